# Optimizing a Trainium2 kernel written in Bass

```python
import jax, jax.numpy as jnp
from jax import lax
import numpy as np

D_MODEL = 1024
BATCH = 2
SEQ = 16384
DEPTH = 4

N_MIXERS = 2
RMS_EPS = 1e-6
LN_EPS = 1e-5

GMLP_EXPAND = 2
GMLP_WIDTH = GMLP_EXPAND * D_MODEL
GMLP_CHUNK = 128
GMLP_GROUPS = 8
GMLP_GROUP_WIDTH = GMLP_WIDTH // GMLP_GROUPS

HEAD_DIM = 64
HEADS_PER_GROUP = D_MODEL // HEAD_DIM
DILATED_GROUPS = ((128, 1), (512, 4), (2048, 16))
N_DIL_GROUPS = len(DILATED_GROUPS)
ATTN_WIDTH = HEADS_PER_GROUP * HEAD_DIM
ATTN_BLOCK = 128
ROPE_THETA = 10000.0

N_A_LAYERS = (DEPTH + 1) // 2
N_B_LAYERS = DEPTH // 2

kernel_name = "hybrid_gmlp_dilated_attn_trunk"


def rms_norm(x, gain):
    xf = x.astype(jnp.float32)
    y = xf * lax.rsqrt(jnp.mean(xf * xf, axis=-1, keepdims=True) + RMS_EPS)
    return (y * gain.astype(jnp.float32)).astype(x.dtype)


def layer_norm(x, gain, bias):
    xf = x.astype(jnp.float32)
    mu = jnp.mean(xf, axis=-1, keepdims=True)
    var = jnp.mean(jnp.square(xf - mu), axis=-1, keepdims=True)
    y = (xf - mu) * lax.rsqrt(var + LN_EPS)
    return (y * gain.astype(jnp.float32) + bias.astype(jnp.float32)).astype(x.dtype)


def rope_tables(seq_len):
    inv_freq = 1.0 / (ROPE_THETA ** (jnp.arange(0, HEAD_DIM, 2, dtype=jnp.float32) / HEAD_DIM))
    ang = jnp.arange(seq_len, dtype=jnp.float32)[:, None] * inv_freq[None, :]
    return jnp.cos(ang), jnp.sin(ang)


def apply_rope(t, cos, sin):
    tf = t.astype(jnp.float32)
    t1, t2 = tf[..., : HEAD_DIM // 2], tf[..., HEAD_DIM // 2:]
    c = cos[None, :, None, :]
    s = sin[None, :, None, :]
    return jnp.concatenate([t1 * c - t2 * s, t2 * c + t1 * s], axis=-1).astype(t.dtype)


def gmlp_mixer(h, w_in, ln_g, ln_b, w_s, b_s, w_out):
    B, S, _ = h.shape
    proj = h @ w_in
    uv = jax.nn.gelu(proj[..., : 2 * GMLP_WIDTH], approximate=False)
    z = proj[..., 2 * GMLP_WIDTH:]
    u, v = uv[..., :GMLP_WIDTH], uv[..., GMLP_WIDTH:]
    v = layer_norm(v, ln_g, ln_b)
    n_chunks = S // GMLP_CHUNK
    vc = v.reshape(B, n_chunks, GMLP_CHUNK, GMLP_GROUPS, GMLP_GROUP_WIDTH)
    causal = jnp.tril(jnp.ones((GMLP_CHUNK, GMLP_CHUNK), dtype=bool))
    w_masked = jnp.where(causal[None], w_s, jnp.zeros_like(w_s))
    sv = jnp.einsum('gts,bnsgc->bntgc', w_masked, vc) + jnp.transpose(b_s)[None, None, :, :, None]
    sv = sv.reshape(B, S, GMLP_WIDTH)
    y = u * sv * jax.nn.silu(z)
    return y @ w_out


def dilated_window_attention(q, k, v, window, dilation):
    B, S, H, Dh = q.shape
    steps = window // dilation
    L = S // dilation
    Lp = ((L + ATTN_BLOCK - 1) // ATTN_BLOCK) * ATTN_BLOCK
    N = Lp // ATTN_BLOCK

    def to_blocks(t):
        t = t.reshape(B, L, dilation, H, Dh).transpose(0, 2, 1, 3, 4)
        t = jnp.pad(t, ((0, 0), (0, 0), (0, Lp - L), (0, 0), (0, 0)))
        return t.reshape(B, dilation, N, ATTN_BLOCK, H, Dh)

    qb, kb, vb = to_blocks(q), to_blocks(k), to_blocks(v)
    pad_prev = ((0, 0), (0, 0), (1, 0), (0, 0), (0, 0), (0, 0))
    kk = jnp.concatenate([jnp.pad(kb, pad_prev)[:, :, :-1], kb], axis=3)
    vv = jnp.concatenate([jnp.pad(vb, pad_prev)[:, :, :-1], vb], axis=3)

    scale = 1.0 / np.sqrt(HEAD_DIM)
    scores = jnp.einsum('brnihk,brnjhk->brnhij', qb.astype(jnp.float32), kk.astype(jnp.float32)) * scale
    i_idx = jnp.arange(ATTN_BLOCK)[:, None]
    j_idx = jnp.arange(2 * ATTN_BLOCK)[None, :]
    n_idx = jnp.arange(N)[:, None, None]
    rel = ATTN_BLOCK + i_idx - j_idx
    valid = (rel >= 0) & (rel <= steps)
    valid = valid[None] & ((n_idx - 1) * ATTN_BLOCK + j_idx[None] >= 0)
    scores = jnp.where(valid[None, None, :, None], scores, -jnp.inf)
    lse = jax.nn.logsumexp(scores, axis=-1)
    p = jnp.exp(scores - lse[..., None])
    o = jnp.einsum('brnhij,brnjhk->brnihk', p.astype(v.dtype), vv)

    o = o.reshape(B, dilation, Lp, H, Dh)[:, :, :L]
    o = o.transpose(0, 2, 1, 3, 4).reshape(B, S, H, Dh)
    lse = lse.transpose(0, 1, 2, 4, 3).reshape(B, dilation, Lp, H)[:, :, :L]
    lse = lse.transpose(0, 2, 1, 3).reshape(B, S, H)
    return o, lse


def dilated_attention_mixer(h, w_in, w_out, cos, sin):
    B, S, _ = h.shape
    proj = h @ w_in
    qkv_width = N_DIL_GROUPS * 3 * ATTN_WIDTH
    qkv = proj[..., :qkv_width].reshape(B, S, N_DIL_GROUPS, 3, HEADS_PER_GROUP, HEAD_DIM)
    z = proj[..., qkv_width:]
    outs, lses = [], []
    for g, (window, dilation) in enumerate(DILATED_GROUPS):
        q = apply_rope(qkv[:, :, g, 0], cos, sin)
        k = apply_rope(qkv[:, :, g, 1], cos, sin)
        o, lse = dilated_window_attention(q, k, qkv[:, :, g, 2], window, dilation)
        outs.append(o)
        lses.append(lse)
    alpha = jax.nn.softmax(jnp.stack(lses, axis=0), axis=0)
    o = jnp.einsum('gbsh,gbshk->bshk', alpha, jnp.stack(outs, axis=0).astype(jnp.float32))
    y = o.astype(h.dtype).reshape(B, S, ATTN_WIDTH) * jax.nn.silu(z)
    return y @ w_out


def setup_inputs(seed: int = 0) -> dict:
    key = jax.random.key(seed)
    ks = jax.random.split(key, 12)
    D = D_MODEL
    x = jax.random.normal(ks[0], (BATCH, SEQ, D), jnp.float32)
    norm_pre = 1.0 + 0.05 * jax.random.normal(ks[1], (DEPTH, D), jnp.float32)
    norm_post = 1.0 + 0.05 * jax.random.normal(ks[2], (DEPTH, D), jnp.float32)
    a_w_in = jax.random.normal(ks[3], (N_A_LAYERS, D, 3 * GMLP_WIDTH), jnp.float32) * D ** -0.5
    a_ln_g = 1.0 + 0.05 * jax.random.normal(ks[4], (N_A_LAYERS, GMLP_WIDTH), jnp.float32)
    a_ln_b = 0.02 * jax.random.normal(ks[5], (N_A_LAYERS, GMLP_WIDTH), jnp.float32)
    a_w_s = jax.random.normal(ks[6], (N_A_LAYERS, GMLP_GROUPS, GMLP_CHUNK, GMLP_CHUNK), jnp.float32) * GMLP_CHUNK ** -0.5
    a_b_s = 1.0 + 0.1 * jax.random.normal(ks[7], (N_A_LAYERS, GMLP_GROUPS, GMLP_CHUNK), jnp.float32)
    a_w_out = jax.random.normal(ks[8], (N_A_LAYERS, GMLP_WIDTH, D), jnp.float32) * GMLP_WIDTH ** -0.5
    b_w_in = jax.random.normal(ks[9], (N_B_LAYERS, D, N_DIL_GROUPS * 3 * ATTN_WIDTH + ATTN_WIDTH), jnp.float32) * D ** -0.5
    b_w_out = jax.random.normal(ks[10], (N_B_LAYERS, ATTN_WIDTH, D), jnp.float32) * ATTN_WIDTH ** -0.5
    return {"x": x, "norm_pre": norm_pre, "norm_post": norm_post,
            "a_w_in": a_w_in, "a_ln_g": a_ln_g, "a_ln_b": a_ln_b, "a_w_s": a_w_s,
            "a_b_s": a_b_s, "a_w_out": a_w_out, "b_w_in": b_w_in, "b_w_out": b_w_out}


def reference(x, norm_pre, norm_post, a_w_in, a_ln_g, a_ln_b, a_w_s, a_b_s, a_w_out, b_w_in, b_w_out):
    cos, sin = rope_tables(x.shape[1])
    for i in range(DEPTH):
        h = rms_norm(x, norm_pre[i])
        j = i // N_MIXERS
        if i % N_MIXERS == 0:
            y = gmlp_mixer(h, a_w_in[j], a_ln_g[j], a_ln_b[j], a_w_s[j], a_b_s[j], a_w_out[j])
        else:
            y = dilated_attention_mixer(h, b_w_in[j], b_w_out[j], cos, sin)
        x = x + rms_norm(y, norm_post[i])
    return x
```

```python
import contextlib
import numpy as np
import ml_dtypes
import concourse.bass as bass
import concourse.mybir as mybir
from concourse.bass_utils import run_bass_kernel_spmd

F32 = mybir.dt.float32
BF16 = mybir.dt.bfloat16
I32 = mybir.dt.int32
AF = mybir.ActivationFunctionType
ALU = mybir.AluOpType

D = 1024
NCORES = 8
TOK = 4096
EW = 2048
RMS_EPS = 1e-6
LN_EPS = 1e-5

ENGS = ("pe", "act", "dve", "pool", "sp")


class _Op:
    __slots__ = ("eng", "fn", "sem", "val", "waits", "dma", "needs_inc")

    def __init__(self, eng, fn, dma):
        self.eng = eng
        self.fn = fn
        self.dma = dma
        self.sem = None
        self.val = 0
        self.waits = []
        self.needs_inc = False


class Sched:
    def __init__(self):
        self.ops = {e: [] for e in ENGS}
        self.last_w = {}
        self.readers = {}
        self.all_ops = []
        self.epoch = 0

    def _deps(self, op, reads, writes):
        deps = []
        raw = set()
        for b in reads:
            w = self.last_w.get(b)
            if w is not None:
                deps.append(w)
                raw.add(id(w))
        for b in writes:
            w = self.last_w.get(b)
            if w is not None:
                deps.append(w)
            deps.extend(self.readers.get(b, {}).values())
        for b in reads:
            self.readers.setdefault(b, {})[op.sem] = op
        for b in writes:
            self.last_w[b] = op
            self.readers[b] = {}
        return deps, raw

    def barrier(self):
        last = {}
        for o in self.all_ops:
            last[o.sem] = o
        self.pending = {e: list(last.values()) for e in ENGS}

    def _pend(self, o, eng):
        pend = getattr(self, "pending", None)
        if pend and pend.get(eng):
            for d in pend[eng]:
                if d.dma is None and d.eng == eng:
                    continue
                d.needs_inc = True
                o.waits.append(d)
            pend[eng] = []

    def op(self, eng, fn, reads=(), writes=()):
        o = _Op(eng, fn, None)
        o.sem = ("eng", eng, self.epoch)
        self._pend(o, eng)
        deps, raw = self._deps(o, reads, writes)
        for d in deps:
            if d is o:
                continue
            if d.dma is None and d.eng == eng:
                if eng == "pe" or id(d) not in raw:
                    continue
            d.needs_inc = True
            o.waits.append(d)
        self.ops[eng].append(o)
        self.all_ops.append(o)
        return o

    def dma(self, queue, stream, fn, reads=(), writes=()):
        o = _Op(queue, fn, stream)
        o.sem = ("dma", stream)
        o.needs_inc = True
        self._pend(o, queue)
        deps, _ = self._deps(o, reads, writes)
        for d in deps:
            if d is o:
                continue
            d.needs_inc = True
            o.waits.append(d)
        self.ops[queue].append(o)
        self.all_ops.append(o)
        return o

    def new_epoch(self):
        self.epoch += 1

    def finalize(self):
        counts = {}
        for o in self.all_ops:
            if o.needs_inc:
                step = 16 if o.dma is not None else 1
                counts[o.sem] = counts.get(o.sem, 0) + step
                o.val = counts[o.sem]
        self.sem_keys = list(counts.keys())
        for k, v in counts.items():
            assert v < 60000, (k, v)
        return counts

    def emit(self, block, sems, final_waits=()):
        handles = {"pe": "tensor", "act": "scalar", "dve": "vector", "pool": "gpsimd", "sp": "sync"}

        def make(engname):
            ops = self.ops[engname]

            def body(eng):
                waited = {}

                def wait(d):
                    if waited.get(d.sem, 0) >= d.val:
                        return
                    eng.wait_ge(sems[d.sem], d.val)
                    waited[d.sem] = d.val

                for o in ops:
                    for d in o.waits:
                        wait(d)
                    ins = o.fn(eng)
                    if o.needs_inc:
                        ins.then_inc(sems[o.sem], 16 if o.dma is not None else 1)
                if engname == "sp":
                    for d in final_waits:
                        wait(d)
            return body

        for engname in ENGS:
            getattr(block, handles[engname])(make(engname))


class Prog:
    def __init__(self):
        self.nc = bass.Bass("TRN2", target_bir_lowering=False)
        self.S = Sched()
        self.cap = 212000
        self.arena = self.nc.alloc_sbuf_tensor("arena", [128, self.cap // 2], BF16)[:]
        self.off = 0
        self.uid = 0
        self.ps_f32 = [self.nc.alloc_psum_tensor("psb%d" % i, [128, 512], F32)[:] for i in range(8)]
        self.ps_b16 = [p.bitcast(BF16) for p in self.ps_f32]
        self.final = []

    def inp(self, name, shape, dtype=F32):
        return self.nc.dram_tensor(name, list(shape), dtype, kind="ExternalInput").ap()

    def outp(self, name, shape, dtype=F32):
        return self.nc.dram_tensor(name, list(shape), dtype, kind="ExternalOutput").ap()

    def carve(self, cols, dtype):
        esz = 4 if dtype in (F32, I32) else 2
        nb = (cols * esz + 63) // 64 * 64
        assert self.off + nb <= self.cap, ("SBUF arena overflow", self.off, nb)
        a = self.arena[:, self.off // 2:(self.off + nb) // 2]
        self.off += nb
        if dtype != BF16:
            a = a.bitcast(dtype)
        return a[:, 0:cols]

    def key(self, base):
        self.uid += 1
        return "%s#%d" % (base, self.uid)

    def rsqrt(self, a_key, a, out_key, out, tmp_key, tmp):
        S = self.S
        S.op("dve", lambda e: e.tensor_scalar(out=out.bitcast(I32), in0=a.bitcast(I32), scalar1=1, scalar2=None,
                                              op0=ALU.arith_shift_right), reads=[a_key], writes=[out_key])
        S.op("dve", lambda e: e.tensor_scalar(out=out.bitcast(I32), in0=out.bitcast(I32), scalar1=0x5f3759df,
                                              scalar2=-1, op0=ALU.subtract, op1=ALU.mult),
             reads=[out_key], writes=[out_key])
        for _ in range(3):
            S.op("dve", lambda e: e.scalar_tensor_tensor(out=tmp, in0=out, scalar=-0.5, in1=out, op0=ALU.mult,
                                                         op1=ALU.mult), reads=[out_key], writes=[tmp_key])
            S.op("dve", lambda e: e.tensor_tensor(out=tmp, in0=tmp, in1=a, op=ALU.mult),
                 reads=[tmp_key, a_key], writes=[tmp_key])
            S.op("dve", lambda e: e.scalar_tensor_tensor(out=out, in0=tmp, scalar=1.5, in1=out, op0=ALU.add,
                                                         op1=ALU.mult), reads=[tmp_key, out_key], writes=[out_key])

    def layer_a(self, x_src, x_dst, ntok, w_in, npre_col, ln_g, ln_b, w_s, bs_col, w_out, npost, ident_d, tril_d):
        S, nc = self.S, self.nc
        off0 = self.off
        NT = ntok // 128
        L = self.key("A")
        K = lambda s: "%s/%s" % (L, s)

        Wb = [self.carve(6144, BF16) for _ in range(8)]
        Wo = [self.carve(1024, BF16) for _ in range(16)]
        lng = self.carve(2048, BF16)
        lnb = self.carve(2048, BF16)
        gpo = self.carve(1024, F32)
        WmT = self.carve(1024, BF16)
        bsc = self.carve(8, F32)
        gpre = self.carve(8, F32)
        idt = self.carve(128, BF16)
        tril = self.carve(128, F32)
        xt = [self.carve(1024, F32) for _ in range(3)]
        xb = self.carve(1024, BF16)
        xT = [self.carve(1024, BF16) for _ in range(2)]
        junk = self.carve(1024, BF16)
        u = self.carve(2048, BF16)
        sz = self.carve(2048, BF16)
        v = self.carve(2048, F32)
        vn = [self.carve(2048, BF16) for _ in range(2)]
        y = self.carve(2048, BF16)
        yT = self.carve(2048, BF16)
        tpost = self.carve(1024, F32)
        st = self.carve(64, F32)
        ss, ms, tmp1 = st[:, 0:1], st[:, 1:2], st[:, 3:4]
        rstd = [st[:, 4:5], st[:, 5:6]]
        bst = st[:, 8:32]
        mv = st[:, 32:34]
        va, rsv, tmp2 = st[:, 34:35], st[:, 35:36], st[:, 36:37]
        ss2, a2, rstd2, tmp3 = st[:, 40:42], st[:, 42:43], st[:, 43:44], st[:, 44:45]
        stage = [v[:, 0:1024], v[:, 1024:2048]]
        VK = [K("v0"), K("v1")]
        P_TX, P_IN, P_SV, P_TY, P_O = 0, (1, 2), (3, 4), 5, (6, 7)
        psf, psb = self.ps_f32, self.ps_b16

        S.dma("sp", "idt", lambda e: e.dma_start(out=idt, in_=ident_d), writes=[K("idt")])
        S.dma("sp", "tril", lambda e: e.dma_start(out=tril, in_=tril_d), writes=[K("tril")])
        S.dma("sp", "gpre", lambda e: e.dma_start(out=gpre, in_=npre_col), writes=[K("gpre")])
        S.dma("sp", "bsc", lambda e: e.dma_start(out=bsc, in_=bs_col), writes=[K("bsc")])
        S.dma("sp", "gpo", lambda e: e.dma_start(out=gpo, in_=npost.partition_broadcast(128)), writes=[K("gpo")])
        cnt = [0]

        def staged(src_ap, cols, consume, view=None):
            i = cnt[0] % 2
            cnt[0] += 1
            sb = stage[i][:, 0:cols]
            sbd = view(sb) if view is not None else sb
            S.dma("sp", "stg%d" % i, lambda e: e.dma_start(out=sbd, in_=src_ap), writes=[VK[i]])
            consume(sb, "act" if i == 0 else "dve", VK[i])

        for k in range(8):
            for q in range(6):
                dst = Wb[k][:, q * 1024:(q + 1) * 1024]

                def cons(sb, eng, skey, dst=dst, k=k):
                    if eng == "act":
                        S.op("act", lambda e: e.activation(out=dst, in_=sb, func=AF.Copy, scale=gpre[:, k:k + 1]),
                             reads=[skey, K("gpre")], writes=[K("Wb%d" % k)])
                    else:
                        S.op("dve", lambda e: e.tensor_scalar(out=dst, in0=sb, scalar1=gpre[:, k:k + 1], scalar2=None,
                                                              op0=ALU.mult), reads=[skey, K("gpre")], writes=[K("Wb%d" % k)])
                staged(w_in[k * 128:(k + 1) * 128, q * 1024:(q + 1) * 1024], 1024, cons)
        for k in range(16):
            def cons(sb, eng, skey, k=k):
                if eng == "act":
                    S.op("act", lambda e: e.activation(out=Wo[k], in_=sb, func=AF.Copy), reads=[skey], writes=[K("Wo")])
                else:
                    S.op("dve", lambda e: e.tensor_copy(out=Wo[k], in_=sb), reads=[skey], writes=[K("Wo")])
            staged(w_out[k * 128:(k + 1) * 128, :], 1024, cons)
        for (src, dstt, nm) in ((ln_g, lng, "lng"), (ln_b, lnb, "lnb")):
            for h in range(2):
                def cons(sb, eng, skey, dstt=dstt, h=h, nm=nm):
                    if eng == "act":
                        S.op("act", lambda e: e.activation(out=dstt[:, h * 1024:(h + 1) * 1024], in_=sb, func=AF.Copy),
                             reads=[skey], writes=[K(nm)])
                    else:
                        S.op("dve", lambda e: e.tensor_copy(out=dstt[:, h * 1024:(h + 1) * 1024], in_=sb),
                             reads=[skey], writes=[K(nm)])
                staged(src[:, h * 1024:(h + 1) * 1024].partition_broadcast(128), 1024, cons)

        def cons_ws(sb, eng, skey):
            sbv = sb.rearrange("p (g s) -> p g s", g=8)
            yv = y[:, 0:1024].rearrange("p (g s) -> p g s", g=8)
            for g in range(8):
                S.op("dve", lambda e, g=g: e.tensor_tensor(out=yv[:, g, :], in0=sbv[:, g, :], in1=tril, op=ALU.mult),
                     reads=[skey, K("tril")], writes=[K("y")])
            for g in range(8):
                S.op("pe", lambda e, g=g: e.transpose(out=psb[P_TX][:, g * 128:(g + 1) * 128], in_=yv[:, g, :], identity=idt),
                     reads=[K("y"), K("idt")], writes=[K("ps%d" % P_TX)])
            S.op("dve", lambda e: e.tensor_copy(out=WmT, in_=psb[P_TX]), reads=[K("ps%d" % P_TX)], writes=[K("WmT")])
        staged(w_s.rearrange("g t s -> t g s"), 1024, cons_ws, view=lambda sb: sb.rearrange("p (g s) -> p g s", g=8))

        def load(i):
            p = i % 3
            S.dma("sp", "xl%d" % p, lambda e: e.dma_start(out=xt[p], in_=x_src[i * 128:(i + 1) * 128, :]),
                  writes=[K("xt%d" % p)])

        def inproj(q, banks, r0):
            r = r0
            for n in banks:
                b = P_IN[r % 2]
                r += 1
                for k in range(8):
                    S.op("pe", lambda e, k=k, n=n, b=b: e.matmul(psf[b], lhsT=xT[q][:, k * 128:(k + 1) * 128],
                                                                rhs=Wb[k][:, n * 512:(n + 1) * 512],
                                                                start=(k == 0), stop=(k == 7)),
                         reads=[K("xT%d" % q), K("Wb%d" % k)], writes=[K("ps%d" % b)])
                if n < 4:
                    dst, func, dk = u[:, n * 512:(n + 1) * 512], AF.Gelu, [K("u")]
                elif n < 8:
                    dst, func, dk = v[:, (n - 4) * 512:(n - 3) * 512], AF.Gelu, VK
                else:
                    dst, func, dk = sz[:, (n - 8) * 512:(n - 7) * 512], AF.Silu, [K("sz")]
                S.op("act", lambda e, dst=dst, func=func, b=b: e.activation(out=dst, in_=psf[b], func=func, scale=rstd[q]),
                     reads=[K("ps%d" % b), K("rstd%d" % q)], writes=dk)

        def pre(i):
            p, q = i % 3, i % 2
            xk = K("xt%d" % p)
            S.op("act", lambda e: e.activation(out=junk, in_=xt[p], func=AF.Square, accum_out=ss),
                 reads=[xk], writes=[K("junk"), K("ss")])
            S.op("dve", lambda e: e.tensor_scalar(out=ms, in0=ss, scalar1=1.0 / D, scalar2=RMS_EPS, op0=ALU.mult,
                                                  op1=ALU.add), reads=[K("ss")], writes=[K("ms")])
            self.rsqrt(K("ms"), ms, K("rstd%d" % q), rstd[q], K("tmp1"), tmp1)
            S.op("pool", lambda e: e.tensor_copy(out=xb, in_=xt[p]), reads=[xk], writes=[K("xb")])

        def stage1(i):
            q = i % 2
            for k in range(8):
                S.op("pe", lambda e, k=k: e.transpose(out=psb[P_TX][:, k * 128:(k + 1) * 128],
                                                      in_=xb[:, k * 128:(k + 1) * 128], identity=idt),
                     reads=[K("xb"), K("idt")], writes=[K("ps%d" % P_TX)])
            S.op("act", lambda e: e.activation(out=xT[q], in_=psb[P_TX], func=AF.Copy),
                 reads=[K("ps%d" % P_TX)], writes=[K("xT%d" % q)])
            inproj(q, (4, 5, 6, 7), 0)
            for c in range(4):
                S.op("dve", lambda e, c=c: e.bn_stats(out=bst[:, c * 6:(c + 1) * 6], in_=v[:, c * 512:(c + 1) * 512]),
                     reads=VK, writes=[K("bst")])
            S.op("dve", lambda e: e.bn_aggr(out=mv, in_=bst.rearrange("p (c s) -> p c s", c=4)),
                 reads=[K("bst")], writes=[K("mv")])
            S.op("dve", lambda e: e.tensor_scalar(out=va, in0=mv[:, 1:2], scalar1=LN_EPS, scalar2=None,
                                                  op0=ALU.add), reads=[K("mv")], writes=[K("va")])
            self.rsqrt(K("va"), va, K("rsv"), rsv, K("tmp2"), tmp2)
            S.op("dve", lambda e: e.scalar_tensor_tensor(out=v, in0=v, scalar=mv[:, 0:1], in1=lng,
                                                         op0=ALU.subtract, op1=ALU.mult),
                 reads=VK + [K("mv"), K("lng")], writes=VK)
            S.op("dve", lambda e: e.scalar_tensor_tensor(out=vn[q], in0=v, scalar=rsv, in1=lnb,
                                                         op0=ALU.mult, op1=ALU.add),
                 reads=VK + [K("rsv"), K("lnb")], writes=[K("vn%d" % q)])

        def stage2a(i):
            q = i % 2
            inproj(q, (0, 1, 2, 3, 8, 9, 10, 11), 0)
            for qd in range(4):
                b = P_SV[qd % 2]
                for gg in range(2):
                    g = 2 * qd + gg
                    S.op("pe", lambda e, g=g, gg=gg, b=b: e.matmul(psf[b][:, gg * 256:(gg + 1) * 256],
                                                                  lhsT=WmT[:, g * 128:(g + 1) * 128],
                                                                  rhs=vn[q][:, g * 256:(g + 1) * 256], start=True, stop=True),
                         reads=[K("vn%d" % q), K("WmT")], writes=[K("ps%d" % b)])
                for gg in range(2):
                    g = 2 * qd + gg
                    S.op("dve", lambda e, g=g, gg=gg, b=b: e.scalar_tensor_tensor(
                        out=y[:, g * 256:(g + 1) * 256], in0=psf[b][:, gg * 256:(gg + 1) * 256], scalar=bsc[:, g:g + 1],
                        in1=u[:, g * 256:(g + 1) * 256], op0=ALU.add, op1=ALU.mult),
                        reads=[K("ps%d" % b), K("bsc"), K("u")], writes=[K("y")])
            S.op("dve", lambda e: e.tensor_tensor(out=y, in0=y, in1=sz, op=ALU.mult), reads=[K("y"), K("sz")], writes=[K("y")])

        def stage2b(i):
            p = i % 3
            xk = K("xt%d" % p)
            for h in range(2):
                for k in range(8):
                    kk = h * 8 + k
                    S.op("pe", lambda e, k=k, kk=kk: e.transpose(out=psb[P_TY][:, k * 128:(k + 1) * 128],
                                                                in_=y[:, kk * 128:(kk + 1) * 128], identity=idt),
                         reads=[K("y"), K("idt")], writes=[K("ps%d" % P_TY)])
                S.op("act", lambda e, h=h: e.activation(out=yT[:, h * 1024:(h + 1) * 1024], in_=psb[P_TY], func=AF.Copy),
                     reads=[K("ps%d" % P_TY)], writes=[K("yT")])
            for n in range(2):
                b = P_O[n]
                for k in range(16):
                    S.op("pe", lambda e, k=k, n=n, b=b: e.matmul(psf[b], lhsT=yT[:, k * 128:(k + 1) * 128],
                                                                rhs=Wo[k][:, n * 512:(n + 1) * 512],
                                                                start=(k == 0), stop=(k == 15)),
                         reads=[K("yT"), K("Wo")], writes=[K("ps%d" % b)])
                S.op("act", lambda e, n=n, b=b: e.activation(out=junk[:, 0:512], in_=psf[b], func=AF.Square,
                                                             accum_out=ss2[:, n:n + 1]),
                     reads=[K("ps%d" % b)], writes=[K("junk"), K("ss2")])
            S.op("dve", lambda e: e.tensor_tensor(out=a2, in0=ss2[:, 0:1], in1=ss2[:, 1:2], op=ALU.add),
                 reads=[K("ss2")], writes=[K("a2")])
            S.op("dve", lambda e: e.tensor_scalar(out=a2, in0=a2, scalar1=1.0 / D, scalar2=RMS_EPS, op0=ALU.mult,
                                                  op1=ALU.add), reads=[K("a2")], writes=[K("a2")])
            self.rsqrt(K("a2"), a2, K("rstd2"), rstd2, K("tmp3"), tmp3)
            for n in range(2):
                b = P_O[n]
                S.op("dve", lambda e, n=n, b=b: e.scalar_tensor_tensor(out=tpost[:, n * 512:(n + 1) * 512], in0=psf[b],
                                                                       scalar=rstd2, in1=gpo[:, n * 512:(n + 1) * 512],
                                                                       op0=ALU.mult, op1=ALU.mult),
                     reads=[K("ps%d" % b), K("rstd2"), K("gpo")], writes=[K("tpost")])
            S.op("pool", lambda e: e.tensor_tensor(out=xt[p], in0=xt[p], in1=tpost, op=ALU.add),
                 reads=[xk, K("tpost")], writes=[xk])
            o = S.dma("pool", "xs%d" % p, lambda e: e.dma_start(out=x_dst[i * 128:(i + 1) * 128, :], in_=xt[p]),
                      reads=[xk], writes=[K("xdst")])
            self.final.append(o)

        load(0)
        if NT > 1:
            load(1)
        pre(0)
        stage1(0)
        for i in range(NT):
            if i + 2 < NT:
                load(i + 2)
            if i + 1 < NT:
                pre(i + 1)
            stage2a(i)
            if i + 1 < NT:
                stage1(i + 1)
            stage2b(i)
        S.barrier()
        self.peak = max(getattr(self, "peak", 0), self.off)
        self.off = off0


    def layer_b(self, x_src, x_halo, x_dst, ysc, w_in, gpre_d, w_out, npost, hv_d, tabs, ident_d, rm_d, mask_d,
                NOWN=4096, dbg_hps=range(8), dbg_gs=range(3), dbg_attn=True):
        S, nc = self.S, self.nc
        off0 = self.off
        L = self.key("B")
        K = lambda s: "%s/%s" % (L, s)
        psf, psb = self.ps_f32, self.ps_b16
        HAL = 2048
        NT = HAL + NOWN
        DIL = (1, 4, 16)
        P_PR, P_RT, P_S2, P_ND, P_V = (0, 1), 2, ((3, 4), (0, 1)), (5, 6), 7

        hT = self.carve(8 * NT, BF16).rearrange("p (k t) -> p k t", k=8)
        offq = self.off
        QT = self.carve(NOWN, BF16)
        KT = self.carve(NT, BF16)
        Vb = self.carve(NT, BF16)
        acc = self.carve(2 * NOWN, F32).rearrange("p (a t) -> p a t", a=2)
        wst = self.carve(3 * 1024, F32)
        wb = [self.carve(3 * 1024, BF16) for _ in range(2)]
        idt = self.carve(128, BF16)
        rm = self.carve(128, BF16)
        maskT = self.carve(512, BF16)
        ones = self.carve(64, BF16)
        hvones = self.carve(64, BF16)
        loones = self.carve(64, BF16)
        hv2 = self.carve(2, F32)
        hv, lo = hv2[:, 0:1], hv2[:, 1:2]
        PT = [self.carve(512, BF16) for _ in range(4)]
        qraw = [self.carve(512, BF16) for _ in range(2)]
        rt1 = [self.carve(512, BF16) for _ in range(2)]
        ctab = [self.carve(512, BF16) for _ in range(2)]
        stab = [self.carve(512, BF16) for _ in range(2)]
        st = self.carve(64, F32)
        print("layer B persistent SBUF bytes", self.off - off0)
        offp = self.off

        S.dma("sp", "idt", lambda e: e.dma_start(out=idt, in_=ident_d), writes=[K("idt")])
        S.dma("sp", "rm", lambda e: e.dma_start(out=rm, in_=rm_d), writes=[K("rm")])
        S.dma("sp", "mask", lambda e: e.dma_start(out=maskT, in_=mask_d), writes=[K("mask")])
        S.dma("sp", "hv", lambda e: e.dma_start(out=hv2, in_=hv_d), writes=[K("hv")])
        S.op("pool", lambda e: e.memset(ones, 1.0), writes=[K("ones")])
        S.op("pool", lambda e: e.memset(hvones, 1.0), writes=[K("hvones")])
        S.op("pool", lambda e: e.tensor_scalar(out=hvones, in0=hvones, scalar1=hv, scalar2=None, op0=ALU.mult),
             reads=[K("hv"), K("hvones")], writes=[K("hvones")])
        S.op("pool", lambda e: e.memset(loones, 1.0), writes=[K("loones")])
        S.op("pool", lambda e: e.tensor_scalar(out=loones, in0=loones, scalar1=lo, scalar2=None, op0=ALU.mult),
             reads=[K("hv"), K("loones")], writes=[K("loones")])

        self.off = offq
        gb = self.carve(1024, F32)
        xt = [self.carve(1024, F32) for _ in range(2)]
        hb = [self.carve(1024, BF16) for _ in range(2)]
        junk = self.carve(1024, BF16)
        ss, ms, rstd, tmp1 = st[:, 0:1], st[:, 1:2], st[:, 2:3], st[:, 3:4]
        S.dma("sp", "gb", lambda e: e.dma_start(out=gb, in_=gpre_d.partition_broadcast(128)), writes=[K("gb")])

        def xrows(i):
            return x_halo[i * 128:(i + 1) * 128, :] if i < 16 else x_src[(i - 16) * 128:(i - 15) * 128, :]

        def p1_load(i):
            p = i % 2
            S.dma("sp", "x1l%d" % p, lambda e: e.dma_start(out=xt[p], in_=xrows(i)), writes=[K("x1t%d" % p)])

        NT1 = NT // 128
        p1_load(0)
        for i in range(NT1):
            p = i % 2
            if i + 1 < NT1:
                p1_load(i + 1)
            S.op("act", lambda e, p=p: e.activation(out=junk, in_=xt[p], func=AF.Square, accum_out=ss),
                 reads=[K("x1t%d" % p)], writes=[K("junk"), K("ss")])
            S.op("dve", lambda e: e.tensor_scalar(out=ms, in0=ss, scalar1=1.0 / D, scalar2=RMS_EPS, op0=ALU.mult,
                                                  op1=ALU.add), reads=[K("ss")], writes=[K("ms")])
            self.rsqrt(K("ms"), ms, K("rstd"), rstd, K("tmp1"), tmp1)
            S.op("dve", lambda e, p=p: e.scalar_tensor_tensor(out=hb[p], in0=xt[p], scalar=rstd, in1=gb, op0=ALU.mult,
                                                              op1=ALU.mult),
                 reads=[K("x1t%d" % p), K("rstd"), K("gb")], writes=[K("hb%d" % p)])
            for k in range(8):
                S.op("pe", lambda e, k=k, p=p: e.transpose(out=psb[P_PR[p]][:, k * 128:(k + 1) * 128],
                                                          in_=hb[p][:, k * 128:(k + 1) * 128], identity=idt),
                     reads=[K("hb%d" % p), K("idt")], writes=[K("ps%d" % P_PR[p])])
            S.op("act", lambda e, i=i, p=p: e.activation(out=hT[:, :, i * 128:(i + 1) * 128],
                                                         in_=psb[P_PR[p]].rearrange("p (k t) -> p k t", k=8), func=AF.Copy),
                 reads=[K("ps%d" % P_PR[p])], writes=[K("hT")])
        S.barrier()

        w5 = w_in[:, 0:9216].rearrange("(k p) (g t h f) -> p k g t h f", p=128, g=3, t=3, h=8)
        wz = w_in[:, 9216:10240].rearrange("(k p) (h f) -> p k h f", p=128, h=8)
        tabi = [0]
        bankc = [0]
        wcnt = [0]

        def load_w(src_ap, ncols):
            slot = wcnt[0] % 2
            wcnt[0] += 1
            dst32 = wst[:, 0:8 * ncols]
            if len(src_ap.shape) == 3:
                S.dma("sp", "wst", lambda e: e.dma_start(out=dst32.rearrange("p (k c) -> p k c", k=8), in_=src_ap),
                      writes=[K("wst")])
            else:
                d4 = dst32.rearrange("p (k t f) -> p k t f", k=8, t=3)
                for t in range(3):
                    S.dma("sp", "wst", lambda e, t=t: e.dma_start(out=d4[:, :, t, :], in_=src_ap[:, :, t, :]),
                          writes=[K("wst")])
            S.op("pool", lambda e: e.tensor_copy(out=wb[slot][:, 0:8 * ncols], in_=dst32), reads=[K("wst")],
                 writes=[K("wb%d" % slot)])
            return wb[slot][:, 0:8 * ncols].rearrange("p (k c) -> p k c", k=8), K("wb%d" % slot)

        def proj_bank(wv, wkey, c0, tau0, n):
            b = P_PR[bankc[0] % 2]
            bankc[0] += 1
            for k in range(8):
                S.op("pe", lambda e, k=k, b=b: e.matmul(psf[b][:, 0:n], lhsT=wv[:, k, c0:c0 + 128], rhs=hT[:, k, tau0:tau0 + n],
                                                       start=(k == 0), stop=(k == 7)),
                     reads=[wkey, K("hT")], writes=[K("ps%d" % b)])
            return b

        def rope_bank(b, n, d, dest, dkey, ctd, std, t0):
            j = tabi[0] % 2
            tabi[0] += 1
            S.dma("sp", "ct%d" % j, lambda e: e.dma_start(out=ctab[j][:, 0:n], in_=ctd[:, t0:t0 + n]), writes=[K("ctab%d" % j)])
            S.dma("sp", "st%d" % j, lambda e: e.dma_start(out=stab[j][:, 0:n], in_=std[:, t0:t0 + n]), writes=[K("stab%d" % j)])
            if d == 1 or n < 512:
                ov, iv = qraw[j][:, 0:n], psf[b][:, 0:n]
            else:
                ov = qraw[j].rearrange("p (r i) -> p r i", r=d)
                iv = psf[b].rearrange("p (i r) -> p r i", r=d)
            S.op("act", lambda e: e.activation(out=ov, in_=iv, func=AF.Copy), reads=[K("ps%d" % b)], writes=[K("qraw%d" % j)])
            S.op("pe", lambda e: e.matmul(psf[P_RT][:, 0:n], lhsT=rm, rhs=qraw[j][:, 0:n], start=True, stop=True),
                 reads=[K("rm"), K("qraw%d" % j)], writes=[K("ps%d" % P_RT)])
            S.op("dve", lambda e: e.tensor_tensor(out=rt1[j][:, 0:n], in0=psf[P_RT][:, 0:n], in1=stab[j][:, 0:n], op=ALU.mult),
                 reads=[K("ps%d" % P_RT), K("stab%d" % j)], writes=[K("rt1%d" % j)])
            S.op("pool", lambda e: e.tensor_tensor(out=qraw[j][:, 0:n], in0=qraw[j][:, 0:n], in1=ctab[j][:, 0:n], op=ALU.mult),
                 reads=[K("qraw%d" % j), K("ctab%d" % j)], writes=[K("qraw%d" % j)])
            if d == 16:
                a0 = rt1[j].rearrange("p (r i) -> p r i", r=16)
                a1 = qraw[j].rearrange("p (r i) -> p r i", r=16)
            else:
                a0, a1 = rt1[j][:, 0:n], qraw[j][:, 0:n]
            S.op("dve", lambda e: e.tensor_tensor(out=dest, in0=a0, in1=a1, op=ALU.add),
                 reads=[K("rt1%d" % j), K("qraw%d" % j)], writes=[dkey])

        ptc = [0]
        sc = [0]
        ndc = [0]
        for hp in dbg_hps:
            S.op("pool", lambda e: e.memset(acc.rearrange("p a t -> p (a t)"), 0.0), writes=[K("acc")])
            for g in dbg_gs:
                d = DIL[g]
                span = 128 * d
                nsp = NOWN // span
                nkb = (nsp + 1) * d
                kt0 = HAL - span
                nk = NOWN + span
                cq, sq, ck, sk = tabs[g]
                wv, wkey = load_w(w5[:, :, g, :, hp, :], 384)
                for bq in range(NOWN // 512):
                    b = proj_bank(wv, wkey, 0, HAL + bq * 512, 512)
                    if d == 16:
                        sp_, i0 = (bq * 512) // span, ((bq * 512) % span) // 16
                        dest = QT[:, sp_ * span:(sp_ + 1) * span].rearrange("p (r i) -> p r i", r=16)[:, :, i0:i0 + 32]
                        rope_bank(b, 512, d, dest, K("QT"), cq, sq, bq * 512)
                    else:
                        rope_bank(b, 512, d, QT[:, bq * 512:(bq + 1) * 512], K("QT"), cq, sq, bq * 512)
                nb_full, rem = nk // 512, nk % 512
                for bk in range(nb_full + (1 if rem else 0)):
                    n = 512 if bk < nb_full else rem
                    b = proj_bank(wv, wkey, 128, kt0 + bk * 512, n)
                    if d == 16:
                        sp_, i0 = (bk * 512) // span, ((bk * 512) % span) // 16
                        dest = KT[:, sp_ * span:(sp_ + 1) * span].rearrange("p (r i) -> p r i", r=16)[:, :, i0:i0 + 32]
                        rope_bank(b, 512, d, dest, K("KT"), ck, sk, bk * 512)
                    else:
                        rope_bank(b, n, d, KT[:, bk * 512:bk * 512 + n], K("KT"), ck, sk, bk * 512)
                for kb4 in range(0, nkb, 4):
                    for q4 in range(4):
                        kb = kb4 + q4
                        if kb >= nkb:
                            break
                        sp_, r = kb // d, kb % d
                        t_start = kt0 + sp_ * span + r
                        for k in range(8):
                            S.op("pe", lambda e, k=k, q4=q4, t_start=t_start, d=d, wv=wv: e.matmul(
                                psf[P_V][:, q4 * 128:(q4 + 1) * 128], lhsT=hT[:, k, t_start:t_start + 127 * d + 1:d],
                                rhs=wv[:, k, 256:384], start=(k == 0), stop=(k == 7)),
                                reads=[wkey, K("hT")], writes=[K("ps%d" % P_V)])
                    nblk = min(4, nkb - kb4)

                    def cls(kb):
                        return 0 if kb < d else (1 if kb < d + 16 else 2)
                    r0 = 0
                    while r0 < nblk:
                        r1 = r0
                        while r1 < nblk and cls(kb4 + r1) == cls(kb4 + r0):
                            r1 += 1
                        c = cls(kb4 + r0)
                        if c == 2:
                            S.op("act", lambda e, kb4=kb4, r0=r0, r1=r1: e.activation(
                                out=Vb[:, (kb4 + r0) * 128:(kb4 + r1) * 128], in_=psf[P_V][:, r0 * 128:r1 * 128], func=AF.Copy),
                                reads=[K("ps%d" % P_V)], writes=[K("Vb")])
                        else:
                            fl = hv if c == 0 else lo
                            S.op("act", lambda e, kb4=kb4, r0=r0, r1=r1, fl=fl: e.activation(
                                out=Vb[:, (kb4 + r0) * 128:(kb4 + r1) * 128], in_=psf[P_V][:, r0 * 128:r1 * 128], func=AF.Copy,
                                scale=fl), reads=[K("ps%d" % P_V), K("hv")], writes=[K("Vb")])
                        r0 = r1
                nqb = nsp * d

                def s_pair(qb0):
                    ba, bb = P_S2[sc[0] % 2]
                    sc[0] += 1
                    for blk in range(2):
                        qb = qb0 + blk
                        for kbi in range(2):
                            kb = qb + kbi * d
                            for hh, b in ((0, ba), (1, bb)):
                                S.op("pe", lambda e, hh=hh, kbi=kbi, kb=kb, b=b, blk=blk, qb=qb: e.matmul(
                                    psf[b][:, (blk * 2 + kbi) * 128:(blk * 2 + kbi + 1) * 128],
                                    lhsT=KT[hh * 64:(hh + 1) * 64, kb * 128:(kb + 1) * 128],
                                    rhs=QT[hh * 64:(hh + 1) * 64, qb * 128:(qb + 1) * 128], start=True, stop=True,
                                    tile_position=(hh * 64, 0)),
                                    reads=[K("KT"), K("QT")], writes=[K("ps%d" % b)])
                    return ba, bb

                def e_pair(banks):
                    js = []
                    for b in banks:
                        j = ptc[0] % 4
                        ptc[0] += 1
                        S.op("act", lambda e, b=b, j=j: e.activation(out=PT[j], in_=psf[b], func=AF.Exp, scale=0.125),
                             reads=[K("ps%d" % b)], writes=[K("PT%d" % j)])
                        S.op("pool", lambda e, j=j: e.tensor_tensor(out=PT[j], in0=PT[j], in1=maskT, op=ALU.mult),
                             reads=[K("PT%d" % j), K("mask")], writes=[K("PT%d" % j)])
                        js.append(j)
                    return js

                def pv_pair(qb0, js, bnd):
                    for blk in range(2):
                        qb = qb0 + blk
                        for hh in range(2):
                            j = js[hh]
                            for what in range(2):
                                for kbi in range(2):
                                    kb = qb + kbi * d
                                    if what == 0:
                                        lhsT, lk = Vb[:, kb * 128 + hh * 64:kb * 128 + (hh + 1) * 64], K("Vb")
                                    elif kb < d:
                                        lhsT, lk = hvones, K("hvones")
                                    elif kb < d + 16:
                                        lhsT, lk = loones, K("loones")
                                    else:
                                        lhsT, lk = ones, K("ones")
                                    S.op("pe", lambda e, hh=hh, what=what, kbi=kbi, lhsT=lhsT, blk=blk, j=j: e.matmul(
                                        psf[bnd][hh * 64:(hh + 1) * 64, (what * 2 + blk) * 128:(what * 2 + blk + 1) * 128],
                                        lhsT=lhsT, rhs=PT[j][:, (blk * 2 + kbi) * 128:(blk * 2 + kbi + 1) * 128],
                                        start=(kbi == 0), stop=(kbi == 1), tile_position=(0, hh * 64)),
                                        reads=[lk, K("PT%d" % j)], writes=[K("ps%d" % bnd)])

                def acc_pair(qb0, bnd):
                    sp_, r = qb0 // d, qb0 % d
                    if d == 1:
                        av = acc[:, :, qb0 * 128:(qb0 + 2) * 128]
                        pv = psf[bnd].rearrange("p (a t) -> p a t", a=2)
                    else:
                        av = acc[:, :, sp_ * span:(sp_ + 1) * span].rearrange("p a (i r) -> p a r i", r=d)[:, :, r:r + 2, :]
                        pv = psf[bnd].rearrange("p (a s i) -> p a s i", a=2, s=2)
                    S.op("dve", lambda e: e.tensor_tensor(out=av, in0=av, in1=pv, op=ALU.add),
                         reads=[K("acc"), K("ps%d" % bnd)], writes=[K("acc")])

                if not dbg_attn:
                    continue
                npair = nqb // 2
                sb = {0: s_pair(0)}
                for pi in range(npair):
                    if pi + 1 < npair:
                        sb[pi + 1] = s_pair(2 * (pi + 1))
                    js = e_pair(sb.pop(pi))
                    bnd = P_ND[ndc[0] % 2]
                    ndc[0] += 1
                    pv_pair(2 * pi, js, bnd)
                    acc_pair(2 * pi, bnd)
            wvz, wzkey = load_w(wz[:, :, hp, :], 128)
            zs = KT[:, 0:NOWN]
            for bq in range(NOWN // 512):
                b = proj_bank(wvz, wzkey, 0, HAL + bq * 512, 512)
                S.op("act", lambda e, b=b, bq=bq: e.activation(out=zs[:, bq * 512:(bq + 1) * 512], in_=psf[b], func=AF.Silu),
                     reads=[K("ps%d" % b)], writes=[K("KT")])
            S.op("dve", lambda e: e.tensor_scalar(out=acc[:, 1, :], in0=acc[:, 1, :], scalar1=1e-30, scalar2=None, op0=ALU.max),
                 reads=[K("acc")], writes=[K("acc")])
            S.op("dve", lambda e: e.reciprocal(out=acc[:, 1, :], in_=acc[:, 1, :]), reads=[K("acc")], writes=[K("acc")])
            S.op("dve", lambda e: e.tensor_tensor(out=acc[:, 0, :], in0=acc[:, 0, :], in1=acc[:, 1, :], op=ALU.mult),
                 reads=[K("acc")], writes=[K("acc")])
            S.op("dve", lambda e: e.tensor_tensor(out=QT, in0=acc[:, 0, :], in1=zs, op=ALU.mult),
                 reads=[K("acc"), K("KT")], writes=[K("QT")])
            S.dma("pool", "ysc", lambda e, hp=hp: e.dma_start(out=ysc[hp], in_=QT), reads=[K("QT")], writes=[K("ysc")])
        S.barrier()

        self.off = off0
        Wo = self.carve(8 * 1024, BF16).rearrange("p (k c) -> p k c", k=8)
        wos = self.carve(1024, F32)
        gpo = self.carve(1024, F32)
        yt = [self.carve(8 * 512, BF16).rearrange("p (h t) -> p h t", h=8) for _ in range(2)]
        x3 = [self.carve(1024, F32) for _ in range(3)]
        tpost = self.carve(1024, F32)
        junk3 = self.carve(512, BF16)
        st3 = self.carve(16, F32)
        ss2, a2, rstd2, tmp3 = st3[:, 0:2], st3[:, 2:3], st3[:, 3:4], st3[:, 4:5]
        P_O = ((0, 1), (2, 3))
        S.dma("sp", "gpo", lambda e: e.dma_start(out=gpo, in_=npost.partition_broadcast(128)), writes=[K("gpo")])
        for k in range(8):
            S.dma("sp", "wos", lambda e, k=k: e.dma_start(out=wos, in_=w_out[k * 128:(k + 1) * 128, :]), writes=[K("wos")])
            S.op("pool", lambda e, k=k: e.tensor_copy(out=Wo[:, k, :], in_=wos), reads=[K("wos")], writes=[K("Wo")])
        yv = ysc.rearrange("h p t -> p h t")
        for i in range(NOWN // 128):
            p, q = i % 3, (i // 4) % 2
            if i % 4 == 0:
                S.dma("sp", "ytl%d" % q, lambda e, i=i, q=q: e.dma_start(out=yt[q], in_=yv[:, :, i * 128:i * 128 + 512]),
                      reads=[K("ysc")], writes=[K("yt%d" % q)])
            S.dma("sp", "x3l%d" % p, lambda e, i=i, p=p: e.dma_start(out=x3[p], in_=x_src[i * 128:(i + 1) * 128, :]),
                  writes=[K("x3%d" % p)])
            c0 = (i % 4) * 128
            pb = P_O[i % 2]
            for n in range(2):
                b = pb[n]
                for k in range(8):
                    S.op("pe", lambda e, k=k, n=n, b=b, q=q, c0=c0: e.matmul(psf[b], lhsT=yt[q][:, k, c0:c0 + 128],
                                                                            rhs=Wo[:, k, n * 512:(n + 1) * 512],
                                                                            start=(k == 0), stop=(k == 7)),
                         reads=[K("yt%d" % q), K("Wo")], writes=[K("ps%d" % b)])
                S.op("act", lambda e, n=n, b=b: e.activation(out=junk3, in_=psf[b], func=AF.Square, accum_out=ss2[:, n:n + 1]),
                     reads=[K("ps%d" % b)], writes=[K("junk3"), K("ss2")])
            S.op("dve", lambda e: e.tensor_tensor(out=a2, in0=ss2[:, 0:1], in1=ss2[:, 1:2], op=ALU.add),
                 reads=[K("ss2")], writes=[K("a2")])
            S.op("dve", lambda e: e.tensor_scalar(out=a2, in0=a2, scalar1=1.0 / D, scalar2=RMS_EPS, op0=ALU.mult,
                                                  op1=ALU.add), reads=[K("a2")], writes=[K("a2")])
            self.rsqrt(K("a2"), a2, K("rstd2"), rstd2, K("tmp3"), tmp3)
            for n in range(2):
                b = pb[n]
                S.op("dve", lambda e, n=n, b=b: e.scalar_tensor_tensor(out=tpost[:, n * 512:(n + 1) * 512], in0=psf[b],
                                                                       scalar=rstd2, in1=gpo[:, n * 512:(n + 1) * 512],
                                                                       op0=ALU.mult, op1=ALU.mult),
                     reads=[K("ps%d" % b), K("rstd2"), K("gpo")], writes=[K("tpost")])
            S.op("pool", lambda e, p=p: e.tensor_tensor(out=x3[p], in0=x3[p], in1=tpost, op=ALU.add),
                 reads=[K("x3%d" % p), K("tpost")], writes=[K("x3%d" % p)])
            o = S.dma("pool", "x3s%d" % p, lambda e, i=i, p=p: e.dma_start(out=x_dst[i * 128:(i + 1) * 128, :], in_=x3[p]),
                      reads=[K("x3%d" % p)], writes=[K("xdst")])
            self.final.append(o)
        S.barrier()
        self.peak = max(getattr(self, "peak", 0), self.off)
        self.off = off0

    def finish(self):
        S, nc = self.S, self.nc
        S.finalize()
        with contextlib.ExitStack() as es:
            sems = {k: es.enter_context(nc.semaphore("s%d" % i)) for i, k in enumerate(S.sem_keys)}
            block = es.enter_context(nc.Block())
            S.emit(block, sems, final_waits=self.final)
        return nc


_CACHE = {}


def consts():
    ident = np.eye(128, dtype=np.float32).astype(ml_dtypes.bfloat16)
    tril = np.tril(np.ones((128, 128), dtype=np.float32))
    return ident, tril


def prog_a(ntok):
    key = ("A", ntok)
    if key not in _CACHE:
        P = Prog()
        x = P.inp("x", [ntok, D])
        w_in = P.inp("w_in", [D, 3 * EW])
        npre = P.inp("npre", [128, 8])
        ln_g = P.inp("ln_g", [1, EW])
        ln_b = P.inp("ln_b", [1, EW])
        w_s = P.inp("w_s", [8, 128, 128])
        bs = P.inp("bs", [128, 8])
        w_out = P.inp("w_out", [EW, D])
        npost = P.inp("npost", [1, D])
        ident = P.inp("ident", [128, 128], BF16)
        tril = P.inp("tril", [128, 128])
        y = P.outp("y", [ntok, D])
        P.layer_a(x, y, ntok, w_in, npre, ln_g, ln_b, w_s, bs, w_out, npost, ident, tril)
        _CACHE[key] = P.finish()
    return _CACHE[key]


def col128(vec):
    return np.ascontiguousarray(np.asarray(vec, dtype=np.float32).reshape(-1, 128).T)


def run_layer_a(xs, i, inputs):
    j = i // 2
    ident, tril = consts()
    ntok = xs[0].shape[0]
    nc = prog_a(ntok)
    common = {
        "w_in": np.ascontiguousarray(inputs["a_w_in"][j]),
        "npre": col128(inputs["norm_pre"][i]),
        "ln_g": np.ascontiguousarray(inputs["a_ln_g"][j][None, :]),
        "ln_b": np.ascontiguousarray(inputs["a_ln_b"][j][None, :]),
        "w_s": np.ascontiguousarray(inputs["a_w_s"][j]),
        "bs": np.ascontiguousarray(np.asarray(inputs["a_b_s"][j]).T),
        "w_out": np.ascontiguousarray(inputs["a_w_out"][j]),
        "npost": np.ascontiguousarray(inputs["norm_post"][i][None, :]),
        "ident": ident, "tril": tril,
    }
    in_maps = [dict(common, x=np.ascontiguousarray(x)) for x in xs]
    res = run_bass_kernel_spmd(nc, in_maps, core_ids=list(range(len(xs))))
    return [r["y"] for r in res.results]


DILS = (1, 4, 16)
HALO = 2048


def rope_consts():
    rm = np.zeros((128, 128), np.float32)
    for fp in range(128):
        if fp % 64 < 32:
            rm[fp + 32, fp] = -1.0
        else:
            rm[fp - 32, fp] = 1.0
    mask = np.zeros((128, 2, 2, 128), np.float32)
    j = np.arange(128)[:, None]
    i = np.arange(128)[None, :]
    for hh in range(2):
        mask[:, hh, 0, :] = (j >= i)
        mask[:, hh, 1, :] = (j <= i)
    return rm.astype(ml_dtypes.bfloat16), mask.reshape(128, 512).astype(ml_dtypes.bfloat16)


def rope_tables_core(pos0, nown=4096):
    inv_freq = (1.0 / (np.float32(10000.0) ** (np.arange(0, 64, 2, dtype=np.float32) / np.float32(64)))).astype(np.float32)
    out = []
    for d in DILS:
        span = 128 * d
        res = []
        for (tau0, ncols) in ((HALO, nown), (HALO - span, nown + span)):
            tau = np.zeros(ncols, np.int64)
            for b0 in range(0, ncols, 512):
                n = min(512, ncols - b0)
                col = np.arange(n)
                r, ii = col // (n // d), col % (n // d)
                tau[b0:b0 + n] = tau0 + b0 + ii * d + r
            pos = (pos0 - HALO + tau).astype(np.float32)
            ang = pos[None, :] * inv_freq[:, None]
            c = np.tile(np.cos(ang), (4, 1)).astype(ml_dtypes.bfloat16)
            s_ = np.tile(np.sin(ang), (4, 1)).astype(ml_dtypes.bfloat16)
            res += [np.ascontiguousarray(c), np.ascontiguousarray(s_)]
        out.append(res)
    return out


DBG = {}


def prog_b():
    key = ("B",)
    if key not in _CACHE:
        P = Prog()
        x = P.inp("x", [TOK, D])
        xh = P.inp("xh", [HALO, D])
        w_in = P.inp("w_in", [D, 10240])
        gpre = P.inp("gpre", [1, D])
        w_out = P.inp("w_out", [D, D])
        npost = P.inp("npost", [1, D])
        hv = P.inp("hv", [128, 2])
        ident = P.inp("ident", [128, 128], BF16)
        rm = P.inp("rm", [128, 128], BF16)
        mask = P.inp("mask", [128, 512], BF16)
        tabs = []
        for g, d in enumerate(DILS):
            nk = 4096 + 128 * d
            tabs.append((P.inp("cq%d" % g, [128, 4096], BF16), P.inp("sq%d" % g, [128, 4096], BF16),
                         P.inp("ck%d" % g, [128, nk], BF16), P.inp("sk%d" % g, [128, nk], BF16)))
        ysc = P.nc.dram_tensor("ysc", [8, 128, 4096], BF16).ap()
        y = P.outp("y", [TOK, D])
        P.layer_b(x, xh, y, ysc, w_in, gpre, w_out, npost, hv, tabs, ident, rm, mask, **DBG)
        _CACHE[key] = P.finish()
    return _CACHE[key]


def run_layer_b(xs, i, inputs, core_ids=None):
    j = i // 2
    ident, _ = consts()
    rm, mask = rope_consts()
    nc = prog_b()
    common = {
        "w_in": np.ascontiguousarray(inputs["b_w_in"][j]),
        "gpre": np.ascontiguousarray(inputs["norm_pre"][i][None, :]),
        "w_out": np.ascontiguousarray(inputs["b_w_out"][j]),
        "npost": np.ascontiguousarray(inputs["norm_post"][i][None, :]),
        "ident": ident, "rm": rm, "mask": mask,
    }
    in_maps = []
    cores = list(range(len(xs))) if core_ids is None else core_ids
    for c in cores:
        m = dict(common)
        m["x"] = np.ascontiguousarray(xs[c])
        first = (c % 4 == 0)
        m["xh"] = np.zeros((HALO, D), np.float32) if first else np.ascontiguousarray(xs[c - 1][TOK - HALO:])
        m["hv"] = np.tile(np.array([[0.0 if first else 1.0, 1.0]], np.float32), (128, 1))
        tb = rope_tables_core((c % 4) * TOK)
        for g in range(3):
            m["cq%d" % g], m["sq%d" % g], m["ck%d" % g], m["sk%d" % g] = tb[g]
        in_maps.append(m)
    res = run_bass_kernel_spmd(nc, in_maps, core_ids=list(range(len(cores))))
    return [r["y"] for r in res.results]


def kernel_unfused(x, norm_pre, norm_post, a_w_in, a_ln_g, a_ln_b, a_w_s, a_b_s, a_w_out, b_w_in, b_w_out):
    inputs = dict(norm_pre=np.asarray(norm_pre), norm_post=np.asarray(norm_post), a_w_in=np.asarray(a_w_in),
                  a_ln_g=np.asarray(a_ln_g), a_ln_b=np.asarray(a_ln_b), a_w_s=np.asarray(a_w_s), a_b_s=np.asarray(a_b_s),
                  a_w_out=np.asarray(a_w_out), b_w_in=np.asarray(b_w_in), b_w_out=np.asarray(b_w_out))
    x = np.asarray(x, dtype=np.float32)
    B, S_, D_ = x.shape
    xs = [np.ascontiguousarray(c) for c in x.reshape(NCORES, TOK, D_)]
    for i in range(4):
        if i % 2 == 0:
            xs = run_layer_a(xs, i, inputs)
        else:
            xs = run_layer_b(xs, i, inputs)
    return np.stack(xs).reshape(B, S_, D_).astype(np.float32)


XR = 8192
B_CALLS = ((2048, 2048), (4096, -2048), (4096, 0))


def prog_fused():
    key = ("F",)
    if key in _CACHE:
        return _CACHE[key]
    P = Prog()
    x = P.inp("x", [XR, D])
    ident = P.inp("ident", [128, 128], BF16)
    tril = P.inp("tril", [128, 128])
    rm = P.inp("rm", [128, 128], BF16)
    mask = P.inp("mask", [128, 512], BF16)
    npost = [P.inp("npost%d" % i, [1, D]) for i in range(4)]
    A = []
    for j in range(2):
        A.append(dict(w_in=P.inp("a_w_in%d" % j, [D, 3 * EW]), npre=P.inp("a_npre%d" % j, [128, 8]),
                      ln_g=P.inp("a_ln_g%d" % j, [1, EW]), ln_b=P.inp("a_ln_b%d" % j, [1, EW]),
                      w_s=P.inp("a_w_s%d" % j, [8, 128, 128]), bs=P.inp("a_bs%d" % j, [128, 8]),
                      w_out=P.inp("a_w_out%d" % j, [EW, D])))
    Bw = []
    for j in range(2):
        Bw.append(dict(w_in=P.inp("b_w_in%d" % j, [D, 10240]), gpre=P.inp("b_gpre%d" % j, [1, D]),
                       w_out=P.inp("b_w_out%d" % j, [D, D])))
    fl = [P.inp("fl%d" % c, [128, 2]) for c in range(3)]
    tabs = []
    for c, (nown, _) in enumerate(B_CALLS):
        tc = []
        for g, d in enumerate(DILS):
            nk = nown + 128 * d
            tc.append((P.inp("cq%d_%d" % (c, g), [128, nown], BF16), P.inp("sq%d_%d" % (c, g), [128, nown], BF16),
                       P.inp("ck%d_%d" % (c, g), [128, nk], BF16), P.inp("sk%d_%d" % (c, g), [128, nk], BF16)))
        tabs.append(tc)
    y = P.outp("y", [TOK, D])
    xres = P.nc.dram_tensor("xres", [XR, D], F32).ap()
    ysc = P.nc.dram_tensor("ysc", [8, 128, 4096], BF16).ap()

    def la(j, i, src, dst, ntok):
        a = A[j]
        P.layer_a(src, dst, ntok, a["w_in"], a["npre"], a["ln_g"], a["ln_b"], a["w_s"], a["bs"], a["w_out"], npost[i],
                  ident, tril)

    def lb(j, i, c, src, halo, dst):
        nown = B_CALLS[c][0]
        b = Bw[j]
        P.layer_b(src, halo, dst, ysc[:, :, 0:nown], b["w_in"], b["gpre"], b["w_out"], npost[i], fl[c], tabs[c],
                  ident, rm, mask, NOWN=nown)

    la(0, 0, x, xres, XR)
    lb(0, 1, 0, xres[6144:8192], xres[4096:6144], xres[6144:8192])
    lb(0, 1, 1, xres[2048:6144], xres[0:2048], xres[2048:6144])
    la(1, 2, xres[2048:8192], xres[2048:8192], 6144)
    P.final = []
    lb(1, 3, 2, xres[4096:8192], xres[2048:4096], y)
    _CACHE[key] = P.finish()
    return _CACHE[key]


def kernel(x, norm_pre, norm_post, a_w_in, a_ln_g, a_ln_b, a_w_s, a_b_s, a_w_out, b_w_in, b_w_out):
    f32 = lambda a: np.ascontiguousarray(np.asarray(a, dtype=np.float32))
    x = f32(x)
    norm_pre, norm_post = f32(norm_pre), f32(norm_post)
    Bn, S_, D_ = x.shape
    per_seq = S_ // TOK
    ident, tril = consts()
    rm, mask = rope_consts()
    common = {"ident": ident, "tril": tril, "rm": rm, "mask": mask}
    for i in range(4):
        common["npost%d" % i] = f32(norm_post[i][None, :])
    for j in range(2):
        common["a_w_in%d" % j] = f32(a_w_in[j])
        common["a_npre%d" % j] = col128(norm_pre[2 * j])
        common["a_ln_g%d" % j] = f32(np.asarray(a_ln_g[j])[None, :])
        common["a_ln_b%d" % j] = f32(np.asarray(a_ln_b[j])[None, :])
        common["a_w_s%d" % j] = f32(a_w_s[j])
        common["a_bs%d" % j] = f32(np.asarray(a_b_s[j]).T)
        common["a_w_out%d" % j] = f32(a_w_out[j])
        common["b_w_in%d" % j] = f32(b_w_in[j])
        common["b_gpre%d" % j] = f32(norm_pre[2 * j + 1][None, :])
        common["b_w_out%d" % j] = f32(b_w_out[j])
    tab_cache = {}
    in_maps = []
    for c in range(NCORES):
        b, k = c // per_seq, c % per_seq
        m = dict(common)
        xe = np.zeros((XR, D_), np.float32)
        lo = k * TOK - TOK
        src_lo = max(lo, 0)
        xe[src_lo - lo:] = x[b, src_lo:(k + 1) * TOK]
        m["x"] = xe
        first = (k == 0)
        flags = ((1.0, 1.0), (0.0, 0.0) if first else (1.0, 1.0), (0.0, 1.0) if first else (1.0, 1.0))
        for ci in range(3):
            m["fl%d" % ci] = np.tile(np.array([flags[ci]], np.float32), (128, 1))
            nown, off = B_CALLS[ci]
            tk = (k, ci)
            if tk not in tab_cache:
                tab_cache[tk] = rope_tables_core(k * TOK + off, nown)
            tb = tab_cache[tk]
            for g in range(3):
                m["cq%d_%d" % (ci, g)], m["sq%d_%d" % (ci, g)], m["ck%d_%d" % (ci, g)], m["sk%d_%d" % (ci, g)] = tb[g]
        in_maps.append(m)
    nc = prog_fused()
    res = run_bass_kernel_spmd(nc, in_maps, core_ids=list(range(NCORES)))
    out = np.stack([r["y"] for r in res.results]).reshape(Bn, S_, D_)
    return out.astype(np.float32)
```

```python
import contextlib
import numpy as np
import ml_dtypes
import concourse.bass as bass
import concourse.mybir as mybir
from concourse.bass_utils import run_bass_kernel_spmd

F32 = mybir.dt.float32
BF16 = mybir.dt.bfloat16
I32 = mybir.dt.int32
AF = mybir.ActivationFunctionType
ALU = mybir.AluOpType

D = 1024
NCORES = 8
TOK = 4096
EW = 2048
RMS_EPS = 1e-6
LN_EPS = 1e-5

ENGS = ("pe", "act", "dve", "pool", "sp")


class _Op:
    __slots__ = ("eng", "fn", "sem", "val", "waits", "dma", "needs_inc")

    def __init__(self, eng, fn, dma):
        self.eng = eng
        self.fn = fn
        self.dma = dma
        self.sem = None
        self.val = 0
        self.waits = []
        self.needs_inc = False


class Sched:
    def __init__(self):
        self.ops = {e: [] for e in ENGS}
        self.last_w = {}
        self.readers = {}
        self.all_ops = []
        self.epoch = 0

    def _deps(self, op, reads, writes):
        deps = []
        raw = set()
        for b in reads:
            w = self.last_w.get(b)
            if w is not None:
                deps.append(w)
                raw.add(id(w))
        for b in writes:
            w = self.last_w.get(b)
            if w is not None:
                deps.append(w)
            deps.extend(self.readers.get(b, {}).values())
        for b in reads:
            self.readers.setdefault(b, {})[op.sem] = op
        for b in writes:
            self.last_w[b] = op
            self.readers[b] = {}
        return deps, raw

    def barrier(self):
        last = {}
        for o in self.all_ops:
            last[o.sem] = o
        self.pending = {e: list(last.values()) for e in ENGS}

    def _pend(self, o, eng):
        pend = getattr(self, "pending", None)
        if pend and pend.get(eng):
            for d in pend[eng]:
                if d.dma is None and d.eng == eng:
                    continue
                d.needs_inc = True
                o.waits.append(d)
            pend[eng] = []

    def op(self, eng, fn, reads=(), writes=()):
        o = _Op(eng, fn, None)
        o.sem = ("eng", eng, self.epoch)
        self._pend(o, eng)
        deps, raw = self._deps(o, reads, writes)
        for d in deps:
            if d is o:
                continue
            if d.dma is None and d.eng == eng:
                if eng == "pe" or id(d) not in raw:
                    continue
            d.needs_inc = True
            o.waits.append(d)
        self.ops[eng].append(o)
        self.all_ops.append(o)
        return o

    def dma(self, queue, stream, fn, reads=(), writes=()):
        o = _Op(queue, fn, stream)
        o.sem = ("dma", stream)
        o.needs_inc = True
        self._pend(o, queue)
        deps, _ = self._deps(o, reads, writes)
        for d in deps:
            if d is o:
                continue
            d.needs_inc = True
            o.waits.append(d)
        self.ops[queue].append(o)
        self.all_ops.append(o)
        return o

    def new_epoch(self):
        self.epoch += 1

    def finalize(self):
        counts = {}
        for o in self.all_ops:
            if o.needs_inc:
                step = 16 if o.dma is not None else 1
                counts[o.sem] = counts.get(o.sem, 0) + step
                o.val = counts[o.sem]
        self.sem_keys = list(counts.keys())
        for k, v in counts.items():
            assert v < 60000, (k, v)
        return counts

    def emit(self, block, sems, final_waits=()):
        handles = {"pe": "tensor", "act": "scalar", "dve": "vector", "pool": "gpsimd", "sp": "sync"}

        def make(engname):
            ops = self.ops[engname]

            def body(eng):
                waited = {}

                def wait(d):
                    if waited.get(d.sem, 0) >= d.val:
                        return
                    eng.wait_ge(sems[d.sem], d.val)
                    waited[d.sem] = d.val

                for o in ops:
                    for d in o.waits:
                        wait(d)
                    ins = o.fn(eng)
                    if o.needs_inc:
                        ins.then_inc(sems[o.sem], 16 if o.dma is not None else 1)
                if engname == "sp":
                    for d in final_waits:
                        wait(d)
            return body

        for engname in ENGS:
            getattr(block, handles[engname])(make(engname))


class Prog:
    def __init__(self):
        self.nc = bass.Bass("TRN2", target_bir_lowering=False)
        self.S = Sched()
        self.cap = 212000
        self.arena = self.nc.alloc_sbuf_tensor("arena", [128, self.cap // 2], BF16)[:]
        self.off = 0
        self.uid = 0
        self.ps_f32 = [self.nc.alloc_psum_tensor("psb%d" % i, [128, 512], F32)[:] for i in range(8)]
        self.ps_b16 = [p.bitcast(BF16) for p in self.ps_f32]
        self.final = []

    def inp(self, name, shape, dtype=F32):
        return self.nc.dram_tensor(name, list(shape), dtype, kind="ExternalInput").ap()

    def outp(self, name, shape, dtype=F32):
        return self.nc.dram_tensor(name, list(shape), dtype, kind="ExternalOutput").ap()

    def carve(self, cols, dtype):
        esz = 4 if dtype in (F32, I32) else 2
        nb = (cols * esz + 63) // 64 * 64
        assert self.off + nb <= self.cap, ("SBUF arena overflow", self.off, nb)
        a = self.arena[:, self.off // 2:(self.off + nb) // 2]
        self.off += nb
        if dtype != BF16:
            a = a.bitcast(dtype)
        return a[:, 0:cols]

    def key(self, base):
        self.uid += 1
        return "%s#%d" % (base, self.uid)

    def rsqrt(self, a_key, a, out_key, out, tmp_key, tmp):
        S = self.S
        S.op("dve", lambda e: e.tensor_scalar(out=out.bitcast(I32), in0=a.bitcast(I32), scalar1=1, scalar2=None,
                                              op0=ALU.arith_shift_right), reads=[a_key], writes=[out_key])
        S.op("dve", lambda e: e.tensor_scalar(out=out.bitcast(I32), in0=out.bitcast(I32), scalar1=0x5f3759df,
                                              scalar2=-1, op0=ALU.subtract, op1=ALU.mult),
             reads=[out_key], writes=[out_key])
        for _ in range(2):
            S.op("dve", lambda e: e.scalar_tensor_tensor(out=tmp, in0=out, scalar=-0.5, in1=out, op0=ALU.mult,
                                                         op1=ALU.mult), reads=[out_key], writes=[tmp_key])
            S.op("dve", lambda e: e.tensor_tensor(out=tmp, in0=tmp, in1=a, op=ALU.mult),
                 reads=[tmp_key, a_key], writes=[tmp_key])
            S.op("dve", lambda e: e.scalar_tensor_tensor(out=out, in0=tmp, scalar=1.5, in1=out, op0=ALU.add,
                                                         op1=ALU.mult), reads=[tmp_key, out_key], writes=[out_key])

    def layer_a(self, x_src, x_dst, ntok, w_in, npre_col, ln_g, ln_b, w_s, bs_col, w_out, npost, ident_d, tril_d):
        S, nc = self.S, self.nc
        off0 = self.off
        NT = ntok // 128
        L = self.key("A")
        K = lambda s: "%s/%s" % (L, s)

        Wb = [self.carve(6144, BF16) for _ in range(8)]
        Wo = [self.carve(1024, BF16) for _ in range(16)]
        lng = self.carve(2048, BF16)
        lnb = self.carve(2048, BF16)
        gpo = self.carve(1024, F32)
        WmT = self.carve(1024, BF16)
        bsc = self.carve(8, F32)
        gpre = self.carve(8, F32)
        idt = self.carve(128, BF16)
        tril = self.carve(128, F32)
        xt = [self.carve(1024, F32) for _ in range(3)]
        xb = self.carve(1024, BF16)
        xT = [self.carve(1024, BF16) for _ in range(2)]
        junk = self.carve(1024, BF16)
        u = self.carve(2048, BF16)
        sz = self.carve(2048, BF16)
        v = self.carve(2048, F32)
        vn = [self.carve(2048, BF16) for _ in range(2)]
        y = self.carve(2048, BF16)
        yT = self.carve(2048, BF16)
        tpost = self.carve(1024, F32)
        st = self.carve(64, F32)
        ss, ms, tmp1 = st[:, 0:1], st[:, 1:2], st[:, 3:4]
        rstd = [st[:, 4:5], st[:, 5:6]]
        bst = st[:, 8:32]
        mv = st[:, 32:34]
        va, rsv, tmp2 = st[:, 34:35], st[:, 35:36], st[:, 36:37]
        ss2, a2, rstd2, tmp3 = st[:, 40:42], st[:, 42:43], st[:, 43:44], st[:, 44:45]
        stage = [v[:, 0:1024], v[:, 1024:2048]]
        VK = [K("v0"), K("v1")]
        P_TX, P_IN, P_SV, P_TY, P_O = 0, (1, 2), (3, 4), 5, (6, 7)
        psf, psb = self.ps_f32, self.ps_b16

        S.dma("sp", "idt", lambda e: e.dma_start(out=idt, in_=ident_d), writes=[K("idt")])
        S.dma("sp", "tril", lambda e: e.dma_start(out=tril, in_=tril_d), writes=[K("tril")])
        S.dma("sp", "gpre", lambda e: e.dma_start(out=gpre, in_=npre_col), writes=[K("gpre")])
        S.dma("sp", "bsc", lambda e: e.dma_start(out=bsc, in_=bs_col), writes=[K("bsc")])
        S.dma("sp", "gpo", lambda e: e.dma_start(out=gpo, in_=npost.partition_broadcast(128)), writes=[K("gpo")])
        cnt = [0]

        def staged(src_ap, cols, consume, view=None):
            i = cnt[0] % 2
            cnt[0] += 1
            sb = stage[i][:, 0:cols]
            sbd = view(sb) if view is not None else sb
            S.dma("sp", "stg%d" % i, lambda e: e.dma_start(out=sbd, in_=src_ap), writes=[VK[i]])
            consume(sb, "act" if i == 0 else "dve", VK[i])

        for k in range(8):
            for q in range(6):
                dst = Wb[k][:, q * 1024:(q + 1) * 1024]

                def cons(sb, eng, skey, dst=dst, k=k):
                    if eng == "act":
                        S.op("act", lambda e: e.activation(out=dst, in_=sb, func=AF.Copy, scale=gpre[:, k:k + 1]),
                             reads=[skey, K("gpre")], writes=[K("Wb%d" % k)])
                    else:
                        S.op("dve", lambda e: e.tensor_scalar(out=dst, in0=sb, scalar1=gpre[:, k:k + 1], scalar2=None,
                                                              op0=ALU.mult), reads=[skey, K("gpre")], writes=[K("Wb%d" % k)])
                staged(w_in[k * 128:(k + 1) * 128, q * 1024:(q + 1) * 1024], 1024, cons)
        for k in range(16):
            def cons(sb, eng, skey, k=k):
                if eng == "act":
                    S.op("act", lambda e: e.activation(out=Wo[k], in_=sb, func=AF.Copy), reads=[skey], writes=[K("Wo")])
                else:
                    S.op("dve", lambda e: e.tensor_copy(out=Wo[k], in_=sb), reads=[skey], writes=[K("Wo")])
            staged(w_out[k * 128:(k + 1) * 128, :], 1024, cons)
        for (src, dstt, nm) in ((ln_g, lng, "lng"), (ln_b, lnb, "lnb")):
            for h in range(2):
                def cons(sb, eng, skey, dstt=dstt, h=h, nm=nm):
                    if eng == "act":
                        S.op("act", lambda e: e.activation(out=dstt[:, h * 1024:(h + 1) * 1024], in_=sb, func=AF.Copy),
                             reads=[skey], writes=[K(nm)])
                    else:
                        S.op("dve", lambda e: e.tensor_copy(out=dstt[:, h * 1024:(h + 1) * 1024], in_=sb),
                             reads=[skey], writes=[K(nm)])
                staged(src[:, h * 1024:(h + 1) * 1024].partition_broadcast(128), 1024, cons)

        def cons_ws(sb, eng, skey):
            sbv = sb.rearrange("p (g s) -> p g s", g=8)
            yv = y[:, 0:1024].rearrange("p (g s) -> p g s", g=8)
            for g in range(8):
                S.op("dve", lambda e, g=g: e.tensor_tensor(out=yv[:, g, :], in0=sbv[:, g, :], in1=tril, op=ALU.mult),
                     reads=[skey, K("tril")], writes=[K("y")])
            for g in range(8):
                S.op("pe", lambda e, g=g: e.transpose(out=psb[P_TX][:, g * 128:(g + 1) * 128], in_=yv[:, g, :], identity=idt),
                     reads=[K("y"), K("idt")], writes=[K("ps%d" % P_TX)])
            S.op("dve", lambda e: e.tensor_copy(out=WmT, in_=psb[P_TX]), reads=[K("ps%d" % P_TX)], writes=[K("WmT")])
        staged(w_s.rearrange("g t s -> t g s"), 1024, cons_ws, view=lambda sb: sb.rearrange("p (g s) -> p g s", g=8))

        def load(i):
            p = i % 3
            S.dma("sp", "xl%d" % p, lambda e: e.dma_start(out=xt[p], in_=x_src[i * 128:(i + 1) * 128, :]),
                  writes=[K("xt%d" % p)])

        def inproj(q, banks, r0):
            r = r0
            for n in banks:
                b = P_IN[r % 2]
                r += 1
                for k in range(8):
                    S.op("pe", lambda e, k=k, n=n, b=b: e.matmul(psf[b], lhsT=xT[q][:, k * 128:(k + 1) * 128],
                                                                rhs=Wb[k][:, n * 512:(n + 1) * 512],
                                                                start=(k == 0), stop=(k == 7)),
                         reads=[K("xT%d" % q), K("Wb%d" % k)], writes=[K("ps%d" % b)])
                if n < 4:
                    dst, func, dk = u[:, n * 512:(n + 1) * 512], AF.Gelu, [K("u")]
                elif n < 8:
                    dst, func, dk = v[:, (n - 4) * 512:(n - 3) * 512], AF.Gelu, VK
                else:
                    dst, func, dk = sz[:, (n - 8) * 512:(n - 7) * 512], AF.Silu, [K("sz")]
                S.op("act", lambda e, dst=dst, func=func, b=b: e.activation(out=dst, in_=psf[b], func=func, scale=rstd[q]),
                     reads=[K("ps%d" % b), K("rstd%d" % q)], writes=dk)

        def pre(i):
            p, q = i % 3, i % 2
            xk = K("xt%d" % p)
            S.op("act", lambda e: e.activation(out=junk, in_=xt[p], func=AF.Square, accum_out=ss),
                 reads=[xk], writes=[K("junk"), K("ss")])
            S.op("dve", lambda e: e.tensor_scalar(out=ms, in0=ss, scalar1=1.0 / D, scalar2=RMS_EPS, op0=ALU.mult,
                                                  op1=ALU.add), reads=[K("ss")], writes=[K("ms")])
            self.rsqrt(K("ms"), ms, K("rstd%d" % q), rstd[q], K("tmp1"), tmp1)
            S.op("pool", lambda e: e.tensor_copy(out=xb, in_=xt[p]), reads=[xk], writes=[K("xb")])

        def stage1(i):
            q = i % 2
            for k in range(8):
                S.op("pe", lambda e, k=k: e.transpose(out=psb[P_TX][:, k * 128:(k + 1) * 128],
                                                      in_=xb[:, k * 128:(k + 1) * 128], identity=idt),
                     reads=[K("xb"), K("idt")], writes=[K("ps%d" % P_TX)])
            S.op("act", lambda e: e.activation(out=xT[q], in_=psb[P_TX], func=AF.Copy),
                 reads=[K("ps%d" % P_TX)], writes=[K("xT%d" % q)])
            inproj(q, (4, 5, 6, 7), 0)
            for c in range(4):
                S.op("dve", lambda e, c=c: e.bn_stats(out=bst[:, c * 6:(c + 1) * 6], in_=v[:, c * 512:(c + 1) * 512]),
                     reads=VK, writes=[K("bst")])
            S.op("dve", lambda e: e.bn_aggr(out=mv, in_=bst.rearrange("p (c s) -> p c s", c=4)),
                 reads=[K("bst")], writes=[K("mv")])
            S.op("dve", lambda e: e.tensor_scalar(out=va, in0=mv[:, 1:2], scalar1=LN_EPS, scalar2=None,
                                                  op0=ALU.add), reads=[K("mv")], writes=[K("va")])
            self.rsqrt(K("va"), va, K("rsv"), rsv, K("tmp2"), tmp2)
            S.op("dve", lambda e: e.scalar_tensor_tensor(out=v, in0=v, scalar=mv[:, 0:1], in1=lng,
                                                         op0=ALU.subtract, op1=ALU.mult),
                 reads=VK + [K("mv"), K("lng")], writes=VK)
            S.op("dve", lambda e: e.scalar_tensor_tensor(out=vn[q], in0=v, scalar=rsv, in1=lnb,
                                                         op0=ALU.mult, op1=ALU.add),
                 reads=VK + [K("rsv"), K("lnb")], writes=[K("vn%d" % q)])

        def stage2a(i):
            q = i % 2
            inproj(q, (0, 1, 2, 3, 8, 9, 10, 11), 0)
            for qd in range(4):
                b = P_SV[qd % 2]
                for gg in range(2):
                    g = 2 * qd + gg
                    S.op("pe", lambda e, g=g, gg=gg, b=b: e.matmul(psf[b][:, gg * 256:(gg + 1) * 256],
                                                                  lhsT=WmT[:, g * 128:(g + 1) * 128],
                                                                  rhs=vn[q][:, g * 256:(g + 1) * 256], start=True, stop=True),
                         reads=[K("vn%d" % q), K("WmT")], writes=[K("ps%d" % b)])
                for gg in range(2):
                    g = 2 * qd + gg
                    S.op("dve", lambda e, g=g, gg=gg, b=b: e.scalar_tensor_tensor(
                        out=y[:, g * 256:(g + 1) * 256], in0=psf[b][:, gg * 256:(gg + 1) * 256], scalar=bsc[:, g:g + 1],
                        in1=u[:, g * 256:(g + 1) * 256], op0=ALU.add, op1=ALU.mult),
                        reads=[K("ps%d" % b), K("bsc"), K("u")], writes=[K("y")])
            S.op("dve", lambda e: e.tensor_tensor(out=y, in0=y, in1=sz, op=ALU.mult), reads=[K("y"), K("sz")], writes=[K("y")])

        def stage2b(i):
            p = i % 3
            xk = K("xt%d" % p)
            for h in range(2):
                for k in range(8):
                    kk = h * 8 + k
                    S.op("pe", lambda e, k=k, kk=kk: e.transpose(out=psb[P_TY][:, k * 128:(k + 1) * 128],
                                                                in_=y[:, kk * 128:(kk + 1) * 128], identity=idt),
                         reads=[K("y"), K("idt")], writes=[K("ps%d" % P_TY)])
                S.op("act", lambda e, h=h: e.activation(out=yT[:, h * 1024:(h + 1) * 1024], in_=psb[P_TY], func=AF.Copy),
                     reads=[K("ps%d" % P_TY)], writes=[K("yT")])
            for n in range(2):
                b = P_O[n]
                for k in range(16):
                    S.op("pe", lambda e, k=k, n=n, b=b: e.matmul(psf[b], lhsT=yT[:, k * 128:(k + 1) * 128],
                                                                rhs=Wo[k][:, n * 512:(n + 1) * 512],
                                                                start=(k == 0), stop=(k == 15)),
                         reads=[K("yT"), K("Wo")], writes=[K("ps%d" % b)])
                S.op("act", lambda e, n=n, b=b: e.activation(out=junk[:, 0:512], in_=psf[b], func=AF.Square,
                                                             accum_out=ss2[:, n:n + 1]),
                     reads=[K("ps%d" % b)], writes=[K("junk"), K("ss2")])
            S.op("dve", lambda e: e.tensor_tensor(out=a2, in0=ss2[:, 0:1], in1=ss2[:, 1:2], op=ALU.add),
                 reads=[K("ss2")], writes=[K("a2")])
            S.op("dve", lambda e: e.tensor_scalar(out=a2, in0=a2, scalar1=1.0 / D, scalar2=RMS_EPS, op0=ALU.mult,
                                                  op1=ALU.add), reads=[K("a2")], writes=[K("a2")])
            self.rsqrt(K("a2"), a2, K("rstd2"), rstd2, K("tmp3"), tmp3)
            for n in range(2):
                b = P_O[n]
                S.op("dve", lambda e, n=n, b=b: e.scalar_tensor_tensor(out=tpost[:, n * 512:(n + 1) * 512], in0=psf[b],
                                                                       scalar=rstd2, in1=gpo[:, n * 512:(n + 1) * 512],
                                                                       op0=ALU.mult, op1=ALU.mult),
                     reads=[K("ps%d" % b), K("rstd2"), K("gpo")], writes=[K("tpost")])
            S.op("pool", lambda e: e.tensor_tensor(out=xt[p], in0=xt[p], in1=tpost, op=ALU.add),
                 reads=[xk, K("tpost")], writes=[xk])
            o = S.dma("pool", "xs%d" % p, lambda e: e.dma_start(out=x_dst[i * 128:(i + 1) * 128, :], in_=xt[p]),
                      reads=[xk], writes=[K("xdst")])
            self.final.append(o)

        load(0)
        if NT > 1:
            load(1)
        pre(0)
        stage1(0)
        for i in range(NT):
            if i + 2 < NT:
                load(i + 2)
            if i + 1 < NT:
                pre(i + 1)
            stage2a(i)
            if i + 1 < NT:
                stage1(i + 1)
            stage2b(i)
        S.barrier()
        self.peak = max(getattr(self, "peak", 0), self.off)
        self.off = off0


    def layer_b(self, x_src, x_halo, x_dst, ysc, w_in, gpre_d, w_out, npost, hv_d, tabs, ident_d, rm_d, mask_d,
                NOWN=4096, dbg_hps=range(8), dbg_gs=range(3), dbg_attn=True):
        S, nc = self.S, self.nc
        off0 = self.off
        L = self.key("B")
        K = lambda s: "%s/%s" % (L, s)
        psf, psb = self.ps_f32, self.ps_b16
        HAL = 2048
        NT = HAL + NOWN
        DIL = (1, 4, 16)
        P_PR, P_RT, P_S2, P_ND, P_V2 = (0, 1), 2, ((3, 4), (0, 1), (2, 7)), (5, 6), (7, 3)

        hT = self.carve(8 * NT, BF16).rearrange("p (k t) -> p k t", k=8)
        offq = self.off
        QT = self.carve(NOWN, BF16)
        KT = self.carve(NT, BF16)
        Vb = self.carve(NT, BF16)
        acc = self.carve(2 * NOWN, F32).rearrange("p (a t) -> p a t", a=2)
        wst = self.carve(3 * 1024, F32)
        wb = [self.carve(3 * 1024, BF16) for _ in range(2)]
        idt = self.carve(128, BF16)
        rm = self.carve(128, BF16)
        maskT = self.carve(512, BF16)
        ones = self.carve(64, BF16)
        hvones = self.carve(64, BF16)
        loones = self.carve(64, BF16)
        hv2 = self.carve(2, F32)
        hv, lo = hv2[:, 0:1], hv2[:, 1:2]
        PT = [self.carve(512, BF16) for _ in range(6)]
        qraw = [self.carve(512, BF16) for _ in range(3)]
        rt1 = [self.carve(512, BF16) for _ in range(3)]
        ctab = [self.carve(512, BF16) for _ in range(3)]
        stab = [self.carve(512, BF16) for _ in range(3)]
        st = self.carve(64, F32)
        print("layer B persistent SBUF bytes", self.off - off0)
        offp = self.off

        S.dma("sp", "idt", lambda e: e.dma_start(out=idt, in_=ident_d), writes=[K("idt")])
        S.dma("sp", "rm", lambda e: e.dma_start(out=rm, in_=rm_d), writes=[K("rm")])
        S.dma("sp", "mask", lambda e: e.dma_start(out=maskT, in_=mask_d), writes=[K("mask")])
        S.dma("sp", "hv", lambda e: e.dma_start(out=hv2, in_=hv_d), writes=[K("hv")])
        S.op("pool", lambda e: e.memset(ones, 1.0), writes=[K("ones")])
        S.op("pool", lambda e: e.memset(hvones, 1.0), writes=[K("hvones")])
        S.op("pool", lambda e: e.tensor_scalar(out=hvones, in0=hvones, scalar1=hv, scalar2=None, op0=ALU.mult),
             reads=[K("hv"), K("hvones")], writes=[K("hvones")])
        S.op("pool", lambda e: e.memset(loones, 1.0), writes=[K("loones")])
        S.op("pool", lambda e: e.tensor_scalar(out=loones, in0=loones, scalar1=lo, scalar2=None, op0=ALU.mult),
             reads=[K("hv"), K("loones")], writes=[K("loones")])

        need1 = 8 * 4096 + 8 * 2048 + 4096 + 2048 + 256
        self.off = offp if offp + need1 <= self.cap else offq
        gb = self.carve(1024, F32)
        xt = [self.carve(1024, F32) for _ in range(8)]
        hb = [self.carve(1024, BF16) for _ in range(8)]
        junk = self.carve(1024, BF16)
        st1 = self.carve(32, F32)
        S.dma("sp", "gb", lambda e: e.dma_start(out=gb, in_=gpre_d.partition_broadcast(128)), writes=[K("gb")])

        def xrows(i):
            return x_halo[i * 128:(i + 1) * 128, :] if i < 16 else x_src[(i - 16) * 128:(i - 15) * 128, :]

        def p1_load(i):
            p = i % 8
            S.dma("sp", "x1l%d" % p, lambda e: e.dma_start(out=xt[p], in_=xrows(i)), writes=[K("x1t%d" % p)])

        NT1 = NT // 128
        NG1 = NT1 // 4

        def p1_x(gi):
            q = gi % 2
            ss, ms, rstd, tmp1 = (st1[:, q * 16 + c * 4:q * 16 + c * 4 + 4] for c in range(4))
            for t in range(4):
                i = gi * 4 + t
                p = i % 8
                S.op("act", lambda e, p=p, t=t, ss=ss: e.activation(out=junk, in_=xt[p], func=AF.Square, accum_out=ss[:, t:t + 1]),
                     reads=[K("x1t%d" % p)], writes=[K("junk"), K("ss%d" % q)])
            S.op("dve", lambda e: e.tensor_scalar(out=ms, in0=ss, scalar1=1.0 / D, scalar2=RMS_EPS, op0=ALU.mult,
                                                  op1=ALU.add), reads=[K("ss%d" % q)], writes=[K("ms%d" % q)])
            self.rsqrt(K("ms%d" % q), ms, K("rstd%d" % q), rstd, K("tmp1%d" % q), tmp1)
            for t in range(4):
                i = gi * 4 + t
                p = i % 8
                S.op("dve", lambda e, p=p, t=t, rstd=rstd: e.scalar_tensor_tensor(out=hb[p], in0=xt[p], scalar=rstd[:, t:t + 1],
                                                                                  in1=gb, op0=ALU.mult, op1=ALU.mult),
                     reads=[K("x1t%d" % p), K("rstd%d" % q), K("gb")], writes=[K("hb%d" % p)])

        def p1_y(gi):
            for t in range(4):
                i = gi * 4 + t
                p = i % 8
                pb = P_PR[i % 2]
                for k in range(8):
                    S.op("pe", lambda e, k=k, p=p, pb=pb: e.transpose(out=psb[pb][:, k * 128:(k + 1) * 128],
                                                                in_=hb[p][:, k * 128:(k + 1) * 128], identity=idt),
                         reads=[K("hb%d" % p), K("idt")], writes=[K("ps%d" % pb)])
                S.op("act", lambda e, i=i, pb=pb: e.activation(out=hT[:, :, i * 128:(i + 1) * 128],
                                                               in_=psb[pb].rearrange("p (k t) -> p k t", k=8), func=AF.Copy),
                     reads=[K("ps%d" % pb)], writes=[K("hT")])

        for i in range(min(8, NT1)):
            p1_load(i)
        p1_x(0)
        for gi in range(NG1):
            if gi + 1 < NG1:
                p1_x(gi + 1)
            p1_y(gi)
            for t in range(4):
                i = (gi + 2) * 4 + t
                if i < NT1:
                    p1_load(i)
        S.barrier()

        w5 = w_in[:, 0:9216].rearrange("(k p) (g t h f) -> p k g t h f", p=128, g=3, t=3, h=8)
        wz = w_in[:, 9216:10240].rearrange("(k p) (h f) -> p k h f", p=128, h=8)
        tabi = [0]
        bankc = [0]
        wcnt = [0]

        def load_w(src_ap, ncols):
            slot = wcnt[0] % 2
            wcnt[0] += 1
            dst32 = wst[:, 0:8 * ncols]
            if len(src_ap.shape) == 3:
                S.dma("sp", "wst", lambda e: e.dma_start(out=dst32.rearrange("p (k c) -> p k c", k=8), in_=src_ap),
                      writes=[K("wst")])
            else:
                d4 = dst32.rearrange("p (k t f) -> p k t f", k=8, t=3)
                for t in range(3):
                    S.dma("sp", "wst", lambda e, t=t: e.dma_start(out=d4[:, :, t, :], in_=src_ap[:, :, t, :]),
                          writes=[K("wst")])
            S.op("pool", lambda e: e.tensor_copy(out=wb[slot][:, 0:8 * ncols], in_=dst32), reads=[K("wst")],
                 writes=[K("wb%d" % slot)])
            return wb[slot][:, 0:8 * ncols].rearrange("p (k c) -> p k c", k=8), K("wb%d" % slot)

        def proj_bank(wv, wkey, c0, tau0, n):
            b = P_PR[bankc[0] % 2]
            bankc[0] += 1
            for k in range(8):
                S.op("pe", lambda e, k=k, b=b: e.matmul(psf[b][:, 0:n], lhsT=wv[:, k, c0:c0 + 128], rhs=hT[:, k, tau0:tau0 + n],
                                                       start=(k == 0), stop=(k == 7)),
                     reads=[wkey, K("hT")], writes=[K("ps%d" % b)])
            flush_rot()
            return b

        pending_rot = []

        def flush_rot():
            while pending_rot:
                pending_rot.pop(0)()

        def rope_bank(b, n, d, dest, dkey, ctd, std, t0):
            j = tabi[0] % 3
            tabi[0] += 1
            S.dma("sp", "ct%d" % j, lambda e: e.dma_start(out=ctab[j][:, 0:n], in_=ctd[:, t0:t0 + n]), writes=[K("ctab%d" % j)])
            S.dma("sp", "st%d" % j, lambda e: e.dma_start(out=stab[j][:, 0:n], in_=std[:, t0:t0 + n]), writes=[K("stab%d" % j)])
            if d == 1 or n < 512:
                ov, iv = qraw[j][:, 0:n], psf[b][:, 0:n]
            else:
                ov = qraw[j].rearrange("p (r i) -> p r i", r=d)
                iv = psf[b].rearrange("p (i r) -> p r i", r=d)
            S.op("act", lambda e: e.activation(out=ov, in_=iv, func=AF.Copy), reads=[K("ps%d" % b)], writes=[K("qraw%d" % j)])

            def second():
                S.op("pe", lambda e: e.matmul(psf[P_RT][:, 0:n], lhsT=rm, rhs=qraw[j][:, 0:n], start=True, stop=True),
                     reads=[K("rm"), K("qraw%d" % j)], writes=[K("ps%d" % P_RT)])
                S.op("dve", lambda e: e.tensor_tensor(out=rt1[j][:, 0:n], in0=psf[P_RT][:, 0:n], in1=stab[j][:, 0:n], op=ALU.mult),
                     reads=[K("ps%d" % P_RT), K("stab%d" % j)], writes=[K("rt1%d" % j)])
                S.op("pool", lambda e: e.tensor_tensor(out=qraw[j][:, 0:n], in0=qraw[j][:, 0:n], in1=ctab[j][:, 0:n], op=ALU.mult),
                     reads=[K("qraw%d" % j), K("ctab%d" % j)], writes=[K("qraw%d" % j)])
                if d == 16:
                    a0 = rt1[j].rearrange("p (r i) -> p r i", r=16)
                    a1 = qraw[j].rearrange("p (r i) -> p r i", r=16)
                else:
                    a0, a1 = rt1[j][:, 0:n], qraw[j][:, 0:n]
                S.op("dve", lambda e: e.tensor_tensor(out=dest, in0=a0, in1=a1, op=ALU.add),
                     reads=[K("rt1%d" % j), K("qraw%d" % j)], writes=[dkey])
            pending_rot.append(second)

        ptc = [0]
        sc = [0]
        ndc = [0]
        for hp in dbg_hps:
            S.op("pool", lambda e: e.memset(acc.rearrange("p a t -> p (a t)"), 0.0), writes=[K("acc")])
            for g in dbg_gs:
                d = DIL[g]
                span = 128 * d
                nsp = NOWN // span
                nkb = (nsp + 1) * d
                kt0 = HAL - span
                nk = NOWN + span
                cq, sq, ck, sk = tabs[g]
                wv, wkey = load_w(w5[:, :, g, :, hp, :], 384)
                for bq in range(NOWN // 512):
                    b = proj_bank(wv, wkey, 0, HAL + bq * 512, 512)
                    if d == 16:
                        sp_, i0 = (bq * 512) // span, ((bq * 512) % span) // 16
                        dest = QT[:, sp_ * span:(sp_ + 1) * span].rearrange("p (r i) -> p r i", r=16)[:, :, i0:i0 + 32]
                        rope_bank(b, 512, d, dest, K("QT"), cq, sq, bq * 512)
                    else:
                        rope_bank(b, 512, d, QT[:, bq * 512:(bq + 1) * 512], K("QT"), cq, sq, bq * 512)
                nb_full, rem = nk // 512, nk % 512
                for bk in range(nb_full + (1 if rem else 0)):
                    n = 512 if bk < nb_full else rem
                    b = proj_bank(wv, wkey, 128, kt0 + bk * 512, n)
                    if d == 16:
                        sp_, i0 = (bk * 512) // span, ((bk * 512) % span) // 16
                        dest = KT[:, sp_ * span:(sp_ + 1) * span].rearrange("p (r i) -> p r i", r=16)[:, :, i0:i0 + 32]
                        rope_bank(b, 512, d, dest, K("KT"), ck, sk, bk * 512)
                    else:
                        rope_bank(b, n, d, KT[:, bk * 512:bk * 512 + n], K("KT"), ck, sk, bk * 512)
                flush_rot()
                for kb4 in range(0, nkb, 4):
                    P_V = P_V2[(kb4 // 4) % 2]
                    for q4 in range(4):
                        kb = kb4 + q4
                        if kb >= nkb:
                            break
                        sp_, r = kb // d, kb % d
                        t_start = kt0 + sp_ * span + r
                        for k in range(8):
                            S.op("pe", lambda e, k=k, q4=q4, t_start=t_start, d=d, wv=wv, P_V=P_V: e.matmul(
                                psf[P_V][:, q4 * 128:(q4 + 1) * 128], lhsT=hT[:, k, t_start:t_start + 127 * d + 1:d],
                                rhs=wv[:, k, 256:384], start=(k == 0), stop=(k == 7)),
                                reads=[wkey, K("hT")], writes=[K("ps%d" % P_V)])
                    nblk = min(4, nkb - kb4)

                    def cls(kb):
                        return 0 if kb < d else (1 if kb < d + 16 else 2)
                    r0 = 0
                    while r0 < nblk:
                        r1 = r0
                        while r1 < nblk and cls(kb4 + r1) == cls(kb4 + r0):
                            r1 += 1
                        c = cls(kb4 + r0)
                        if c == 2:
                            S.op("act", lambda e, kb4=kb4, r0=r0, r1=r1, P_V=P_V: e.activation(
                                out=Vb[:, (kb4 + r0) * 128:(kb4 + r1) * 128], in_=psf[P_V][:, r0 * 128:r1 * 128], func=AF.Copy),
                                reads=[K("ps%d" % P_V)], writes=[K("Vb")])
                        else:
                            fl = hv if c == 0 else lo
                            S.op("act", lambda e, kb4=kb4, r0=r0, r1=r1, fl=fl, P_V=P_V: e.activation(
                                out=Vb[:, (kb4 + r0) * 128:(kb4 + r1) * 128], in_=psf[P_V][:, r0 * 128:r1 * 128], func=AF.Copy,
                                scale=fl), reads=[K("ps%d" % P_V), K("hv")], writes=[K("Vb")])
                        r0 = r1
                nqb = nsp * d

                def s_pair(qb0):
                    ba, bb = P_S2[sc[0] % 3]
                    sc[0] += 1
                    for blk in range(2):
                        qb = qb0 + blk
                        for kbi in range(2):
                            kb = qb + kbi * d
                            for hh, b in ((0, ba), (1, bb)):
                                S.op("pe", lambda e, hh=hh, kbi=kbi, kb=kb, b=b, blk=blk, qb=qb: e.matmul(
                                    psf[b][:, (blk * 2 + kbi) * 128:(blk * 2 + kbi + 1) * 128],
                                    lhsT=KT[hh * 64:(hh + 1) * 64, kb * 128:(kb + 1) * 128],
                                    rhs=QT[hh * 64:(hh + 1) * 64, qb * 128:(qb + 1) * 128], start=True, stop=True,
                                    tile_position=(hh * 64, 0)),
                                    reads=[K("KT"), K("QT")], writes=[K("ps%d" % b)])
                    return ba, bb

                def e_pair(banks):
                    js = []
                    for b in banks:
                        j = ptc[0] % 6
                        ptc[0] += 1
                        S.op("act", lambda e, b=b, j=j: e.activation(out=PT[j], in_=psf[b], func=AF.Exp, scale=0.125),
                             reads=[K("ps%d" % b)], writes=[K("PT%d" % j)])
                        S.op("pool" if j % 2 == 0 else "dve", lambda e, j=j: e.tensor_tensor(out=PT[j], in0=PT[j], in1=maskT, op=ALU.mult),
                             reads=[K("PT%d" % j), K("mask")], writes=[K("PT%d" % j)])
                        js.append(j)
                    return js

                def pv_pair(qb0, js, bnd):
                    for blk in range(2):
                        qb = qb0 + blk
                        for what in range(2):
                            for kbi in range(2):
                                kb = qb + kbi * d
                                for hh in range(2):
                                    j = js[hh]
                                    if what == 0:
                                        lhsT, lk = Vb[:, kb * 128 + hh * 64:kb * 128 + (hh + 1) * 64], K("Vb")
                                    elif kb < d:
                                        lhsT, lk = hvones, K("hvones")
                                    elif kb < d + 16:
                                        lhsT, lk = loones, K("loones")
                                    else:
                                        lhsT, lk = ones, K("ones")
                                    S.op("pe", lambda e, hh=hh, what=what, kbi=kbi, lhsT=lhsT, blk=blk, j=j: e.matmul(
                                        psf[bnd][hh * 64:(hh + 1) * 64, (what * 2 + blk) * 128:(what * 2 + blk + 1) * 128],
                                        lhsT=lhsT, rhs=PT[j][:, (blk * 2 + kbi) * 128:(blk * 2 + kbi + 1) * 128],
                                        start=(kbi == 0), stop=(kbi == 1), tile_position=(0, hh * 64)),
                                        reads=[lk, K("PT%d" % j)], writes=[K("ps%d" % bnd)])

                def acc_pair(qb0, bnd):
                    sp_, r = qb0 // d, qb0 % d
                    if d == 1:
                        av = acc[:, :, qb0 * 128:(qb0 + 2) * 128]
                        pv = psf[bnd].rearrange("p (a t) -> p a t", a=2)
                    else:
                        av = acc[:, :, sp_ * span:(sp_ + 1) * span].rearrange("p a (i r) -> p a r i", r=d)[:, :, r:r + 2, :]
                        pv = psf[bnd].rearrange("p (a s i) -> p a s i", a=2, s=2)
                    S.op("dve", lambda e: e.tensor_tensor(out=av, in0=av, in1=pv, op=ALU.add),
                         reads=[K("acc"), K("ps%d" % bnd)], writes=[K("acc")])

                if not dbg_attn:
                    continue
                npair = nqb // 2
                sb = {0: s_pair(0)}
                if npair > 1:
                    sb[1] = s_pair(2)
                for pi in range(npair):
                    if pi + 2 < npair:
                        sb[pi + 2] = s_pair(2 * (pi + 2))
                    js = e_pair(sb.pop(pi))
                    bnd = P_ND[ndc[0] % 2]
                    ndc[0] += 1
                    pv_pair(2 * pi, js, bnd)
                    acc_pair(2 * pi, bnd)
            wvz, wzkey = load_w(wz[:, :, hp, :], 128)
            zs = KT[:, 0:NOWN]
            for bq in range(NOWN // 512):
                b = proj_bank(wvz, wzkey, 0, HAL + bq * 512, 512)
                S.op("act", lambda e, b=b, bq=bq: e.activation(out=zs[:, bq * 512:(bq + 1) * 512], in_=psf[b], func=AF.Silu),
                     reads=[K("ps%d" % b)], writes=[K("KT")])
            S.op("dve", lambda e: e.tensor_scalar(out=acc[:, 1, :], in0=acc[:, 1, :], scalar1=1e-18, scalar2=None, op0=ALU.max),
                 reads=[K("acc")], writes=[K("acc")])
            S.op("act", lambda e: e.activation(out=acc[:, 1, :], in_=acc[:, 1, :], func=AF.Ln), reads=[K("acc")], writes=[K("acc")])
            S.op("act", lambda e: e.activation(out=acc[:, 1, :], in_=acc[:, 1, :], func=AF.Exp, scale=-1.0),
                 reads=[K("acc")], writes=[K("acc")])
            S.op("dve", lambda e: e.tensor_tensor(out=acc[:, 0, :], in0=acc[:, 0, :], in1=acc[:, 1, :], op=ALU.mult),
                 reads=[K("acc")], writes=[K("acc")])
            S.op("dve", lambda e: e.tensor_tensor(out=QT, in0=acc[:, 0, :], in1=zs, op=ALU.mult),
                 reads=[K("acc"), K("KT")], writes=[K("QT")])
            S.dma("pool", "ysc", lambda e, hp=hp: e.dma_start(out=ysc[hp], in_=QT), reads=[K("QT")], writes=[K("ysc")])
        S.barrier()

        self.off = off0
        Wo = self.carve(8 * 1024, BF16).rearrange("p (k c) -> p k c", k=8)
        wos = self.carve(1024, F32)
        gpo = self.carve(1024, F32)
        yt = [self.carve(8 * 512, BF16).rearrange("p (h t) -> p h t", h=8) for _ in range(2)]
        x3 = [self.carve(1024, F32) for _ in range(3)]
        tpost = self.carve(1024, F32)
        junk3 = self.carve(512, BF16)
        st3 = self.carve(16, F32)
        ss2, a2, rstd2, tmp3 = st3[:, 0:2], st3[:, 2:3], st3[:, 3:4], st3[:, 4:5]
        P_O = ((0, 1), (2, 3))
        S.dma("sp", "gpo", lambda e: e.dma_start(out=gpo, in_=npost.partition_broadcast(128)), writes=[K("gpo")])
        for k in range(8):
            S.dma("sp", "wos", lambda e, k=k: e.dma_start(out=wos, in_=w_out[k * 128:(k + 1) * 128, :]), writes=[K("wos")])
            S.op("pool", lambda e, k=k: e.tensor_copy(out=Wo[:, k, :], in_=wos), reads=[K("wos")], writes=[K("Wo")])
        yv = ysc.rearrange("h p t -> p h t")
        for i in range(NOWN // 128):
            p, q = i % 3, (i // 4) % 2
            if i % 4 == 0:
                S.dma("sp", "ytl%d" % q, lambda e, i=i, q=q: e.dma_start(out=yt[q], in_=yv[:, :, i * 128:i * 128 + 512]),
                      reads=[K("ysc")], writes=[K("yt%d" % q)])
            S.dma("sp", "x3l%d" % p, lambda e, i=i, p=p: e.dma_start(out=x3[p], in_=x_src[i * 128:(i + 1) * 128, :]),
                  writes=[K("x3%d" % p)])
            c0 = (i % 4) * 128
            pb = P_O[i % 2]
            for n in range(2):
                b = pb[n]
                for k in range(8):
                    S.op("pe", lambda e, k=k, n=n, b=b, q=q, c0=c0: e.matmul(psf[b], lhsT=yt[q][:, k, c0:c0 + 128],
                                                                            rhs=Wo[:, k, n * 512:(n + 1) * 512],
                                                                            start=(k == 0), stop=(k == 7)),
                         reads=[K("yt%d" % q), K("Wo")], writes=[K("ps%d" % b)])
                S.op("act", lambda e, n=n, b=b: e.activation(out=junk3, in_=psf[b], func=AF.Square, accum_out=ss2[:, n:n + 1]),
                     reads=[K("ps%d" % b)], writes=[K("junk3"), K("ss2")])
            S.op("dve", lambda e: e.tensor_tensor(out=a2, in0=ss2[:, 0:1], in1=ss2[:, 1:2], op=ALU.add),
                 reads=[K("ss2")], writes=[K("a2")])
            S.op("dve", lambda e: e.tensor_scalar(out=a2, in0=a2, scalar1=1.0 / D, scalar2=RMS_EPS, op0=ALU.mult,
                                                  op1=ALU.add), reads=[K("a2")], writes=[K("a2")])
            self.rsqrt(K("a2"), a2, K("rstd2"), rstd2, K("tmp3"), tmp3)
            for n in range(2):
                b = pb[n]
                S.op("dve", lambda e, n=n, b=b: e.scalar_tensor_tensor(out=tpost[:, n * 512:(n + 1) * 512], in0=psf[b],
                                                                       scalar=rstd2, in1=gpo[:, n * 512:(n + 1) * 512],
                                                                       op0=ALU.mult, op1=ALU.mult),
                     reads=[K("ps%d" % b), K("rstd2"), K("gpo")], writes=[K("tpost")])
            S.op("pool", lambda e, p=p: e.tensor_tensor(out=x3[p], in0=x3[p], in1=tpost, op=ALU.add),
                 reads=[K("x3%d" % p), K("tpost")], writes=[K("x3%d" % p)])
            o = S.dma("pool", "x3s%d" % p, lambda e, i=i, p=p: e.dma_start(out=x_dst[i * 128:(i + 1) * 128, :], in_=x3[p]),
                      reads=[K("x3%d" % p)], writes=[K("xdst")])
            self.final.append(o)
        S.barrier()
        self.peak = max(getattr(self, "peak", 0), self.off)
        self.off = off0

    def finish(self):
        S, nc = self.S, self.nc
        S.finalize()
        with contextlib.ExitStack() as es:
            sems = {k: es.enter_context(nc.semaphore("s%d" % i)) for i, k in enumerate(S.sem_keys)}
            block = es.enter_context(nc.Block())
            S.emit(block, sems, final_waits=self.final)
        return nc


_CACHE = {}


def consts():
    ident = np.eye(128, dtype=np.float32).astype(ml_dtypes.bfloat16)
    tril = np.tril(np.ones((128, 128), dtype=np.float32))
    return ident, tril


def prog_a(ntok):
    key = ("A", ntok)
    if key not in _CACHE:
        P = Prog()
        x = P.inp("x", [ntok, D])
        w_in = P.inp("w_in", [D, 3 * EW])
        npre = P.inp("npre", [128, 8])
        ln_g = P.inp("ln_g", [1, EW])
        ln_b = P.inp("ln_b", [1, EW])
        w_s = P.inp("w_s", [8, 128, 128])
        bs = P.inp("bs", [128, 8])
        w_out = P.inp("w_out", [EW, D])
        npost = P.inp("npost", [1, D])
        ident = P.inp("ident", [128, 128], BF16)
        tril = P.inp("tril", [128, 128])
        y = P.outp("y", [ntok, D])
        P.layer_a(x, y, ntok, w_in, npre, ln_g, ln_b, w_s, bs, w_out, npost, ident, tril)
        _CACHE[key] = P.finish()
    return _CACHE[key]


def col128(vec):
    return np.ascontiguousarray(np.asarray(vec, dtype=np.float32).reshape(-1, 128).T)


def run_layer_a(xs, i, inputs):
    j = i // 2
    ident, tril = consts()
    ntok = xs[0].shape[0]
    nc = prog_a(ntok)
    common = {
        "w_in": np.ascontiguousarray(inputs["a_w_in"][j]),
        "npre": col128(inputs["norm_pre"][i]),
        "ln_g": np.ascontiguousarray(inputs["a_ln_g"][j][None, :]),
        "ln_b": np.ascontiguousarray(inputs["a_ln_b"][j][None, :]),
        "w_s": np.ascontiguousarray(inputs["a_w_s"][j]),
        "bs": np.ascontiguousarray(np.asarray(inputs["a_b_s"][j]).T),
        "w_out": np.ascontiguousarray(inputs["a_w_out"][j]),
        "npost": np.ascontiguousarray(inputs["norm_post"][i][None, :]),
        "ident": ident, "tril": tril,
    }
    in_maps = [dict(common, x=np.ascontiguousarray(x)) for x in xs]
    res = run_bass_kernel_spmd(nc, in_maps, core_ids=list(range(len(xs))))
    return [r["y"] for r in res.results]


DILS = (1, 4, 16)
HALO = 2048


def rope_consts():
    rm = np.zeros((128, 128), np.float32)
    for fp in range(128):
        if fp % 64 < 32:
            rm[fp + 32, fp] = -1.0
        else:
            rm[fp - 32, fp] = 1.0
    mask = np.zeros((128, 2, 2, 128), np.float32)
    j = np.arange(128)[:, None]
    i = np.arange(128)[None, :]
    for hh in range(2):
        mask[:, hh, 0, :] = (j >= i)
        mask[:, hh, 1, :] = (j <= i)
    return rm.astype(ml_dtypes.bfloat16), mask.reshape(128, 512).astype(ml_dtypes.bfloat16)


def rope_tables_core(pos0, nown=4096):
    inv_freq = (1.0 / (np.float32(10000.0) ** (np.arange(0, 64, 2, dtype=np.float32) / np.float32(64)))).astype(np.float32)
    out = []
    for d in DILS:
        span = 128 * d
        res = []
        for (tau0, ncols) in ((HALO, nown), (HALO - span, nown + span)):
            tau = np.zeros(ncols, np.int64)
            for b0 in range(0, ncols, 512):
                n = min(512, ncols - b0)
                col = np.arange(n)
                r, ii = col // (n // d), col % (n // d)
                tau[b0:b0 + n] = tau0 + b0 + ii * d + r
            pos = (pos0 - HALO + tau).astype(np.float32)
            ang = pos[None, :] * inv_freq[:, None]
            c = np.tile(np.cos(ang), (4, 1)).astype(ml_dtypes.bfloat16)
            s_ = np.tile(np.sin(ang), (4, 1)).astype(ml_dtypes.bfloat16)
            res += [np.ascontiguousarray(c), np.ascontiguousarray(s_)]
        out.append(res)
    return out


DBG = {}


def prog_b():
    key = ("B",)
    if key not in _CACHE:
        P = Prog()
        x = P.inp("x", [TOK, D])
        xh = P.inp("xh", [HALO, D])
        w_in = P.inp("w_in", [D, 10240])
        gpre = P.inp("gpre", [1, D])
        w_out = P.inp("w_out", [D, D])
        npost = P.inp("npost", [1, D])
        hv = P.inp("hv", [128, 2])
        ident = P.inp("ident", [128, 128], BF16)
        rm = P.inp("rm", [128, 128], BF16)
        mask = P.inp("mask", [128, 512], BF16)
        tabs = []
        for g, d in enumerate(DILS):
            nk = 4096 + 128 * d
            tabs.append((P.inp("cq%d" % g, [128, 4096], BF16), P.inp("sq%d" % g, [128, 4096], BF16),
                         P.inp("ck%d" % g, [128, nk], BF16), P.inp("sk%d" % g, [128, nk], BF16)))
        ysc = P.nc.dram_tensor("ysc", [8, 128, 4096], BF16).ap()
        y = P.outp("y", [TOK, D])
        P.layer_b(x, xh, y, ysc, w_in, gpre, w_out, npost, hv, tabs, ident, rm, mask, **DBG)
        _CACHE[key] = P.finish()
    return _CACHE[key]


def run_layer_b(xs, i, inputs, core_ids=None):
    j = i // 2
    ident, _ = consts()
    rm, mask = rope_consts()
    nc = prog_b()
    common = {
        "w_in": np.ascontiguousarray(inputs["b_w_in"][j]),
        "gpre": np.ascontiguousarray(inputs["norm_pre"][i][None, :]),
        "w_out": np.ascontiguousarray(inputs["b_w_out"][j]),
        "npost": np.ascontiguousarray(inputs["norm_post"][i][None, :]),
        "ident": ident, "rm": rm, "mask": mask,
    }
    in_maps = []
    cores = list(range(len(xs))) if core_ids is None else core_ids
    for c in cores:
        m = dict(common)
        m["x"] = np.ascontiguousarray(xs[c])
        first = (c % 4 == 0)
        m["xh"] = np.zeros((HALO, D), np.float32) if first else np.ascontiguousarray(xs[c - 1][TOK - HALO:])
        m["hv"] = np.tile(np.array([[0.0 if first else 1.0, 1.0]], np.float32), (128, 1))
        tb = rope_tables_core((c % 4) * TOK)
        for g in range(3):
            m["cq%d" % g], m["sq%d" % g], m["ck%d" % g], m["sk%d" % g] = tb[g]
        in_maps.append(m)
    res = run_bass_kernel_spmd(nc, in_maps, core_ids=list(range(len(cores))))
    return [r["y"] for r in res.results]


def kernel_unfused(x, norm_pre, norm_post, a_w_in, a_ln_g, a_ln_b, a_w_s, a_b_s, a_w_out, b_w_in, b_w_out):
    inputs = dict(norm_pre=np.asarray(norm_pre), norm_post=np.asarray(norm_post), a_w_in=np.asarray(a_w_in),
                  a_ln_g=np.asarray(a_ln_g), a_ln_b=np.asarray(a_ln_b), a_w_s=np.asarray(a_w_s), a_b_s=np.asarray(a_b_s),
                  a_w_out=np.asarray(a_w_out), b_w_in=np.asarray(b_w_in), b_w_out=np.asarray(b_w_out))
    x = np.asarray(x, dtype=np.float32)
    B, S_, D_ = x.shape
    xs = [np.ascontiguousarray(c) for c in x.reshape(NCORES, TOK, D_)]
    for i in range(4):
        if i % 2 == 0:
            xs = run_layer_a(xs, i, inputs)
        else:
            xs = run_layer_b(xs, i, inputs)
    return np.stack(xs).reshape(B, S_, D_).astype(np.float32)


XR = 8192
B_CALLS = ((2048, 2048), (4096, -2048), (4096, 0))


def prog_fused():
    key = ("F",)
    if key in _CACHE:
        return _CACHE[key]
    P = Prog()
    x = P.inp("x", [XR, D])
    ident = P.inp("ident", [128, 128], BF16)
    tril = P.inp("tril", [128, 128])
    rm = P.inp("rm", [128, 128], BF16)
    mask = P.inp("mask", [128, 512], BF16)
    npost = [P.inp("npost%d" % i, [1, D]) for i in range(4)]
    A = []
    for j in range(2):
        A.append(dict(w_in=P.inp("a_w_in%d" % j, [D, 3 * EW]), npre=P.inp("a_npre%d" % j, [128, 8]),
                      ln_g=P.inp("a_ln_g%d" % j, [1, EW]), ln_b=P.inp("a_ln_b%d" % j, [1, EW]),
                      w_s=P.inp("a_w_s%d" % j, [8, 128, 128]), bs=P.inp("a_bs%d" % j, [128, 8]),
                      w_out=P.inp("a_w_out%d" % j, [EW, D])))
    Bw = []
    for j in range(2):
        Bw.append(dict(w_in=P.inp("b_w_in%d" % j, [D, 10240]), gpre=P.inp("b_gpre%d" % j, [1, D]),
                       w_out=P.inp("b_w_out%d" % j, [D, D])))
    fl = [P.inp("fl%d" % c, [128, 2]) for c in range(3)]
    tabs = []
    for c, (nown, _) in enumerate(B_CALLS):
        tc = []
        for g, d in enumerate(DILS):
            nk = nown + 128 * d
            tc.append((P.inp("cq%d_%d" % (c, g), [128, nown], BF16), P.inp("sq%d_%d" % (c, g), [128, nown], BF16),
                       P.inp("ck%d_%d" % (c, g), [128, nk], BF16), P.inp("sk%d_%d" % (c, g), [128, nk], BF16)))
        tabs.append(tc)
    y = P.outp("y", [TOK, D])
    xres = P.nc.dram_tensor("xres", [XR, D], F32).ap()
    ysc = P.nc.dram_tensor("ysc", [8, 128, 4096], BF16).ap()

    def la(j, i, src, dst, ntok):
        a = A[j]
        P.layer_a(src, dst, ntok, a["w_in"], a["npre"], a["ln_g"], a["ln_b"], a["w_s"], a["bs"], a["w_out"], npost[i],
                  ident, tril)

    def lb(j, i, c, src, halo, dst):
        nown = B_CALLS[c][0]
        b = Bw[j]
        P.layer_b(src, halo, dst, ysc[:, :, 0:nown], b["w_in"], b["gpre"], b["w_out"], npost[i], fl[c], tabs[c],
                  ident, rm, mask, NOWN=nown)

    la(0, 0, x, xres, XR)
    lb(0, 1, 0, xres[6144:8192], xres[4096:6144], xres[6144:8192])
    lb(0, 1, 1, xres[2048:6144], xres[0:2048], xres[2048:6144])
    la(1, 2, xres[2048:8192], xres[2048:8192], 6144)
    P.final = []
    lb(1, 3, 2, xres[4096:8192], xres[2048:4096], y)
    _CACHE[key] = P.finish()
    return _CACHE[key]


def kernel(x, norm_pre, norm_post, a_w_in, a_ln_g, a_ln_b, a_w_s, a_b_s, a_w_out, b_w_in, b_w_out):
    f32 = lambda a: np.ascontiguousarray(np.asarray(a, dtype=np.float32))
    x = f32(x)
    norm_pre, norm_post = f32(norm_pre), f32(norm_post)
    Bn, S_, D_ = x.shape
    per_seq = S_ // TOK
    ident, tril = consts()
    rm, mask = rope_consts()
    common = {"ident": ident, "tril": tril, "rm": rm, "mask": mask}
    for i in range(4):
        common["npost%d" % i] = f32(norm_post[i][None, :])
    for j in range(2):
        common["a_w_in%d" % j] = f32(a_w_in[j])
        common["a_npre%d" % j] = col128(norm_pre[2 * j])
        common["a_ln_g%d" % j] = f32(np.asarray(a_ln_g[j])[None, :])
        common["a_ln_b%d" % j] = f32(np.asarray(a_ln_b[j])[None, :])
        common["a_w_s%d" % j] = f32(a_w_s[j])
        common["a_bs%d" % j] = f32(np.asarray(a_b_s[j]).T)
        common["a_w_out%d" % j] = f32(a_w_out[j])
        common["b_w_in%d" % j] = f32(b_w_in[j])
        common["b_gpre%d" % j] = f32(norm_pre[2 * j + 1][None, :])
        common["b_w_out%d" % j] = f32(b_w_out[j])
    tab_cache = {}
    in_maps = []
    for c in range(NCORES):
        b, k = c // per_seq, c % per_seq
        m = dict(common)
        xe = np.zeros((XR, D_), np.float32)
        lo = k * TOK - TOK
        src_lo = max(lo, 0)
        xe[src_lo - lo:] = x[b, src_lo:(k + 1) * TOK]
        m["x"] = xe
        first = (k == 0)
        flags = ((1.0, 1.0), (0.0, 0.0) if first else (1.0, 1.0), (0.0, 1.0) if first else (1.0, 1.0))
        for ci in range(3):
            m["fl%d" % ci] = np.tile(np.array([flags[ci]], np.float32), (128, 1))
            nown, off = B_CALLS[ci]
            tk = (k, ci)
            if tk not in tab_cache:
                tab_cache[tk] = rope_tables_core(k * TOK + off, nown)
            tb = tab_cache[tk]
            for g in range(3):
                m["cq%d_%d" % (ci, g)], m["sq%d_%d" % (ci, g)], m["ck%d_%d" % (ci, g)], m["sk%d_%d" % (ci, g)] = tb[g]
        in_maps.append(m)
    nc = prog_fused()
    res = run_bass_kernel_spmd(nc, in_maps, core_ids=list(range(NCORES)))
    out = np.stack([r["y"] for r in res.results]).reshape(Bn, S_, D_)
    return out.astype(np.float32)
```

```python
import contextlib
import numpy as np
import ml_dtypes
import concourse.bass as bass
import concourse.mybir as mybir
from concourse.bass_utils import run_bass_kernel_spmd

F32 = mybir.dt.float32
BF16 = mybir.dt.bfloat16
I32 = mybir.dt.int32
AF = mybir.ActivationFunctionType
ALU = mybir.AluOpType

D = 1024
NCORES = 8
TOK = 4096
EW = 2048
RMS_EPS = 1e-6
LN_EPS = 1e-5

ENGS = ("pe", "act", "dve", "pool", "sp")


class _Op:
    __slots__ = ("eng", "fn", "sem", "val", "waits", "dma", "needs_inc")

    def __init__(self, eng, fn, dma):
        self.eng = eng
        self.fn = fn
        self.dma = dma
        self.sem = None
        self.val = 0
        self.waits = []
        self.needs_inc = False


class Sched:
    def __init__(self):
        self.ops = {e: [] for e in ENGS}
        self.last_w = {}
        self.readers = {}
        self.all_ops = []
        self.epoch = 0

    def _deps(self, op, reads, writes):
        deps = []
        raw = set()
        for b in reads:
            w = self.last_w.get(b)
            if w is not None:
                deps.append(w)
                raw.add(id(w))
        for b in writes:
            w = self.last_w.get(b)
            if w is not None:
                deps.append(w)
            deps.extend(self.readers.get(b, {}).values())
        for b in reads:
            self.readers.setdefault(b, {})[op.sem] = op
        for b in writes:
            self.last_w[b] = op
            self.readers[b] = {}
        return deps, raw

    def barrier(self):
        last = {}
        for o in self.all_ops:
            last[o.sem] = o
        self.pending = {e: list(last.values()) for e in ENGS}

    def _pend(self, o, eng):
        pend = getattr(self, "pending", None)
        if pend and pend.get(eng):
            for d in pend[eng]:
                if d.dma is None and d.eng == eng:
                    continue
                d.needs_inc = True
                o.waits.append(d)
            pend[eng] = []

    def op(self, eng, fn, reads=(), writes=()):
        o = _Op(eng, fn, None)
        o.sem = ("eng", eng, self.epoch)
        self._pend(o, eng)
        deps, raw = self._deps(o, reads, writes)
        for d in deps:
            if d is o:
                continue
            if d.dma is None and d.eng == eng:
                if eng == "pe" or id(d) not in raw:
                    continue
            d.needs_inc = True
            o.waits.append(d)
        self.ops[eng].append(o)
        self.all_ops.append(o)
        return o

    def dma(self, queue, stream, fn, reads=(), writes=()):
        o = _Op(queue, fn, stream)
        o.sem = ("dma", stream)
        o.needs_inc = True
        self._pend(o, queue)
        deps, _ = self._deps(o, reads, writes)
        for d in deps:
            if d is o:
                continue
            d.needs_inc = True
            o.waits.append(d)
        self.ops[queue].append(o)
        self.all_ops.append(o)
        return o

    def new_epoch(self):
        self.epoch += 1

    def finalize(self):
        counts = {}
        for o in self.all_ops:
            if o.needs_inc:
                step = 16 if o.dma is not None else 1
                counts[o.sem] = counts.get(o.sem, 0) + step
                o.val = counts[o.sem]
        self.sem_keys = list(counts.keys())
        for k, v in counts.items():
            assert v < 60000, (k, v)
        return counts

    def emit(self, block, sems, final_waits=()):
        handles = {"pe": "tensor", "act": "scalar", "dve": "vector", "pool": "gpsimd", "sp": "sync"}

        def make(engname):
            ops = self.ops[engname]

            def body(eng):
                waited = {}

                def wait(d):
                    if waited.get(d.sem, 0) >= d.val:
                        return
                    eng.wait_ge(sems[d.sem], d.val)
                    waited[d.sem] = d.val

                for o in ops:
                    for d in o.waits:
                        wait(d)
                    ins = o.fn(eng)
                    if o.needs_inc:
                        ins.then_inc(sems[o.sem], 16 if o.dma is not None else 1)
                if engname == "sp":
                    for d in final_waits:
                        wait(d)
            return body

        for engname in ENGS:
            getattr(block, handles[engname])(make(engname))


class Prog:
    def __init__(self):
        self.nc = bass.Bass("TRN2", target_bir_lowering=False)
        self.S = Sched()
        self.cap = 212000
        self.arena = self.nc.alloc_sbuf_tensor("arena", [128, self.cap // 2], BF16)[:]
        self.off = 0
        self.uid = 0
        self.ps_f32 = [self.nc.alloc_psum_tensor("psb%d" % i, [128, 512], F32)[:] for i in range(8)]
        self.ps_b16 = [p.bitcast(BF16) for p in self.ps_f32]
        self.final = []

    def inp(self, name, shape, dtype=F32):
        return self.nc.dram_tensor(name, list(shape), dtype, kind="ExternalInput").ap()

    def outp(self, name, shape, dtype=F32):
        return self.nc.dram_tensor(name, list(shape), dtype, kind="ExternalOutput").ap()

    def carve(self, cols, dtype):
        esz = 4 if dtype in (F32, I32) else 2
        nb = (cols * esz + 63) // 64 * 64
        assert self.off + nb <= self.cap, ("SBUF arena overflow", self.off, nb)
        a = self.arena[:, self.off // 2:(self.off + nb) // 2]
        self.off += nb
        if dtype != BF16:
            a = a.bitcast(dtype)
        return a[:, 0:cols]

    def key(self, base):
        self.uid += 1
        return "%s#%d" % (base, self.uid)

    def rsqrt(self, a_key, a, out_key, out, tmp_key, tmp):
        S = self.S
        S.op("dve", lambda e: e.tensor_scalar(out=out.bitcast(I32), in0=a.bitcast(I32), scalar1=1, scalar2=None,
                                              op0=ALU.arith_shift_right), reads=[a_key], writes=[out_key])
        S.op("dve", lambda e: e.tensor_scalar(out=out.bitcast(I32), in0=out.bitcast(I32), scalar1=0x5f3759df,
                                              scalar2=-1, op0=ALU.subtract, op1=ALU.mult),
             reads=[out_key], writes=[out_key])
        for _ in range(2):
            S.op("dve", lambda e: e.scalar_tensor_tensor(out=tmp, in0=out, scalar=-0.5, in1=out, op0=ALU.mult,
                                                         op1=ALU.mult), reads=[out_key], writes=[tmp_key])
            S.op("dve", lambda e: e.tensor_tensor(out=tmp, in0=tmp, in1=a, op=ALU.mult),
                 reads=[tmp_key, a_key], writes=[tmp_key])
            S.op("dve", lambda e: e.scalar_tensor_tensor(out=out, in0=tmp, scalar=1.5, in1=out, op0=ALU.add,
                                                         op1=ALU.mult), reads=[tmp_key, out_key], writes=[out_key])

    def layer_a(self, x_src, x_dst, ntok, w_in, npre_col, ln_g, ln_b, w_s, bs_col, w_out, npost, ident_d, tril_d):
        S, nc = self.S, self.nc
        off0 = self.off
        NT = ntok // 128
        L = self.key("A")
        K = lambda s: "%s/%s" % (L, s)

        Wb = [self.carve(6144, BF16) for _ in range(8)]
        Wo = [self.carve(1024, BF16) for _ in range(16)]
        lng = self.carve(2048, BF16)
        lnb = self.carve(2048, BF16)
        gpo = self.carve(1024, F32)
        WmT = self.carve(1024, BF16)
        bsc = self.carve(8, F32)
        gpre = self.carve(8, F32)
        idt = self.carve(128, BF16)
        tril = self.carve(128, F32)
        xt = [self.carve(1024, F32) for _ in range(3)]
        xb = self.carve(1024, BF16)
        xT = [self.carve(1024, BF16) for _ in range(2)]
        junk = self.carve(1024, BF16)
        u = self.carve(2048, BF16)
        sz = self.carve(2048, BF16)
        v = self.carve(2048, F32)
        vn = [self.carve(2048, BF16) for _ in range(2)]
        y = self.carve(2048, BF16)
        yT = self.carve(2048, BF16)
        tpost = self.carve(1024, F32)
        st = self.carve(64, F32)
        ss, ms, tmp1 = st[:, 0:1], st[:, 1:2], st[:, 3:4]
        rstd = [st[:, 4:5], st[:, 5:6]]
        bst = st[:, 8:32]
        mv = st[:, 32:34]
        va, rsv, tmp2 = st[:, 34:35], st[:, 35:36], st[:, 36:37]
        ss2, a2, rstd2, tmp3 = st[:, 40:42], st[:, 42:43], st[:, 43:44], st[:, 44:45]
        stage = [v[:, 0:1024], v[:, 1024:2048]]
        VK = [K("v0"), K("v1")]
        P_TX, P_IN, P_SV, P_TY, P_O = 0, (1, 2), (3, 4), 5, (6, 7)
        psf, psb = self.ps_f32, self.ps_b16

        S.dma("sp", "idt", lambda e: e.dma_start(out=idt, in_=ident_d), writes=[K("idt")])
        S.dma("sp", "tril", lambda e: e.dma_start(out=tril, in_=tril_d), writes=[K("tril")])
        S.dma("sp", "gpre", lambda e: e.dma_start(out=gpre, in_=npre_col), writes=[K("gpre")])
        S.dma("sp", "bsc", lambda e: e.dma_start(out=bsc, in_=bs_col), writes=[K("bsc")])
        S.dma("sp", "gpo", lambda e: e.dma_start(out=gpo, in_=npost.partition_broadcast(128)), writes=[K("gpo")])
        cnt = [0]

        def staged(src_ap, cols, consume, view=None):
            i = cnt[0] % 2
            cnt[0] += 1
            sb = stage[i][:, 0:cols]
            sbd = view(sb) if view is not None else sb
            S.dma("sp", "stg%d" % i, lambda e: e.dma_start(out=sbd, in_=src_ap), writes=[VK[i]])
            consume(sb, "act" if i == 0 else "dve", VK[i])

        for k in range(8):
            for q in range(6):
                dst = Wb[k][:, q * 1024:(q + 1) * 1024]

                def cons(sb, eng, skey, dst=dst, k=k):
                    if eng == "act":
                        S.op("act", lambda e: e.activation(out=dst, in_=sb, func=AF.Copy, scale=gpre[:, k:k + 1]),
                             reads=[skey, K("gpre")], writes=[K("Wb%d" % k)])
                    else:
                        S.op("dve", lambda e: e.tensor_scalar(out=dst, in0=sb, scalar1=gpre[:, k:k + 1], scalar2=None,
                                                              op0=ALU.mult), reads=[skey, K("gpre")], writes=[K("Wb%d" % k)])
                staged(w_in[k * 128:(k + 1) * 128, q * 1024:(q + 1) * 1024], 1024, cons)
        for k in range(16):
            def cons(sb, eng, skey, k=k):
                if eng == "act":
                    S.op("act", lambda e: e.activation(out=Wo[k], in_=sb, func=AF.Copy), reads=[skey], writes=[K("Wo")])
                else:
                    S.op("dve", lambda e: e.tensor_copy(out=Wo[k], in_=sb), reads=[skey], writes=[K("Wo")])
            staged(w_out[k * 128:(k + 1) * 128, :], 1024, cons)
        for (src, dstt, nm) in ((ln_g, lng, "lng"), (ln_b, lnb, "lnb")):
            for h in range(2):
                def cons(sb, eng, skey, dstt=dstt, h=h, nm=nm):
                    if eng == "act":
                        S.op("act", lambda e: e.activation(out=dstt[:, h * 1024:(h + 1) * 1024], in_=sb, func=AF.Copy),
                             reads=[skey], writes=[K(nm)])
                    else:
                        S.op("dve", lambda e: e.tensor_copy(out=dstt[:, h * 1024:(h + 1) * 1024], in_=sb),
                             reads=[skey], writes=[K(nm)])
                staged(src[:, h * 1024:(h + 1) * 1024].partition_broadcast(128), 1024, cons)

        def cons_ws(sb, eng, skey):
            sbv = sb.rearrange("p (g s) -> p g s", g=8)
            yv = y[:, 0:1024].rearrange("p (g s) -> p g s", g=8)
            for g in range(8):
                S.op("dve", lambda e, g=g: e.tensor_tensor(out=yv[:, g, :], in0=sbv[:, g, :], in1=tril, op=ALU.mult),
                     reads=[skey, K("tril")], writes=[K("y")])
            for g in range(8):
                S.op("pe", lambda e, g=g: e.transpose(out=psb[P_TX][:, g * 128:(g + 1) * 128], in_=yv[:, g, :], identity=idt),
                     reads=[K("y"), K("idt")], writes=[K("ps%d" % P_TX)])
            S.op("dve", lambda e: e.tensor_copy(out=WmT, in_=psb[P_TX]), reads=[K("ps%d" % P_TX)], writes=[K("WmT")])
        staged(w_s.rearrange("g t s -> t g s"), 1024, cons_ws, view=lambda sb: sb.rearrange("p (g s) -> p g s", g=8))

        def load(i):
            p = i % 3
            S.dma("sp", "xl%d" % p, lambda e: e.dma_start(out=xt[p], in_=x_src[i * 128:(i + 1) * 128, :]),
                  writes=[K("xt%d" % p)])

        def inproj(q, banks, r0):
            r = r0
            for n in banks:
                b = P_IN[r % 2]
                r += 1
                for k in range(8):
                    S.op("pe", lambda e, k=k, n=n, b=b: e.matmul(psf[b], lhsT=xT[q][:, k * 128:(k + 1) * 128],
                                                                rhs=Wb[k][:, n * 512:(n + 1) * 512],
                                                                start=(k == 0), stop=(k == 7)),
                         reads=[K("xT%d" % q), K("Wb%d" % k)], writes=[K("ps%d" % b)])
                if n < 4:
                    dst, func, dk = u[:, n * 512:(n + 1) * 512], AF.Gelu, [K("u%d" % n)]
                elif n < 8:
                    dst, func, dk = v[:, (n - 4) * 512:(n - 3) * 512], AF.Gelu, VK
                else:
                    dst, func, dk = sz[:, (n - 8) * 512:(n - 7) * 512], AF.Silu, [K("sz%d" % (n - 8))]
                S.op("act", lambda e, dst=dst, func=func, b=b: e.activation(out=dst, in_=psf[b], func=func, scale=rstd[q]),
                     reads=[K("ps%d" % b), K("rstd%d" % q)], writes=dk)

        def pre(i):
            p, q = i % 3, i % 2
            xk = K("xt%d" % p)
            S.op("act", lambda e: e.activation(out=junk, in_=xt[p], func=AF.Square, accum_out=ss),
                 reads=[xk], writes=[K("junk"), K("ss")])
            S.op("dve", lambda e: e.tensor_scalar(out=ms, in0=ss, scalar1=1.0 / D, scalar2=RMS_EPS, op0=ALU.mult,
                                                  op1=ALU.add), reads=[K("ss")], writes=[K("ms")])
            self.rsqrt(K("ms"), ms, K("rstd%d" % q), rstd[q], K("tmp1"), tmp1)
            S.op("pool", lambda e: e.tensor_copy(out=xb, in_=xt[p]), reads=[xk], writes=[K("xb")])

        def stage1(i):
            q = i % 2
            for k in range(8):
                S.op("pe", lambda e, k=k: e.transpose(out=psb[P_TX][:, k * 128:(k + 1) * 128],
                                                      in_=xb[:, k * 128:(k + 1) * 128], identity=idt),
                     reads=[K("xb"), K("idt")], writes=[K("ps%d" % P_TX)])
            S.op("act", lambda e: e.activation(out=xT[q], in_=psb[P_TX], func=AF.Copy),
                 reads=[K("ps%d" % P_TX)], writes=[K("xT%d" % q)])
            inproj(q, (4, 5, 6, 7), 0)
            for c in range(4):
                S.op("dve", lambda e, c=c: e.bn_stats(out=bst[:, c * 6:(c + 1) * 6], in_=v[:, c * 512:(c + 1) * 512]),
                     reads=VK, writes=[K("bst")])
            S.op("dve", lambda e: e.bn_aggr(out=mv, in_=bst.rearrange("p (c s) -> p c s", c=4)),
                 reads=[K("bst")], writes=[K("mv")])
            S.op("dve", lambda e: e.tensor_scalar(out=va, in0=mv[:, 1:2], scalar1=LN_EPS, scalar2=None,
                                                  op0=ALU.add), reads=[K("mv")], writes=[K("va")])
            self.rsqrt(K("va"), va, K("rsv"), rsv, K("tmp2"), tmp2)
            S.op("dve", lambda e: e.scalar_tensor_tensor(out=v, in0=v, scalar=mv[:, 0:1], in1=lng,
                                                         op0=ALU.subtract, op1=ALU.mult),
                 reads=VK + [K("mv"), K("lng")], writes=VK)
            S.op("dve", lambda e: e.scalar_tensor_tensor(out=vn[q], in0=v, scalar=rsv, in1=lnb,
                                                         op0=ALU.mult, op1=ALU.add),
                 reads=VK + [K("rsv"), K("lnb")], writes=[K("vn%d" % q)])

        def stage2a(i):
            q = i % 2
            inproj(q, (0, 1, 2, 3, 8, 9, 10, 11), 0)
            for qd in range(4):
                b = P_SV[qd % 2]
                for gg in range(2):
                    g = 2 * qd + gg
                    S.op("pe", lambda e, g=g, gg=gg, b=b: e.matmul(psf[b][:, gg * 256:(gg + 1) * 256],
                                                                  lhsT=WmT[:, g * 128:(g + 1) * 128],
                                                                  rhs=vn[q][:, g * 256:(g + 1) * 256], start=True, stop=True),
                         reads=[K("vn%d" % q), K("WmT")], writes=[K("ps%d" % b)])
                for gg in range(2):
                    g = 2 * qd + gg
                    S.op("dve", lambda e, g=g, gg=gg, b=b: e.scalar_tensor_tensor(
                        out=y[:, g * 256:(g + 1) * 256], in0=psf[b][:, gg * 256:(gg + 1) * 256], scalar=bsc[:, g:g + 1],
                        in1=u[:, g * 256:(g + 1) * 256], op0=ALU.add, op1=ALU.mult),
                        reads=[K("ps%d" % b), K("bsc"), K("u%d" % qd)], writes=[K("y%d" % qd)])
            S.op("dve", lambda e: e.tensor_tensor(out=y, in0=y, in1=sz, op=ALU.mult),
                 reads=[K("y")] + [K("y%d" % c) for c in range(4)] + [K("sz%d" % c) for c in range(4)],
                 writes=[K("y")] + [K("y%d" % c) for c in range(4)])

        def stage2b(i):
            p = i % 3
            xk = K("xt%d" % p)
            for h in range(2):
                for k in range(8):
                    kk = h * 8 + k
                    S.op("pe", lambda e, k=k, kk=kk: e.transpose(out=psb[P_TY][:, k * 128:(k + 1) * 128],
                                                                in_=y[:, kk * 128:(kk + 1) * 128], identity=idt),
                         reads=[K("y"), K("idt")] + [K("y%d" % c) for c in range(4)], writes=[K("ps%d" % P_TY)])
                S.op("act", lambda e, h=h: e.activation(out=yT[:, h * 1024:(h + 1) * 1024], in_=psb[P_TY], func=AF.Copy),
                     reads=[K("ps%d" % P_TY)], writes=[K("yT")])
            for n in range(2):
                b = P_O[n]
                for k in range(16):
                    S.op("pe", lambda e, k=k, n=n, b=b: e.matmul(psf[b], lhsT=yT[:, k * 128:(k + 1) * 128],
                                                                rhs=Wo[k][:, n * 512:(n + 1) * 512],
                                                                start=(k == 0), stop=(k == 15)),
                         reads=[K("yT"), K("Wo")], writes=[K("ps%d" % b)])
                S.op("act", lambda e, n=n, b=b: e.activation(out=junk[:, 0:512], in_=psf[b], func=AF.Square,
                                                             accum_out=ss2[:, n:n + 1]),
                     reads=[K("ps%d" % b)], writes=[K("junk"), K("ss2")])
            S.op("dve", lambda e: e.tensor_tensor(out=a2, in0=ss2[:, 0:1], in1=ss2[:, 1:2], op=ALU.add),
                 reads=[K("ss2")], writes=[K("a2")])
            S.op("dve", lambda e: e.tensor_scalar(out=a2, in0=a2, scalar1=1.0 / D, scalar2=RMS_EPS, op0=ALU.mult,
                                                  op1=ALU.add), reads=[K("a2")], writes=[K("a2")])
            self.rsqrt(K("a2"), a2, K("rstd2"), rstd2, K("tmp3"), tmp3)
            for n in range(2):
                b = P_O[n]
                S.op("dve", lambda e, n=n, b=b: e.scalar_tensor_tensor(out=tpost[:, n * 512:(n + 1) * 512], in0=psf[b],
                                                                       scalar=rstd2, in1=gpo[:, n * 512:(n + 1) * 512],
                                                                       op0=ALU.mult, op1=ALU.mult),
                     reads=[K("ps%d" % b), K("rstd2"), K("gpo")], writes=[K("tpost")])
            S.op("pool", lambda e: e.tensor_tensor(out=xt[p], in0=xt[p], in1=tpost, op=ALU.add),
                 reads=[xk, K("tpost")], writes=[xk])
            o = S.dma("pool", "xs%d" % p, lambda e: e.dma_start(out=x_dst[i * 128:(i + 1) * 128, :], in_=xt[p]),
                      reads=[xk], writes=[K("xdst")])
            self.final.append(o)

        load(0)
        if NT > 1:
            load(1)
        pre(0)
        stage1(0)
        for i in range(NT):
            if i + 2 < NT:
                load(i + 2)
            if i + 1 < NT:
                pre(i + 1)
            stage2a(i)
            if i + 1 < NT:
                stage1(i + 1)
            stage2b(i)
        S.barrier()
        self.peak = max(getattr(self, "peak", 0), self.off)
        self.off = off0


    def layer_b(self, x_src, x_halo, x_dst, ysc, vsc, w_in, gpre_d, w_out, npost, hv_d, tabs, ident_d, rm_d, mask_d,
                NOWN=4096, dbg_hps=range(8), dbg_gs=range(3), dbg_attn=True):
        S, nc = self.S, self.nc
        off0 = self.off
        L = self.key("B")
        K = lambda s: "%s/%s" % (L, s)
        psf, psb = self.ps_f32, self.ps_b16
        HAL = 2048
        NT = HAL + NOWN
        DIL = (1, 4, 16)
        P_PR, P_RT, P_S2, P_ND, P_V2 = (0, 1), 2, ((3, 4), (0, 1), (2, 7)), (5, 6), (7, 3)

        hT = self.carve(8 * NT, BF16).rearrange("p (k t) -> p k t", k=8)
        offq = self.off
        QT = self.carve(NOWN, BF16)
        KT = self.carve(NT, BF16)
        Vb = self.carve(NT, BF16)
        acc = self.carve(2 * NOWN, F32).rearrange("p (a t) -> p a t", a=2)
        offw = self.off
        wst = self.carve(3 * 1024, F32)
        wb = [self.carve(3 * 1024, BF16) for _ in range(2)]
        idt = self.carve(128, BF16)
        rm = self.carve(128, BF16)
        maskT = self.carve(512, BF16)
        ones = self.carve(64, BF16)
        hvones = self.carve(64, BF16)
        loones = self.carve(64, BF16)
        hv2 = self.carve(2, F32)
        hv, lo = hv2[:, 0:1], hv2[:, 1:2]
        PT = [self.carve(512, BF16) for _ in range(6)]
        qraw = [self.carve(512, BF16) for _ in range(3)]
        rt1 = [self.carve(512, BF16) for _ in range(3)]
        ctab = [self.carve(512, BF16) for _ in range(3)]
        stab = [self.carve(512, BF16) for _ in range(3)]
        st = self.carve(64, F32)
        print("layer B persistent SBUF bytes", self.off - off0)
        offp = self.off

        S.dma("sp", "idt", lambda e: e.dma_start(out=idt, in_=ident_d), writes=[K("idt")])
        S.dma("sp", "rm", lambda e: e.dma_start(out=rm, in_=rm_d), writes=[K("rm")])
        S.dma("sp", "mask", lambda e: e.dma_start(out=maskT, in_=mask_d), writes=[K("mask")])
        S.dma("sp", "hv", lambda e: e.dma_start(out=hv2, in_=hv_d), writes=[K("hv")])
        S.op("pool", lambda e: e.memset(ones, 1.0), writes=[K("ones")])
        S.op("pool", lambda e: e.memset(hvones, 1.0), writes=[K("hvones")])
        S.op("pool", lambda e: e.tensor_scalar(out=hvones, in0=hvones, scalar1=hv, scalar2=None, op0=ALU.mult),
             reads=[K("hv"), K("hvones")], writes=[K("hvones")])
        S.op("pool", lambda e: e.memset(loones, 1.0), writes=[K("loones")])
        S.op("pool", lambda e: e.tensor_scalar(out=loones, in0=loones, scalar1=lo, scalar2=None, op0=ALU.mult),
             reads=[K("hv"), K("loones")], writes=[K("loones")])

        need1 = 8 * 4096 + 8 * 2048 + 4096 + 2048 + 256
        def place(need):
            if offp + need <= self.cap:
                self.off = offp
            else:
                assert offq + need <= offw, ("overlay does not fit", need, offw - offq)
                self.off = offq
        place(need1)
        gb = self.carve(1024, F32)
        xt = [self.carve(1024, F32) for _ in range(8)]
        hb = [self.carve(1024, BF16) for _ in range(8)]
        junk = self.carve(1024, BF16)
        st1 = self.carve(32, F32)
        S.dma("sp", "gb", lambda e: e.dma_start(out=gb, in_=gpre_d.partition_broadcast(128)), writes=[K("gb")])

        def xrows(i):
            return x_halo[i * 128:(i + 1) * 128, :] if i < 16 else x_src[(i - 16) * 128:(i - 15) * 128, :]

        def p1_load(i):
            p = i % 8
            S.dma("sp", "x1l%d" % p, lambda e: e.dma_start(out=xt[p], in_=xrows(i)), writes=[K("x1t%d" % p)])

        NT1 = NT // 128
        NG1 = NT1 // 4

        def p1_x(gi):
            q = gi % 2
            ss, ms, rstd, tmp1 = (st1[:, q * 16 + c * 4:q * 16 + c * 4 + 4] for c in range(4))
            for t in range(4):
                i = gi * 4 + t
                p = i % 8
                S.op("act", lambda e, p=p, t=t, ss=ss: e.activation(out=junk, in_=xt[p], func=AF.Square, accum_out=ss[:, t:t + 1]),
                     reads=[K("x1t%d" % p)], writes=[K("junk"), K("ss%d" % q)])
            S.op("dve", lambda e: e.tensor_scalar(out=ms, in0=ss, scalar1=1.0 / D, scalar2=RMS_EPS, op0=ALU.mult,
                                                  op1=ALU.add), reads=[K("ss%d" % q)], writes=[K("ms%d" % q)])
            self.rsqrt(K("ms%d" % q), ms, K("rstd%d" % q), rstd, K("tmp1%d" % q), tmp1)
            for t in range(4):
                i = gi * 4 + t
                p = i % 8
                S.op("dve", lambda e, p=p, t=t, rstd=rstd: e.scalar_tensor_tensor(out=hb[p], in0=xt[p], scalar=rstd[:, t:t + 1],
                                                                                  in1=gb, op0=ALU.mult, op1=ALU.mult),
                     reads=[K("x1t%d" % p), K("rstd%d" % q), K("gb")], writes=[K("hb%d" % p)])

        def p1_y(gi):
            for t in range(4):
                i = gi * 4 + t
                p = i % 8
                pb = P_PR[i % 2]
                for k in range(8):
                    S.op("pe", lambda e, k=k, p=p, pb=pb: e.transpose(out=psb[pb][:, k * 128:(k + 1) * 128],
                                                                in_=hb[p][:, k * 128:(k + 1) * 128], identity=idt),
                         reads=[K("hb%d" % p), K("idt")], writes=[K("ps%d" % pb)])
                S.op("act", lambda e, i=i, pb=pb: e.activation(out=hT[:, :, i * 128:(i + 1) * 128],
                                                               in_=psb[pb].rearrange("p (k t) -> p k t", k=8), func=AF.Copy),
                     reads=[K("ps%d" % pb)], writes=[K("hT")])

        for i in range(min(8, NT1)):
            p1_load(i)
        p1_x(0)
        for gi in range(NG1):
            if gi + 1 < NG1:
                p1_x(gi + 1)
            p1_y(gi)
            for t in range(4):
                i = (gi + 2) * 4 + t
                if i < NT1:
                    p1_load(i)
        S.barrier()

        KBG = 8 if NOWN >= 4096 else 4
        need15 = 2 * 8 * 1024 * 2 + 2 * KBG * 1024 * 2
        place(need15)
        Wv2 = [self.carve(8 * 1024, BF16).rearrange("p (k c) -> p k c", k=8) for _ in range(2)]
        vstg = [self.carve(KBG * 1024, BF16).rearrange("p (b c) -> p b c", b=KBG) for _ in range(2)]
        vcnt = [0]
        def load_wv(gidx):
            g = list(dbg_gs)[gidx]
            Wv = Wv2[gidx % 2]
            for k in range(8):
                slot = k % 3
                stg = wst[:, slot * 1024:(slot + 1) * 1024]
                S.dma("sp", "wvs%d" % slot, lambda e, k=k, g=g, stg=stg: e.dma_start(
                    out=stg, in_=w_in[k * 128:(k + 1) * 128, g * 3072 + 2048:g * 3072 + 3072]), writes=[K("wst%d" % slot)])
                if k % 2 == 0:
                    S.op("act", lambda e, k=k, stg=stg, Wv=Wv: e.activation(out=Wv[:, k, :], in_=stg, func=AF.Copy),
                         reads=[K("wst%d" % slot)], writes=[K("Wv%d" % (gidx % 2))])
                else:
                    S.op("dve", lambda e, k=k, stg=stg, Wv=Wv: e.tensor_copy(out=Wv[:, k, :], in_=stg), reads=[K("wst%d" % slot)],
                         writes=[K("Wv%d" % (gidx % 2))])

        if len(list(dbg_gs)) > 0:
            load_wv(0)
        for gidx, g in enumerate(dbg_gs):
            d = DIL[g]
            span = 128 * d
            nkb = (NOWN // span + 1) * d
            kt0 = HAL - span
            Wv = Wv2[gidx % 2]
            wvkey = K("Wv%d" % (gidx % 2))
            if gidx + 1 < len(list(dbg_gs)):
                load_wv(gidx + 1)
            for kb0 in range(0, nkb, KBG):
                nb = min(KBG, nkb - kb0)
                vs = vcnt[0] % 2
                vcnt[0] += 1
                for bl in range(nb):
                    kb = kb0 + bl
                    sp_, r = kb // d, kb % d
                    t_start = kt0 + sp_ * span + r
                    banks = P_PR if kb % 2 == 0 else (P_RT, P_V2[0])
                    for k in range(8):
                        for n in range(2):
                            S.op("pe", lambda e, k=k, n=n, t_start=t_start, d=d, bk=banks[n], Wv=Wv: e.matmul(
                                psf[bk], lhsT=hT[:, k, t_start:t_start + 127 * d + 1:d], rhs=Wv[:, k, n * 512:(n + 1) * 512],
                                start=(k == 0), stop=(k == 7)), reads=[wvkey, K("hT")], writes=[K("ps%d" % banks[n])])
                    c = 0 if kb < d else (1 if kb < d + 16 else 2)
                    fl = hv if c == 0 else lo
                    for n in range(2):
                        dst = vstg[vs][:, bl, n * 512:(n + 1) * 512]
                        bk = banks[n]
                        if n == 0:
                            if c == 2:
                                S.op("act", lambda e, dst=dst, bk=bk: e.activation(out=dst, in_=psf[bk], func=AF.Copy),
                                     reads=[K("ps%d" % bk)], writes=[K("vstg%d" % vs)])
                            else:
                                S.op("act", lambda e, dst=dst, bk=bk, fl=fl: e.activation(out=dst, in_=psf[bk], func=AF.Copy,
                                                                                        scale=fl),
                                     reads=[K("ps%d" % bk), K("hv")], writes=[K("vstg%d" % vs)])
                        else:
                            if c == 2:
                                S.op("dve", lambda e, dst=dst, bk=bk: e.tensor_copy(out=dst, in_=psf[bk]),
                                     reads=[K("ps%d" % bk)], writes=[K("vstg%d" % vs)])
                            else:
                                S.op("dve", lambda e, dst=dst, bk=bk, fl=fl: e.tensor_scalar(out=dst, in0=psf[bk], scalar1=fl,
                                                                                          scalar2=None, op0=ALU.mult),
                                     reads=[K("ps%d" % bk), K("hv")], writes=[K("vstg%d" % vs)])
                for h8 in range(8):
                    S.dma("sp", "vst%d" % vs, lambda e, g=g, kb0=kb0, nb=nb, vs=vs, h8=h8: e.dma_start(
                        out=vsc[g, h8, :, kb0:kb0 + nb, :], in_=vstg[vs][:, 0:nb, h8 * 128:(h8 + 1) * 128]),
                        reads=[K("vstg%d" % vs)], writes=[K("vsc")])
        S.barrier()

        w5 = w_in[:, 0:9216].rearrange("(k p) (g t h f) -> p k g t h f", p=128, g=3, t=3, h=8)
        wz = w_in[:, 9216:10240].rearrange("(k p) (h f) -> p k h f", p=128, h=8)
        tabi = [0]
        bankc = [0]
        wcnt = [0]

        pending_cast = []

        def flush_cast():
            while pending_cast:
                pending_cast.pop(0)()

        wjobs = []
        for hp_ in dbg_hps:
            for g_ in dbg_gs:
                wjobs.append((w5[:, :, g_, :, hp_, :], 384))
            wjobs.append((wz[:, :, hp_, :], 128))
        wready = {}

        def load_w(src_ap=None, ncols=None):
            i = wcnt[0]
            wcnt[0] += 1
            if i == 0:
                wready[0] = _load_w(0, *wjobs[0])
            if i + 1 < len(wjobs):
                wready[i + 1] = _load_w(i + 1, *wjobs[i + 1])
            return wready.pop(i)

        def _load_w(i, src_ap, ncols):
            slot = i % 2
            dst32 = wst[:, 0:8 * ncols]
            if len(src_ap.shape) == 3:
                S.dma("sp", "wst", lambda e: e.dma_start(out=dst32.rearrange("p (k c) -> p k c", k=8), in_=src_ap),
                      writes=[K("wst")])
            else:
                d4 = dst32.rearrange("p (k t f) -> p k t f", k=8, t=3)
                for t in range(3):
                    S.dma("sp", "wst", lambda e, t=t: e.dma_start(out=d4[:, :, t, :], in_=src_ap[:, :, t, :]),
                          writes=[K("wst")])
            def cast():
                S.op("act", lambda e: e.activation(out=wb[slot][:, 0:8 * ncols], in_=dst32, func=AF.Copy), reads=[K("wst")],
                     writes=[K("wb%d" % slot)])
            if i == 0:
                cast()
            else:
                pending_cast.append(cast)
            return wb[slot][:, 0:8 * ncols].rearrange("p (k c) -> p k c", k=8), K("wb%d" % slot)

        def proj_bank(wv, wkey, c0, tau0, n):
            b = P_PR[bankc[0] % 2]
            bankc[0] += 1
            for k in range(8):
                S.op("pe", lambda e, k=k, b=b: e.matmul(psf[b][:, 0:n], lhsT=wv[:, k, c0:c0 + 128], rhs=hT[:, k, tau0:tau0 + n],
                                                       start=(k == 0), stop=(k == 7)),
                     reads=[wkey, K("hT")], writes=[K("ps%d" % b)])
            flush_rot()
            return b

        pending_rot = []

        def flush_rot():
            while pending_rot:
                pending_rot.pop(0)()

        def rope_bank(b, n, d, dest, dkey, ctd, std, t0):
            j = tabi[0] % 3
            tabi[0] += 1
            S.dma("sp", "ct%d" % j, lambda e: e.dma_start(out=ctab[j][:, 0:n], in_=ctd[:, t0:t0 + n]), writes=[K("ctab%d" % j)])
            S.dma("sp", "st%d" % j, lambda e: e.dma_start(out=stab[j][:, 0:n], in_=std[:, t0:t0 + n]), writes=[K("stab%d" % j)])
            if d == 1 or n < 512:
                ov, iv = qraw[j][:, 0:n], psf[b][:, 0:n]
            else:
                ov = qraw[j].rearrange("p (r i) -> p r i", r=d)
                iv = psf[b].rearrange("p (i r) -> p r i", r=d)
            S.op("act", lambda e: e.activation(out=ov, in_=iv, func=AF.Copy), reads=[K("ps%d" % b)], writes=[K("qraw%d" % j)])

            def second():
                S.op("pe", lambda e: e.matmul(psf[P_RT][:, 0:n], lhsT=rm, rhs=qraw[j][:, 0:n], start=True, stop=True),
                     reads=[K("rm"), K("qraw%d" % j)], writes=[K("ps%d" % P_RT)])
                S.op("dve", lambda e: e.tensor_tensor(out=rt1[j][:, 0:n], in0=psf[P_RT][:, 0:n], in1=stab[j][:, 0:n], op=ALU.mult),
                     reads=[K("ps%d" % P_RT), K("stab%d" % j)], writes=[K("rt1%d" % j)])
                S.op("pool", lambda e: e.tensor_tensor(out=qraw[j][:, 0:n], in0=qraw[j][:, 0:n], in1=ctab[j][:, 0:n], op=ALU.mult),
                     reads=[K("qraw%d" % j), K("ctab%d" % j)], writes=[K("qraw%d" % j)])
                if d == 16:
                    a0 = rt1[j].rearrange("p (r i) -> p r i", r=16)
                    a1 = qraw[j].rearrange("p (r i) -> p r i", r=16)
                else:
                    a0, a1 = rt1[j][:, 0:n], qraw[j][:, 0:n]
                S.op("dve", lambda e: e.tensor_tensor(out=dest, in0=a0, in1=a1, op=ALU.add),
                     reads=[K("rt1%d" % j), K("qraw%d" % j)], writes=[dkey])
            pending_rot.append(second)

        ptc = [0]
        sc = [0]
        ndc = [0]
        pending_tail = []

        def flush_tail():
            while pending_tail:
                pending_tail.pop(0)()

        for hp in dbg_hps:
            if not pending_tail:
                S.op("pool", lambda e: e.memset(acc.rearrange("p a t -> p (a t)"), 0.0), writes=[K("acc")])
            for g in dbg_gs:
                d = DIL[g]
                span = 128 * d
                nsp = NOWN // span
                nkb = (nsp + 1) * d
                kt0 = HAL - span
                nk = NOWN + span
                cq, sq, ck, sk = tabs[g]
                wv, wkey = load_w(w5[:, :, g, :, hp, :], 384)
                for bq in range(NOWN // 512):
                    if bq == 3 and pending_tail:
                        flush_tail()
                        S.op("pool", lambda e: e.memset(acc.rearrange("p a t -> p (a t)"), 0.0), writes=[K("acc")])
                    b = proj_bank(wv, wkey, 0, HAL + bq * 512, 512)
                    if d == 16:
                        sp_, i0 = (bq * 512) // span, ((bq * 512) % span) // 16
                        dest = QT[:, sp_ * span:(sp_ + 1) * span].rearrange("p (r i) -> p r i", r=16)[:, :, i0:i0 + 32]
                        rope_bank(b, 512, d, dest, K("QT"), cq, sq, bq * 512)
                    else:
                        rope_bank(b, 512, d, QT[:, bq * 512:(bq + 1) * 512], K("QT"), cq, sq, bq * 512)
                nb_full, rem = nk // 512, nk % 512
                for bk in range(nb_full + (1 if rem else 0)):
                    n = 512 if bk < nb_full else rem
                    b = proj_bank(wv, wkey, 128, kt0 + bk * 512, n)
                    if d == 16:
                        sp_, i0 = (bk * 512) // span, ((bk * 512) % span) // 16
                        dest = KT[:, sp_ * span:(sp_ + 1) * span].rearrange("p (r i) -> p r i", r=16)[:, :, i0:i0 + 32]
                        rope_bank(b, 512, d, dest, K("KT"), ck, sk, bk * 512)
                    else:
                        rope_bank(b, n, d, KT[:, bk * 512:bk * 512 + n], K("KT"), ck, sk, bk * 512)
                flush_rot()
                S.dma("sp", "vbl", lambda e, g=g, hp=hp, nkb=nkb: e.dma_start(
                    out=Vb[:, 0:nkb * 128], in_=vsc[g, hp, :, 0:nkb, :].rearrange("p b f -> p (b f)")),
                    reads=[K("vsc")], writes=[K("Vb")])
                nqb = nsp * d

                def s_pair(qb0):
                    ba, bb = P_S2[sc[0] % 3]
                    sc[0] += 1
                    for blk in range(2):
                        qb = qb0 + blk
                        for kbi in range(2):
                            kb = qb + kbi * d
                            for hh, b in ((0, ba), (1, bb)):
                                S.op("pe", lambda e, hh=hh, kbi=kbi, kb=kb, b=b, blk=blk, qb=qb: e.matmul(
                                    psf[b][:, (blk * 2 + kbi) * 128:(blk * 2 + kbi + 1) * 128],
                                    lhsT=KT[hh * 64:(hh + 1) * 64, kb * 128:(kb + 1) * 128],
                                    rhs=QT[hh * 64:(hh + 1) * 64, qb * 128:(qb + 1) * 128], start=True, stop=True,
                                    tile_position=(hh * 64, 0)),
                                    reads=[K("KT"), K("QT")], writes=[K("ps%d" % b)])
                    return ba, bb

                def e_pair(banks):
                    js = []
                    for b in banks:
                        j = ptc[0] % 6
                        ptc[0] += 1
                        S.op("act", lambda e, b=b, j=j: e.activation(out=PT[j], in_=psf[b], func=AF.Exp, scale=0.125),
                             reads=[K("ps%d" % b)], writes=[K("PT%d" % j)])
                        if j % 2 == 0:
                            S.op("pool", lambda e, j=j: e.tensor_tensor(out=PT[j], in0=PT[j], in1=maskT, op=ALU.mult),
                                 reads=[K("PT%d" % j), K("mask")], writes=[K("PT%d" % j)])
                        else:
                            S.op("dve", lambda e, j=j: e.tensor_tensor(out=PT[j], in0=PT[j], in1=maskT, op=ALU.mult),
                                 reads=[K("PT%d" % j), K("mask")], writes=[K("PT%d" % j)])
                        js.append(j)
                    return js

                def pv_pair(qb0, js, bnd):
                    for blk in range(2):
                        qb = qb0 + blk
                        for what in range(2):
                            for kbi in range(2):
                                kb = qb + kbi * d
                                for hh in range(2):
                                    j = js[hh]
                                    if what == 0:
                                        lhsT, lk = Vb[:, kb * 128 + hh * 64:kb * 128 + (hh + 1) * 64], K("Vb")
                                    elif kb < d:
                                        lhsT, lk = hvones, K("hvones")
                                    elif kb < d + 16:
                                        lhsT, lk = loones, K("loones")
                                    else:
                                        lhsT, lk = ones, K("ones")
                                    S.op("pe", lambda e, hh=hh, what=what, kbi=kbi, lhsT=lhsT, blk=blk, j=j: e.matmul(
                                        psf[bnd][hh * 64:(hh + 1) * 64, (what * 2 + blk) * 128:(what * 2 + blk + 1) * 128],
                                        lhsT=lhsT, rhs=PT[j][:, (blk * 2 + kbi) * 128:(blk * 2 + kbi + 1) * 128],
                                        start=(kbi == 0), stop=(kbi == 1), tile_position=(0, hh * 64)),
                                        reads=[lk, K("PT%d" % j)], writes=[K("ps%d" % bnd)])

                def acc_pair(qb0, bnd):
                    sp_, r = qb0 // d, qb0 % d
                    if d == 1:
                        av = acc[:, :, qb0 * 128:(qb0 + 2) * 128]
                        pv = psf[bnd].rearrange("p (a t) -> p a t", a=2)
                    else:
                        av = acc[:, :, sp_ * span:(sp_ + 1) * span].rearrange("p a (i r) -> p a r i", r=d)[:, :, r:r + 2, :]
                        pv = psf[bnd].rearrange("p (a s i) -> p a s i", a=2, s=2)
                    S.op("dve", lambda e: e.tensor_tensor(out=av, in0=av, in1=pv, op=ALU.add),
                         reads=[K("acc"), K("ps%d" % bnd)], writes=[K("acc")])

                flush_cast()
                if not dbg_attn:
                    continue
                npair = nqb // 2
                sb = {0: s_pair(0)}
                if npair > 1:
                    sb[1] = s_pair(2)
                for pi in range(npair):
                    if pi + 2 < npair:
                        sb[pi + 2] = s_pair(2 * (pi + 2))
                    js = e_pair(sb.pop(pi))
                    bnd = P_ND[ndc[0] % 2]
                    ndc[0] += 1
                    pv_pair(2 * pi, js, bnd)
                    acc_pair(2 * pi, bnd)
            wvz, wzkey = load_w(wz[:, :, hp, :], 128)
            zs = KT[:, 0:NOWN]
            for bq in range(NOWN // 512):
                b = proj_bank(wvz, wzkey, 0, HAL + bq * 512, 512)
                S.op("act", lambda e, b=b, bq=bq: e.activation(out=zs[:, bq * 512:(bq + 1) * 512], in_=psf[b], func=AF.Silu),
                     reads=[K("ps%d" % b)], writes=[K("KT")])
            def tail(hp=hp):
                S.op("dve", lambda e: e.tensor_scalar(out=acc[:, 1, :], in0=acc[:, 1, :], scalar1=1e-18, scalar2=None, op0=ALU.max),
                     reads=[K("acc")], writes=[K("acc")])
                S.op("act", lambda e: e.activation(out=acc[:, 1, :], in_=acc[:, 1, :], func=AF.Ln), reads=[K("acc")], writes=[K("acc")])
                S.op("act", lambda e: e.activation(out=acc[:, 1, :], in_=acc[:, 1, :], func=AF.Exp, scale=-1.0),
                     reads=[K("acc")], writes=[K("acc")])
                S.op("dve", lambda e: e.tensor_tensor(out=acc[:, 0, :], in0=acc[:, 0, :], in1=acc[:, 1, :], op=ALU.mult),
                     reads=[K("acc")], writes=[K("acc")])
                S.op("dve", lambda e: e.tensor_tensor(out=zs, in0=acc[:, 0, :], in1=zs, op=ALU.mult),
                     reads=[K("acc"), K("KT")], writes=[K("KT")])
                S.dma("pool", "ysc", lambda e: e.dma_start(out=ysc[hp], in_=zs), reads=[K("KT")], writes=[K("ysc")])
            pending_tail.append(tail)
            flush_tail()
            flush_cast()
        flush_tail()
        S.barrier()

        self.off = off0
        Wo = self.carve(8 * 1024, BF16).rearrange("p (k c) -> p k c", k=8)
        wos = self.carve(1024, F32)
        gpo = self.carve(1024, F32)
        yt = [self.carve(8 * 512, BF16).rearrange("p (h t) -> p h t", h=8) for _ in range(2)]
        x3 = [self.carve(1024, F32) for _ in range(8)]
        osb = [self.carve(1024, F32) for _ in range(8)]
        tpost = [self.carve(1024, F32) for _ in range(2)]
        junk3 = self.carve(512, BF16)
        st3 = self.carve(64, F32)
        P_O = ((0, 1), (2, 3))
        S.dma("sp", "gpo", lambda e: e.dma_start(out=gpo, in_=npost.partition_broadcast(128)), writes=[K("gpo")])
        for k in range(8):
            S.dma("sp", "wos", lambda e, k=k: e.dma_start(out=wos, in_=w_out[k * 128:(k + 1) * 128, :]), writes=[K("wos")])
            S.op("act", lambda e, k=k: e.activation(out=Wo[:, k, :], in_=wos, func=AF.Copy), reads=[K("wos")], writes=[K("Wo")])
        yv = ysc.rearrange("h p t -> p h t")
        NG3 = NOWN // 512

        def p3_yload(gi):
            q = gi % 2
            S.dma("sp", "ytl%d" % q, lambda e: e.dma_start(out=yt[q], in_=yv[:, :, gi * 512:(gi + 1) * 512]),
                  reads=[K("ysc")], writes=[K("yt%d" % q)])

        def p3_load(gi):
            for t in range(4):
                i = gi * 4 + t
                p = i % 8
                S.dma("sp", "x3l%d" % p, lambda e, i=i, p=p: e.dma_start(out=x3[p], in_=x_src[i * 128:(i + 1) * 128, :]),
                      writes=[K("x3%d" % p)])

        def p3_mm(gi):
            q = gi % 2
            ss2 = st3[:, q * 32:q * 32 + 8]
            for t in range(4):
                i = gi * 4 + t
                p = i % 8
                c0 = t * 128
                pb = P_O[i % 2]
                for n in range(2):
                    bk = pb[n]
                    for k in range(8):
                        S.op("pe", lambda e, k=k, n=n, bk=bk, c0=c0: e.matmul(psf[bk], lhsT=yt[q][:, k, c0:c0 + 128],
                                                                            rhs=Wo[:, k, n * 512:(n + 1) * 512],
                                                                            start=(k == 0), stop=(k == 7)),
                             reads=[K("yt%d" % q), K("Wo")], writes=[K("ps%d" % bk)])
                    S.op("act", lambda e, n=n, bk=bk, t=t, ss2=ss2: e.activation(out=junk3, in_=psf[bk], func=AF.Square,
                                                                               accum_out=ss2[:, 2 * t + n:2 * t + n + 1]),
                         reads=[K("ps%d" % bk)], writes=[K("junk3"), K("ss2%d" % q)])
                    S.op("act", lambda e, n=n, bk=bk, p=p: e.activation(out=osb[p][:, n * 512:(n + 1) * 512], in_=psf[bk],
                                                                       func=AF.Copy),
                         reads=[K("ps%d" % bk)], writes=[K("osb%d" % p)])

        def p3_post(gi):
            q = gi % 2
            ss2 = st3[:, q * 32:q * 32 + 8]
            a2, rstd2, tmp3 = (st3[:, q * 32 + 8 + c * 4:q * 32 + 12 + c * 4] for c in range(3))
            ssv = ss2.rearrange("p (t n) -> p t n", n=2)
            S.op("dve", lambda e: e.tensor_tensor(out=a2, in0=ssv[:, :, 0], in1=ssv[:, :, 1], op=ALU.add),
                 reads=[K("ss2%d" % q)], writes=[K("a2%d" % q)])
            S.op("dve", lambda e: e.tensor_scalar(out=a2, in0=a2, scalar1=1.0 / D, scalar2=RMS_EPS, op0=ALU.mult,
                                                  op1=ALU.add), reads=[K("a2%d" % q)], writes=[K("a2%d" % q)])
            self.rsqrt(K("a2%d" % q), a2, K("rstd2%d" % q), rstd2, K("tmp3%d" % q), tmp3)
            for t in range(4):
                i = gi * 4 + t
                p = i % 8
                tp = tpost[i % 2]
                S.op("dve", lambda e, p=p, t=t, tp=tp, rstd2=rstd2: e.scalar_tensor_tensor(out=tp, in0=osb[p], scalar=rstd2[:, t:t + 1],
                                                                                      in1=gpo, op0=ALU.mult, op1=ALU.mult),
                     reads=[K("osb%d" % p), K("rstd2%d" % q), K("gpo")], writes=[K("tpost%d" % (i % 2))])
                S.op("pool", lambda e, p=p, tp=tp: e.tensor_tensor(out=x3[p], in0=x3[p], in1=tp, op=ALU.add),
                     reads=[K("x3%d" % p), K("tpost%d" % (i % 2))], writes=[K("x3%d" % p)])
                o = S.dma("pool", "x3s%d" % p, lambda e, i=i, p=p: e.dma_start(out=x_dst[i * 128:(i + 1) * 128, :], in_=x3[p]),
                          reads=[K("x3%d" % p)], writes=[K("xdst")])
                self.final.append(o)

        p3_yload(0)
        p3_load(0)
        if NG3 > 1:
            p3_yload(1)
            p3_load(1)
        p3_mm(0)
        for gi in range(NG3):
            if gi + 1 < NG3:
                p3_mm(gi + 1)
            if gi + 2 < NG3:
                p3_yload(gi + 2)
            p3_post(gi)
            if gi + 2 < NG3:
                p3_load(gi + 2)
        S.barrier()
        self.peak = max(getattr(self, "peak", 0), self.off)
        self.off = off0

    def finish(self):
        S, nc = self.S, self.nc
        S.finalize()
        with contextlib.ExitStack() as es:
            sems = {k: es.enter_context(nc.semaphore("s%d" % i)) for i, k in enumerate(S.sem_keys)}
            block = es.enter_context(nc.Block())
            S.emit(block, sems, final_waits=self.final)
        return nc


_CACHE = {}


def consts():
    ident = np.eye(128, dtype=np.float32).astype(ml_dtypes.bfloat16)
    tril = np.tril(np.ones((128, 128), dtype=np.float32))
    return ident, tril


def prog_a(ntok):
    key = ("A", ntok)
    if key not in _CACHE:
        P = Prog()
        x = P.inp("x", [ntok, D])
        w_in = P.inp("w_in", [D, 3 * EW])
        npre = P.inp("npre", [128, 8])
        ln_g = P.inp("ln_g", [1, EW])
        ln_b = P.inp("ln_b", [1, EW])
        w_s = P.inp("w_s", [8, 128, 128])
        bs = P.inp("bs", [128, 8])
        w_out = P.inp("w_out", [EW, D])
        npost = P.inp("npost", [1, D])
        ident = P.inp("ident", [128, 128], BF16)
        tril = P.inp("tril", [128, 128])
        y = P.outp("y", [ntok, D])
        P.layer_a(x, y, ntok, w_in, npre, ln_g, ln_b, w_s, bs, w_out, npost, ident, tril)
        _CACHE[key] = P.finish()
    return _CACHE[key]


def col128(vec):
    return np.ascontiguousarray(np.asarray(vec, dtype=np.float32).reshape(-1, 128).T)


def run_layer_a(xs, i, inputs):
    j = i // 2
    ident, tril = consts()
    ntok = xs[0].shape[0]
    nc = prog_a(ntok)
    common = {
        "w_in": np.ascontiguousarray(inputs["a_w_in"][j]),
        "npre": col128(inputs["norm_pre"][i]),
        "ln_g": np.ascontiguousarray(inputs["a_ln_g"][j][None, :]),
        "ln_b": np.ascontiguousarray(inputs["a_ln_b"][j][None, :]),
        "w_s": np.ascontiguousarray(inputs["a_w_s"][j]),
        "bs": np.ascontiguousarray(np.asarray(inputs["a_b_s"][j]).T),
        "w_out": np.ascontiguousarray(inputs["a_w_out"][j]),
        "npost": np.ascontiguousarray(inputs["norm_post"][i][None, :]),
        "ident": ident, "tril": tril,
    }
    in_maps = [dict(common, x=np.ascontiguousarray(x)) for x in xs]
    res = run_bass_kernel_spmd(nc, in_maps, core_ids=list(range(len(xs))))
    return [r["y"] for r in res.results]


DILS = (1, 4, 16)
HALO = 2048


def rope_consts():
    rm = np.zeros((128, 128), np.float32)
    for fp in range(128):
        if fp % 64 < 32:
            rm[fp + 32, fp] = -1.0
        else:
            rm[fp - 32, fp] = 1.0
    mask = np.zeros((128, 2, 2, 128), np.float32)
    j = np.arange(128)[:, None]
    i = np.arange(128)[None, :]
    for hh in range(2):
        mask[:, hh, 0, :] = (j >= i)
        mask[:, hh, 1, :] = (j <= i)
    return rm.astype(ml_dtypes.bfloat16), mask.reshape(128, 512).astype(ml_dtypes.bfloat16)


def rope_tables_core(pos0, nown=4096):
    inv_freq = (1.0 / (np.float32(10000.0) ** (np.arange(0, 64, 2, dtype=np.float32) / np.float32(64)))).astype(np.float32)
    out = []
    for d in DILS:
        span = 128 * d
        res = []
        for (tau0, ncols) in ((HALO, nown), (HALO - span, nown + span)):
            tau = np.zeros(ncols, np.int64)
            for b0 in range(0, ncols, 512):
                n = min(512, ncols - b0)
                col = np.arange(n)
                r, ii = col // (n // d), col % (n // d)
                tau[b0:b0 + n] = tau0 + b0 + ii * d + r
            pos = (pos0 - HALO + tau).astype(np.float32)
            ang = pos[None, :] * inv_freq[:, None]
            c = np.tile(np.cos(ang), (4, 1)).astype(ml_dtypes.bfloat16)
            s_ = np.tile(np.sin(ang), (4, 1)).astype(ml_dtypes.bfloat16)
            res += [np.ascontiguousarray(c), np.ascontiguousarray(s_)]
        out.append(res)
    return out


DBG = {}


def prog_b():
    key = ("B",)
    if key not in _CACHE:
        P = Prog()
        x = P.inp("x", [TOK, D])
        xh = P.inp("xh", [HALO, D])
        w_in = P.inp("w_in", [D, 10240])
        gpre = P.inp("gpre", [1, D])
        w_out = P.inp("w_out", [D, D])
        npost = P.inp("npost", [1, D])
        hv = P.inp("hv", [128, 2])
        ident = P.inp("ident", [128, 128], BF16)
        rm = P.inp("rm", [128, 128], BF16)
        mask = P.inp("mask", [128, 512], BF16)
        tabs = []
        for g, d in enumerate(DILS):
            nk = 4096 + 128 * d
            tabs.append((P.inp("cq%d" % g, [128, 4096], BF16), P.inp("sq%d" % g, [128, 4096], BF16),
                         P.inp("ck%d" % g, [128, nk], BF16), P.inp("sk%d" % g, [128, nk], BF16)))
        ysc = P.nc.dram_tensor("ysc", [8, 128, 4096], BF16).ap()
        vsc = P.nc.dram_tensor("vsc", [3, 8, 128, 48, 128], BF16).ap()
        y = P.outp("y", [TOK, D])
        P.layer_b(x, xh, y, ysc, vsc, w_in, gpre, w_out, npost, hv, tabs, ident, rm, mask, **DBG)
        _CACHE[key] = P.finish()
    return _CACHE[key]


def run_layer_b(xs, i, inputs, core_ids=None):
    j = i // 2
    ident, _ = consts()
    rm, mask = rope_consts()
    nc = prog_b()
    common = {
        "w_in": np.ascontiguousarray(inputs["b_w_in"][j]),
        "gpre": np.ascontiguousarray(inputs["norm_pre"][i][None, :]),
        "w_out": np.ascontiguousarray(inputs["b_w_out"][j]),
        "npost": np.ascontiguousarray(inputs["norm_post"][i][None, :]),
        "ident": ident, "rm": rm, "mask": mask,
    }
    in_maps = []
    cores = list(range(len(xs))) if core_ids is None else core_ids
    for c in cores:
        m = dict(common)
        m["x"] = np.ascontiguousarray(xs[c])
        first = (c % 4 == 0)
        m["xh"] = np.zeros((HALO, D), np.float32) if first else np.ascontiguousarray(xs[c - 1][TOK - HALO:])
        m["hv"] = np.tile(np.array([[0.0 if first else 1.0, 1.0]], np.float32), (128, 1))
        tb = rope_tables_core((c % 4) * TOK)
        for g in range(3):
            m["cq%d" % g], m["sq%d" % g], m["ck%d" % g], m["sk%d" % g] = tb[g]
        in_maps.append(m)
    res = run_bass_kernel_spmd(nc, in_maps, core_ids=list(range(len(cores))))
    return [r["y"] for r in res.results]


def kernel_unfused(x, norm_pre, norm_post, a_w_in, a_ln_g, a_ln_b, a_w_s, a_b_s, a_w_out, b_w_in, b_w_out):
    inputs = dict(norm_pre=np.asarray(norm_pre), norm_post=np.asarray(norm_post), a_w_in=np.asarray(a_w_in),
                  a_ln_g=np.asarray(a_ln_g), a_ln_b=np.asarray(a_ln_b), a_w_s=np.asarray(a_w_s), a_b_s=np.asarray(a_b_s),
                  a_w_out=np.asarray(a_w_out), b_w_in=np.asarray(b_w_in), b_w_out=np.asarray(b_w_out))
    x = np.asarray(x, dtype=np.float32)
    B, S_, D_ = x.shape
    xs = [np.ascontiguousarray(c) for c in x.reshape(NCORES, TOK, D_)]
    for i in range(4):
        if i % 2 == 0:
            xs = run_layer_a(xs, i, inputs)
        else:
            xs = run_layer_b(xs, i, inputs)
    return np.stack(xs).reshape(B, S_, D_).astype(np.float32)


XR = 8192
B_CALLS = ((2048, 2048), (4096, -2048), (4096, 0))


def prog_fused():
    key = ("F",)
    if key in _CACHE:
        return _CACHE[key]
    P = Prog()
    x = P.inp("x", [XR, D])
    ident = P.inp("ident", [128, 128], BF16)
    tril = P.inp("tril", [128, 128])
    rm = P.inp("rm", [128, 128], BF16)
    mask = P.inp("mask", [128, 512], BF16)
    npost = [P.inp("npost%d" % i, [1, D]) for i in range(4)]
    A = []
    for j in range(2):
        A.append(dict(w_in=P.inp("a_w_in%d" % j, [D, 3 * EW]), npre=P.inp("a_npre%d" % j, [128, 8]),
                      ln_g=P.inp("a_ln_g%d" % j, [1, EW]), ln_b=P.inp("a_ln_b%d" % j, [1, EW]),
                      w_s=P.inp("a_w_s%d" % j, [8, 128, 128]), bs=P.inp("a_bs%d" % j, [128, 8]),
                      w_out=P.inp("a_w_out%d" % j, [EW, D])))
    Bw = []
    for j in range(2):
        Bw.append(dict(w_in=P.inp("b_w_in%d" % j, [D, 10240]), gpre=P.inp("b_gpre%d" % j, [1, D]),
                       w_out=P.inp("b_w_out%d" % j, [D, D])))
    fl = [P.inp("fl%d" % c, [128, 2]) for c in range(3)]
    tabs = []
    for c, (nown, _) in enumerate(B_CALLS):
        tc = []
        for g, d in enumerate(DILS):
            nk = nown + 128 * d
            tc.append((P.inp("cq%d_%d" % (c, g), [128, nown], BF16), P.inp("sq%d_%d" % (c, g), [128, nown], BF16),
                       P.inp("ck%d_%d" % (c, g), [128, nk], BF16), P.inp("sk%d_%d" % (c, g), [128, nk], BF16)))
        tabs.append(tc)
    y = P.outp("y", [TOK, D])
    xres = P.nc.dram_tensor("xres", [XR, D], F32).ap()
    ysc = P.nc.dram_tensor("ysc", [8, 128, 4096], BF16).ap()
    vsc = P.nc.dram_tensor("vsc", [3, 8, 128, 48, 128], BF16).ap()

    def la(j, i, src, dst, ntok):
        a = A[j]
        P.layer_a(src, dst, ntok, a["w_in"], a["npre"], a["ln_g"], a["ln_b"], a["w_s"], a["bs"], a["w_out"], npost[i],
                  ident, tril)

    def lb(j, i, c, src, halo, dst):
        nown = B_CALLS[c][0]
        b = Bw[j]
        P.layer_b(src, halo, dst, ysc[:, :, 0:nown], vsc, b["w_in"], b["gpre"], b["w_out"], npost[i], fl[c], tabs[c],
                  ident, rm, mask, NOWN=nown)

    la(0, 0, x, xres, XR)
    lb(0, 1, 0, xres[6144:8192], xres[4096:6144], xres[6144:8192])
    lb(0, 1, 1, xres[2048:6144], xres[0:2048], xres[2048:6144])
    la(1, 2, xres[2048:8192], xres[2048:8192], 6144)
    P.final = []
    lb(1, 3, 2, xres[4096:8192], xres[2048:4096], y)
    _CACHE[key] = P.finish()
    return _CACHE[key]


def kernel(x, norm_pre, norm_post, a_w_in, a_ln_g, a_ln_b, a_w_s, a_b_s, a_w_out, b_w_in, b_w_out):
    f32 = lambda a: np.ascontiguousarray(np.asarray(a, dtype=np.float32))
    x = f32(x)
    norm_pre, norm_post = f32(norm_pre), f32(norm_post)
    Bn, S_, D_ = x.shape
    per_seq = S_ // TOK
    ident, tril = consts()
    rm, mask = rope_consts()
    common = {"ident": ident, "tril": tril, "rm": rm, "mask": mask}
    for i in range(4):
        common["npost%d" % i] = f32(norm_post[i][None, :])
    for j in range(2):
        common["a_w_in%d" % j] = f32(a_w_in[j])
        common["a_npre%d" % j] = col128(norm_pre[2 * j])
        common["a_ln_g%d" % j] = f32(np.asarray(a_ln_g[j])[None, :])
        common["a_ln_b%d" % j] = f32(np.asarray(a_ln_b[j])[None, :])
        common["a_w_s%d" % j] = f32(a_w_s[j])
        common["a_bs%d" % j] = f32(np.asarray(a_b_s[j]).T)
        common["a_w_out%d" % j] = f32(a_w_out[j])
        common["b_w_in%d" % j] = f32(b_w_in[j])
        common["b_gpre%d" % j] = f32(norm_pre[2 * j + 1][None, :])
        common["b_w_out%d" % j] = f32(b_w_out[j])
    tab_cache = {}
    in_maps = []
    for c in range(NCORES):
        b, k = c // per_seq, c % per_seq
        m = dict(common)
        xe = np.zeros((XR, D_), np.float32)
        lo = k * TOK - TOK
        src_lo = max(lo, 0)
        xe[src_lo - lo:] = x[b, src_lo:(k + 1) * TOK]
        m["x"] = xe
        first = (k == 0)
        flags = ((1.0, 1.0), (0.0, 0.0) if first else (1.0, 1.0), (0.0, 1.0) if first else (1.0, 1.0))
        for ci in range(3):
            m["fl%d" % ci] = np.tile(np.array([flags[ci]], np.float32), (128, 1))
            nown, off = B_CALLS[ci]
            tk = (k, ci)
            if tk not in tab_cache:
                tab_cache[tk] = rope_tables_core(k * TOK + off, nown)
            tb = tab_cache[tk]
            for g in range(3):
                m["cq%d_%d" % (ci, g)], m["sq%d_%d" % (ci, g)], m["ck%d_%d" % (ci, g)], m["sk%d_%d" % (ci, g)] = tb[g]
        in_maps.append(m)
    nc = prog_fused()
    res = run_bass_kernel_spmd(nc, in_maps, core_ids=list(range(NCORES)))
    out = np.stack([r["y"] for r in res.results]).reshape(Bn, S_, D_)
    return out.astype(np.float32)
```

```python
import contextlib
import numpy as np
import ml_dtypes
import concourse.bass as bass
import concourse.mybir as mybir
from concourse.bass_utils import run_bass_kernel_spmd

F32 = mybir.dt.float32
BF16 = mybir.dt.bfloat16
I32 = mybir.dt.int32
AF = mybir.ActivationFunctionType
ALU = mybir.AluOpType

D = 1024
NCORES = 8
TOK = 4096
EW = 2048
RMS_EPS = 1e-6
LN_EPS = 1e-5

ENGS = ("pe", "act", "dve", "pool", "sp")


class _Op:
    __slots__ = ("eng", "fn", "sem", "val", "waits", "dma", "needs_inc")

    def __init__(self, eng, fn, dma):
        self.eng = eng
        self.fn = fn
        self.dma = dma
        self.sem = None
        self.val = 0
        self.waits = []
        self.needs_inc = False


class Sched:
    def __init__(self):
        self.ops = {e: [] for e in ENGS}
        self.last_w = {}
        self.readers = {}
        self.all_ops = []
        self.epoch = 0

    def _deps(self, op, reads, writes):
        deps = []
        raw = set()
        for b in reads:
            w = self.last_w.get(b)
            if w is not None:
                deps.append(w)
                raw.add(id(w))
        for b in writes:
            w = self.last_w.get(b)
            if w is not None:
                deps.append(w)
            deps.extend(self.readers.get(b, {}).values())
        for b in reads:
            self.readers.setdefault(b, {})[op.sem] = op
        for b in writes:
            self.last_w[b] = op
            self.readers[b] = {}
        return deps, raw

    def barrier(self):
        last = {}
        for o in self.all_ops:
            last[o.sem] = o
        self.pending = {e: list(last.values()) for e in ENGS}

    def _pend(self, o, eng):
        pend = getattr(self, "pending", None)
        if pend and pend.get(eng):
            for d in pend[eng]:
                if d.dma is None and d.eng == eng:
                    continue
                d.needs_inc = True
                o.waits.append(d)
            pend[eng] = []

    def op(self, eng, fn, reads=(), writes=()):
        o = _Op(eng, fn, None)
        o.sem = ("eng", eng, self.epoch)
        self._pend(o, eng)
        deps, raw = self._deps(o, reads, writes)
        for d in deps:
            if d is o:
                continue
            if d.dma is None and d.eng == eng:
                if eng == "pe" or id(d) not in raw:
                    continue
            d.needs_inc = True
            o.waits.append(d)
        self.ops[eng].append(o)
        self.all_ops.append(o)
        return o

    def dma(self, queue, stream, fn, reads=(), writes=()):
        o = _Op(queue, fn, stream)
        o.sem = ("dma", stream)
        o.needs_inc = True
        self._pend(o, queue)
        deps, _ = self._deps(o, reads, writes)
        for d in deps:
            if d is o:
                continue
            d.needs_inc = True
            o.waits.append(d)
        self.ops[queue].append(o)
        self.all_ops.append(o)
        return o

    def new_epoch(self):
        self.epoch += 1

    def finalize(self):
        counts = {}
        for o in self.all_ops:
            if o.needs_inc:
                step = 16 if o.dma is not None else 1
                counts[o.sem] = counts.get(o.sem, 0) + step
                o.val = counts[o.sem]
        self.sem_keys = list(counts.keys())
        for k, v in counts.items():
            assert v < 60000, (k, v)
        return counts

    def emit(self, block, sems, final_waits=()):
        handles = {"pe": "tensor", "act": "scalar", "dve": "vector", "pool": "gpsimd", "sp": "sync"}

        def make(engname):
            ops = self.ops[engname]

            def body(eng):
                waited = {}

                def wait(d):
                    if waited.get(d.sem, 0) >= d.val:
                        return
                    eng.wait_ge(sems[d.sem], d.val)
                    waited[d.sem] = d.val

                for o in ops:
                    for d in o.waits:
                        wait(d)
                    ins = o.fn(eng)
                    if o.needs_inc:
                        ins.then_inc(sems[o.sem], 16 if o.dma is not None else 1)
                if engname == "sp":
                    for d in final_waits:
                        wait(d)
            return body

        for engname in ENGS:
            getattr(block, handles[engname])(make(engname))


class Prog:
    def __init__(self):
        self.nc = bass.Bass("TRN2", target_bir_lowering=False)
        self.S = Sched()
        self.cap = 212000
        self.arena = self.nc.alloc_sbuf_tensor("arena", [128, self.cap // 2], BF16)[:]
        self.off = 0
        self.uid = 0
        self.ps_f32 = [self.nc.alloc_psum_tensor("psb%d" % i, [128, 512], F32)[:] for i in range(8)]
        self.ps_b16 = [p.bitcast(BF16) for p in self.ps_f32]
        self.final = []

    def inp(self, name, shape, dtype=F32):
        return self.nc.dram_tensor(name, list(shape), dtype, kind="ExternalInput").ap()

    def outp(self, name, shape, dtype=F32):
        return self.nc.dram_tensor(name, list(shape), dtype, kind="ExternalOutput").ap()

    def carve(self, cols, dtype):
        esz = 4 if dtype in (F32, I32) else 2
        nb = (cols * esz + 63) // 64 * 64
        assert self.off + nb <= self.cap, ("SBUF arena overflow", self.off, nb)
        a = self.arena[:, self.off // 2:(self.off + nb) // 2]
        self.off += nb
        if dtype != BF16:
            a = a.bitcast(dtype)
        return a[:, 0:cols]

    def key(self, base):
        self.uid += 1
        return "%s#%d" % (base, self.uid)

    def rsqrt(self, a_key, a, out_key, out, tmp_key, tmp):
        S = self.S
        S.op("dve", lambda e: e.tensor_scalar(out=out.bitcast(I32), in0=a.bitcast(I32), scalar1=1, scalar2=None,
                                              op0=ALU.arith_shift_right), reads=[a_key], writes=[out_key])
        S.op("dve", lambda e: e.tensor_scalar(out=out.bitcast(I32), in0=out.bitcast(I32), scalar1=0x5f3759df,
                                              scalar2=-1, op0=ALU.subtract, op1=ALU.mult),
             reads=[out_key], writes=[out_key])
        for _ in range(2):
            S.op("dve", lambda e: e.scalar_tensor_tensor(out=tmp, in0=out, scalar=-0.5, in1=out, op0=ALU.mult,
                                                         op1=ALU.mult), reads=[out_key], writes=[tmp_key])
            S.op("dve", lambda e: e.tensor_tensor(out=tmp, in0=tmp, in1=a, op=ALU.mult),
                 reads=[tmp_key, a_key], writes=[tmp_key])
            S.op("dve", lambda e: e.scalar_tensor_tensor(out=out, in0=tmp, scalar=1.5, in1=out, op0=ALU.add,
                                                         op1=ALU.mult), reads=[tmp_key, out_key], writes=[out_key])

    def layer_a(self, x_src, x_dst, ntok, w_in, npre_col, ln_g, ln_b, w_s, bs_col, w_out, npost, ident_d, tril_d):
        S, nc = self.S, self.nc
        off0 = self.off
        NT = ntok // 128
        L = self.key("A")
        K = lambda s: "%s/%s" % (L, s)

        Wb = [self.carve(6144, BF16) for _ in range(8)]
        Wo = [self.carve(1024, BF16) for _ in range(16)]
        lng = self.carve(2048, BF16)
        lnb = self.carve(2048, BF16)
        gpo = self.carve(1024, F32)
        WmT = self.carve(1024, BF16)
        bsc = self.carve(8, F32)
        gpre = self.carve(8, F32)
        idt = self.carve(128, BF16)
        tril = self.carve(128, F32)
        xt = [self.carve(1024, F32) for _ in range(3)]
        xb = self.carve(1024, BF16)
        xT = [self.carve(1024, BF16) for _ in range(2)]
        junk = self.carve(1024, BF16)
        u = self.carve(2048, BF16)
        sz = self.carve(2048, BF16)
        v = self.carve(2048, F32)
        vn = [self.carve(2048, BF16) for _ in range(2)]
        y = self.carve(2048, BF16)
        yT = self.carve(2048, BF16)
        tpost = self.carve(1024, F32)
        st = self.carve(64, F32)
        ss, ms, tmp1 = st[:, 0:1], st[:, 1:2], st[:, 3:4]
        rstd = [st[:, 4:5], st[:, 5:6]]
        bst = st[:, 8:32]
        mv = st[:, 32:34]
        va, rsv, tmp2 = st[:, 34:35], st[:, 35:36], st[:, 36:37]
        ss2, a2, rstd2, tmp3 = st[:, 40:42], st[:, 42:43], st[:, 43:44], st[:, 44:45]
        stage = [v[:, 0:1024], v[:, 1024:2048]]
        VK = [K("v0"), K("v1")]
        P_TX, P_IN, P_SV, P_TY, P_O = 0, (1, 2), (3, 4), 5, (6, 7)
        psf, psb = self.ps_f32, self.ps_b16

        S.dma("sp", "idt", lambda e: e.dma_start(out=idt, in_=ident_d), writes=[K("idt")])
        S.dma("sp", "tril", lambda e: e.dma_start(out=tril, in_=tril_d), writes=[K("tril")])
        S.dma("sp", "gpre", lambda e: e.dma_start(out=gpre, in_=npre_col), writes=[K("gpre")])
        S.dma("sp", "bsc", lambda e: e.dma_start(out=bsc, in_=bs_col), writes=[K("bsc")])
        S.dma("sp", "gpo", lambda e: e.dma_start(out=gpo, in_=npost.partition_broadcast(128)), writes=[K("gpo")])
        cnt = [0]

        def staged(src_ap, cols, consume, view=None):
            i = cnt[0] % 2
            cnt[0] += 1
            sb = stage[i][:, 0:cols]
            sbd = view(sb) if view is not None else sb
            S.dma("sp", "stg%d" % i, lambda e: e.dma_start(out=sbd, in_=src_ap), writes=[VK[i]])
            consume(sb, "act" if i == 0 else "dve", VK[i])

        for k in range(8):
            for q in range(6):
                dst = Wb[k][:, q * 1024:(q + 1) * 1024]

                def cons(sb, eng, skey, dst=dst, k=k):
                    if eng == "act":
                        S.op("act", lambda e: e.activation(out=dst, in_=sb, func=AF.Copy, scale=gpre[:, k:k + 1]),
                             reads=[skey, K("gpre")], writes=[K("Wb%d" % k)])
                    else:
                        S.op("dve", lambda e: e.tensor_scalar(out=dst, in0=sb, scalar1=gpre[:, k:k + 1], scalar2=None,
                                                              op0=ALU.mult), reads=[skey, K("gpre")], writes=[K("Wb%d" % k)])
                staged(w_in[k * 128:(k + 1) * 128, q * 1024:(q + 1) * 1024], 1024, cons)
        for k in range(16):
            def cons(sb, eng, skey, k=k):
                if eng == "act":
                    S.op("act", lambda e: e.activation(out=Wo[k], in_=sb, func=AF.Copy), reads=[skey], writes=[K("Wo")])
                else:
                    S.op("dve", lambda e: e.tensor_copy(out=Wo[k], in_=sb), reads=[skey], writes=[K("Wo")])
            staged(w_out[k * 128:(k + 1) * 128, :], 1024, cons)
        for (src, dstt, nm) in ((ln_g, lng, "lng"), (ln_b, lnb, "lnb")):
            for h in range(2):
                def cons(sb, eng, skey, dstt=dstt, h=h, nm=nm):
                    if eng == "act":
                        S.op("act", lambda e: e.activation(out=dstt[:, h * 1024:(h + 1) * 1024], in_=sb, func=AF.Copy),
                             reads=[skey], writes=[K(nm)])
                    else:
                        S.op("dve", lambda e: e.tensor_copy(out=dstt[:, h * 1024:(h + 1) * 1024], in_=sb),
                             reads=[skey], writes=[K(nm)])
                staged(src[:, h * 1024:(h + 1) * 1024].partition_broadcast(128), 1024, cons)

        def cons_ws(sb, eng, skey):
            sbv = sb.rearrange("p (g s) -> p g s", g=8)
            yv = y[:, 0:1024].rearrange("p (g s) -> p g s", g=8)
            for g in range(8):
                S.op("dve", lambda e, g=g: e.tensor_tensor(out=yv[:, g, :], in0=sbv[:, g, :], in1=tril, op=ALU.mult),
                     reads=[skey, K("tril")], writes=[K("y")])
            for g in range(8):
                S.op("pe", lambda e, g=g: e.transpose(out=psb[P_TX][:, g * 128:(g + 1) * 128], in_=yv[:, g, :], identity=idt),
                     reads=[K("y"), K("idt")], writes=[K("ps%d" % P_TX)])
            S.op("dve", lambda e: e.tensor_copy(out=WmT, in_=psb[P_TX]), reads=[K("ps%d" % P_TX)], writes=[K("WmT")])
        staged(w_s.rearrange("g t s -> t g s"), 1024, cons_ws, view=lambda sb: sb.rearrange("p (g s) -> p g s", g=8))

        def load(i):
            p = i % 3
            S.dma("sp", "xl%d" % p, lambda e: e.dma_start(out=xt[p], in_=x_src[i * 128:(i + 1) * 128, :]),
                  writes=[K("xt%d" % p)])

        def inproj(q, banks, r0):
            r = r0
            for n in banks:
                b = P_IN[r % 2]
                r += 1
                for k in range(8):
                    S.op("pe", lambda e, k=k, n=n, b=b: e.matmul(psf[b], lhsT=xT[q][:, k * 128:(k + 1) * 128],
                                                                rhs=Wb[k][:, n * 512:(n + 1) * 512],
                                                                start=(k == 0), stop=(k == 7)),
                         reads=[K("xT%d" % q), K("Wb%d" % k)], writes=[K("ps%d" % b)])
                if n < 4:
                    dst, func, dk = u[:, n * 512:(n + 1) * 512], AF.Gelu, [K("u%d" % n)]
                elif n < 8:
                    dst, func, dk = v[:, (n - 4) * 512:(n - 3) * 512], AF.Gelu, VK
                else:
                    dst, func, dk = sz[:, (n - 8) * 512:(n - 7) * 512], AF.Silu, [K("sz%d" % (n - 8))]
                S.op("act", lambda e, dst=dst, func=func, b=b: e.activation(out=dst, in_=psf[b], func=func, scale=rstd[q]),
                     reads=[K("ps%d" % b), K("rstd%d" % q)], writes=dk)

        def pre(i):
            p, q = i % 3, i % 2
            xk = K("xt%d" % p)
            S.op("act", lambda e: e.activation(out=junk, in_=xt[p], func=AF.Square, accum_out=ss),
                 reads=[xk], writes=[K("junk"), K("ss")])
            S.op("dve", lambda e: e.tensor_scalar(out=ms, in0=ss, scalar1=1.0 / D, scalar2=RMS_EPS, op0=ALU.mult,
                                                  op1=ALU.add), reads=[K("ss")], writes=[K("ms")])
            self.rsqrt(K("ms"), ms, K("rstd%d" % q), rstd[q], K("tmp1"), tmp1)
            S.op("pool", lambda e: e.tensor_copy(out=xb, in_=xt[p]), reads=[xk], writes=[K("xb")])

        def stage1(i):
            q = i % 2
            for k in range(8):
                S.op("pe", lambda e, k=k: e.transpose(out=psb[P_TX][:, k * 128:(k + 1) * 128],
                                                      in_=xb[:, k * 128:(k + 1) * 128], identity=idt),
                     reads=[K("xb"), K("idt")], writes=[K("ps%d" % P_TX)])
            S.op("act", lambda e: e.activation(out=xT[q], in_=psb[P_TX], func=AF.Copy),
                 reads=[K("ps%d" % P_TX)], writes=[K("xT%d" % q)])
            inproj(q, (4, 5, 6, 7), 0)
            for c in range(4):
                S.op("dve", lambda e, c=c: e.bn_stats(out=bst[:, c * 6:(c + 1) * 6], in_=v[:, c * 512:(c + 1) * 512]),
                     reads=VK, writes=[K("bst")])
            S.op("dve", lambda e: e.bn_aggr(out=mv, in_=bst.rearrange("p (c s) -> p c s", c=4)),
                 reads=[K("bst")], writes=[K("mv")])
            S.op("dve", lambda e: e.tensor_scalar(out=va, in0=mv[:, 1:2], scalar1=LN_EPS, scalar2=None,
                                                  op0=ALU.add), reads=[K("mv")], writes=[K("va")])
            self.rsqrt(K("va"), va, K("rsv"), rsv, K("tmp2"), tmp2)
            S.op("dve", lambda e: e.scalar_tensor_tensor(out=v, in0=v, scalar=mv[:, 0:1], in1=lng,
                                                         op0=ALU.subtract, op1=ALU.mult),
                 reads=VK + [K("mv"), K("lng")], writes=VK)
            S.op("dve", lambda e: e.scalar_tensor_tensor(out=vn[q], in0=v, scalar=rsv, in1=lnb,
                                                         op0=ALU.mult, op1=ALU.add),
                 reads=VK + [K("rsv"), K("lnb")], writes=[K("vn%d" % q)])

        def stage2a(i):
            q = i % 2
            inproj(q, (0, 1, 2, 3, 8, 9, 10, 11), 0)
            for qd in range(4):
                b = P_SV[qd % 2]
                for gg in range(2):
                    g = 2 * qd + gg
                    S.op("pe", lambda e, g=g, gg=gg, b=b: e.matmul(psf[b][:, gg * 256:(gg + 1) * 256],
                                                                  lhsT=WmT[:, g * 128:(g + 1) * 128],
                                                                  rhs=vn[q][:, g * 256:(g + 1) * 256], start=True, stop=True),
                         reads=[K("vn%d" % q), K("WmT")], writes=[K("ps%d" % b)])
                for gg in range(2):
                    g = 2 * qd + gg
                    S.op("dve", lambda e, g=g, gg=gg, b=b: e.scalar_tensor_tensor(
                        out=y[:, g * 256:(g + 1) * 256], in0=psf[b][:, gg * 256:(gg + 1) * 256], scalar=bsc[:, g:g + 1],
                        in1=u[:, g * 256:(g + 1) * 256], op0=ALU.add, op1=ALU.mult),
                        reads=[K("ps%d" % b), K("bsc"), K("u%d" % qd)], writes=[K("y%d" % qd)])
            S.op("dve", lambda e: e.tensor_tensor(out=y, in0=y, in1=sz, op=ALU.mult),
                 reads=[K("y")] + [K("y%d" % c) for c in range(4)] + [K("sz%d" % c) for c in range(4)],
                 writes=[K("y")] + [K("y%d" % c) for c in range(4)])

        def stage2b(i):
            p = i % 3
            xk = K("xt%d" % p)
            for h in range(2):
                for k in range(8):
                    kk = h * 8 + k
                    S.op("pe", lambda e, k=k, kk=kk: e.transpose(out=psb[P_TY][:, k * 128:(k + 1) * 128],
                                                                in_=y[:, kk * 128:(kk + 1) * 128], identity=idt),
                         reads=[K("y"), K("idt")] + [K("y%d" % c) for c in range(4)], writes=[K("ps%d" % P_TY)])
                S.op("act", lambda e, h=h: e.activation(out=yT[:, h * 1024:(h + 1) * 1024], in_=psb[P_TY], func=AF.Copy),
                     reads=[K("ps%d" % P_TY)], writes=[K("yT")])
            for n in range(2):
                b = P_O[n]
                for k in range(16):
                    S.op("pe", lambda e, k=k, n=n, b=b: e.matmul(psf[b], lhsT=yT[:, k * 128:(k + 1) * 128],
                                                                rhs=Wo[k][:, n * 512:(n + 1) * 512],
                                                                start=(k == 0), stop=(k == 15)),
                         reads=[K("yT"), K("Wo")], writes=[K("ps%d" % b)])
                S.op("act", lambda e, n=n, b=b: e.activation(out=junk[:, 0:512], in_=psf[b], func=AF.Square,
                                                             accum_out=ss2[:, n:n + 1]),
                     reads=[K("ps%d" % b)], writes=[K("junk"), K("ss2")])
            S.op("dve", lambda e: e.tensor_tensor(out=a2, in0=ss2[:, 0:1], in1=ss2[:, 1:2], op=ALU.add),
                 reads=[K("ss2")], writes=[K("a2")])
            S.op("dve", lambda e: e.tensor_scalar(out=a2, in0=a2, scalar1=1.0 / D, scalar2=RMS_EPS, op0=ALU.mult,
                                                  op1=ALU.add), reads=[K("a2")], writes=[K("a2")])
            self.rsqrt(K("a2"), a2, K("rstd2"), rstd2, K("tmp3"), tmp3)
            for n in range(2):
                b = P_O[n]
                S.op("dve", lambda e, n=n, b=b: e.scalar_tensor_tensor(out=tpost[:, n * 512:(n + 1) * 512], in0=psf[b],
                                                                       scalar=rstd2, in1=gpo[:, n * 512:(n + 1) * 512],
                                                                       op0=ALU.mult, op1=ALU.mult),
                     reads=[K("ps%d" % b), K("rstd2"), K("gpo")], writes=[K("tpost")])
            S.op("pool", lambda e: e.tensor_tensor(out=xt[p], in0=xt[p], in1=tpost, op=ALU.add),
                 reads=[xk, K("tpost")], writes=[xk])
            o = S.dma("pool", "xs%d" % p, lambda e: e.dma_start(out=x_dst[i * 128:(i + 1) * 128, :], in_=xt[p]),
                      reads=[xk], writes=[K("xdst")])
            self.final.append(o)

        load(0)
        if NT > 1:
            load(1)
        pre(0)
        stage1(0)
        for i in range(NT):
            if i + 2 < NT:
                load(i + 2)
            if i + 1 < NT:
                pre(i + 1)
            stage2a(i)
            if i + 1 < NT:
                stage1(i + 1)
            stage2b(i)
        S.barrier()
        self.peak = max(getattr(self, "peak", 0), self.off)
        self.off = off0


    def layer_b(self, x_src, x_halo, x_dst, ysc, vsc, w_in, gpre_d, w_out, npost, hv_d, tabs, ident_d, rm_d, mask_d,
                NOWN=4096, dbg_hps=range(8), dbg_gs=range(3), dbg_attn=True):
        S, nc = self.S, self.nc
        off0 = self.off
        L = self.key("B")
        K = lambda s: "%s/%s" % (L, s)
        psf, psb = self.ps_f32, self.ps_b16
        HAL = 2048
        NT = HAL + NOWN
        DIL = (1, 4, 16)
        P_PR, P_RT, P_S2, P_ND, P_V2 = (0, 1), 2, ((3, 4), (0, 1), (2, 7)), (5, 6), (7, 3)

        hT = self.carve(8 * NT, BF16).rearrange("p (k t) -> p k t", k=8)
        offq = self.off
        QT = self.carve(NOWN, BF16)
        KT = self.carve(NT, BF16)
        Vb = self.carve(NT, BF16)
        acc = self.carve(2 * NOWN, F32).rearrange("p (a t) -> p a t", a=2)
        offw = self.off
        wst = self.carve(3 * 1024, F32)
        wb = [self.carve(3 * 1024, BF16) for _ in range(2)]
        idt = self.carve(128, BF16)
        rm = self.carve(128, BF16)
        maskT = self.carve(512, BF16)
        ones = self.carve(64, BF16)
        hvones = self.carve(64, BF16)
        loones = self.carve(64, BF16)
        hv2 = self.carve(2, F32)
        hv, lo = hv2[:, 0:1], hv2[:, 1:2]
        PT = [self.carve(512, BF16) for _ in range(6)]
        qraw = [self.carve(512, BF16) for _ in range(3)]
        rt1 = [self.carve(512, BF16) for _ in range(3)]
        ctab = [self.carve(512, BF16) for _ in range(3)]
        stab = [self.carve(512, BF16) for _ in range(3)]
        st = self.carve(64, F32)
        print("layer B persistent SBUF bytes", self.off - off0)
        offp = self.off

        S.dma("sp", "idt", lambda e: e.dma_start(out=idt, in_=ident_d), writes=[K("idt")])
        S.dma("sp", "rm", lambda e: e.dma_start(out=rm, in_=rm_d), writes=[K("rm")])
        S.dma("sp", "mask", lambda e: e.dma_start(out=maskT, in_=mask_d), writes=[K("mask")])
        S.dma("sp", "hv", lambda e: e.dma_start(out=hv2, in_=hv_d), writes=[K("hv")])
        S.op("pool", lambda e: e.memset(ones, 1.0), writes=[K("ones")])
        S.op("pool", lambda e: e.memset(hvones, 1.0), writes=[K("hvones")])
        S.op("pool", lambda e: e.tensor_scalar(out=hvones, in0=hvones, scalar1=hv, scalar2=None, op0=ALU.mult),
             reads=[K("hv"), K("hvones")], writes=[K("hvones")])
        S.op("pool", lambda e: e.memset(loones, 1.0), writes=[K("loones")])
        S.op("pool", lambda e: e.tensor_scalar(out=loones, in0=loones, scalar1=lo, scalar2=None, op0=ALU.mult),
             reads=[K("hv"), K("loones")], writes=[K("loones")])

        need1 = 8 * 4096 + 8 * 2048 + 4096 + 2048 + 256
        def place(need):
            if offp + need <= self.cap:
                self.off = offp
            else:
                assert offq + need <= offw, ("overlay does not fit", need, offw - offq)
                self.off = offq
        place(need1)
        gb = self.carve(1024, F32)
        xt = [self.carve(1024, F32) for _ in range(8)]
        hb = [self.carve(1024, BF16) for _ in range(8)]
        junk = self.carve(1024, BF16)
        st1 = self.carve(32, F32)
        S.dma("sp", "gb", lambda e: e.dma_start(out=gb, in_=gpre_d.partition_broadcast(128)), writes=[K("gb")])

        def xrows(i):
            return x_halo[i * 128:(i + 1) * 128, :] if i < 16 else x_src[(i - 16) * 128:(i - 15) * 128, :]

        def p1_load(i):
            p = i % 8
            S.dma("sp", "x1l%d" % p, lambda e: e.dma_start(out=xt[p], in_=xrows(i)), writes=[K("x1t%d" % p)])

        NT1 = NT // 128
        NG1 = NT1 // 4

        def p1_x(gi):
            q = gi % 2
            ss, ms, rstd, tmp1 = (st1[:, q * 16 + c * 4:q * 16 + c * 4 + 4] for c in range(4))
            for t in range(4):
                i = gi * 4 + t
                p = i % 8
                S.op("act", lambda e, p=p, t=t, ss=ss: e.activation(out=junk, in_=xt[p], func=AF.Square, accum_out=ss[:, t:t + 1]),
                     reads=[K("x1t%d" % p)], writes=[K("junk"), K("ss%d" % q)])
            S.op("dve", lambda e: e.tensor_scalar(out=ms, in0=ss, scalar1=1.0 / D, scalar2=RMS_EPS, op0=ALU.mult,
                                                  op1=ALU.add), reads=[K("ss%d" % q)], writes=[K("ms%d" % q)])
            self.rsqrt(K("ms%d" % q), ms, K("rstd%d" % q), rstd, K("tmp1%d" % q), tmp1)
            for t in range(4):
                i = gi * 4 + t
                p = i % 8
                S.op("dve", lambda e, p=p, t=t, rstd=rstd: e.scalar_tensor_tensor(out=hb[p], in0=xt[p], scalar=rstd[:, t:t + 1],
                                                                                  in1=gb, op0=ALU.mult, op1=ALU.mult),
                     reads=[K("x1t%d" % p), K("rstd%d" % q), K("gb")], writes=[K("hb%d" % p)])

        def p1_y(gi):
            for t in range(4):
                i = gi * 4 + t
                p = i % 8
                pb = P_PR[i % 2]
                for k in range(8):
                    S.op("pe", lambda e, k=k, p=p, pb=pb: e.transpose(out=psb[pb][:, k * 128:(k + 1) * 128],
                                                                in_=hb[p][:, k * 128:(k + 1) * 128], identity=idt),
                         reads=[K("hb%d" % p), K("idt")], writes=[K("ps%d" % pb)])
                S.op("act", lambda e, i=i, pb=pb: e.activation(out=hT[:, :, i * 128:(i + 1) * 128],
                                                               in_=psb[pb].rearrange("p (k t) -> p k t", k=8), func=AF.Copy),
                     reads=[K("ps%d" % pb)], writes=[K("hT")])

        for i in range(min(8, NT1)):
            p1_load(i)
        p1_x(0)
        for gi in range(NG1):
            if gi + 1 < NG1:
                p1_x(gi + 1)
            p1_y(gi)
            for t in range(4):
                i = (gi + 2) * 4 + t
                if i < NT1:
                    p1_load(i)
        S.barrier()

        KBG = 8 if NOWN >= 4096 else 4
        need15 = 2 * 8 * 1024 * 2 + 2 * KBG * 1024 * 2
        place(need15)
        Wv2 = [self.carve(8 * 1024, BF16).rearrange("p (k c) -> p k c", k=8) for _ in range(2)]
        vstg = [self.carve(KBG * 1024, BF16).rearrange("p (b c) -> p b c", b=KBG) for _ in range(2)]
        vcnt = [0]
        def load_wv(gidx):
            g = list(dbg_gs)[gidx]
            Wv = Wv2[gidx % 2]
            for k in range(8):
                slot = k % 3
                stg = wst[:, slot * 1024:(slot + 1) * 1024]
                S.dma("sp", "wvs%d" % slot, lambda e, k=k, g=g, stg=stg: e.dma_start(
                    out=stg, in_=w_in[k * 128:(k + 1) * 128, g * 3072 + 2048:g * 3072 + 3072]), writes=[K("wst%d" % slot)])
                if k % 2 == 0:
                    S.op("act", lambda e, k=k, stg=stg, Wv=Wv: e.activation(out=Wv[:, k, :], in_=stg, func=AF.Copy),
                         reads=[K("wst%d" % slot)], writes=[K("Wv%d" % (gidx % 2))])
                else:
                    S.op("dve", lambda e, k=k, stg=stg, Wv=Wv: e.tensor_copy(out=Wv[:, k, :], in_=stg), reads=[K("wst%d" % slot)],
                         writes=[K("Wv%d" % (gidx % 2))])

        if len(list(dbg_gs)) > 0:
            load_wv(0)
        for gidx, g in enumerate(dbg_gs):
            d = DIL[g]
            span = 128 * d
            nkb = (NOWN // span + 1) * d
            kt0 = HAL - span
            Wv = Wv2[gidx % 2]
            wvkey = K("Wv%d" % (gidx % 2))
            if gidx + 1 < len(list(dbg_gs)):
                load_wv(gidx + 1)
            for kb0 in range(0, nkb, KBG):
                nb = min(KBG, nkb - kb0)
                vs = vcnt[0] % 2
                vcnt[0] += 1
                for bl in range(nb):
                    kb = kb0 + bl
                    sp_, r = kb // d, kb % d
                    t_start = kt0 + sp_ * span + r
                    banks = P_PR if kb % 2 == 0 else (P_RT, P_V2[0])
                    for k in range(8):
                        for n in range(2):
                            S.op("pe", lambda e, k=k, n=n, t_start=t_start, d=d, bk=banks[n], Wv=Wv: e.matmul(
                                psf[bk], lhsT=hT[:, k, t_start:t_start + 127 * d + 1:d], rhs=Wv[:, k, n * 512:(n + 1) * 512],
                                start=(k == 0), stop=(k == 7)), reads=[wvkey, K("hT")], writes=[K("ps%d" % banks[n])])
                    c = 0 if kb < d else (1 if kb < d + 16 else 2)
                    fl = hv if c == 0 else lo
                    for n in range(2):
                        dst = vstg[vs][:, bl, n * 512:(n + 1) * 512]
                        bk = banks[n]
                        if n == 0:
                            if c == 2:
                                S.op("act", lambda e, dst=dst, bk=bk: e.activation(out=dst, in_=psf[bk], func=AF.Copy),
                                     reads=[K("ps%d" % bk)], writes=[K("vstg%d" % vs)])
                            else:
                                S.op("act", lambda e, dst=dst, bk=bk, fl=fl: e.activation(out=dst, in_=psf[bk], func=AF.Copy,
                                                                                        scale=fl),
                                     reads=[K("ps%d" % bk), K("hv")], writes=[K("vstg%d" % vs)])
                        else:
                            if c == 2:
                                S.op("dve", lambda e, dst=dst, bk=bk: e.tensor_copy(out=dst, in_=psf[bk]),
                                     reads=[K("ps%d" % bk)], writes=[K("vstg%d" % vs)])
                            else:
                                S.op("dve", lambda e, dst=dst, bk=bk, fl=fl: e.tensor_scalar(out=dst, in0=psf[bk], scalar1=fl,
                                                                                          scalar2=None, op0=ALU.mult),
                                     reads=[K("ps%d" % bk), K("hv")], writes=[K("vstg%d" % vs)])
                for h8 in range(8):
                    S.dma("sp", "vst%d" % vs, lambda e, g=g, kb0=kb0, nb=nb, vs=vs, h8=h8: e.dma_start(
                        out=vsc[g, h8, :, kb0:kb0 + nb, :], in_=vstg[vs][:, 0:nb, h8 * 128:(h8 + 1) * 128]),
                        reads=[K("vstg%d" % vs)], writes=[K("vsc")])
        S.barrier()

        w5 = w_in[:, 0:9216].rearrange("(k p) (g t h f) -> p k g t h f", p=128, g=3, t=3, h=8)
        wz = w_in[:, 9216:10240].rearrange("(k p) (h f) -> p k h f", p=128, h=8)
        tabi = [0]
        bankc = [0]
        wcnt = [0]

        pending_cast = []

        def flush_cast():
            while pending_cast:
                pending_cast.pop(0)()

        wjobs = []
        for hp_ in dbg_hps:
            for g_ in dbg_gs:
                wjobs.append((w5[:, :, g_, :, hp_, :], 384))
            wjobs.append((wz[:, :, hp_, :], 128))
        wready = {}

        def load_w(src_ap=None, ncols=None):
            i = wcnt[0]
            wcnt[0] += 1
            if i == 0:
                wready[0] = _load_w(0, *wjobs[0])
            if i + 1 < len(wjobs):
                wready[i + 1] = _load_w(i + 1, *wjobs[i + 1])
            return wready.pop(i)

        def _load_w(i, src_ap, ncols):
            slot = i % 2
            dst32 = wst[:, 0:8 * ncols]
            if len(src_ap.shape) == 3:
                S.dma("sp", "wst", lambda e: e.dma_start(out=dst32.rearrange("p (k c) -> p k c", k=8), in_=src_ap),
                      writes=[K("wst")])
            else:
                d4 = dst32.rearrange("p (k t f) -> p k t f", k=8, t=3)
                for t in range(3):
                    S.dma("sp", "wst", lambda e, t=t: e.dma_start(out=d4[:, :, t, :], in_=src_ap[:, :, t, :]),
                          writes=[K("wst")])
            def cast():
                S.op("act", lambda e: e.activation(out=wb[slot][:, 0:8 * ncols], in_=dst32, func=AF.Copy), reads=[K("wst")],
                     writes=[K("wb%d" % slot)])
            if i == 0:
                cast()
            else:
                pending_cast.append(cast)
            return wb[slot][:, 0:8 * ncols].rearrange("p (k c) -> p k c", k=8), K("wb%d" % slot)

        def proj_bank(wv, wkey, c0, tau0, n):
            b = P_PR[bankc[0] % 2]
            bankc[0] += 1
            for k in range(8):
                S.op("pe", lambda e, k=k, b=b: e.matmul(psf[b][:, 0:n], lhsT=wv[:, k, c0:c0 + 128], rhs=hT[:, k, tau0:tau0 + n],
                                                       start=(k == 0), stop=(k == 7)),
                     reads=[wkey, K("hT")], writes=[K("ps%d" % b)])
            flush_rot()
            return b

        pending_rot = []

        def flush_rot():
            while pending_rot:
                pending_rot.pop(0)()

        def rope_bank(b, n, d, dest, dkey, ctd, std, t0):
            j = tabi[0] % 3
            tabi[0] += 1
            S.dma("sp", "ct%d" % j, lambda e: e.dma_start(out=ctab[j][:, 0:n], in_=ctd[:, t0:t0 + n]), writes=[K("ctab%d" % j)])
            S.dma("sp", "st%d" % j, lambda e: e.dma_start(out=stab[j][:, 0:n], in_=std[:, t0:t0 + n]), writes=[K("stab%d" % j)])
            if d == 1 or n < 512:
                ov, iv = qraw[j][:, 0:n], psf[b][:, 0:n]
            else:
                ov = qraw[j].rearrange("p (r i) -> p r i", r=d)
                iv = psf[b].rearrange("p (i r) -> p r i", r=d)
            S.op("act", lambda e: e.activation(out=ov, in_=iv, func=AF.Copy), reads=[K("ps%d" % b)], writes=[K("qraw%d" % j)])

            def second():
                S.op("pe", lambda e: e.matmul(psf[P_RT][:, 0:n], lhsT=rm, rhs=qraw[j][:, 0:n], start=True, stop=True),
                     reads=[K("rm"), K("qraw%d" % j)], writes=[K("ps%d" % P_RT)])
                S.op("dve", lambda e: e.tensor_tensor(out=rt1[j][:, 0:n], in0=psf[P_RT][:, 0:n], in1=stab[j][:, 0:n], op=ALU.mult),
                     reads=[K("ps%d" % P_RT), K("stab%d" % j)], writes=[K("rt1%d" % j)])
                S.op("pool", lambda e: e.tensor_tensor(out=qraw[j][:, 0:n], in0=qraw[j][:, 0:n], in1=ctab[j][:, 0:n], op=ALU.mult),
                     reads=[K("qraw%d" % j), K("ctab%d" % j)], writes=[K("qraw%d" % j)])
                if d == 16:
                    a0 = rt1[j].rearrange("p (r i) -> p r i", r=16)
                    a1 = qraw[j].rearrange("p (r i) -> p r i", r=16)
                else:
                    a0, a1 = rt1[j][:, 0:n], qraw[j][:, 0:n]
                S.op("dve", lambda e: e.tensor_tensor(out=dest, in0=a0, in1=a1, op=ALU.add),
                     reads=[K("rt1%d" % j), K("qraw%d" % j)], writes=[dkey])
            pending_rot.append(second)

        ptc = [0]
        sc = [0]
        ndc = [0]
        pending_tail = []

        def flush_tail():
            while pending_tail:
                pending_tail.pop(0)()

        for hp in dbg_hps:
            if not pending_tail:
                S.op("pool", lambda e: e.memset(acc.rearrange("p a t -> p (a t)"), 0.0), writes=[K("acc")])
            for g in dbg_gs:
                d = DIL[g]
                span = 128 * d
                nsp = NOWN // span
                nkb = (nsp + 1) * d
                kt0 = HAL - span
                nk = NOWN + span
                cq, sq, ck, sk = tabs[g]
                wv, wkey = load_w(w5[:, :, g, :, hp, :], 384)
                for bq in range(NOWN // 512):
                    if bq == 3 and pending_tail:
                        flush_tail()
                        S.op("pool", lambda e: e.memset(acc.rearrange("p a t -> p (a t)"), 0.0), writes=[K("acc")])
                    b = proj_bank(wv, wkey, 0, HAL + bq * 512, 512)
                    if d == 16:
                        sp_, i0 = (bq * 512) // span, ((bq * 512) % span) // 16
                        dest = QT[:, sp_ * span:(sp_ + 1) * span].rearrange("p (r i) -> p r i", r=16)[:, :, i0:i0 + 32]
                        rope_bank(b, 512, d, dest, K("QT"), cq, sq, bq * 512)
                    else:
                        rope_bank(b, 512, d, QT[:, bq * 512:(bq + 1) * 512], K("QT"), cq, sq, bq * 512)
                nb_full, rem = nk // 512, nk % 512
                for bk in range(nb_full + (1 if rem else 0)):
                    n = 512 if bk < nb_full else rem
                    b = proj_bank(wv, wkey, 128, kt0 + bk * 512, n)
                    if d == 16:
                        sp_, i0 = (bk * 512) // span, ((bk * 512) % span) // 16
                        dest = KT[:, sp_ * span:(sp_ + 1) * span].rearrange("p (r i) -> p r i", r=16)[:, :, i0:i0 + 32]
                        rope_bank(b, 512, d, dest, K("KT"), ck, sk, bk * 512)
                    else:
                        rope_bank(b, n, d, KT[:, bk * 512:bk * 512 + n], K("KT"), ck, sk, bk * 512)
                flush_rot()
                S.dma("sp", "vbl", lambda e, g=g, hp=hp, nkb=nkb: e.dma_start(
                    out=Vb[:, 0:nkb * 128], in_=vsc[g, hp, :, 0:nkb, :].rearrange("p b f -> p (b f)")),
                    reads=[K("vsc")], writes=[K("Vb")])
                nqb = nsp * d

                def s_pair(qb0):
                    ba, bb = P_S2[sc[0] % 3]
                    sc[0] += 1
                    for blk in range(2):
                        qb = qb0 + blk
                        for kbi in range(2):
                            kb = qb + kbi * d
                            for hh, b in ((0, ba), (1, bb)):
                                S.op("pe", lambda e, hh=hh, kbi=kbi, kb=kb, b=b, blk=blk, qb=qb: e.matmul(
                                    psf[b][:, (blk * 2 + kbi) * 128:(blk * 2 + kbi + 1) * 128],
                                    lhsT=KT[hh * 64:(hh + 1) * 64, kb * 128:(kb + 1) * 128],
                                    rhs=QT[hh * 64:(hh + 1) * 64, qb * 128:(qb + 1) * 128], start=True, stop=True,
                                    tile_position=(hh * 64, 0)),
                                    reads=[K("KT"), K("QT")], writes=[K("ps%d" % b)])
                    return ba, bb

                def e_pair(banks):
                    js = []
                    for b in banks:
                        j = ptc[0] % 6
                        ptc[0] += 1
                        S.op("act", lambda e, b=b, j=j: e.activation(out=PT[j], in_=psf[b], func=AF.Exp, scale=0.125),
                             reads=[K("ps%d" % b)], writes=[K("PT%d" % j)])
                        if j % 2 == 0:
                            S.op("pool", lambda e, j=j: e.tensor_tensor(out=PT[j], in0=PT[j], in1=maskT, op=ALU.mult),
                                 reads=[K("PT%d" % j), K("mask")], writes=[K("PT%d" % j)])
                        else:
                            S.op("dve", lambda e, j=j: e.tensor_tensor(out=PT[j], in0=PT[j], in1=maskT, op=ALU.mult),
                                 reads=[K("PT%d" % j), K("mask")], writes=[K("PT%d" % j)])
                        js.append(j)
                    return js

                def pv_pair(qb0, js, bnd):
                    for blk in range(2):
                        qb = qb0 + blk
                        for what in range(2):
                            for kbi in range(2):
                                kb = qb + kbi * d
                                for hh in range(2):
                                    j = js[hh]
                                    if what == 0:
                                        lhsT, lk = Vb[:, kb * 128 + hh * 64:kb * 128 + (hh + 1) * 64], K("Vb")
                                    elif kb < d:
                                        lhsT, lk = hvones, K("hvones")
                                    elif kb < d + 16:
                                        lhsT, lk = loones, K("loones")
                                    else:
                                        lhsT, lk = ones, K("ones")
                                    S.op("pe", lambda e, hh=hh, what=what, kbi=kbi, lhsT=lhsT, blk=blk, j=j: e.matmul(
                                        psf[bnd][hh * 64:(hh + 1) * 64, (what * 2 + blk) * 128:(what * 2 + blk + 1) * 128],
                                        lhsT=lhsT, rhs=PT[j][:, (blk * 2 + kbi) * 128:(blk * 2 + kbi + 1) * 128],
                                        start=(kbi == 0), stop=(kbi == 1), tile_position=(0, hh * 64)),
                                        reads=[lk, K("PT%d" % j)], writes=[K("ps%d" % bnd)])

                def acc_pair(qb0, bnd):
                    sp_, r = qb0 // d, qb0 % d
                    if d == 1:
                        av = acc[:, :, qb0 * 128:(qb0 + 2) * 128]
                        pv = psf[bnd].rearrange("p (a t) -> p a t", a=2)
                    else:
                        av = acc[:, :, sp_ * span:(sp_ + 1) * span].rearrange("p a (i r) -> p a r i", r=d)[:, :, r:r + 2, :]
                        pv = psf[bnd].rearrange("p (a s i) -> p a s i", a=2, s=2)
                    S.op("dve", lambda e: e.tensor_tensor(out=av, in0=av, in1=pv, op=ALU.add),
                         reads=[K("acc"), K("ps%d" % bnd)], writes=[K("acc")])

                flush_cast()
                if not dbg_attn:
                    continue
                npair = nqb // 2
                sb = {0: s_pair(0)}
                if npair > 1:
                    sb[1] = s_pair(2)
                jss = {0: e_pair(sb.pop(0))}
                for pi in range(npair):
                    if pi + 2 < npair:
                        sb[pi + 2] = s_pair(2 * (pi + 2))
                    if pi + 1 < npair:
                        jss[pi + 1] = e_pair(sb.pop(pi + 1))
                    bnd = P_ND[ndc[0] % 2]
                    ndc[0] += 1
                    pv_pair(2 * pi, jss.pop(pi), bnd)
                    acc_pair(2 * pi, bnd)
            wvz, wzkey = load_w(wz[:, :, hp, :], 128)
            zs = KT[:, 0:NOWN]
            for bq in range(NOWN // 512):
                b = proj_bank(wvz, wzkey, 0, HAL + bq * 512, 512)
                S.op("act", lambda e, b=b, bq=bq: e.activation(out=zs[:, bq * 512:(bq + 1) * 512], in_=psf[b], func=AF.Silu),
                     reads=[K("ps%d" % b)], writes=[K("KT")])
            def tail(hp=hp):
                S.op("dve", lambda e: e.tensor_scalar(out=acc[:, 1, :], in0=acc[:, 1, :], scalar1=1e-18, scalar2=None, op0=ALU.max),
                     reads=[K("acc")], writes=[K("acc")])
                S.op("act", lambda e: e.activation(out=acc[:, 1, :], in_=acc[:, 1, :], func=AF.Ln), reads=[K("acc")], writes=[K("acc")])
                S.op("act", lambda e: e.activation(out=acc[:, 1, :], in_=acc[:, 1, :], func=AF.Exp, scale=-1.0),
                     reads=[K("acc")], writes=[K("acc")])
                S.op("dve", lambda e: e.tensor_tensor(out=acc[:, 0, :], in0=acc[:, 0, :], in1=acc[:, 1, :], op=ALU.mult),
                     reads=[K("acc")], writes=[K("acc")])
                S.op("dve", lambda e: e.tensor_tensor(out=zs, in0=acc[:, 0, :], in1=zs, op=ALU.mult),
                     reads=[K("acc"), K("KT")], writes=[K("KT")])
                S.dma("pool", "ysc", lambda e: e.dma_start(out=ysc[hp], in_=zs), reads=[K("KT")], writes=[K("ysc")])
            pending_tail.append(tail)
            flush_tail()
            flush_cast()
        flush_tail()
        S.barrier()

        self.off = off0
        Wo = self.carve(8 * 1024, BF16).rearrange("p (k c) -> p k c", k=8)
        wos = self.carve(1024, F32)
        gpo = self.carve(1024, F32)
        yt = [self.carve(8 * 512, BF16).rearrange("p (h t) -> p h t", h=8) for _ in range(2)]
        x3 = [self.carve(1024, F32) for _ in range(8)]
        osb = [self.carve(1024, F32) for _ in range(8)]
        tpost = [self.carve(1024, F32) for _ in range(2)]
        junk3 = self.carve(512, BF16)
        st3 = self.carve(64, F32)
        P_O = ((0, 1), (2, 3))
        S.dma("sp", "gpo", lambda e: e.dma_start(out=gpo, in_=npost.partition_broadcast(128)), writes=[K("gpo")])
        for k in range(8):
            S.dma("sp", "wos", lambda e, k=k: e.dma_start(out=wos, in_=w_out[k * 128:(k + 1) * 128, :]), writes=[K("wos")])
            S.op("act", lambda e, k=k: e.activation(out=Wo[:, k, :], in_=wos, func=AF.Copy), reads=[K("wos")], writes=[K("Wo")])
        yv = ysc.rearrange("h p t -> p h t")
        NG3 = NOWN // 512

        def p3_yload(gi):
            q = gi % 2
            S.dma("sp", "ytl%d" % q, lambda e: e.dma_start(out=yt[q], in_=yv[:, :, gi * 512:(gi + 1) * 512]),
                  reads=[K("ysc")], writes=[K("yt%d" % q)])

        def p3_load(gi):
            for t in range(4):
                i = gi * 4 + t
                p = i % 8
                S.dma("sp", "x3l%d" % p, lambda e, i=i, p=p: e.dma_start(out=x3[p], in_=x_src[i * 128:(i + 1) * 128, :]),
                      writes=[K("x3%d" % p)])

        def p3_mm(gi):
            q = gi % 2
            ss2 = st3[:, q * 32:q * 32 + 8]
            for t in range(4):
                i = gi * 4 + t
                p = i % 8
                c0 = t * 128
                pb = P_O[i % 2]
                for n in range(2):
                    bk = pb[n]
                    for k in range(8):
                        S.op("pe", lambda e, k=k, n=n, bk=bk, c0=c0: e.matmul(psf[bk], lhsT=yt[q][:, k, c0:c0 + 128],
                                                                            rhs=Wo[:, k, n * 512:(n + 1) * 512],
                                                                            start=(k == 0), stop=(k == 7)),
                             reads=[K("yt%d" % q), K("Wo")], writes=[K("ps%d" % bk)])
                    S.op("act", lambda e, n=n, bk=bk, t=t, ss2=ss2: e.activation(out=junk3, in_=psf[bk], func=AF.Square,
                                                                               accum_out=ss2[:, 2 * t + n:2 * t + n + 1]),
                         reads=[K("ps%d" % bk)], writes=[K("junk3"), K("ss2%d" % q)])
                    S.op("act", lambda e, n=n, bk=bk, p=p: e.activation(out=osb[p][:, n * 512:(n + 1) * 512], in_=psf[bk],
                                                                       func=AF.Copy),
                         reads=[K("ps%d" % bk)], writes=[K("osb%d" % p)])

        def p3_post(gi):
            q = gi % 2
            ss2 = st3[:, q * 32:q * 32 + 8]
            a2, rstd2, tmp3 = (st3[:, q * 32 + 8 + c * 4:q * 32 + 12 + c * 4] for c in range(3))
            ssv = ss2.rearrange("p (t n) -> p t n", n=2)
            S.op("dve", lambda e: e.tensor_tensor(out=a2, in0=ssv[:, :, 0], in1=ssv[:, :, 1], op=ALU.add),
                 reads=[K("ss2%d" % q)], writes=[K("a2%d" % q)])
            S.op("dve", lambda e: e.tensor_scalar(out=a2, in0=a2, scalar1=1.0 / D, scalar2=RMS_EPS, op0=ALU.mult,
                                                  op1=ALU.add), reads=[K("a2%d" % q)], writes=[K("a2%d" % q)])
            self.rsqrt(K("a2%d" % q), a2, K("rstd2%d" % q), rstd2, K("tmp3%d" % q), tmp3)
            for t in range(4):
                i = gi * 4 + t
                p = i % 8
                tp = tpost[i % 2]
                S.op("dve", lambda e, p=p, t=t, tp=tp, rstd2=rstd2: e.scalar_tensor_tensor(out=tp, in0=osb[p], scalar=rstd2[:, t:t + 1],
                                                                                      in1=gpo, op0=ALU.mult, op1=ALU.mult),
                     reads=[K("osb%d" % p), K("rstd2%d" % q), K("gpo")], writes=[K("tpost%d" % (i % 2))])
                S.op("pool", lambda e, p=p, tp=tp: e.tensor_tensor(out=x3[p], in0=x3[p], in1=tp, op=ALU.add),
                     reads=[K("x3%d" % p), K("tpost%d" % (i % 2))], writes=[K("x3%d" % p)])
                o = S.dma("pool", "x3s%d" % p, lambda e, i=i, p=p: e.dma_start(out=x_dst[i * 128:(i + 1) * 128, :], in_=x3[p]),
                          reads=[K("x3%d" % p)], writes=[K("xdst")])
                self.final.append(o)

        p3_yload(0)
        p3_load(0)
        if NG3 > 1:
            p3_yload(1)
            p3_load(1)
        p3_mm(0)
        for gi in range(NG3):
            if gi + 1 < NG3:
                p3_mm(gi + 1)
            if gi + 2 < NG3:
                p3_yload(gi + 2)
            p3_post(gi)
            if gi + 2 < NG3:
                p3_load(gi + 2)
        S.barrier()
        self.peak = max(getattr(self, "peak", 0), self.off)
        self.off = off0

    def finish(self):
        S, nc = self.S, self.nc
        S.finalize()
        with contextlib.ExitStack() as es:
            sems = {k: es.enter_context(nc.semaphore("s%d" % i)) for i, k in enumerate(S.sem_keys)}
            block = es.enter_context(nc.Block())
            S.emit(block, sems, final_waits=self.final)
        return nc


_CACHE = {}


def consts():
    ident = np.eye(128, dtype=np.float32).astype(ml_dtypes.bfloat16)
    tril = np.tril(np.ones((128, 128), dtype=np.float32))
    return ident, tril


def prog_a(ntok):
    key = ("A", ntok)
    if key not in _CACHE:
        P = Prog()
        x = P.inp("x", [ntok, D])
        w_in = P.inp("w_in", [D, 3 * EW])
        npre = P.inp("npre", [128, 8])
        ln_g = P.inp("ln_g", [1, EW])
        ln_b = P.inp("ln_b", [1, EW])
        w_s = P.inp("w_s", [8, 128, 128])
        bs = P.inp("bs", [128, 8])
        w_out = P.inp("w_out", [EW, D])
        npost = P.inp("npost", [1, D])
        ident = P.inp("ident", [128, 128], BF16)
        tril = P.inp("tril", [128, 128])
        y = P.outp("y", [ntok, D])
        P.layer_a(x, y, ntok, w_in, npre, ln_g, ln_b, w_s, bs, w_out, npost, ident, tril)
        _CACHE[key] = P.finish()
    return _CACHE[key]


def col128(vec):
    return np.ascontiguousarray(np.asarray(vec, dtype=np.float32).reshape(-1, 128).T)


def run_layer_a(xs, i, inputs):
    j = i // 2
    ident, tril = consts()
    ntok = xs[0].shape[0]
    nc = prog_a(ntok)
    common = {
        "w_in": np.ascontiguousarray(inputs["a_w_in"][j]),
        "npre": col128(inputs["norm_pre"][i]),
        "ln_g": np.ascontiguousarray(inputs["a_ln_g"][j][None, :]),
        "ln_b": np.ascontiguousarray(inputs["a_ln_b"][j][None, :]),
        "w_s": np.ascontiguousarray(inputs["a_w_s"][j]),
        "bs": np.ascontiguousarray(np.asarray(inputs["a_b_s"][j]).T),
        "w_out": np.ascontiguousarray(inputs["a_w_out"][j]),
        "npost": np.ascontiguousarray(inputs["norm_post"][i][None, :]),
        "ident": ident, "tril": tril,
    }
    in_maps = [dict(common, x=np.ascontiguousarray(x)) for x in xs]
    res = run_bass_kernel_spmd(nc, in_maps, core_ids=list(range(len(xs))))
    return [r["y"] for r in res.results]


DILS = (1, 4, 16)
HALO = 2048


def rope_consts():
    rm = np.zeros((128, 128), np.float32)
    for fp in range(128):
        if fp % 64 < 32:
            rm[fp + 32, fp] = -1.0
        else:
            rm[fp - 32, fp] = 1.0
    mask = np.zeros((128, 2, 2, 128), np.float32)
    j = np.arange(128)[:, None]
    i = np.arange(128)[None, :]
    for hh in range(2):
        mask[:, hh, 0, :] = (j >= i)
        mask[:, hh, 1, :] = (j <= i)
    return rm.astype(ml_dtypes.bfloat16), mask.reshape(128, 512).astype(ml_dtypes.bfloat16)


def rope_tables_core(pos0, nown=4096):
    inv_freq = (1.0 / (np.float32(10000.0) ** (np.arange(0, 64, 2, dtype=np.float32) / np.float32(64)))).astype(np.float32)
    out = []
    for d in DILS:
        span = 128 * d
        res = []
        for (tau0, ncols) in ((HALO, nown), (HALO - span, nown + span)):
            tau = np.zeros(ncols, np.int64)
            for b0 in range(0, ncols, 512):
                n = min(512, ncols - b0)
                col = np.arange(n)
                r, ii = col // (n // d), col % (n // d)
                tau[b0:b0 + n] = tau0 + b0 + ii * d + r
            pos = (pos0 - HALO + tau).astype(np.float32)
            ang = pos[None, :] * inv_freq[:, None]
            c = np.tile(np.cos(ang), (4, 1)).astype(ml_dtypes.bfloat16)
            s_ = np.tile(np.sin(ang), (4, 1)).astype(ml_dtypes.bfloat16)
            res += [np.ascontiguousarray(c), np.ascontiguousarray(s_)]
        out.append(res)
    return out


DBG = {}


def prog_b():
    key = ("B",)
    if key not in _CACHE:
        P = Prog()
        x = P.inp("x", [TOK, D])
        xh = P.inp("xh", [HALO, D])
        w_in = P.inp("w_in", [D, 10240])
        gpre = P.inp("gpre", [1, D])
        w_out = P.inp("w_out", [D, D])
        npost = P.inp("npost", [1, D])
        hv = P.inp("hv", [128, 2])
        ident = P.inp("ident", [128, 128], BF16)
        rm = P.inp("rm", [128, 128], BF16)
        mask = P.inp("mask", [128, 512], BF16)
        tabs = []
        for g, d in enumerate(DILS):
            nk = 4096 + 128 * d
            tabs.append((P.inp("cq%d" % g, [128, 4096], BF16), P.inp("sq%d" % g, [128, 4096], BF16),
                         P.inp("ck%d" % g, [128, nk], BF16), P.inp("sk%d" % g, [128, nk], BF16)))
        ysc = P.nc.dram_tensor("ysc", [8, 128, 4096], BF16).ap()
        vsc = P.nc.dram_tensor("vsc", [3, 8, 128, 48, 128], BF16).ap()
        y = P.outp("y", [TOK, D])
        P.layer_b(x, xh, y, ysc, vsc, w_in, gpre, w_out, npost, hv, tabs, ident, rm, mask, **DBG)
        _CACHE[key] = P.finish()
    return _CACHE[key]


def run_layer_b(xs, i, inputs, core_ids=None):
    j = i // 2
    ident, _ = consts()
    rm, mask = rope_consts()
    nc = prog_b()
    common = {
        "w_in": np.ascontiguousarray(inputs["b_w_in"][j]),
        "gpre": np.ascontiguousarray(inputs["norm_pre"][i][None, :]),
        "w_out": np.ascontiguousarray(inputs["b_w_out"][j]),
        "npost": np.ascontiguousarray(inputs["norm_post"][i][None, :]),
        "ident": ident, "rm": rm, "mask": mask,
    }
    in_maps = []
    cores = list(range(len(xs))) if core_ids is None else core_ids
    for c in cores:
        m = dict(common)
        m["x"] = np.ascontiguousarray(xs[c])
        first = (c % 4 == 0)
        m["xh"] = np.zeros((HALO, D), np.float32) if first else np.ascontiguousarray(xs[c - 1][TOK - HALO:])
        m["hv"] = np.tile(np.array([[0.0 if first else 1.0, 1.0]], np.float32), (128, 1))
        tb = rope_tables_core((c % 4) * TOK)
        for g in range(3):
            m["cq%d" % g], m["sq%d" % g], m["ck%d" % g], m["sk%d" % g] = tb[g]
        in_maps.append(m)
    res = run_bass_kernel_spmd(nc, in_maps, core_ids=list(range(len(cores))))
    return [r["y"] for r in res.results]


def kernel_unfused(x, norm_pre, norm_post, a_w_in, a_ln_g, a_ln_b, a_w_s, a_b_s, a_w_out, b_w_in, b_w_out):
    inputs = dict(norm_pre=np.asarray(norm_pre), norm_post=np.asarray(norm_post), a_w_in=np.asarray(a_w_in),
                  a_ln_g=np.asarray(a_ln_g), a_ln_b=np.asarray(a_ln_b), a_w_s=np.asarray(a_w_s), a_b_s=np.asarray(a_b_s),
                  a_w_out=np.asarray(a_w_out), b_w_in=np.asarray(b_w_in), b_w_out=np.asarray(b_w_out))
    x = np.asarray(x, dtype=np.float32)
    B, S_, D_ = x.shape
    xs = [np.ascontiguousarray(c) for c in x.reshape(NCORES, TOK, D_)]
    for i in range(4):
        if i % 2 == 0:
            xs = run_layer_a(xs, i, inputs)
        else:
            xs = run_layer_b(xs, i, inputs)
    return np.stack(xs).reshape(B, S_, D_).astype(np.float32)


XR = 8192
B_CALLS = ((2048, 2048), (4096, -2048), (4096, 0))


def prog_fused():
    key = ("F",)
    if key in _CACHE:
        return _CACHE[key]
    P = Prog()
    x = P.inp("x", [XR, D])
    ident = P.inp("ident", [128, 128], BF16)
    tril = P.inp("tril", [128, 128])
    rm = P.inp("rm", [128, 128], BF16)
    mask = P.inp("mask", [128, 512], BF16)
    npost = [P.inp("npost%d" % i, [1, D]) for i in range(4)]
    A = []
    for j in range(2):
        A.append(dict(w_in=P.inp("a_w_in%d" % j, [D, 3 * EW]), npre=P.inp("a_npre%d" % j, [128, 8]),
                      ln_g=P.inp("a_ln_g%d" % j, [1, EW]), ln_b=P.inp("a_ln_b%d" % j, [1, EW]),
                      w_s=P.inp("a_w_s%d" % j, [8, 128, 128]), bs=P.inp("a_bs%d" % j, [128, 8]),
                      w_out=P.inp("a_w_out%d" % j, [EW, D])))
    Bw = []
    for j in range(2):
        Bw.append(dict(w_in=P.inp("b_w_in%d" % j, [D, 10240]), gpre=P.inp("b_gpre%d" % j, [1, D]),
                       w_out=P.inp("b_w_out%d" % j, [D, D])))
    fl = [P.inp("fl%d" % c, [128, 2]) for c in range(3)]
    tabs = []
    for c, (nown, _) in enumerate(B_CALLS):
        tc = []
        for g, d in enumerate(DILS):
            nk = nown + 128 * d
            tc.append((P.inp("cq%d_%d" % (c, g), [128, nown], BF16), P.inp("sq%d_%d" % (c, g), [128, nown], BF16),
                       P.inp("ck%d_%d" % (c, g), [128, nk], BF16), P.inp("sk%d_%d" % (c, g), [128, nk], BF16)))
        tabs.append(tc)
    y = P.outp("y", [TOK, D])
    xres = P.nc.dram_tensor("xres", [XR, D], F32).ap()
    ysc = P.nc.dram_tensor("ysc", [8, 128, 4096], BF16).ap()
    vsc = P.nc.dram_tensor("vsc", [3, 8, 128, 48, 128], BF16).ap()

    def la(j, i, src, dst, ntok):
        a = A[j]
        P.layer_a(src, dst, ntok, a["w_in"], a["npre"], a["ln_g"], a["ln_b"], a["w_s"], a["bs"], a["w_out"], npost[i],
                  ident, tril)

    def lb(j, i, c, src, halo, dst):
        nown = B_CALLS[c][0]
        b = Bw[j]
        P.layer_b(src, halo, dst, ysc[:, :, 0:nown], vsc, b["w_in"], b["gpre"], b["w_out"], npost[i], fl[c], tabs[c],
                  ident, rm, mask, NOWN=nown)

    la(0, 0, x, xres, XR)
    lb(0, 1, 0, xres[6144:8192], xres[4096:6144], xres[6144:8192])
    lb(0, 1, 1, xres[2048:6144], xres[0:2048], xres[2048:6144])
    la(1, 2, xres[2048:8192], xres[2048:8192], 6144)
    P.final = []
    lb(1, 3, 2, xres[4096:8192], xres[2048:4096], y)
    _CACHE[key] = P.finish()
    return _CACHE[key]


def kernel(x, norm_pre, norm_post, a_w_in, a_ln_g, a_ln_b, a_w_s, a_b_s, a_w_out, b_w_in, b_w_out):
    f32 = lambda a: np.ascontiguousarray(np.asarray(a, dtype=np.float32))
    x = f32(x)
    norm_pre, norm_post = f32(norm_pre), f32(norm_post)
    Bn, S_, D_ = x.shape
    per_seq = S_ // TOK
    ident, tril = consts()
    rm, mask = rope_consts()
    common = {"ident": ident, "tril": tril, "rm": rm, "mask": mask}
    for i in range(4):
        common["npost%d" % i] = f32(norm_post[i][None, :])
    for j in range(2):
        common["a_w_in%d" % j] = f32(a_w_in[j])
        common["a_npre%d" % j] = col128(norm_pre[2 * j])
        common["a_ln_g%d" % j] = f32(np.asarray(a_ln_g[j])[None, :])
        common["a_ln_b%d" % j] = f32(np.asarray(a_ln_b[j])[None, :])
        common["a_w_s%d" % j] = f32(a_w_s[j])
        common["a_bs%d" % j] = f32(np.asarray(a_b_s[j]).T)
        common["a_w_out%d" % j] = f32(a_w_out[j])
        common["b_w_in%d" % j] = f32(b_w_in[j])
        common["b_gpre%d" % j] = f32(norm_pre[2 * j + 1][None, :])
        common["b_w_out%d" % j] = f32(b_w_out[j])
    tab_cache = {}
    in_maps = []
    for c in range(NCORES):
        b, k = c // per_seq, c % per_seq
        m = dict(common)
        xe = np.zeros((XR, D_), np.float32)
        lo = k * TOK - TOK
        src_lo = max(lo, 0)
        xe[src_lo - lo:] = x[b, src_lo:(k + 1) * TOK]
        m["x"] = xe
        first = (k == 0)
        flags = ((1.0, 1.0), (0.0, 0.0) if first else (1.0, 1.0), (0.0, 1.0) if first else (1.0, 1.0))
        for ci in range(3):
            m["fl%d" % ci] = np.tile(np.array([flags[ci]], np.float32), (128, 1))
            nown, off = B_CALLS[ci]
            tk = (k, ci)
            if tk not in tab_cache:
                tab_cache[tk] = rope_tables_core(k * TOK + off, nown)
            tb = tab_cache[tk]
            for g in range(3):
                m["cq%d_%d" % (ci, g)], m["sq%d_%d" % (ci, g)], m["ck%d_%d" % (ci, g)], m["sk%d_%d" % (ci, g)] = tb[g]
        in_maps.append(m)
    nc = prog_fused()
    res = run_bass_kernel_spmd(nc, in_maps, core_ids=list(range(NCORES)))
    out = np.stack([r["y"] for r in res.results]).reshape(Bn, S_, D_)
    return out.astype(np.float32)
```

```python
import contextlib
import numpy as np
import ml_dtypes
import concourse.bass as bass
import concourse.mybir as mybir
from concourse.bass_utils import run_bass_kernel_spmd

F32 = mybir.dt.float32
BF16 = mybir.dt.bfloat16
I32 = mybir.dt.int32
AF = mybir.ActivationFunctionType
ALU = mybir.AluOpType

D = 1024
NCORES = 8
TOK = 4096
EW = 2048
RMS_EPS = 1e-6
LN_EPS = 1e-5

ENGS = ("pe", "act", "dve", "pool", "sp")


class _Op:
    __slots__ = ("eng", "fn", "sem", "val", "waits", "dma", "needs_inc")

    def __init__(self, eng, fn, dma):
        self.eng = eng
        self.fn = fn
        self.dma = dma
        self.sem = None
        self.val = 0
        self.waits = []
        self.needs_inc = False


class Sched:
    def __init__(self):
        self.ops = {e: [] for e in ENGS}
        self.last_w = {}
        self.readers = {}
        self.all_ops = []
        self.epoch = 0

    def _deps(self, op, reads, writes):
        deps = []
        raw = set()
        for b in reads:
            w = self.last_w.get(b)
            if w is not None:
                deps.append(w)
                raw.add(id(w))
        for b in writes:
            w = self.last_w.get(b)
            if w is not None:
                deps.append(w)
            deps.extend(self.readers.get(b, {}).values())
        for b in reads:
            self.readers.setdefault(b, {})[op.sem] = op
        for b in writes:
            self.last_w[b] = op
            self.readers[b] = {}
        return deps, raw

    def barrier(self):
        last = {}
        for o in self.all_ops:
            last[o.sem] = o
        self.pending = {e: list(last.values()) for e in ENGS}

    def _pend(self, o, eng):
        pend = getattr(self, "pending", None)
        if pend and pend.get(eng):
            for d in pend[eng]:
                if d.dma is None and d.eng == eng:
                    continue
                d.needs_inc = True
                o.waits.append(d)
            pend[eng] = []

    def op(self, eng, fn, reads=(), writes=()):
        o = _Op(eng, fn, None)
        o.sem = ("eng", eng, self.epoch)
        self._pend(o, eng)
        deps, raw = self._deps(o, reads, writes)
        for d in deps:
            if d is o:
                continue
            if d.dma is None and d.eng == eng:
                if eng == "pe" or id(d) not in raw:
                    continue
            d.needs_inc = True
            o.waits.append(d)
        self.ops[eng].append(o)
        self.all_ops.append(o)
        return o

    def dma(self, queue, stream, fn, reads=(), writes=()):
        o = _Op(queue, fn, stream)
        o.sem = ("dma", stream)
        o.needs_inc = True
        self._pend(o, queue)
        deps, _ = self._deps(o, reads, writes)
        for d in deps:
            if d is o:
                continue
            d.needs_inc = True
            o.waits.append(d)
        self.ops[queue].append(o)
        self.all_ops.append(o)
        return o

    def new_epoch(self):
        self.epoch += 1

    def finalize(self):
        counts = {}
        for o in self.all_ops:
            if o.needs_inc:
                step = 16 if o.dma is not None else 1
                counts[o.sem] = counts.get(o.sem, 0) + step
                o.val = counts[o.sem]
        self.sem_keys = list(counts.keys())
        for k, v in counts.items():
            assert v < 60000, (k, v)
        return counts

    def emit(self, block, sems, final_waits=()):
        handles = {"pe": "tensor", "act": "scalar", "dve": "vector", "pool": "gpsimd", "sp": "sync"}

        def make(engname):
            ops = self.ops[engname]

            def body(eng):
                waited = {}

                def wait(d):
                    if waited.get(d.sem, 0) >= d.val:
                        return
                    eng.wait_ge(sems[d.sem], d.val)
                    waited[d.sem] = d.val

                for o in ops:
                    for d in o.waits:
                        wait(d)
                    ins = o.fn(eng)
                    if o.needs_inc:
                        ins.then_inc(sems[o.sem], 16 if o.dma is not None else 1)
                if engname == "sp":
                    for d in final_waits:
                        wait(d)
            return body

        for engname in ENGS:
            getattr(block, handles[engname])(make(engname))


class Prog:
    def __init__(self):
        self.nc = bass.Bass("TRN2", target_bir_lowering=False)
        self.S = Sched()
        self.cap = 212000
        self.arena = self.nc.alloc_sbuf_tensor("arena", [128, self.cap // 2], BF16)[:]
        self.off = 0
        self.uid = 0
        self.ps_f32 = [self.nc.alloc_psum_tensor("psb%d" % i, [128, 512], F32)[:] for i in range(8)]
        self.ps_b16 = [p.bitcast(BF16) for p in self.ps_f32]
        self.final = []

    def inp(self, name, shape, dtype=F32):
        return self.nc.dram_tensor(name, list(shape), dtype, kind="ExternalInput").ap()

    def outp(self, name, shape, dtype=F32):
        return self.nc.dram_tensor(name, list(shape), dtype, kind="ExternalOutput").ap()

    def carve(self, cols, dtype):
        esz = 4 if dtype in (F32, I32) else 2
        nb = (cols * esz + 63) // 64 * 64
        assert self.off + nb <= self.cap, ("SBUF arena overflow", self.off, nb)
        a = self.arena[:, self.off // 2:(self.off + nb) // 2]
        self.off += nb
        if dtype != BF16:
            a = a.bitcast(dtype)
        return a[:, 0:cols]

    def key(self, base):
        self.uid += 1
        return "%s#%d" % (base, self.uid)

    def rsqrt(self, a_key, a, out_key, out, tmp_key, tmp):
        S = self.S
        S.op("dve", lambda e: e.tensor_scalar(out=out.bitcast(I32), in0=a.bitcast(I32), scalar1=1, scalar2=None,
                                              op0=ALU.arith_shift_right), reads=[a_key], writes=[out_key])
        S.op("dve", lambda e: e.tensor_scalar(out=out.bitcast(I32), in0=out.bitcast(I32), scalar1=0x5f3759df,
                                              scalar2=-1, op0=ALU.subtract, op1=ALU.mult),
             reads=[out_key], writes=[out_key])
        for _ in range(2):
            S.op("dve", lambda e: e.scalar_tensor_tensor(out=tmp, in0=out, scalar=-0.5, in1=out, op0=ALU.mult,
                                                         op1=ALU.mult), reads=[out_key], writes=[tmp_key])
            S.op("dve", lambda e: e.tensor_tensor(out=tmp, in0=tmp, in1=a, op=ALU.mult),
                 reads=[tmp_key, a_key], writes=[tmp_key])
            S.op("dve", lambda e: e.scalar_tensor_tensor(out=out, in0=tmp, scalar=1.5, in1=out, op0=ALU.add,
                                                         op1=ALU.mult), reads=[tmp_key, out_key], writes=[out_key])

    def layer_a(self, x_src, x_dst, ntok, w_in, npre_col, ln_g, ln_b, w_s, bs_col, w_out, npost, ident_d, tril_d):
        S, nc = self.S, self.nc
        off0 = self.off
        NT = ntok // 128
        L = self.key("A")
        K = lambda s: "%s/%s" % (L, s)

        Wb = [self.carve(6144, BF16) for _ in range(8)]
        Wo = [self.carve(1024, BF16) for _ in range(16)]
        lng = self.carve(2048, BF16)
        lnb = self.carve(2048, BF16)
        gpo = self.carve(1024, F32)
        WmT = self.carve(1024, BF16)
        bsc = self.carve(8, F32)
        gpre = self.carve(8, F32)
        idt = self.carve(128, BF16)
        tril = self.carve(128, F32)
        xt = [self.carve(1024, F32) for _ in range(3)]
        xb = self.carve(1024, BF16)
        xT = [self.carve(1024, BF16) for _ in range(2)]
        junk = self.carve(1024, BF16)
        u = self.carve(2048, BF16)
        sz = self.carve(2048, BF16)
        v = self.carve(2048, F32)
        vn = [self.carve(2048, BF16) for _ in range(2)]
        y = self.carve(2048, BF16)
        yT = self.carve(2048, BF16)
        tpost = self.carve(1024, F32)
        st = self.carve(64, F32)
        ss, ms, tmp1 = st[:, 0:1], st[:, 1:2], st[:, 3:4]
        rstd = [st[:, 4:5], st[:, 5:6]]
        bst = st[:, 8:32]
        mv = st[:, 32:34]
        va, rsv, tmp2 = st[:, 34:35], st[:, 35:36], st[:, 36:37]
        ss2, a2, rstd2, tmp3 = st[:, 40:42], st[:, 42:43], st[:, 43:44], st[:, 44:45]
        stage = [v[:, 0:1024], v[:, 1024:2048]]
        VK = [K("v0"), K("v1")]
        P_TX, P_IN, P_SV, P_TY, P_O = 0, (1, 2), (3, 4), 5, (6, 7)
        P_TY2 = (5, 0)
        psf, psb = self.ps_f32, self.ps_b16

        S.dma("sp", "idt", lambda e: e.dma_start(out=idt, in_=ident_d), writes=[K("idt")])
        S.dma("sp", "tril", lambda e: e.dma_start(out=tril, in_=tril_d), writes=[K("tril")])
        S.dma("sp", "gpre", lambda e: e.dma_start(out=gpre, in_=npre_col), writes=[K("gpre")])
        S.dma("sp", "bsc", lambda e: e.dma_start(out=bsc, in_=bs_col), writes=[K("bsc")])
        S.dma("sp", "gpo", lambda e: e.dma_start(out=gpo, in_=npost.partition_broadcast(128)), writes=[K("gpo")])
        cnt = [0]

        def staged(src_ap, cols, consume, view=None):
            i = cnt[0] % 2
            cnt[0] += 1
            sb = stage[i][:, 0:cols]
            sbd = view(sb) if view is not None else sb
            S.dma("sp", "stg%d" % i, lambda e: e.dma_start(out=sbd, in_=src_ap), writes=[VK[i]])
            consume(sb, "act" if i == 0 else "dve", VK[i])

        for k in range(8):
            for q in range(6):
                dst = Wb[k][:, q * 1024:(q + 1) * 1024]

                def cons(sb, eng, skey, dst=dst, k=k):
                    if eng == "act":
                        S.op("act", lambda e: e.activation(out=dst, in_=sb, func=AF.Copy, scale=gpre[:, k:k + 1]),
                             reads=[skey, K("gpre")], writes=[K("Wb%d" % k)])
                    else:
                        S.op("dve", lambda e: e.tensor_scalar(out=dst, in0=sb, scalar1=gpre[:, k:k + 1], scalar2=None,
                                                              op0=ALU.mult), reads=[skey, K("gpre")], writes=[K("Wb%d" % k)])
                staged(w_in[k * 128:(k + 1) * 128, q * 1024:(q + 1) * 1024], 1024, cons)
        for k in range(16):
            def cons(sb, eng, skey, k=k):
                if eng == "act":
                    S.op("act", lambda e: e.activation(out=Wo[k], in_=sb, func=AF.Copy), reads=[skey], writes=[K("Wo")])
                else:
                    S.op("dve", lambda e: e.tensor_copy(out=Wo[k], in_=sb), reads=[skey], writes=[K("Wo")])
            staged(w_out[k * 128:(k + 1) * 128, :], 1024, cons)
        for (src, dstt, nm) in ((ln_g, lng, "lng"), (ln_b, lnb, "lnb")):
            for h in range(2):
                def cons(sb, eng, skey, dstt=dstt, h=h, nm=nm):
                    if eng == "act":
                        S.op("act", lambda e: e.activation(out=dstt[:, h * 1024:(h + 1) * 1024], in_=sb, func=AF.Copy),
                             reads=[skey], writes=[K(nm)])
                    else:
                        S.op("dve", lambda e: e.tensor_copy(out=dstt[:, h * 1024:(h + 1) * 1024], in_=sb),
                             reads=[skey], writes=[K(nm)])
                staged(src[:, h * 1024:(h + 1) * 1024].partition_broadcast(128), 1024, cons)

        def cons_ws(sb, eng, skey):
            sbv = sb.rearrange("p (g s) -> p g s", g=8)
            yv = y[:, 0:1024].rearrange("p (g s) -> p g s", g=8)
            for g in range(8):
                S.op("dve", lambda e, g=g: e.tensor_tensor(out=yv[:, g, :], in0=sbv[:, g, :], in1=tril, op=ALU.mult),
                     reads=[skey, K("tril")], writes=[K("y")])
            for g in range(8):
                S.op("pe", lambda e, g=g: e.transpose(out=psb[P_TX][:, g * 128:(g + 1) * 128], in_=yv[:, g, :], identity=idt),
                     reads=[K("y"), K("idt")], writes=[K("ps%d" % P_TX)])
            S.op("dve", lambda e: e.tensor_copy(out=WmT, in_=psb[P_TX]), reads=[K("ps%d" % P_TX)], writes=[K("WmT")])
        staged(w_s.rearrange("g t s -> t g s"), 1024, cons_ws, view=lambda sb: sb.rearrange("p (g s) -> p g s", g=8))

        def load(i):
            p = i % 3
            S.dma("sp", "xl%d" % p, lambda e: e.dma_start(out=xt[p], in_=x_src[i * 128:(i + 1) * 128, :]),
                  writes=[K("xt%d" % p)])

        def inproj(q, banks, r0):
            r = r0
            for n in banks:
                b = P_IN[r % 2]
                r += 1
                for k in range(8):
                    S.op("pe", lambda e, k=k, n=n, b=b: e.matmul(psf[b], lhsT=xT[q][:, k * 128:(k + 1) * 128],
                                                                rhs=Wb[k][:, n * 512:(n + 1) * 512],
                                                                start=(k == 0), stop=(k == 7)),
                         reads=[K("xT%d" % q), K("Wb%d" % k)], writes=[K("ps%d" % b)])
                if n < 4:
                    dst, func, dk = u[:, n * 512:(n + 1) * 512], AF.Gelu, [K("u%d" % n)]
                elif n < 8:
                    dst, func, dk = v[:, (n - 4) * 512:(n - 3) * 512], AF.Gelu, VK
                else:
                    dst, func, dk = sz[:, (n - 8) * 512:(n - 7) * 512], AF.Silu, [K("sz%d" % (n - 8))]
                S.op("act", lambda e, dst=dst, func=func, b=b: e.activation(out=dst, in_=psf[b], func=func, scale=rstd[q]),
                     reads=[K("ps%d" % b), K("rstd%d" % q)], writes=dk)

        def pre(i):
            p, q = i % 3, i % 2
            xk = K("xt%d" % p)
            S.op("act", lambda e: e.activation(out=junk, in_=xt[p], func=AF.Square, accum_out=ss),
                 reads=[xk], writes=[K("junk"), K("ss")])
            S.op("dve", lambda e: e.tensor_scalar(out=ms, in0=ss, scalar1=1.0 / D, scalar2=RMS_EPS, op0=ALU.mult,
                                                  op1=ALU.add), reads=[K("ss")], writes=[K("ms")])
            self.rsqrt(K("ms"), ms, K("rstd%d" % q), rstd[q], K("tmp1"), tmp1)
            S.op("pool", lambda e: e.tensor_copy(out=xb, in_=xt[p]), reads=[xk], writes=[K("xb")])

        def stage1_t(i):
            q = i % 2
            for k in range(8):
                S.op("pe", lambda e, k=k: e.transpose(out=psb[P_TX][:, k * 128:(k + 1) * 128],
                                                      in_=xb[:, k * 128:(k + 1) * 128], identity=idt),
                     reads=[K("xb"), K("idt")], writes=[K("ps%d" % P_TX)])
            S.op("act", lambda e: e.activation(out=xT[q], in_=psb[P_TX], func=AF.Copy),
                 reads=[K("ps%d" % P_TX)], writes=[K("xT%d" % q)])

        def stage1(i):
            q = i % 2
            if i > 0:
                stage1_t(i)
            inproj(q, (4, 5, 6, 7), 0)
            for c in range(4):
                S.op("dve", lambda e, c=c: e.bn_stats(out=bst[:, c * 6:(c + 1) * 6], in_=v[:, c * 512:(c + 1) * 512]),
                     reads=VK, writes=[K("bst")])
            S.op("dve", lambda e: e.bn_aggr(out=mv, in_=bst.rearrange("p (c s) -> p c s", c=4)),
                 reads=[K("bst")], writes=[K("mv")])
            S.op("dve", lambda e: e.tensor_scalar(out=va, in0=mv[:, 1:2], scalar1=LN_EPS, scalar2=None,
                                                  op0=ALU.add), reads=[K("mv")], writes=[K("va")])
            self.rsqrt(K("va"), va, K("rsv"), rsv, K("tmp2"), tmp2)
            S.op("dve", lambda e: e.scalar_tensor_tensor(out=v, in0=v, scalar=mv[:, 0:1], in1=lng,
                                                         op0=ALU.subtract, op1=ALU.mult),
                 reads=VK + [K("mv"), K("lng")], writes=VK)
            S.op("dve", lambda e: e.scalar_tensor_tensor(out=vn[q], in0=v, scalar=rsv, in1=lnb,
                                                         op0=ALU.mult, op1=ALU.add),
                 reads=VK + [K("rsv"), K("lnb")], writes=[K("vn%d" % q)])

        def stage2a(i):
            q = i % 2
            inproj(q, (0, 1, 2, 3, 8, 9, 10, 11), 0)
            for qd in range(4):
                b = P_SV[qd % 2]
                for gg in range(2):
                    g = 2 * qd + gg
                    S.op("pe", lambda e, g=g, gg=gg, b=b: e.matmul(psf[b][:, gg * 256:(gg + 1) * 256],
                                                                  lhsT=WmT[:, g * 128:(g + 1) * 128],
                                                                  rhs=vn[q][:, g * 256:(g + 1) * 256], start=True, stop=True),
                         reads=[K("vn%d" % q), K("WmT")], writes=[K("ps%d" % b)])
                for gg in range(2):
                    g = 2 * qd + gg
                    S.op("dve", lambda e, g=g, gg=gg, b=b: e.scalar_tensor_tensor(
                        out=y[:, g * 256:(g + 1) * 256], in0=psf[b][:, gg * 256:(gg + 1) * 256], scalar=bsc[:, g:g + 1],
                        in1=u[:, g * 256:(g + 1) * 256], op0=ALU.add, op1=ALU.mult),
                        reads=[K("ps%d" % b), K("bsc"), K("u%d" % qd)], writes=[K("y%d" % qd)])
            S.op("dve", lambda e: e.tensor_tensor(out=y, in0=y, in1=sz, op=ALU.mult),
                 reads=[K("y")] + [K("y%d" % c) for c in range(4)] + [K("sz%d" % c) for c in range(4)],
                 writes=[K("y")] + [K("y%d" % c) for c in range(4)])

        def stage2b(i):
            p = i % 3
            xk = K("xt%d" % p)
            for h in range(2):
                for k in range(8):
                    kk = h * 8 + k
                    S.op("pe", lambda e, k=k, kk=kk, h=h: e.transpose(out=psb[P_TY2[h]][:, k * 128:(k + 1) * 128],
                                                                     in_=y[:, kk * 128:(kk + 1) * 128], identity=idt),
                         reads=[K("y"), K("idt")] + [K("y%d" % c) for c in range(4)], writes=[K("ps%d" % P_TY2[h])])
            for h in range(2):
                S.op("act", lambda e, h=h: e.activation(out=yT[:, h * 1024:(h + 1) * 1024], in_=psb[P_TY2[h]], func=AF.Copy),
                     reads=[K("ps%d" % P_TY2[h])], writes=[K("yT%d" % h)])
            for n in range(2):
                b = P_O[n]
                for k in range(16):
                    S.op("pe", lambda e, k=k, n=n, b=b: e.matmul(psf[b], lhsT=yT[:, k * 128:(k + 1) * 128],
                                                                rhs=Wo[k][:, n * 512:(n + 1) * 512],
                                                                start=(k == 0), stop=(k == 15)),
                         reads=[K("yT%d" % (k // 8)), K("Wo")], writes=[K("ps%d" % b)])
                S.op("act", lambda e, n=n, b=b: e.activation(out=junk[:, 0:512], in_=psf[b], func=AF.Square,
                                                             accum_out=ss2[:, n:n + 1]),
                     reads=[K("ps%d" % b)], writes=[K("junk"), K("ss2")])
            S.op("dve", lambda e: e.tensor_tensor(out=a2, in0=ss2[:, 0:1], in1=ss2[:, 1:2], op=ALU.add),
                 reads=[K("ss2")], writes=[K("a2")])
            S.op("dve", lambda e: e.tensor_scalar(out=a2, in0=a2, scalar1=1.0 / D, scalar2=RMS_EPS, op0=ALU.mult,
                                                  op1=ALU.add), reads=[K("a2")], writes=[K("a2")])
            self.rsqrt(K("a2"), a2, K("rstd2"), rstd2, K("tmp3"), tmp3)
            for n in range(2):
                b = P_O[n]
                S.op("dve", lambda e, n=n, b=b: e.scalar_tensor_tensor(out=tpost[:, n * 512:(n + 1) * 512], in0=psf[b],
                                                                       scalar=rstd2, in1=gpo[:, n * 512:(n + 1) * 512],
                                                                       op0=ALU.mult, op1=ALU.mult),
                     reads=[K("ps%d" % b), K("rstd2"), K("gpo")], writes=[K("tpost")])
            S.op("pool", lambda e: e.tensor_tensor(out=xt[p], in0=xt[p], in1=tpost, op=ALU.add),
                 reads=[xk, K("tpost")], writes=[xk])
            o = S.dma("pool", "xs%d" % p, lambda e: e.dma_start(out=x_dst[i * 128:(i + 1) * 128, :], in_=xt[p]),
                      reads=[xk], writes=[K("xdst")])
            self.final.append(o)

        load(0)
        if NT > 1:
            load(1)
        pre(0)
        stage1_t(0)
        stage1(0)
        for i in range(NT):
            if i + 2 < NT:
                load(i + 2)
            if i + 1 < NT:
                pre(i + 1)
            stage2a(i)
            if i + 1 < NT:
                stage1(i + 1)
            stage2b(i)
        S.barrier()
        self.peak = max(getattr(self, "peak", 0), self.off)
        self.off = off0


    def layer_b(self, x_src, x_halo, x_dst, ysc, vsc, w_in, gpre_d, w_out, npost, hv_d, tabs, ident_d, rm_d, mask_d,
                NOWN=4096, dbg_hps=range(8), dbg_gs=range(3), dbg_attn=True):
        S, nc = self.S, self.nc
        off0 = self.off
        L = self.key("B")
        K = lambda s: "%s/%s" % (L, s)
        psf, psb = self.ps_f32, self.ps_b16
        HAL = 2048
        NT = HAL + NOWN
        DIL = (1, 4, 16)
        P_PR, P_RT, P_S2, P_ND, P_V2 = (0, 1), 2, ((3, 4), (0, 1), (2, 7)), (5, 6), (7, 3)

        hT = self.carve(8 * NT, BF16).rearrange("p (k t) -> p k t", k=8)
        offq = self.off
        QT = self.carve(NOWN, BF16)
        KT = self.carve(NT, BF16)
        Vb = self.carve(NT, BF16)
        acc = self.carve(2 * NOWN, F32).rearrange("p (a t) -> p a t", a=2)
        offw = self.off
        wst = self.carve(3 * 1024, F32)
        wb = [self.carve(3 * 1024, BF16) for _ in range(2)]
        idt = self.carve(128, BF16)
        rm = self.carve(128, BF16)
        maskT = self.carve(512, BF16)
        ones = self.carve(64, BF16)
        hvones = self.carve(64, BF16)
        loones = self.carve(64, BF16)
        hv2 = self.carve(2, F32)
        hv, lo = hv2[:, 0:1], hv2[:, 1:2]
        PT = [self.carve(512, BF16) for _ in range(6)]
        qraw = [self.carve(512, BF16) for _ in range(3)]
        rt1 = [self.carve(512, BF16) for _ in range(3)]
        ctab = [self.carve(512, BF16) for _ in range(3)]
        stab = [self.carve(512, BF16) for _ in range(3)]
        st = self.carve(64, F32)
        print("layer B persistent SBUF bytes", self.off - off0)
        offp = self.off

        S.dma("sp", "idt", lambda e: e.dma_start(out=idt, in_=ident_d), writes=[K("idt")])
        S.dma("sp", "rm", lambda e: e.dma_start(out=rm, in_=rm_d), writes=[K("rm")])
        S.dma("sp", "mask", lambda e: e.dma_start(out=maskT, in_=mask_d), writes=[K("mask")])
        S.dma("sp", "hv", lambda e: e.dma_start(out=hv2, in_=hv_d), writes=[K("hv")])
        S.op("pool", lambda e: e.memset(ones, 1.0), writes=[K("ones")])
        S.op("pool", lambda e: e.memset(hvones, 1.0), writes=[K("hvones")])
        S.op("pool", lambda e: e.tensor_scalar(out=hvones, in0=hvones, scalar1=hv, scalar2=None, op0=ALU.mult),
             reads=[K("hv"), K("hvones")], writes=[K("hvones")])
        S.op("pool", lambda e: e.memset(loones, 1.0), writes=[K("loones")])
        S.op("pool", lambda e: e.tensor_scalar(out=loones, in0=loones, scalar1=lo, scalar2=None, op0=ALU.mult),
             reads=[K("hv"), K("loones")], writes=[K("loones")])

        need1 = 8 * 4096 + 8 * 2048 + 4096 + 2048 + 256
        def place(need):
            if offp + need <= self.cap:
                self.off = offp
            else:
                assert offq + need <= offw, ("overlay does not fit", need, offw - offq)
                self.off = offq
        place(need1)
        gb = self.carve(1024, F32)
        xt = [self.carve(1024, F32) for _ in range(8)]
        hb = [self.carve(1024, BF16) for _ in range(8)]
        junk = self.carve(1024, BF16)
        st1 = self.carve(32, F32)
        S.dma("sp", "gb", lambda e: e.dma_start(out=gb, in_=gpre_d.partition_broadcast(128)), writes=[K("gb")])

        def xrows(i):
            return x_halo[i * 128:(i + 1) * 128, :] if i < 16 else x_src[(i - 16) * 128:(i - 15) * 128, :]

        def p1_load(i):
            p = i % 8
            S.dma("sp", "x1l%d" % p, lambda e: e.dma_start(out=xt[p], in_=xrows(i)), writes=[K("x1t%d" % p)])

        NT1 = NT // 128
        NG1 = NT1 // 4

        def p1_x(gi):
            q = gi % 2
            ss, ms, rstd, tmp1 = (st1[:, q * 16 + c * 4:q * 16 + c * 4 + 4] for c in range(4))
            for t in range(4):
                i = gi * 4 + t
                p = i % 8
                S.op("act", lambda e, p=p, t=t, ss=ss: e.activation(out=junk, in_=xt[p], func=AF.Square, accum_out=ss[:, t:t + 1]),
                     reads=[K("x1t%d" % p)], writes=[K("junk"), K("ss%d" % q)])
            S.op("dve", lambda e: e.tensor_scalar(out=ms, in0=ss, scalar1=1.0 / D, scalar2=RMS_EPS, op0=ALU.mult,
                                                  op1=ALU.add), reads=[K("ss%d" % q)], writes=[K("ms%d" % q)])
            self.rsqrt(K("ms%d" % q), ms, K("rstd%d" % q), rstd, K("tmp1%d" % q), tmp1)
            for t in range(4):
                i = gi * 4 + t
                p = i % 8
                S.op("dve", lambda e, p=p, t=t, rstd=rstd: e.scalar_tensor_tensor(out=hb[p], in0=xt[p], scalar=rstd[:, t:t + 1],
                                                                                  in1=gb, op0=ALU.mult, op1=ALU.mult),
                     reads=[K("x1t%d" % p), K("rstd%d" % q), K("gb")], writes=[K("hb%d" % p)])

        def p1_y(gi):
            for t in range(4):
                i = gi * 4 + t
                p = i % 8
                pb = P_PR[i % 2]
                for k in range(8):
                    S.op("pe", lambda e, k=k, p=p, pb=pb: e.transpose(out=psb[pb][:, k * 128:(k + 1) * 128],
                                                                in_=hb[p][:, k * 128:(k + 1) * 128], identity=idt),
                         reads=[K("hb%d" % p), K("idt")], writes=[K("ps%d" % pb)])
                S.op("act", lambda e, i=i, pb=pb: e.activation(out=hT[:, :, i * 128:(i + 1) * 128],
                                                               in_=psb[pb].rearrange("p (k t) -> p k t", k=8), func=AF.Copy),
                     reads=[K("ps%d" % pb)], writes=[K("hT")])

        for i in range(min(8, NT1)):
            p1_load(i)
        p1_x(0)
        for gi in range(NG1):
            if gi + 1 < NG1:
                p1_x(gi + 1)
            p1_y(gi)
            for t in range(4):
                i = (gi + 2) * 4 + t
                if i < NT1:
                    p1_load(i)
        S.barrier()

        KBG = 8 if NOWN >= 4096 else 4
        need15 = 2 * 8 * 1024 * 2 + 2 * KBG * 1024 * 2
        place(need15)
        Wv2 = [self.carve(8 * 1024, BF16).rearrange("p (k c) -> p k c", k=8) for _ in range(2)]
        vstg = [self.carve(KBG * 1024, BF16).rearrange("p (b c) -> p b c", b=KBG) for _ in range(2)]
        vcnt = [0]
        def load_wv(gidx):
            g = list(dbg_gs)[gidx]
            Wv = Wv2[gidx % 2]
            for k in range(8):
                slot = k % 3
                stg = wst[:, slot * 1024:(slot + 1) * 1024]
                S.dma("sp", "wvs%d" % slot, lambda e, k=k, g=g, stg=stg: e.dma_start(
                    out=stg, in_=w_in[k * 128:(k + 1) * 128, g * 3072 + 2048:g * 3072 + 3072]), writes=[K("wst%d" % slot)])
                if k % 2 == 0:
                    S.op("act", lambda e, k=k, stg=stg, Wv=Wv: e.activation(out=Wv[:, k, :], in_=stg, func=AF.Copy),
                         reads=[K("wst%d" % slot)], writes=[K("Wv%d" % (gidx % 2))])
                else:
                    S.op("dve", lambda e, k=k, stg=stg, Wv=Wv: e.tensor_copy(out=Wv[:, k, :], in_=stg), reads=[K("wst%d" % slot)],
                         writes=[K("Wv%d" % (gidx % 2))])

        if len(list(dbg_gs)) > 0:
            load_wv(0)
        for gidx, g in enumerate(dbg_gs):
            d = DIL[g]
            span = 128 * d
            nkb = (NOWN // span + 1) * d
            kt0 = HAL - span
            Wv = Wv2[gidx % 2]
            wvkey = K("Wv%d" % (gidx % 2))
            if gidx + 1 < len(list(dbg_gs)):
                load_wv(gidx + 1)
            for kb0 in range(0, nkb, KBG):
                nb = min(KBG, nkb - kb0)
                vs = vcnt[0] % 2
                vcnt[0] += 1
                for bl in range(nb):
                    kb = kb0 + bl
                    sp_, r = kb // d, kb % d
                    t_start = kt0 + sp_ * span + r
                    banks = P_PR if kb % 2 == 0 else (P_RT, P_V2[0])
                    for k in range(8):
                        for n in range(2):
                            S.op("pe", lambda e, k=k, n=n, t_start=t_start, d=d, bk=banks[n], Wv=Wv: e.matmul(
                                psf[bk], lhsT=hT[:, k, t_start:t_start + 127 * d + 1:d], rhs=Wv[:, k, n * 512:(n + 1) * 512],
                                start=(k == 0), stop=(k == 7)), reads=[wvkey, K("hT")], writes=[K("ps%d" % banks[n])])
                    c = 0 if kb < d else (1 if kb < d + 16 else 2)
                    fl = hv if c == 0 else lo
                    for n in range(2):
                        dst = vstg[vs][:, bl, n * 512:(n + 1) * 512]
                        bk = banks[n]
                        if n == 0:
                            if c == 2:
                                S.op("act", lambda e, dst=dst, bk=bk: e.activation(out=dst, in_=psf[bk], func=AF.Copy),
                                     reads=[K("ps%d" % bk)], writes=[K("vstg%d" % vs)])
                            else:
                                S.op("act", lambda e, dst=dst, bk=bk, fl=fl: e.activation(out=dst, in_=psf[bk], func=AF.Copy,
                                                                                        scale=fl),
                                     reads=[K("ps%d" % bk), K("hv")], writes=[K("vstg%d" % vs)])
                        else:
                            if c == 2:
                                S.op("dve", lambda e, dst=dst, bk=bk: e.tensor_copy(out=dst, in_=psf[bk]),
                                     reads=[K("ps%d" % bk)], writes=[K("vstg%d" % vs)])
                            else:
                                S.op("dve", lambda e, dst=dst, bk=bk, fl=fl: e.tensor_scalar(out=dst, in0=psf[bk], scalar1=fl,
                                                                                          scalar2=None, op0=ALU.mult),
                                     reads=[K("ps%d" % bk), K("hv")], writes=[K("vstg%d" % vs)])
                for h8 in range(8):
                    S.dma("sp", "vst%d" % vs, lambda e, g=g, kb0=kb0, nb=nb, vs=vs, h8=h8: e.dma_start(
                        out=vsc[g, h8, :, kb0:kb0 + nb, :], in_=vstg[vs][:, 0:nb, h8 * 128:(h8 + 1) * 128]),
                        reads=[K("vstg%d" % vs)], writes=[K("vsc")])
        S.barrier()

        w5 = w_in[:, 0:9216].rearrange("(k p) (g t h f) -> p k g t h f", p=128, g=3, t=3, h=8)
        wz = w_in[:, 9216:10240].rearrange("(k p) (h f) -> p k h f", p=128, h=8)
        tabi = [0]
        bankc = [0]
        wcnt = [0]

        pending_cast = []

        def flush_cast():
            while pending_cast:
                pending_cast.pop(0)()

        wjobs = []
        for hp_ in dbg_hps:
            for g_ in dbg_gs:
                wjobs.append((w5[:, :, g_, :, hp_, :], 384))
            wjobs.append((wz[:, :, hp_, :], 128))
        wready = {}

        def load_w(src_ap=None, ncols=None):
            i = wcnt[0]
            wcnt[0] += 1
            if i == 0:
                wready[0] = _load_w(0, *wjobs[0])
            if i + 1 < len(wjobs):
                wready[i + 1] = _load_w(i + 1, *wjobs[i + 1])
            return wready.pop(i)

        def _load_w(i, src_ap, ncols):
            slot = i % 2
            dst32 = wst[:, 0:8 * ncols]
            if len(src_ap.shape) == 3:
                S.dma("sp", "wst", lambda e: e.dma_start(out=dst32.rearrange("p (k c) -> p k c", k=8), in_=src_ap),
                      writes=[K("wst")])
            else:
                d4 = dst32.rearrange("p (k t f) -> p k t f", k=8, t=3)
                for t in range(3):
                    S.dma("sp", "wst", lambda e, t=t: e.dma_start(out=d4[:, :, t, :], in_=src_ap[:, :, t, :]),
                          writes=[K("wst")])
            def cast():
                S.op("act", lambda e: e.activation(out=wb[slot][:, 0:8 * ncols], in_=dst32, func=AF.Copy), reads=[K("wst")],
                     writes=[K("wb%d" % slot)])
            if i == 0:
                cast()
            else:
                pending_cast.append(cast)
            return wb[slot][:, 0:8 * ncols].rearrange("p (k c) -> p k c", k=8), K("wb%d" % slot)

        def proj_bank(wv, wkey, c0, tau0, n):
            b = P_PR[bankc[0] % 2]
            bankc[0] += 1
            for k in range(8):
                S.op("pe", lambda e, k=k, b=b: e.matmul(psf[b][:, 0:n], lhsT=wv[:, k, c0:c0 + 128], rhs=hT[:, k, tau0:tau0 + n],
                                                       start=(k == 0), stop=(k == 7)),
                     reads=[wkey, K("hT")], writes=[K("ps%d" % b)])
            flush_rot()
            return b

        pending_rot = []

        def flush_rot():
            while pending_rot:
                pending_rot.pop(0)()

        def rope_bank(b, n, d, dest, dkey, ctd, std, t0):
            j = tabi[0] % 3
            tabi[0] += 1
            S.dma("sp", "ct%d" % j, lambda e: e.dma_start(out=ctab[j][:, 0:n], in_=ctd[:, t0:t0 + n]), writes=[K("ctab%d" % j)])
            S.dma("sp", "st%d" % j, lambda e: e.dma_start(out=stab[j][:, 0:n], in_=std[:, t0:t0 + n]), writes=[K("stab%d" % j)])
            if d == 1 or n < 512:
                ov, iv = qraw[j][:, 0:n], psf[b][:, 0:n]
            else:
                ov = qraw[j].rearrange("p (r i) -> p r i", r=d)
                iv = psf[b].rearrange("p (i r) -> p r i", r=d)
            S.op("act", lambda e: e.activation(out=ov, in_=iv, func=AF.Copy), reads=[K("ps%d" % b)], writes=[K("qraw%d" % j)])

            def second():
                S.op("pe", lambda e: e.matmul(psf[P_RT][:, 0:n], lhsT=rm, rhs=qraw[j][:, 0:n], start=True, stop=True),
                     reads=[K("rm"), K("qraw%d" % j)], writes=[K("ps%d" % P_RT)])
                S.op("dve", lambda e: e.tensor_tensor(out=rt1[j][:, 0:n], in0=psf[P_RT][:, 0:n], in1=stab[j][:, 0:n], op=ALU.mult),
                     reads=[K("ps%d" % P_RT), K("stab%d" % j)], writes=[K("rt1%d" % j)])
                S.op("pool", lambda e: e.tensor_tensor(out=qraw[j][:, 0:n], in0=qraw[j][:, 0:n], in1=ctab[j][:, 0:n], op=ALU.mult),
                     reads=[K("qraw%d" % j), K("ctab%d" % j)], writes=[K("qraw%d" % j)])
                if d == 16:
                    a0 = rt1[j].rearrange("p (r i) -> p r i", r=16)
                    a1 = qraw[j].rearrange("p (r i) -> p r i", r=16)
                else:
                    a0, a1 = rt1[j][:, 0:n], qraw[j][:, 0:n]
                S.op("dve", lambda e: e.tensor_tensor(out=dest, in0=a0, in1=a1, op=ALU.add),
                     reads=[K("rt1%d" % j), K("qraw%d" % j)], writes=[dkey])
            pending_rot.append(second)

        ptc = [0]
        sc = [0]
        ndc = [0]
        pending_tail = []

        def flush_tail():
            while pending_tail:
                pending_tail.pop(0)()

        for hp in dbg_hps:
            if not pending_tail:
                S.op("pool", lambda e: e.memset(acc.rearrange("p a t -> p (a t)"), 0.0), writes=[K("acc")])
            for g in dbg_gs:
                d = DIL[g]
                span = 128 * d
                nsp = NOWN // span
                nkb = (nsp + 1) * d
                kt0 = HAL - span
                nk = NOWN + span
                cq, sq, ck, sk = tabs[g]
                wv, wkey = load_w(w5[:, :, g, :, hp, :], 384)
                for bq in range(NOWN // 512):
                    if bq == 3 and pending_tail:
                        flush_tail()
                        S.op("pool", lambda e: e.memset(acc.rearrange("p a t -> p (a t)"), 0.0), writes=[K("acc")])
                    b = proj_bank(wv, wkey, 0, HAL + bq * 512, 512)
                    if d == 16:
                        sp_, i0 = (bq * 512) // span, ((bq * 512) % span) // 16
                        dest = QT[:, sp_ * span:(sp_ + 1) * span].rearrange("p (r i) -> p r i", r=16)[:, :, i0:i0 + 32]
                        rope_bank(b, 512, d, dest, K("QT"), cq, sq, bq * 512)
                    else:
                        rope_bank(b, 512, d, QT[:, bq * 512:(bq + 1) * 512], K("QT"), cq, sq, bq * 512)
                nb_full, rem = nk // 512, nk % 512
                for bk in range(nb_full + (1 if rem else 0)):
                    n = 512 if bk < nb_full else rem
                    b = proj_bank(wv, wkey, 128, kt0 + bk * 512, n)
                    if d == 16:
                        sp_, i0 = (bk * 512) // span, ((bk * 512) % span) // 16
                        dest = KT[:, sp_ * span:(sp_ + 1) * span].rearrange("p (r i) -> p r i", r=16)[:, :, i0:i0 + 32]
                        rope_bank(b, 512, d, dest, K("KT"), ck, sk, bk * 512)
                    else:
                        rope_bank(b, n, d, KT[:, bk * 512:bk * 512 + n], K("KT"), ck, sk, bk * 512)
                flush_rot()
                S.dma("sp", "vbl", lambda e, g=g, hp=hp, nkb=nkb: e.dma_start(
                    out=Vb[:, 0:nkb * 128], in_=vsc[g, hp, :, 0:nkb, :].rearrange("p b f -> p (b f)")),
                    reads=[K("vsc")], writes=[K("Vb")])
                nqb = nsp * d

                def s_pair(qb0):
                    ba, bb = P_S2[sc[0] % 3]
                    sc[0] += 1
                    for blk in range(2):
                        qb = qb0 + blk
                        for kbi in range(2):
                            kb = qb + kbi * d
                            for hh, b in ((0, ba), (1, bb)):
                                S.op("pe", lambda e, hh=hh, kbi=kbi, kb=kb, b=b, blk=blk, qb=qb: e.matmul(
                                    psf[b][:, (blk * 2 + kbi) * 128:(blk * 2 + kbi + 1) * 128],
                                    lhsT=KT[hh * 64:(hh + 1) * 64, kb * 128:(kb + 1) * 128],
                                    rhs=QT[hh * 64:(hh + 1) * 64, qb * 128:(qb + 1) * 128], start=True, stop=True,
                                    tile_position=(hh * 64, 0)),
                                    reads=[K("KT"), K("QT")], writes=[K("ps%d" % b)])
                    return ba, bb

                def e_pair(banks):
                    js = []
                    for b in banks:
                        j = ptc[0] % 6
                        ptc[0] += 1
                        S.op("act", lambda e, b=b, j=j: e.activation(out=PT[j], in_=psf[b], func=AF.Exp, scale=0.125),
                             reads=[K("ps%d" % b)], writes=[K("PT%d" % j)])
                        if j % 2 == 0:
                            S.op("pool", lambda e, j=j: e.tensor_tensor(out=PT[j], in0=PT[j], in1=maskT, op=ALU.mult),
                                 reads=[K("PT%d" % j), K("mask")], writes=[K("PT%d" % j)])
                        else:
                            S.op("dve", lambda e, j=j: e.tensor_tensor(out=PT[j], in0=PT[j], in1=maskT, op=ALU.mult),
                                 reads=[K("PT%d" % j), K("mask")], writes=[K("PT%d" % j)])
                        js.append(j)
                    return js

                def pv_pair(qb0, js, bnd):
                    for blk in range(2):
                        qb = qb0 + blk
                        for what in range(2):
                            for kbi in range(2):
                                kb = qb + kbi * d
                                for hh in range(2):
                                    j = js[hh]
                                    if what == 0:
                                        lhsT, lk = Vb[:, kb * 128 + hh * 64:kb * 128 + (hh + 1) * 64], K("Vb")
                                    elif kb < d:
                                        lhsT, lk = hvones, K("hvones")
                                    elif kb < d + 16:
                                        lhsT, lk = loones, K("loones")
                                    else:
                                        lhsT, lk = ones, K("ones")
                                    S.op("pe", lambda e, hh=hh, what=what, kbi=kbi, lhsT=lhsT, blk=blk, j=j: e.matmul(
                                        psf[bnd][hh * 64:(hh + 1) * 64, (what * 2 + blk) * 128:(what * 2 + blk + 1) * 128],
                                        lhsT=lhsT, rhs=PT[j][:, (blk * 2 + kbi) * 128:(blk * 2 + kbi + 1) * 128],
                                        start=(kbi == 0), stop=(kbi == 1), tile_position=(0, hh * 64)),
                                        reads=[lk, K("PT%d" % j)], writes=[K("ps%d" % bnd)])

                def acc_pair(qb0, bnd):
                    sp_, r = qb0 // d, qb0 % d
                    if d == 1:
                        av = acc[:, :, qb0 * 128:(qb0 + 2) * 128]
                        pv = psf[bnd].rearrange("p (a t) -> p a t", a=2)
                    else:
                        av = acc[:, :, sp_ * span:(sp_ + 1) * span].rearrange("p a (i r) -> p a r i", r=d)[:, :, r:r + 2, :]
                        pv = psf[bnd].rearrange("p (a s i) -> p a s i", a=2, s=2)
                    S.op("dve", lambda e: e.tensor_tensor(out=av, in0=av, in1=pv, op=ALU.add),
                         reads=[K("acc"), K("ps%d" % bnd)], writes=[K("acc")])

                flush_cast()
                if not dbg_attn:
                    continue
                npair = nqb // 2
                sb = {0: s_pair(0)}
                if npair > 1:
                    sb[1] = s_pair(2)
                jss = {0: e_pair(sb.pop(0))}
                for pi in range(npair):
                    if pi + 2 < npair:
                        sb[pi + 2] = s_pair(2 * (pi + 2))
                    if pi + 1 < npair:
                        jss[pi + 1] = e_pair(sb.pop(pi + 1))
                    bnd = P_ND[ndc[0] % 2]
                    ndc[0] += 1
                    pv_pair(2 * pi, jss.pop(pi), bnd)
                    acc_pair(2 * pi, bnd)
            wvz, wzkey = load_w(wz[:, :, hp, :], 128)
            zs = KT[:, 0:NOWN]
            for bq in range(NOWN // 512):
                b = proj_bank(wvz, wzkey, 0, HAL + bq * 512, 512)
                S.op("act", lambda e, b=b, bq=bq: e.activation(out=zs[:, bq * 512:(bq + 1) * 512], in_=psf[b], func=AF.Silu),
                     reads=[K("ps%d" % b)], writes=[K("KT")])
            def tail(hp=hp):
                S.op("dve", lambda e: e.tensor_scalar(out=acc[:, 1, :], in0=acc[:, 1, :], scalar1=1e-18, scalar2=None, op0=ALU.max),
                     reads=[K("acc")], writes=[K("acc")])
                S.op("act", lambda e: e.activation(out=acc[:, 1, :], in_=acc[:, 1, :], func=AF.Ln), reads=[K("acc")], writes=[K("acc")])
                S.op("act", lambda e: e.activation(out=acc[:, 1, :], in_=acc[:, 1, :], func=AF.Exp, scale=-1.0),
                     reads=[K("acc")], writes=[K("acc")])
                S.op("dve", lambda e: e.tensor_tensor(out=acc[:, 0, :], in0=acc[:, 0, :], in1=acc[:, 1, :], op=ALU.mult),
                     reads=[K("acc")], writes=[K("acc")])
                S.op("dve", lambda e: e.tensor_tensor(out=zs, in0=acc[:, 0, :], in1=zs, op=ALU.mult),
                     reads=[K("acc"), K("KT")], writes=[K("KT")])
                S.dma("pool", "ysc", lambda e: e.dma_start(out=ysc[hp], in_=zs), reads=[K("KT")], writes=[K("ysc")])
            pending_tail.append(tail)
            flush_tail()
            flush_cast()
        flush_tail()
        S.barrier()

        self.off = off0
        Wo = self.carve(8 * 1024, BF16).rearrange("p (k c) -> p k c", k=8)
        wos = self.carve(1024, F32)
        gpo = self.carve(1024, F32)
        yt = [self.carve(8 * 512, BF16).rearrange("p (h t) -> p h t", h=8) for _ in range(2)]
        x3 = [self.carve(1024, F32) for _ in range(8)]
        osb = [self.carve(1024, F32) for _ in range(8)]
        tpost = [self.carve(1024, F32) for _ in range(2)]
        junk3 = self.carve(512, BF16)
        st3 = self.carve(64, F32)
        P_O = ((0, 1), (2, 3))
        S.dma("sp", "gpo", lambda e: e.dma_start(out=gpo, in_=npost.partition_broadcast(128)), writes=[K("gpo")])
        for k in range(8):
            S.dma("sp", "wos", lambda e, k=k: e.dma_start(out=wos, in_=w_out[k * 128:(k + 1) * 128, :]), writes=[K("wos")])
            S.op("act", lambda e, k=k: e.activation(out=Wo[:, k, :], in_=wos, func=AF.Copy), reads=[K("wos")], writes=[K("Wo")])
        yv = ysc.rearrange("h p t -> p h t")
        NG3 = NOWN // 512

        def p3_yload(gi):
            q = gi % 2
            S.dma("sp", "ytl%d" % q, lambda e: e.dma_start(out=yt[q], in_=yv[:, :, gi * 512:(gi + 1) * 512]),
                  reads=[K("ysc")], writes=[K("yt%d" % q)])

        def p3_load(gi):
            for t in range(4):
                i = gi * 4 + t
                p = i % 8
                S.dma("sp", "x3l%d" % p, lambda e, i=i, p=p: e.dma_start(out=x3[p], in_=x_src[i * 128:(i + 1) * 128, :]),
                      writes=[K("x3%d" % p)])

        def p3_mm(gi):
            q = gi % 2
            ss2 = st3[:, q * 32:q * 32 + 8]
            for t in range(4):
                i = gi * 4 + t
                p = i % 8
                c0 = t * 128
                pb = P_O[i % 2]
                for n in range(2):
                    bk = pb[n]
                    for k in range(8):
                        S.op("pe", lambda e, k=k, n=n, bk=bk, c0=c0: e.matmul(psf[bk], lhsT=yt[q][:, k, c0:c0 + 128],
                                                                            rhs=Wo[:, k, n * 512:(n + 1) * 512],
                                                                            start=(k == 0), stop=(k == 7)),
                             reads=[K("yt%d" % q), K("Wo")], writes=[K("ps%d" % bk)])
                    S.op("act", lambda e, n=n, bk=bk, t=t, ss2=ss2: e.activation(out=junk3, in_=psf[bk], func=AF.Square,
                                                                               accum_out=ss2[:, 2 * t + n:2 * t + n + 1]),
                         reads=[K("ps%d" % bk)], writes=[K("junk3"), K("ss2%d" % q)])
                    S.op("act", lambda e, n=n, bk=bk, p=p: e.activation(out=osb[p][:, n * 512:(n + 1) * 512], in_=psf[bk],
                                                                       func=AF.Copy),
                         reads=[K("ps%d" % bk)], writes=[K("osb%d" % p)])

        def p3_post(gi):
            q = gi % 2
            ss2 = st3[:, q * 32:q * 32 + 8]
            a2, rstd2, tmp3 = (st3[:, q * 32 + 8 + c * 4:q * 32 + 12 + c * 4] for c in range(3))
            ssv = ss2.rearrange("p (t n) -> p t n", n=2)
            S.op("dve", lambda e: e.tensor_tensor(out=a2, in0=ssv[:, :, 0], in1=ssv[:, :, 1], op=ALU.add),
                 reads=[K("ss2%d" % q)], writes=[K("a2%d" % q)])
            S.op("dve", lambda e: e.tensor_scalar(out=a2, in0=a2, scalar1=1.0 / D, scalar2=RMS_EPS, op0=ALU.mult,
                                                  op1=ALU.add), reads=[K("a2%d" % q)], writes=[K("a2%d" % q)])
            self.rsqrt(K("a2%d" % q), a2, K("rstd2%d" % q), rstd2, K("tmp3%d" % q), tmp3)
            for t in range(4):
                i = gi * 4 + t
                p = i % 8
                tp = tpost[i % 2]
                S.op("dve", lambda e, p=p, t=t, tp=tp, rstd2=rstd2: e.scalar_tensor_tensor(out=tp, in0=osb[p], scalar=rstd2[:, t:t + 1],
                                                                                      in1=gpo, op0=ALU.mult, op1=ALU.mult),
                     reads=[K("osb%d" % p), K("rstd2%d" % q), K("gpo")], writes=[K("tpost%d" % (i % 2))])
                S.op("pool", lambda e, p=p, tp=tp: e.tensor_tensor(out=x3[p], in0=x3[p], in1=tp, op=ALU.add),
                     reads=[K("x3%d" % p), K("tpost%d" % (i % 2))], writes=[K("x3%d" % p)])
                o = S.dma("pool", "x3s%d" % p, lambda e, i=i, p=p: e.dma_start(out=x_dst[i * 128:(i + 1) * 128, :], in_=x3[p]),
                          reads=[K("x3%d" % p)], writes=[K("xdst")])
                self.final.append(o)

        p3_yload(0)
        p3_load(0)
        if NG3 > 1:
            p3_yload(1)
            p3_load(1)
        p3_mm(0)
        for gi in range(NG3):
            if gi + 1 < NG3:
                p3_mm(gi + 1)
            if gi + 2 < NG3:
                p3_yload(gi + 2)
            p3_post(gi)
            if gi + 2 < NG3:
                p3_load(gi + 2)
        S.barrier()
        self.peak = max(getattr(self, "peak", 0), self.off)
        self.off = off0

    def finish(self):
        S, nc = self.S, self.nc
        S.finalize()
        with contextlib.ExitStack() as es:
            sems = {k: es.enter_context(nc.semaphore("s%d" % i)) for i, k in enumerate(S.sem_keys)}
            block = es.enter_context(nc.Block())
            S.emit(block, sems, final_waits=self.final)
        return nc


_CACHE = {}


def consts():
    ident = np.eye(128, dtype=np.float32).astype(ml_dtypes.bfloat16)
    tril = np.tril(np.ones((128, 128), dtype=np.float32))
    return ident, tril


def prog_a(ntok):
    key = ("A", ntok)
    if key not in _CACHE:
        P = Prog()
        x = P.inp("x", [ntok, D])
        w_in = P.inp("w_in", [D, 3 * EW])
        npre = P.inp("npre", [128, 8])
        ln_g = P.inp("ln_g", [1, EW])
        ln_b = P.inp("ln_b", [1, EW])
        w_s = P.inp("w_s", [8, 128, 128])
        bs = P.inp("bs", [128, 8])
        w_out = P.inp("w_out", [EW, D])
        npost = P.inp("npost", [1, D])
        ident = P.inp("ident", [128, 128], BF16)
        tril = P.inp("tril", [128, 128])
        y = P.outp("y", [ntok, D])
        P.layer_a(x, y, ntok, w_in, npre, ln_g, ln_b, w_s, bs, w_out, npost, ident, tril)
        _CACHE[key] = P.finish()
    return _CACHE[key]


def col128(vec):
    return np.ascontiguousarray(np.asarray(vec, dtype=np.float32).reshape(-1, 128).T)


def run_layer_a(xs, i, inputs):
    j = i // 2
    ident, tril = consts()
    ntok = xs[0].shape[0]
    nc = prog_a(ntok)
    common = {
        "w_in": np.ascontiguousarray(inputs["a_w_in"][j]),
        "npre": col128(inputs["norm_pre"][i]),
        "ln_g": np.ascontiguousarray(inputs["a_ln_g"][j][None, :]),
        "ln_b": np.ascontiguousarray(inputs["a_ln_b"][j][None, :]),
        "w_s": np.ascontiguousarray(inputs["a_w_s"][j]),
        "bs": np.ascontiguousarray(np.asarray(inputs["a_b_s"][j]).T),
        "w_out": np.ascontiguousarray(inputs["a_w_out"][j]),
        "npost": np.ascontiguousarray(inputs["norm_post"][i][None, :]),
        "ident": ident, "tril": tril,
    }
    in_maps = [dict(common, x=np.ascontiguousarray(x)) for x in xs]
    res = run_bass_kernel_spmd(nc, in_maps, core_ids=list(range(len(xs))))
    return [r["y"] for r in res.results]


DILS = (1, 4, 16)
HALO = 2048


def rope_consts():
    rm = np.zeros((128, 128), np.float32)
    for fp in range(128):
        if fp % 64 < 32:
            rm[fp + 32, fp] = -1.0
        else:
            rm[fp - 32, fp] = 1.0
    mask = np.zeros((128, 2, 2, 128), np.float32)
    j = np.arange(128)[:, None]
    i = np.arange(128)[None, :]
    for hh in range(2):
        mask[:, hh, 0, :] = (j >= i)
        mask[:, hh, 1, :] = (j <= i)
    return rm.astype(ml_dtypes.bfloat16), mask.reshape(128, 512).astype(ml_dtypes.bfloat16)


def rope_tables_core(pos0, nown=4096):
    inv_freq = (1.0 / (np.float32(10000.0) ** (np.arange(0, 64, 2, dtype=np.float32) / np.float32(64)))).astype(np.float32)
    out = []
    for d in DILS:
        span = 128 * d
        res = []
        for (tau0, ncols) in ((HALO, nown), (HALO - span, nown + span)):
            tau = np.zeros(ncols, np.int64)
            for b0 in range(0, ncols, 512):
                n = min(512, ncols - b0)
                col = np.arange(n)
                r, ii = col // (n // d), col % (n // d)
                tau[b0:b0 + n] = tau0 + b0 + ii * d + r
            pos = (pos0 - HALO + tau).astype(np.float32)
            ang = pos[None, :] * inv_freq[:, None]
            c = np.tile(np.cos(ang), (4, 1)).astype(ml_dtypes.bfloat16)
            s_ = np.tile(np.sin(ang), (4, 1)).astype(ml_dtypes.bfloat16)
            res += [np.ascontiguousarray(c), np.ascontiguousarray(s_)]
        out.append(res)
    return out


DBG = {}


def prog_b():
    key = ("B",)
    if key not in _CACHE:
        P = Prog()
        x = P.inp("x", [TOK, D])
        xh = P.inp("xh", [HALO, D])
        w_in = P.inp("w_in", [D, 10240])
        gpre = P.inp("gpre", [1, D])
        w_out = P.inp("w_out", [D, D])
        npost = P.inp("npost", [1, D])
        hv = P.inp("hv", [128, 2])
        ident = P.inp("ident", [128, 128], BF16)
        rm = P.inp("rm", [128, 128], BF16)
        mask = P.inp("mask", [128, 512], BF16)
        tabs = []
        for g, d in enumerate(DILS):
            nk = 4096 + 128 * d
            tabs.append((P.inp("cq%d" % g, [128, 4096], BF16), P.inp("sq%d" % g, [128, 4096], BF16),
                         P.inp("ck%d" % g, [128, nk], BF16), P.inp("sk%d" % g, [128, nk], BF16)))
        ysc = P.nc.dram_tensor("ysc", [8, 128, 4096], BF16).ap()
        vsc = P.nc.dram_tensor("vsc", [3, 8, 128, 48, 128], BF16).ap()
        y = P.outp("y", [TOK, D])
        P.layer_b(x, xh, y, ysc, vsc, w_in, gpre, w_out, npost, hv, tabs, ident, rm, mask, **DBG)
        _CACHE[key] = P.finish()
    return _CACHE[key]


def run_layer_b(xs, i, inputs, core_ids=None):
    j = i // 2
    ident, _ = consts()
    rm, mask = rope_consts()
    nc = prog_b()
    common = {
        "w_in": np.ascontiguousarray(inputs["b_w_in"][j]),
        "gpre": np.ascontiguousarray(inputs["norm_pre"][i][None, :]),
        "w_out": np.ascontiguousarray(inputs["b_w_out"][j]),
        "npost": np.ascontiguousarray(inputs["norm_post"][i][None, :]),
        "ident": ident, "rm": rm, "mask": mask,
    }
    in_maps = []
    cores = list(range(len(xs))) if core_ids is None else core_ids
    for c in cores:
        m = dict(common)
        m["x"] = np.ascontiguousarray(xs[c])
        first = (c % 4 == 0)
        m["xh"] = np.zeros((HALO, D), np.float32) if first else np.ascontiguousarray(xs[c - 1][TOK - HALO:])
        m["hv"] = np.tile(np.array([[0.0 if first else 1.0, 1.0]], np.float32), (128, 1))
        tb = rope_tables_core((c % 4) * TOK)
        for g in range(3):
            m["cq%d" % g], m["sq%d" % g], m["ck%d" % g], m["sk%d" % g] = tb[g]
        in_maps.append(m)
    res = run_bass_kernel_spmd(nc, in_maps, core_ids=list(range(len(cores))))
    return [r["y"] for r in res.results]


def kernel_unfused(x, norm_pre, norm_post, a_w_in, a_ln_g, a_ln_b, a_w_s, a_b_s, a_w_out, b_w_in, b_w_out):
    inputs = dict(norm_pre=np.asarray(norm_pre), norm_post=np.asarray(norm_post), a_w_in=np.asarray(a_w_in),
                  a_ln_g=np.asarray(a_ln_g), a_ln_b=np.asarray(a_ln_b), a_w_s=np.asarray(a_w_s), a_b_s=np.asarray(a_b_s),
                  a_w_out=np.asarray(a_w_out), b_w_in=np.asarray(b_w_in), b_w_out=np.asarray(b_w_out))
    x = np.asarray(x, dtype=np.float32)
    B, S_, D_ = x.shape
    xs = [np.ascontiguousarray(c) for c in x.reshape(NCORES, TOK, D_)]
    for i in range(4):
        if i % 2 == 0:
            xs = run_layer_a(xs, i, inputs)
        else:
            xs = run_layer_b(xs, i, inputs)
    return np.stack(xs).reshape(B, S_, D_).astype(np.float32)


XR = 8192
B_CALLS = ((2048, 2048), (4096, -2048), (4096, 0))


def prog_fused():
    key = ("F",)
    if key in _CACHE:
        return _CACHE[key]
    P = Prog()
    x = P.inp("x", [XR, D])
    ident = P.inp("ident", [128, 128], BF16)
    tril = P.inp("tril", [128, 128])
    rm = P.inp("rm", [128, 128], BF16)
    mask = P.inp("mask", [128, 512], BF16)
    npost = [P.inp("npost%d" % i, [1, D]) for i in range(4)]
    A = []
    for j in range(2):
        A.append(dict(w_in=P.inp("a_w_in%d" % j, [D, 3 * EW]), npre=P.inp("a_npre%d" % j, [128, 8]),
                      ln_g=P.inp("a_ln_g%d" % j, [1, EW]), ln_b=P.inp("a_ln_b%d" % j, [1, EW]),
                      w_s=P.inp("a_w_s%d" % j, [8, 128, 128]), bs=P.inp("a_bs%d" % j, [128, 8]),
                      w_out=P.inp("a_w_out%d" % j, [EW, D])))
    Bw = []
    for j in range(2):
        Bw.append(dict(w_in=P.inp("b_w_in%d" % j, [D, 10240]), gpre=P.inp("b_gpre%d" % j, [1, D]),
                       w_out=P.inp("b_w_out%d" % j, [D, D])))
    fl = [P.inp("fl%d" % c, [128, 2]) for c in range(3)]
    tabs = []
    for c, (nown, _) in enumerate(B_CALLS):
        tc = []
        for g, d in enumerate(DILS):
            nk = nown + 128 * d
            tc.append((P.inp("cq%d_%d" % (c, g), [128, nown], BF16), P.inp("sq%d_%d" % (c, g), [128, nown], BF16),
                       P.inp("ck%d_%d" % (c, g), [128, nk], BF16), P.inp("sk%d_%d" % (c, g), [128, nk], BF16)))
        tabs.append(tc)
    y = P.outp("y", [TOK, D])
    xres = P.nc.dram_tensor("xres", [XR, D], F32).ap()
    ysc = P.nc.dram_tensor("ysc", [8, 128, 4096], BF16).ap()
    vsc = P.nc.dram_tensor("vsc", [3, 8, 128, 48, 128], BF16).ap()

    def la(j, i, src, dst, ntok):
        a = A[j]
        P.layer_a(src, dst, ntok, a["w_in"], a["npre"], a["ln_g"], a["ln_b"], a["w_s"], a["bs"], a["w_out"], npost[i],
                  ident, tril)

    def lb(j, i, c, src, halo, dst):
        nown = B_CALLS[c][0]
        b = Bw[j]
        P.layer_b(src, halo, dst, ysc[:, :, 0:nown], vsc, b["w_in"], b["gpre"], b["w_out"], npost[i], fl[c], tabs[c],
                  ident, rm, mask, NOWN=nown)

    la(0, 0, x, xres, XR)
    lb(0, 1, 0, xres[6144:8192], xres[4096:6144], xres[6144:8192])
    lb(0, 1, 1, xres[2048:6144], xres[0:2048], xres[2048:6144])
    la(1, 2, xres[2048:8192], xres[2048:8192], 6144)
    P.final = []
    lb(1, 3, 2, xres[4096:8192], xres[2048:4096], y)
    _CACHE[key] = P.finish()
    return _CACHE[key]


def kernel(x, norm_pre, norm_post, a_w_in, a_ln_g, a_ln_b, a_w_s, a_b_s, a_w_out, b_w_in, b_w_out):
    f32 = lambda a: np.ascontiguousarray(np.asarray(a, dtype=np.float32))
    x = f32(x)
    norm_pre, norm_post = f32(norm_pre), f32(norm_post)
    Bn, S_, D_ = x.shape
    per_seq = S_ // TOK
    ident, tril = consts()
    rm, mask = rope_consts()
    common = {"ident": ident, "tril": tril, "rm": rm, "mask": mask}
    for i in range(4):
        common["npost%d" % i] = f32(norm_post[i][None, :])
    for j in range(2):
        common["a_w_in%d" % j] = f32(a_w_in[j])
        common["a_npre%d" % j] = col128(norm_pre[2 * j])
        common["a_ln_g%d" % j] = f32(np.asarray(a_ln_g[j])[None, :])
        common["a_ln_b%d" % j] = f32(np.asarray(a_ln_b[j])[None, :])
        common["a_w_s%d" % j] = f32(a_w_s[j])
        common["a_bs%d" % j] = f32(np.asarray(a_b_s[j]).T)
        common["a_w_out%d" % j] = f32(a_w_out[j])
        common["b_w_in%d" % j] = f32(b_w_in[j])
        common["b_gpre%d" % j] = f32(norm_pre[2 * j + 1][None, :])
        common["b_w_out%d" % j] = f32(b_w_out[j])
    tab_cache = {}
    in_maps = []
    for c in range(NCORES):
        b, k = c // per_seq, c % per_seq
        m = dict(common)
        xe = np.zeros((XR, D_), np.float32)
        lo = k * TOK - TOK
        src_lo = max(lo, 0)
        xe[src_lo - lo:] = x[b, src_lo:(k + 1) * TOK]
        m["x"] = xe
        first = (k == 0)
        flags = ((1.0, 1.0), (0.0, 0.0) if first else (1.0, 1.0), (0.0, 1.0) if first else (1.0, 1.0))
        for ci in range(3):
            m["fl%d" % ci] = np.tile(np.array([flags[ci]], np.float32), (128, 1))
            nown, off = B_CALLS[ci]
            tk = (k, ci)
            if tk not in tab_cache:
                tab_cache[tk] = rope_tables_core(k * TOK + off, nown)
            tb = tab_cache[tk]
            for g in range(3):
                m["cq%d_%d" % (ci, g)], m["sq%d_%d" % (ci, g)], m["ck%d_%d" % (ci, g)], m["sk%d_%d" % (ci, g)] = tb[g]
        in_maps.append(m)
    nc = prog_fused()
    res = run_bass_kernel_spmd(nc, in_maps, core_ids=list(range(NCORES)))
    out = np.stack([r["y"] for r in res.results]).reshape(Bn, S_, D_)
    return out.astype(np.float32)
```

```python
import contextlib
import numpy as np
import ml_dtypes
import concourse.bass as bass
import concourse.mybir as mybir
from concourse.bass_utils import run_bass_kernel_spmd

F32 = mybir.dt.float32
BF16 = mybir.dt.bfloat16
I32 = mybir.dt.int32
AF = mybir.ActivationFunctionType
ALU = mybir.AluOpType

D = 1024
NCORES = 8
TOK = 4096
EW = 2048
RMS_EPS = 1e-6
LN_EPS = 1e-5

ENGS = ("pe", "act", "dve", "pool", "sp")


class _Op:
    __slots__ = ("eng", "fn", "sem", "val", "waits", "dma", "needs_inc")

    def __init__(self, eng, fn, dma):
        self.eng = eng
        self.fn = fn
        self.dma = dma
        self.sem = None
        self.val = 0
        self.waits = []
        self.needs_inc = False


class Sched:
    def __init__(self):
        self.ops = {e: [] for e in ENGS}
        self.last_w = {}
        self.readers = {}
        self.all_ops = []
        self.epoch = 0

    def _deps(self, op, reads, writes):
        deps = []
        raw = set()
        for b in reads:
            w = self.last_w.get(b)
            if w is not None:
                deps.append(w)
                raw.add(id(w))
        for b in writes:
            w = self.last_w.get(b)
            if w is not None:
                deps.append(w)
            deps.extend(self.readers.get(b, {}).values())
        for b in reads:
            self.readers.setdefault(b, {})[op.sem] = op
        for b in writes:
            self.last_w[b] = op
            self.readers[b] = {}
        return deps, raw

    def barrier(self):
        last = {}
        for o in self.all_ops:
            last[o.sem] = o
        self.pending = {e: list(last.values()) for e in ENGS}

    def _pend(self, o, eng):
        pend = getattr(self, "pending", None)
        if pend and pend.get(eng):
            for d in pend[eng]:
                if d.dma is None and d.eng == eng:
                    continue
                d.needs_inc = True
                o.waits.append(d)
            pend[eng] = []

    def op(self, eng, fn, reads=(), writes=()):
        o = _Op(eng, fn, None)
        o.sem = ("eng", eng, self.epoch)
        self._pend(o, eng)
        deps, raw = self._deps(o, reads, writes)
        for d in deps:
            if d is o:
                continue
            if d.dma is None and d.eng == eng:
                if eng == "pe" or id(d) not in raw:
                    continue
            d.needs_inc = True
            o.waits.append(d)
        self.ops[eng].append(o)
        self.all_ops.append(o)
        return o

    def dma(self, queue, stream, fn, reads=(), writes=()):
        o = _Op(queue, fn, stream)
        o.sem = ("dma", stream)
        o.needs_inc = True
        self._pend(o, queue)
        deps, _ = self._deps(o, reads, writes)
        for d in deps:
            if d is o:
                continue
            d.needs_inc = True
            o.waits.append(d)
        self.ops[queue].append(o)
        self.all_ops.append(o)
        return o

    def new_epoch(self):
        self.epoch += 1

    def finalize(self):
        counts = {}
        for o in self.all_ops:
            if o.needs_inc:
                step = 16 if o.dma is not None else 1
                counts[o.sem] = counts.get(o.sem, 0) + step
                o.val = counts[o.sem]
        self.sem_keys = list(counts.keys())
        for k, v in counts.items():
            assert v < 60000, (k, v)
        return counts

    def emit(self, block, sems, final_waits=()):
        handles = {"pe": "tensor", "act": "scalar", "dve": "vector", "pool": "gpsimd", "sp": "sync"}

        def make(engname):
            ops = self.ops[engname]

            def body(eng):
                waited = {}

                def wait(d):
                    if waited.get(d.sem, 0) >= d.val:
                        return
                    eng.wait_ge(sems[d.sem], d.val)
                    waited[d.sem] = d.val

                for o in ops:
                    for d in o.waits:
                        wait(d)
                    ins = o.fn(eng)
                    if o.needs_inc:
                        ins.then_inc(sems[o.sem], 16 if o.dma is not None else 1)
                if engname == "sp":
                    for d in final_waits:
                        wait(d)
            return body

        for engname in ENGS:
            getattr(block, handles[engname])(make(engname))


class Prog:
    def __init__(self):
        self.nc = bass.Bass("TRN2", target_bir_lowering=False)
        self.S = Sched()
        self.cap = 212000
        self.arena = self.nc.alloc_sbuf_tensor("arena", [128, self.cap // 2], BF16)[:]
        self.off = 0
        self.uid = 0
        self.ps_f32 = [self.nc.alloc_psum_tensor("psb%d" % i, [128, 512], F32)[:] for i in range(8)]
        self.ps_b16 = [p.bitcast(BF16) for p in self.ps_f32]
        self.final = []

    def inp(self, name, shape, dtype=F32):
        return self.nc.dram_tensor(name, list(shape), dtype, kind="ExternalInput").ap()

    def outp(self, name, shape, dtype=F32):
        return self.nc.dram_tensor(name, list(shape), dtype, kind="ExternalOutput").ap()

    def carve(self, cols, dtype):
        esz = 4 if dtype in (F32, I32) else 2
        nb = (cols * esz + 63) // 64 * 64
        assert self.off + nb <= self.cap, ("SBUF arena overflow", self.off, nb)
        a = self.arena[:, self.off // 2:(self.off + nb) // 2]
        self.off += nb
        if dtype != BF16:
            a = a.bitcast(dtype)
        return a[:, 0:cols]

    def key(self, base):
        self.uid += 1
        return "%s#%d" % (base, self.uid)

    def rsqrt(self, a_key, a, out_key, out, tmp_key, tmp):
        S = self.S
        S.op("dve", lambda e: e.tensor_scalar(out=out.bitcast(I32), in0=a.bitcast(I32), scalar1=1, scalar2=None,
                                              op0=ALU.arith_shift_right), reads=[a_key], writes=[out_key])
        S.op("dve", lambda e: e.tensor_scalar(out=out.bitcast(I32), in0=out.bitcast(I32), scalar1=0x5f3759df,
                                              scalar2=-1, op0=ALU.subtract, op1=ALU.mult),
             reads=[out_key], writes=[out_key])
        for _ in range(2):
            S.op("dve", lambda e: e.scalar_tensor_tensor(out=tmp, in0=out, scalar=-0.5, in1=out, op0=ALU.mult,
                                                         op1=ALU.mult), reads=[out_key], writes=[tmp_key])
            S.op("dve", lambda e: e.tensor_tensor(out=tmp, in0=tmp, in1=a, op=ALU.mult),
                 reads=[tmp_key, a_key], writes=[tmp_key])
            S.op("dve", lambda e: e.scalar_tensor_tensor(out=out, in0=tmp, scalar=1.5, in1=out, op0=ALU.add,
                                                         op1=ALU.mult), reads=[tmp_key, out_key], writes=[out_key])

    def layer_a(self, x_src, x_dst, ntok, w_in, npre_col, ln_g, ln_b, w_s, bs_col, w_out, npost, ident_d, tril_d):
        S, nc = self.S, self.nc
        off0 = self.off
        NT = ntok // 128
        L = self.key("A")
        K = lambda s: "%s/%s" % (L, s)

        Wb = [self.carve(6144, BF16) for _ in range(8)]
        Wo = [self.carve(1024, BF16) for _ in range(16)]
        lng = self.carve(2048, BF16)
        lnb = self.carve(2048, BF16)
        gpo = self.carve(1024, F32)
        WmT = self.carve(1024, BF16)
        bsc = self.carve(8, F32)
        gpre = self.carve(8, F32)
        idt = self.carve(128, BF16)
        tril = self.carve(128, F32)
        xt = [self.carve(1024, F32) for _ in range(3)]
        xb = self.carve(1024, BF16)
        xT = [self.carve(1024, BF16) for _ in range(2)]
        junk = self.carve(1024, BF16)
        u = self.carve(2048, BF16)
        sz = self.carve(2048, BF16)
        v = self.carve(2048, F32)
        vn = [self.carve(2048, BF16) for _ in range(2)]
        y = self.carve(2048, BF16)
        yT = self.carve(2048, BF16)
        tpost = self.carve(1024, F32)
        st = self.carve(64, F32)
        ss, ms, tmp1 = st[:, 0:1], st[:, 1:2], st[:, 3:4]
        rstd = [st[:, 4:5], st[:, 5:6]]
        bst = st[:, 8:32]
        mv = st[:, 32:34]
        va, rsv, tmp2 = st[:, 34:35], st[:, 35:36], st[:, 36:37]
        ss2, a2, rstd2, tmp3 = st[:, 40:42], st[:, 42:43], st[:, 43:44], st[:, 44:45]
        stage = [v[:, 0:1024], v[:, 1024:2048]]
        VK = [K("v0"), K("v1")]
        P_TX, P_IN, P_SV, P_TY, P_O = 0, (1, 2), (3, 4), 5, (6, 7)
        P_TY2 = (5, 0)
        psf, psb = self.ps_f32, self.ps_b16

        S.dma("sp", "idt", lambda e: e.dma_start(out=idt, in_=ident_d), writes=[K("idt")])
        S.dma("sp", "tril", lambda e: e.dma_start(out=tril, in_=tril_d), writes=[K("tril")])
        S.dma("sp", "gpre", lambda e: e.dma_start(out=gpre, in_=npre_col), writes=[K("gpre")])
        S.dma("sp", "bsc", lambda e: e.dma_start(out=bsc, in_=bs_col), writes=[K("bsc")])
        S.dma("sp", "gpo", lambda e: e.dma_start(out=gpo, in_=npost.partition_broadcast(128)), writes=[K("gpo")])
        cnt = [0]

        def staged(src_ap, cols, consume, view=None):
            i = cnt[0] % 2
            cnt[0] += 1
            sb = stage[i][:, 0:cols]
            sbd = view(sb) if view is not None else sb
            S.dma("sp", "stg%d" % i, lambda e: e.dma_start(out=sbd, in_=src_ap), writes=[VK[i]])
            consume(sb, "act" if i == 0 else "dve", VK[i])

        for k in range(8):
            for q in range(6):
                dst = Wb[k][:, q * 1024:(q + 1) * 1024]

                def cons(sb, eng, skey, dst=dst, k=k):
                    if eng == "act":
                        S.op("act", lambda e: e.activation(out=dst, in_=sb, func=AF.Copy, scale=gpre[:, k:k + 1]),
                             reads=[skey, K("gpre")], writes=[K("Wb%d" % k)])
                    else:
                        S.op("dve", lambda e: e.tensor_scalar(out=dst, in0=sb, scalar1=gpre[:, k:k + 1], scalar2=None,
                                                              op0=ALU.mult), reads=[skey, K("gpre")], writes=[K("Wb%d" % k)])
                staged(w_in[k * 128:(k + 1) * 128, q * 1024:(q + 1) * 1024], 1024, cons)
        for k in range(16):
            def cons(sb, eng, skey, k=k):
                if eng == "act":
                    S.op("act", lambda e: e.activation(out=Wo[k], in_=sb, func=AF.Copy), reads=[skey], writes=[K("Wo")])
                else:
                    S.op("dve", lambda e: e.tensor_copy(out=Wo[k], in_=sb), reads=[skey], writes=[K("Wo")])
            staged(w_out[k * 128:(k + 1) * 128, :], 1024, cons)
        for (src, dstt, nm) in ((ln_g, lng, "lng"), (ln_b, lnb, "lnb")):
            for h in range(2):
                def cons(sb, eng, skey, dstt=dstt, h=h, nm=nm):
                    if eng == "act":
                        S.op("act", lambda e: e.activation(out=dstt[:, h * 1024:(h + 1) * 1024], in_=sb, func=AF.Copy),
                             reads=[skey], writes=[K(nm)])
                    else:
                        S.op("dve", lambda e: e.tensor_copy(out=dstt[:, h * 1024:(h + 1) * 1024], in_=sb),
                             reads=[skey], writes=[K(nm)])
                staged(src[:, h * 1024:(h + 1) * 1024].partition_broadcast(128), 1024, cons)

        def cons_ws(sb, eng, skey):
            sbv = sb.rearrange("p (g s) -> p g s", g=8)
            yv = y[:, 0:1024].rearrange("p (g s) -> p g s", g=8)
            for g in range(8):
                S.op("dve", lambda e, g=g: e.tensor_tensor(out=yv[:, g, :], in0=sbv[:, g, :], in1=tril, op=ALU.mult),
                     reads=[skey, K("tril")], writes=[K("y")])
            for g in range(8):
                S.op("pe", lambda e, g=g: e.transpose(out=psb[P_TX][:, g * 128:(g + 1) * 128], in_=yv[:, g, :], identity=idt),
                     reads=[K("y"), K("idt")], writes=[K("ps%d" % P_TX)])
            S.op("dve", lambda e: e.tensor_copy(out=WmT, in_=psb[P_TX]), reads=[K("ps%d" % P_TX)], writes=[K("WmT")])
        staged(w_s.rearrange("g t s -> t g s"), 1024, cons_ws, view=lambda sb: sb.rearrange("p (g s) -> p g s", g=8))

        def load(i):
            p = i % 3
            S.dma("sp", "xl%d" % p, lambda e: e.dma_start(out=xt[p], in_=x_src[i * 128:(i + 1) * 128, :]),
                  writes=[K("xt%d" % p)])

        def inproj(q, banks, r0):
            r = r0
            for n in banks:
                b = P_IN[r % 2]
                r += 1
                for k in range(8):
                    S.op("pe", lambda e, k=k, n=n, b=b: e.matmul(psf[b], lhsT=xT[q][:, k * 128:(k + 1) * 128],
                                                                rhs=Wb[k][:, n * 512:(n + 1) * 512],
                                                                start=(k == 0), stop=(k == 7)),
                         reads=[K("xT%d" % q), K("Wb%d" % k)], writes=[K("ps%d" % b)])
                if n < 4:
                    dst, func, dk = u[:, n * 512:(n + 1) * 512], AF.Gelu, [K("u%d" % n)]
                elif n < 8:
                    dst, func, dk = v[:, (n - 4) * 512:(n - 3) * 512], AF.Gelu, VK
                else:
                    dst, func, dk = sz[:, (n - 8) * 512:(n - 7) * 512], AF.Silu, [K("sz%d" % (n - 8))]
                S.op("act", lambda e, dst=dst, func=func, b=b: e.activation(out=dst, in_=psf[b], func=func, scale=rstd[q]),
                     reads=[K("ps%d" % b), K("rstd%d" % q)], writes=dk)

        def pre(i):
            p, q = i % 3, i % 2
            xk = K("xt%d" % p)
            S.op("act", lambda e: e.activation(out=junk, in_=xt[p], func=AF.Square, accum_out=ss),
                 reads=[xk], writes=[K("junk"), K("ss")])
            S.op("dve", lambda e: e.tensor_scalar(out=ms, in0=ss, scalar1=1.0 / D, scalar2=RMS_EPS, op0=ALU.mult,
                                                  op1=ALU.add), reads=[K("ss")], writes=[K("ms")])
            self.rsqrt(K("ms"), ms, K("rstd%d" % q), rstd[q], K("tmp1"), tmp1)
            S.op("pool", lambda e: e.tensor_copy(out=xb, in_=xt[p]), reads=[xk], writes=[K("xb")])

        def stage1_t(i):
            q = i % 2
            for k in range(8):
                S.op("pe", lambda e, k=k: e.transpose(out=psb[P_TX][:, k * 128:(k + 1) * 128],
                                                      in_=xb[:, k * 128:(k + 1) * 128], identity=idt),
                     reads=[K("xb"), K("idt")], writes=[K("ps%d" % P_TX)])
            S.op("act", lambda e: e.activation(out=xT[q], in_=psb[P_TX], func=AF.Copy),
                 reads=[K("ps%d" % P_TX)], writes=[K("xT%d" % q)])

        def stage1(i):
            q = i % 2
            inproj(q, (4, 5, 6, 7), 0)
            for c in range(4):
                S.op("dve", lambda e, c=c: e.bn_stats(out=bst[:, c * 6:(c + 1) * 6], in_=v[:, c * 512:(c + 1) * 512]),
                     reads=VK, writes=[K("bst")])
            S.op("dve", lambda e: e.bn_aggr(out=mv, in_=bst.rearrange("p (c s) -> p c s", c=4)),
                 reads=[K("bst")], writes=[K("mv")])
            S.op("dve", lambda e: e.tensor_scalar(out=va, in0=mv[:, 1:2], scalar1=LN_EPS, scalar2=None,
                                                  op0=ALU.add), reads=[K("mv")], writes=[K("va")])
            self.rsqrt(K("va"), va, K("rsv"), rsv, K("tmp2"), tmp2)
            S.op("dve", lambda e: e.scalar_tensor_tensor(out=v, in0=v, scalar=mv[:, 0:1], in1=lng,
                                                         op0=ALU.subtract, op1=ALU.mult),
                 reads=VK + [K("mv"), K("lng")], writes=VK)
            S.op("dve", lambda e: e.scalar_tensor_tensor(out=vn[q], in0=v, scalar=rsv, in1=lnb,
                                                         op0=ALU.mult, op1=ALU.add),
                 reads=VK + [K("rsv"), K("lnb")], writes=[K("vn%d" % q)])

        def stage2a(i):
            q = i % 2
            inproj(q, (0, 1, 2, 3, 8, 9, 10, 11), 0)
            if i + 1 < NT:
                stage1_t(i + 1)
            for qd in range(4):
                b = P_SV[qd % 2]
                for gg in range(2):
                    g = 2 * qd + gg
                    S.op("pe", lambda e, g=g, gg=gg, b=b: e.matmul(psf[b][:, gg * 256:(gg + 1) * 256],
                                                                  lhsT=WmT[:, g * 128:(g + 1) * 128],
                                                                  rhs=vn[q][:, g * 256:(g + 1) * 256], start=True, stop=True),
                         reads=[K("vn%d" % q), K("WmT")], writes=[K("ps%d" % b)])
                for gg in range(2):
                    g = 2 * qd + gg
                    S.op("dve", lambda e, g=g, gg=gg, b=b: e.scalar_tensor_tensor(
                        out=y[:, g * 256:(g + 1) * 256], in0=psf[b][:, gg * 256:(gg + 1) * 256], scalar=bsc[:, g:g + 1],
                        in1=u[:, g * 256:(g + 1) * 256], op0=ALU.add, op1=ALU.mult),
                        reads=[K("ps%d" % b), K("bsc"), K("u%d" % qd)], writes=[K("y%d" % qd)])
            S.op("dve", lambda e: e.tensor_tensor(out=y, in0=y, in1=sz, op=ALU.mult),
                 reads=[K("y")] + [K("y%d" % c) for c in range(4)] + [K("sz%d" % c) for c in range(4)],
                 writes=[K("y")] + [K("y%d" % c) for c in range(4)])

        def stage2b(i):
            p = i % 3
            xk = K("xt%d" % p)
            for h in range(2):
                for k in range(8):
                    kk = h * 8 + k
                    S.op("pe", lambda e, k=k, kk=kk, h=h: e.transpose(out=psb[P_TY2[h]][:, k * 128:(k + 1) * 128],
                                                                     in_=y[:, kk * 128:(kk + 1) * 128], identity=idt),
                         reads=[K("y"), K("idt")] + [K("y%d" % c) for c in range(4)], writes=[K("ps%d" % P_TY2[h])])
            for h in range(2):
                S.op("act", lambda e, h=h: e.activation(out=yT[:, h * 1024:(h + 1) * 1024], in_=psb[P_TY2[h]], func=AF.Copy),
                     reads=[K("ps%d" % P_TY2[h])], writes=[K("yT%d" % h)])
            for n in range(2):
                b = P_O[n]
                for k in range(16):
                    S.op("pe", lambda e, k=k, n=n, b=b: e.matmul(psf[b], lhsT=yT[:, k * 128:(k + 1) * 128],
                                                                rhs=Wo[k][:, n * 512:(n + 1) * 512],
                                                                start=(k == 0), stop=(k == 15)),
                         reads=[K("yT%d" % (k // 8)), K("Wo")], writes=[K("ps%d" % b)])
                S.op("act", lambda e, n=n, b=b: e.activation(out=junk[:, 0:512], in_=psf[b], func=AF.Square,
                                                             accum_out=ss2[:, n:n + 1]),
                     reads=[K("ps%d" % b)], writes=[K("junk"), K("ss2")])
            S.op("dve", lambda e: e.tensor_tensor(out=a2, in0=ss2[:, 0:1], in1=ss2[:, 1:2], op=ALU.add),
                 reads=[K("ss2")], writes=[K("a2")])
            S.op("dve", lambda e: e.tensor_scalar(out=a2, in0=a2, scalar1=1.0 / D, scalar2=RMS_EPS, op0=ALU.mult,
                                                  op1=ALU.add), reads=[K("a2")], writes=[K("a2")])
            self.rsqrt(K("a2"), a2, K("rstd2"), rstd2, K("tmp3"), tmp3)
            for n in range(2):
                b = P_O[n]
                S.op("dve", lambda e, n=n, b=b: e.scalar_tensor_tensor(out=tpost[:, n * 512:(n + 1) * 512], in0=psf[b],
                                                                       scalar=rstd2, in1=gpo[:, n * 512:(n + 1) * 512],
                                                                       op0=ALU.mult, op1=ALU.mult),
                     reads=[K("ps%d" % b), K("rstd2"), K("gpo")], writes=[K("tpost")])
            S.op("pool", lambda e: e.tensor_tensor(out=xt[p], in0=xt[p], in1=tpost, op=ALU.add),
                 reads=[xk, K("tpost")], writes=[xk])
            o = S.dma("pool", "xs%d" % p, lambda e: e.dma_start(out=x_dst[i * 128:(i + 1) * 128, :], in_=xt[p]),
                      reads=[xk], writes=[K("xdst")])
            self.final.append(o)

        load(0)
        if NT > 1:
            load(1)
        pre(0)
        stage1_t(0)
        stage1(0)
        for i in range(NT):
            if i + 2 < NT:
                load(i + 2)
            if i + 1 < NT:
                pre(i + 1)
            stage2a(i)
            if i + 1 < NT:
                stage1(i + 1)
            stage2b(i)
        S.barrier()
        self.peak = max(getattr(self, "peak", 0), self.off)
        self.off = off0


    def layer_b(self, x_src, x_halo, x_dst, ysc, vsc, w_in, gpre_d, w_out, npost, hv_d, tabs, ident_d, rm_d, mask_d,
                NOWN=4096, dbg_hps=range(8), dbg_gs=range(3), dbg_attn=True):
        S, nc = self.S, self.nc
        off0 = self.off
        L = self.key("B")
        K = lambda s: "%s/%s" % (L, s)
        psf, psb = self.ps_f32, self.ps_b16
        HAL = 2048
        NT = HAL + NOWN
        DIL = (1, 4, 16)
        P_PR, P_RT, P_S2, P_ND, P_V2 = (0, 1), 2, ((3, 4), (0, 1), (2, 7)), (5, 6), (7, 3)

        hT = self.carve(8 * NT, BF16).rearrange("p (k t) -> p k t", k=8)
        offq = self.off
        QT = self.carve(NOWN, BF16)
        KT = self.carve(NT, BF16)
        Vb = self.carve(NT, BF16)
        acc = self.carve(2 * NOWN, F32).rearrange("p (a t) -> p a t", a=2)
        offw = self.off
        wst = self.carve(3 * 1024, F32)
        wb = [self.carve(3 * 1024, BF16) for _ in range(2)]
        idt = self.carve(128, BF16)
        rm = self.carve(128, BF16)
        maskT = self.carve(512, BF16)
        ones = self.carve(64, BF16)
        hvones = self.carve(64, BF16)
        loones = self.carve(64, BF16)
        hv2 = self.carve(2, F32)
        hv, lo = hv2[:, 0:1], hv2[:, 1:2]
        PT = [self.carve(512, BF16) for _ in range(6)]
        qraw = [self.carve(512, BF16) for _ in range(3)]
        rt1 = [self.carve(512, BF16) for _ in range(3)]
        ctab = [self.carve(512, BF16) for _ in range(3)]
        stab = [self.carve(512, BF16) for _ in range(3)]
        st = self.carve(64, F32)
        print("layer B persistent SBUF bytes", self.off - off0)
        offp = self.off

        S.dma("sp", "idt", lambda e: e.dma_start(out=idt, in_=ident_d), writes=[K("idt")])
        S.dma("sp", "rm", lambda e: e.dma_start(out=rm, in_=rm_d), writes=[K("rm")])
        S.dma("sp", "mask", lambda e: e.dma_start(out=maskT, in_=mask_d), writes=[K("mask")])
        S.dma("sp", "hv", lambda e: e.dma_start(out=hv2, in_=hv_d), writes=[K("hv")])
        S.op("pool", lambda e: e.memset(ones, 1.0), writes=[K("ones")])
        S.op("pool", lambda e: e.memset(hvones, 1.0), writes=[K("hvones")])
        S.op("pool", lambda e: e.tensor_scalar(out=hvones, in0=hvones, scalar1=hv, scalar2=None, op0=ALU.mult),
             reads=[K("hv"), K("hvones")], writes=[K("hvones")])
        S.op("pool", lambda e: e.memset(loones, 1.0), writes=[K("loones")])
        S.op("pool", lambda e: e.tensor_scalar(out=loones, in0=loones, scalar1=lo, scalar2=None, op0=ALU.mult),
             reads=[K("hv"), K("loones")], writes=[K("loones")])

        need1 = 8 * 4096 + 8 * 2048 + 4096 + 2048 + 256
        def place(need):
            if offp + need <= self.cap:
                self.off = offp
            else:
                assert offq + need <= offw, ("overlay does not fit", need, offw - offq)
                self.off = offq
        place(need1)
        gb = self.carve(1024, F32)
        xt = [self.carve(1024, F32) for _ in range(8)]
        hb = [self.carve(1024, BF16) for _ in range(8)]
        junk = self.carve(1024, BF16)
        st1 = self.carve(32, F32)
        S.dma("sp", "gb", lambda e: e.dma_start(out=gb, in_=gpre_d.partition_broadcast(128)), writes=[K("gb")])

        def xrows(i):
            return x_halo[i * 128:(i + 1) * 128, :] if i < 16 else x_src[(i - 16) * 128:(i - 15) * 128, :]

        def p1_load(i):
            p = i % 8
            S.dma("sp", "x1l%d" % p, lambda e: e.dma_start(out=xt[p], in_=xrows(i)), writes=[K("x1t%d" % p)])

        NT1 = NT // 128
        NG1 = NT1 // 4

        def p1_x(gi):
            q = gi % 2
            ss, ms, rstd, tmp1 = (st1[:, q * 16 + c * 4:q * 16 + c * 4 + 4] for c in range(4))
            for t in range(4):
                i = gi * 4 + t
                p = i % 8
                S.op("act", lambda e, p=p, t=t, ss=ss: e.activation(out=junk, in_=xt[p], func=AF.Square, accum_out=ss[:, t:t + 1]),
                     reads=[K("x1t%d" % p)], writes=[K("junk"), K("ss%d" % q)])
            S.op("dve", lambda e: e.tensor_scalar(out=ms, in0=ss, scalar1=1.0 / D, scalar2=RMS_EPS, op0=ALU.mult,
                                                  op1=ALU.add), reads=[K("ss%d" % q)], writes=[K("ms%d" % q)])
            self.rsqrt(K("ms%d" % q), ms, K("rstd%d" % q), rstd, K("tmp1%d" % q), tmp1)
            for t in range(4):
                i = gi * 4 + t
                p = i % 8
                S.op("dve", lambda e, p=p, t=t, rstd=rstd: e.scalar_tensor_tensor(out=hb[p], in0=xt[p], scalar=rstd[:, t:t + 1],
                                                                                  in1=gb, op0=ALU.mult, op1=ALU.mult),
                     reads=[K("x1t%d" % p), K("rstd%d" % q), K("gb")], writes=[K("hb%d" % p)])

        def p1_y(gi):
            for t in range(4):
                i = gi * 4 + t
                p = i % 8
                pb = P_PR[i % 2]
                for k in range(8):
                    S.op("pe", lambda e, k=k, p=p, pb=pb: e.transpose(out=psb[pb][:, k * 128:(k + 1) * 128],
                                                                in_=hb[p][:, k * 128:(k + 1) * 128], identity=idt),
                         reads=[K("hb%d" % p), K("idt")], writes=[K("ps%d" % pb)])
                S.op("act", lambda e, i=i, pb=pb: e.activation(out=hT[:, :, i * 128:(i + 1) * 128],
                                                               in_=psb[pb].rearrange("p (k t) -> p k t", k=8), func=AF.Copy),
                     reads=[K("ps%d" % pb)], writes=[K("hT")])

        for i in range(min(8, NT1)):
            p1_load(i)
        p1_x(0)
        for gi in range(NG1):
            if gi + 1 < NG1:
                p1_x(gi + 1)
            p1_y(gi)
            for t in range(4):
                i = (gi + 2) * 4 + t
                if i < NT1:
                    p1_load(i)
        S.barrier()

        KBG = 8 if NOWN >= 4096 else 4
        need15 = 2 * 8 * 1024 * 2 + 2 * KBG * 1024 * 2
        place(need15)
        Wv2 = [self.carve(8 * 1024, BF16).rearrange("p (k c) -> p k c", k=8) for _ in range(2)]
        vstg = [self.carve(KBG * 1024, BF16).rearrange("p (b c) -> p b c", b=KBG) for _ in range(2)]
        vcnt = [0]
        def load_wv(gidx):
            g = list(dbg_gs)[gidx]
            Wv = Wv2[gidx % 2]
            for k in range(8):
                slot = k % 3
                stg = wst[:, slot * 1024:(slot + 1) * 1024]
                S.dma("sp", "wvs%d" % slot, lambda e, k=k, g=g, stg=stg: e.dma_start(
                    out=stg, in_=w_in[k * 128:(k + 1) * 128, g * 3072 + 2048:g * 3072 + 3072]), writes=[K("wst%d" % slot)])
                if k % 2 == 0:
                    S.op("act", lambda e, k=k, stg=stg, Wv=Wv: e.activation(out=Wv[:, k, :], in_=stg, func=AF.Copy),
                         reads=[K("wst%d" % slot)], writes=[K("Wv%d" % (gidx % 2))])
                else:
                    S.op("dve", lambda e, k=k, stg=stg, Wv=Wv: e.tensor_copy(out=Wv[:, k, :], in_=stg), reads=[K("wst%d" % slot)],
                         writes=[K("Wv%d" % (gidx % 2))])

        if len(list(dbg_gs)) > 0:
            load_wv(0)
        for gidx, g in enumerate(dbg_gs):
            d = DIL[g]
            span = 128 * d
            nkb = (NOWN // span + 1) * d
            kt0 = HAL - span
            Wv = Wv2[gidx % 2]
            wvkey = K("Wv%d" % (gidx % 2))
            if gidx + 1 < len(list(dbg_gs)):
                load_wv(gidx + 1)
            for kb0 in range(0, nkb, KBG):
                nb = min(KBG, nkb - kb0)
                vs = vcnt[0] % 2
                vcnt[0] += 1
                for bl in range(nb):
                    kb = kb0 + bl
                    sp_, r = kb // d, kb % d
                    t_start = kt0 + sp_ * span + r
                    banks = P_PR if kb % 2 == 0 else (P_RT, P_V2[0])
                    for k in range(8):
                        for n in range(2):
                            S.op("pe", lambda e, k=k, n=n, t_start=t_start, d=d, bk=banks[n], Wv=Wv: e.matmul(
                                psf[bk], lhsT=hT[:, k, t_start:t_start + 127 * d + 1:d], rhs=Wv[:, k, n * 512:(n + 1) * 512],
                                start=(k == 0), stop=(k == 7)), reads=[wvkey, K("hT")], writes=[K("ps%d" % banks[n])])
                    c = 0 if kb < d else (1 if kb < d + 16 else 2)
                    fl = hv if c == 0 else lo
                    for n in range(2):
                        dst = vstg[vs][:, bl, n * 512:(n + 1) * 512]
                        bk = banks[n]
                        if n == 0:
                            if c == 2:
                                S.op("act", lambda e, dst=dst, bk=bk: e.activation(out=dst, in_=psf[bk], func=AF.Copy),
                                     reads=[K("ps%d" % bk)], writes=[K("vstg%d" % vs)])
                            else:
                                S.op("act", lambda e, dst=dst, bk=bk, fl=fl: e.activation(out=dst, in_=psf[bk], func=AF.Copy,
                                                                                        scale=fl),
                                     reads=[K("ps%d" % bk), K("hv")], writes=[K("vstg%d" % vs)])
                        else:
                            if c == 2:
                                S.op("dve", lambda e, dst=dst, bk=bk: e.tensor_copy(out=dst, in_=psf[bk]),
                                     reads=[K("ps%d" % bk)], writes=[K("vstg%d" % vs)])
                            else:
                                S.op("dve", lambda e, dst=dst, bk=bk, fl=fl: e.tensor_scalar(out=dst, in0=psf[bk], scalar1=fl,
                                                                                          scalar2=None, op0=ALU.mult),
                                     reads=[K("ps%d" % bk), K("hv")], writes=[K("vstg%d" % vs)])
                for h8 in range(8):
                    S.dma("sp", "vst%d" % vs, lambda e, g=g, kb0=kb0, nb=nb, vs=vs, h8=h8: e.dma_start(
                        out=vsc[g, h8, :, kb0:kb0 + nb, :], in_=vstg[vs][:, 0:nb, h8 * 128:(h8 + 1) * 128]),
                        reads=[K("vstg%d" % vs)], writes=[K("vsc")])
        S.barrier()

        w5 = w_in[:, 0:9216].rearrange("(k p) (g t h f) -> p k g t h f", p=128, g=3, t=3, h=8)
        wz = w_in[:, 9216:10240].rearrange("(k p) (h f) -> p k h f", p=128, h=8)
        tabi = [0]
        bankc = [0]
        wcnt = [0]

        pending_cast = []

        def flush_cast():
            while pending_cast:
                pending_cast.pop(0)()

        wjobs = []
        for hp_ in dbg_hps:
            for g_ in dbg_gs:
                wjobs.append((w5[:, :, g_, :, hp_, :], 384))
            wjobs.append((wz[:, :, hp_, :], 128))
        wready = {}

        def load_w(src_ap=None, ncols=None):
            i = wcnt[0]
            wcnt[0] += 1
            if i == 0:
                wready[0] = _load_w(0, *wjobs[0])
            if i + 1 < len(wjobs):
                wready[i + 1] = _load_w(i + 1, *wjobs[i + 1])
            return wready.pop(i)

        def _load_w(i, src_ap, ncols):
            slot = i % 2
            dst32 = wst[:, 0:8 * ncols]
            if len(src_ap.shape) == 3:
                S.dma("sp", "wst", lambda e: e.dma_start(out=dst32.rearrange("p (k c) -> p k c", k=8), in_=src_ap),
                      writes=[K("wst")])
            else:
                d4 = dst32.rearrange("p (k t f) -> p k t f", k=8, t=3)
                for t in range(3):
                    S.dma("sp", "wst", lambda e, t=t: e.dma_start(out=d4[:, :, t, :], in_=src_ap[:, :, t, :]),
                          writes=[K("wst")])
            def cast():
                S.op("act", lambda e: e.activation(out=wb[slot][:, 0:8 * ncols], in_=dst32, func=AF.Copy), reads=[K("wst")],
                     writes=[K("wb%d" % slot)])
            if i == 0:
                cast()
            else:
                pending_cast.append(cast)
            return wb[slot][:, 0:8 * ncols].rearrange("p (k c) -> p k c", k=8), K("wb%d" % slot)

        def proj_bank(wv, wkey, c0, tau0, n):
            b = P_PR[bankc[0] % 2]
            bankc[0] += 1
            for k in range(8):
                S.op("pe", lambda e, k=k, b=b: e.matmul(psf[b][:, 0:n], lhsT=wv[:, k, c0:c0 + 128], rhs=hT[:, k, tau0:tau0 + n],
                                                       start=(k == 0), stop=(k == 7)),
                     reads=[wkey, K("hT")], writes=[K("ps%d" % b)])
            flush_rot()
            return b

        pending_rot = []

        def flush_rot():
            while pending_rot:
                pending_rot.pop(0)()

        def rope_bank(b, n, d, dest, dkey, ctd, std, t0):
            j = tabi[0] % 3
            tabi[0] += 1
            S.dma("sp", "ct%d" % j, lambda e: e.dma_start(out=ctab[j][:, 0:n], in_=ctd[:, t0:t0 + n]), writes=[K("ctab%d" % j)])
            S.dma("sp", "st%d" % j, lambda e: e.dma_start(out=stab[j][:, 0:n], in_=std[:, t0:t0 + n]), writes=[K("stab%d" % j)])
            if d == 1 or n < 512:
                ov, iv = qraw[j][:, 0:n], psf[b][:, 0:n]
            else:
                ov = qraw[j].rearrange("p (r i) -> p r i", r=d)
                iv = psf[b].rearrange("p (i r) -> p r i", r=d)
            S.op("act", lambda e: e.activation(out=ov, in_=iv, func=AF.Copy), reads=[K("ps%d" % b)], writes=[K("qraw%d" % j)])

            def second():
                S.op("pe", lambda e: e.matmul(psf[P_RT][:, 0:n], lhsT=rm, rhs=qraw[j][:, 0:n], start=True, stop=True),
                     reads=[K("rm"), K("qraw%d" % j)], writes=[K("ps%d" % P_RT)])
                S.op("dve", lambda e: e.tensor_tensor(out=rt1[j][:, 0:n], in0=psf[P_RT][:, 0:n], in1=stab[j][:, 0:n], op=ALU.mult),
                     reads=[K("ps%d" % P_RT), K("stab%d" % j)], writes=[K("rt1%d" % j)])
                S.op("pool", lambda e: e.tensor_tensor(out=qraw[j][:, 0:n], in0=qraw[j][:, 0:n], in1=ctab[j][:, 0:n], op=ALU.mult),
                     reads=[K("qraw%d" % j), K("ctab%d" % j)], writes=[K("qraw%d" % j)])
                if d == 16:
                    a0 = rt1[j].rearrange("p (r i) -> p r i", r=16)
                    a1 = qraw[j].rearrange("p (r i) -> p r i", r=16)
                else:
                    a0, a1 = rt1[j][:, 0:n], qraw[j][:, 0:n]
                S.op("dve", lambda e: e.tensor_tensor(out=dest, in0=a0, in1=a1, op=ALU.add),
                     reads=[K("rt1%d" % j), K("qraw%d" % j)], writes=[dkey])
            pending_rot.append(second)

        ptc = [0]
        sc = [0]
        ndc = [0]
        pending_tail = []

        def flush_tail():
            while pending_tail:
                pending_tail.pop(0)()

        for hp in dbg_hps:
            if not pending_tail:
                S.op("pool", lambda e: e.memset(acc.rearrange("p a t -> p (a t)"), 0.0), writes=[K("acc")])
            for g in dbg_gs:
                d = DIL[g]
                span = 128 * d
                nsp = NOWN // span
                nkb = (nsp + 1) * d
                kt0 = HAL - span
                nk = NOWN + span
                cq, sq, ck, sk = tabs[g]
                wv, wkey = load_w(w5[:, :, g, :, hp, :], 384)
                for bq in range(NOWN // 512):
                    if bq == 3 and pending_tail:
                        flush_tail()
                        S.op("pool", lambda e: e.memset(acc.rearrange("p a t -> p (a t)"), 0.0), writes=[K("acc")])
                    b = proj_bank(wv, wkey, 0, HAL + bq * 512, 512)
                    if d == 16:
                        sp_, i0 = (bq * 512) // span, ((bq * 512) % span) // 16
                        dest = QT[:, sp_ * span:(sp_ + 1) * span].rearrange("p (r i) -> p r i", r=16)[:, :, i0:i0 + 32]
                        rope_bank(b, 512, d, dest, K("QT"), cq, sq, bq * 512)
                    else:
                        rope_bank(b, 512, d, QT[:, bq * 512:(bq + 1) * 512], K("QT"), cq, sq, bq * 512)
                nb_full, rem = nk // 512, nk % 512
                for bk in range(nb_full + (1 if rem else 0)):
                    n = 512 if bk < nb_full else rem
                    b = proj_bank(wv, wkey, 128, kt0 + bk * 512, n)
                    if d == 16:
                        sp_, i0 = (bk * 512) // span, ((bk * 512) % span) // 16
                        dest = KT[:, sp_ * span:(sp_ + 1) * span].rearrange("p (r i) -> p r i", r=16)[:, :, i0:i0 + 32]
                        rope_bank(b, 512, d, dest, K("KT"), ck, sk, bk * 512)
                    else:
                        rope_bank(b, n, d, KT[:, bk * 512:bk * 512 + n], K("KT"), ck, sk, bk * 512)
                flush_rot()
                S.dma("sp", "vbl", lambda e, g=g, hp=hp, nkb=nkb: e.dma_start(
                    out=Vb[:, 0:nkb * 128], in_=vsc[g, hp, :, 0:nkb, :].rearrange("p b f -> p (b f)")),
                    reads=[K("vsc")], writes=[K("Vb")])
                nqb = nsp * d

                def s_pair(qb0):
                    ba, bb = P_S2[sc[0] % 3]
                    sc[0] += 1
                    for blk in range(2):
                        qb = qb0 + blk
                        for kbi in range(2):
                            kb = qb + kbi * d
                            for hh, b in ((0, ba), (1, bb)):
                                S.op("pe", lambda e, hh=hh, kbi=kbi, kb=kb, b=b, blk=blk, qb=qb: e.matmul(
                                    psf[b][:, (blk * 2 + kbi) * 128:(blk * 2 + kbi + 1) * 128],
                                    lhsT=KT[hh * 64:(hh + 1) * 64, kb * 128:(kb + 1) * 128],
                                    rhs=QT[hh * 64:(hh + 1) * 64, qb * 128:(qb + 1) * 128], start=True, stop=True,
                                    tile_position=(hh * 64, 0)),
                                    reads=[K("KT"), K("QT")], writes=[K("ps%d" % b)])
                    return ba, bb

                def e_pair(banks):
                    js = []
                    for b in banks:
                        j = ptc[0] % 6
                        ptc[0] += 1
                        S.op("act", lambda e, b=b, j=j: e.activation(out=PT[j], in_=psf[b], func=AF.Exp, scale=0.125),
                             reads=[K("ps%d" % b)], writes=[K("PT%d" % j)])
                        if j % 2 == 0:
                            S.op("pool", lambda e, j=j: e.tensor_tensor(out=PT[j], in0=PT[j], in1=maskT, op=ALU.mult),
                                 reads=[K("PT%d" % j), K("mask")], writes=[K("PT%d" % j)])
                        else:
                            S.op("dve", lambda e, j=j: e.tensor_tensor(out=PT[j], in0=PT[j], in1=maskT, op=ALU.mult),
                                 reads=[K("PT%d" % j), K("mask")], writes=[K("PT%d" % j)])
                        js.append(j)
                    return js

                def pv_pair(qb0, js, bnd):
                    for blk in range(2):
                        qb = qb0 + blk
                        for what in range(2):
                            for kbi in range(2):
                                kb = qb + kbi * d
                                for hh in range(2):
                                    j = js[hh]
                                    if what == 0:
                                        lhsT, lk = Vb[:, kb * 128 + hh * 64:kb * 128 + (hh + 1) * 64], K("Vb")
                                    elif kb < d:
                                        lhsT, lk = hvones, K("hvones")
                                    elif kb < d + 16:
                                        lhsT, lk = loones, K("loones")
                                    else:
                                        lhsT, lk = ones, K("ones")
                                    S.op("pe", lambda e, hh=hh, what=what, kbi=kbi, lhsT=lhsT, blk=blk, j=j: e.matmul(
                                        psf[bnd][hh * 64:(hh + 1) * 64, (what * 2 + blk) * 128:(what * 2 + blk + 1) * 128],
                                        lhsT=lhsT, rhs=PT[j][:, (blk * 2 + kbi) * 128:(blk * 2 + kbi + 1) * 128],
                                        start=(kbi == 0), stop=(kbi == 1), tile_position=(0, hh * 64)),
                                        reads=[lk, K("PT%d" % j)], writes=[K("ps%d" % bnd)])

                def acc_pair(qb0, bnd):
                    sp_, r = qb0 // d, qb0 % d
                    if d == 1:
                        av = acc[:, :, qb0 * 128:(qb0 + 2) * 128]
                        pv = psf[bnd].rearrange("p (a t) -> p a t", a=2)
                    else:
                        av = acc[:, :, sp_ * span:(sp_ + 1) * span].rearrange("p a (i r) -> p a r i", r=d)[:, :, r:r + 2, :]
                        pv = psf[bnd].rearrange("p (a s i) -> p a s i", a=2, s=2)
                    S.op("dve", lambda e: e.tensor_tensor(out=av, in0=av, in1=pv, op=ALU.add),
                         reads=[K("acc"), K("ps%d" % bnd)], writes=[K("acc")])

                flush_cast()
                if not dbg_attn:
                    continue
                npair = nqb // 2
                sb = {0: s_pair(0)}
                if npair > 1:
                    sb[1] = s_pair(2)
                jss = {0: e_pair(sb.pop(0))}
                for pi in range(npair):
                    if pi + 2 < npair:
                        sb[pi + 2] = s_pair(2 * (pi + 2))
                    if pi + 1 < npair:
                        jss[pi + 1] = e_pair(sb.pop(pi + 1))
                    bnd = P_ND[ndc[0] % 2]
                    ndc[0] += 1
                    pv_pair(2 * pi, jss.pop(pi), bnd)
                    acc_pair(2 * pi, bnd)
            wvz, wzkey = load_w(wz[:, :, hp, :], 128)
            zs = KT[:, 0:NOWN]
            for bq in range(NOWN // 512):
                b = proj_bank(wvz, wzkey, 0, HAL + bq * 512, 512)
                S.op("act", lambda e, b=b, bq=bq: e.activation(out=zs[:, bq * 512:(bq + 1) * 512], in_=psf[b], func=AF.Silu),
                     reads=[K("ps%d" % b)], writes=[K("KT")])
            def tail(hp=hp):
                S.op("dve", lambda e: e.tensor_scalar(out=acc[:, 1, :], in0=acc[:, 1, :], scalar1=1e-18, scalar2=None, op0=ALU.max),
                     reads=[K("acc")], writes=[K("acc")])
                S.op("act", lambda e: e.activation(out=acc[:, 1, :], in_=acc[:, 1, :], func=AF.Ln), reads=[K("acc")], writes=[K("acc")])
                S.op("act", lambda e: e.activation(out=acc[:, 1, :], in_=acc[:, 1, :], func=AF.Exp, scale=-1.0),
                     reads=[K("acc")], writes=[K("acc")])
                S.op("dve", lambda e: e.tensor_tensor(out=acc[:, 0, :], in0=acc[:, 0, :], in1=acc[:, 1, :], op=ALU.mult),
                     reads=[K("acc")], writes=[K("acc")])
                S.op("dve", lambda e: e.tensor_tensor(out=zs, in0=acc[:, 0, :], in1=zs, op=ALU.mult),
                     reads=[K("acc"), K("KT")], writes=[K("KT")])
                S.dma("pool", "ysc", lambda e: e.dma_start(out=ysc[hp], in_=zs), reads=[K("KT")], writes=[K("ysc")])
            pending_tail.append(tail)
            flush_tail()
            flush_cast()
        flush_tail()
        S.barrier()

        self.off = off0
        Wo = self.carve(8 * 1024, BF16).rearrange("p (k c) -> p k c", k=8)
        wos = self.carve(1024, F32)
        gpo = self.carve(1024, F32)
        yt = [self.carve(8 * 512, BF16).rearrange("p (h t) -> p h t", h=8) for _ in range(2)]
        x3 = [self.carve(1024, F32) for _ in range(8)]
        osb = [self.carve(1024, F32) for _ in range(8)]
        tpost = [self.carve(1024, F32) for _ in range(2)]
        junk3 = self.carve(512, BF16)
        st3 = self.carve(64, F32)
        P_O = ((0, 1), (2, 3))
        S.dma("sp", "gpo", lambda e: e.dma_start(out=gpo, in_=npost.partition_broadcast(128)), writes=[K("gpo")])
        for k in range(8):
            S.dma("sp", "wos", lambda e, k=k: e.dma_start(out=wos, in_=w_out[k * 128:(k + 1) * 128, :]), writes=[K("wos")])
            S.op("act", lambda e, k=k: e.activation(out=Wo[:, k, :], in_=wos, func=AF.Copy), reads=[K("wos")], writes=[K("Wo")])
        yv = ysc.rearrange("h p t -> p h t")
        NG3 = NOWN // 512

        def p3_yload(gi):
            q = gi % 2
            S.dma("sp", "ytl%d" % q, lambda e: e.dma_start(out=yt[q], in_=yv[:, :, gi * 512:(gi + 1) * 512]),
                  reads=[K("ysc")], writes=[K("yt%d" % q)])

        def p3_load(gi):
            for t in range(4):
                i = gi * 4 + t
                p = i % 8
                S.dma("sp", "x3l%d" % p, lambda e, i=i, p=p: e.dma_start(out=x3[p], in_=x_src[i * 128:(i + 1) * 128, :]),
                      writes=[K("x3%d" % p)])

        def p3_mm(gi):
            q = gi % 2
            ss2 = st3[:, q * 32:q * 32 + 8]
            for t in range(4):
                i = gi * 4 + t
                p = i % 8
                c0 = t * 128
                pb = P_O[i % 2]
                for n in range(2):
                    bk = pb[n]
                    for k in range(8):
                        S.op("pe", lambda e, k=k, n=n, bk=bk, c0=c0: e.matmul(psf[bk], lhsT=yt[q][:, k, c0:c0 + 128],
                                                                            rhs=Wo[:, k, n * 512:(n + 1) * 512],
                                                                            start=(k == 0), stop=(k == 7)),
                             reads=[K("yt%d" % q), K("Wo")], writes=[K("ps%d" % bk)])
                    S.op("act", lambda e, n=n, bk=bk, t=t, ss2=ss2: e.activation(out=junk3, in_=psf[bk], func=AF.Square,
                                                                               accum_out=ss2[:, 2 * t + n:2 * t + n + 1]),
                         reads=[K("ps%d" % bk)], writes=[K("junk3"), K("ss2%d" % q)])
                    S.op("act", lambda e, n=n, bk=bk, p=p: e.activation(out=osb[p][:, n * 512:(n + 1) * 512], in_=psf[bk],
                                                                       func=AF.Copy),
                         reads=[K("ps%d" % bk)], writes=[K("osb%d" % p)])

        def p3_post(gi):
            q = gi % 2
            ss2 = st3[:, q * 32:q * 32 + 8]
            a2, rstd2, tmp3 = (st3[:, q * 32 + 8 + c * 4:q * 32 + 12 + c * 4] for c in range(3))
            ssv = ss2.rearrange("p (t n) -> p t n", n=2)
            S.op("dve", lambda e: e.tensor_tensor(out=a2, in0=ssv[:, :, 0], in1=ssv[:, :, 1], op=ALU.add),
                 reads=[K("ss2%d" % q)], writes=[K("a2%d" % q)])
            S.op("dve", lambda e: e.tensor_scalar(out=a2, in0=a2, scalar1=1.0 / D, scalar2=RMS_EPS, op0=ALU.mult,
                                                  op1=ALU.add), reads=[K("a2%d" % q)], writes=[K("a2%d" % q)])
            self.rsqrt(K("a2%d" % q), a2, K("rstd2%d" % q), rstd2, K("tmp3%d" % q), tmp3)
            for t in range(4):
                i = gi * 4 + t
                p = i % 8
                tp = tpost[i % 2]
                S.op("dve", lambda e, p=p, t=t, tp=tp, rstd2=rstd2: e.scalar_tensor_tensor(out=tp, in0=osb[p], scalar=rstd2[:, t:t + 1],
                                                                                      in1=gpo, op0=ALU.mult, op1=ALU.mult),
                     reads=[K("osb%d" % p), K("rstd2%d" % q), K("gpo")], writes=[K("tpost%d" % (i % 2))])
                S.op("pool", lambda e, p=p, tp=tp: e.tensor_tensor(out=x3[p], in0=x3[p], in1=tp, op=ALU.add),
                     reads=[K("x3%d" % p), K("tpost%d" % (i % 2))], writes=[K("x3%d" % p)])
                o = S.dma("pool", "x3s%d" % p, lambda e, i=i, p=p: e.dma_start(out=x_dst[i * 128:(i + 1) * 128, :], in_=x3[p]),
                          reads=[K("x3%d" % p)], writes=[K("xdst")])
                self.final.append(o)

        p3_yload(0)
        p3_load(0)
        if NG3 > 1:
            p3_yload(1)
            p3_load(1)
        p3_mm(0)
        for gi in range(NG3):
            if gi + 1 < NG3:
                p3_mm(gi + 1)
            if gi + 2 < NG3:
                p3_yload(gi + 2)
            p3_post(gi)
            if gi + 2 < NG3:
                p3_load(gi + 2)
        S.barrier()
        self.peak = max(getattr(self, "peak", 0), self.off)
        self.off = off0

    def finish(self):
        S, nc = self.S, self.nc
        S.finalize()
        with contextlib.ExitStack() as es:
            sems = {k: es.enter_context(nc.semaphore("s%d" % i)) for i, k in enumerate(S.sem_keys)}
            block = es.enter_context(nc.Block())
            S.emit(block, sems, final_waits=self.final)
        return nc


_CACHE = {}


def consts():
    ident = np.eye(128, dtype=np.float32).astype(ml_dtypes.bfloat16)
    tril = np.tril(np.ones((128, 128), dtype=np.float32))
    return ident, tril


def prog_a(ntok):
    key = ("A", ntok)
    if key not in _CACHE:
        P = Prog()
        x = P.inp("x", [ntok, D])
        w_in = P.inp("w_in", [D, 3 * EW])
        npre = P.inp("npre", [128, 8])
        ln_g = P.inp("ln_g", [1, EW])
        ln_b = P.inp("ln_b", [1, EW])
        w_s = P.inp("w_s", [8, 128, 128])
        bs = P.inp("bs", [128, 8])
        w_out = P.inp("w_out", [EW, D])
        npost = P.inp("npost", [1, D])
        ident = P.inp("ident", [128, 128], BF16)
        tril = P.inp("tril", [128, 128])
        y = P.outp("y", [ntok, D])
        P.layer_a(x, y, ntok, w_in, npre, ln_g, ln_b, w_s, bs, w_out, npost, ident, tril)
        _CACHE[key] = P.finish()
    return _CACHE[key]


def col128(vec):
    return np.ascontiguousarray(np.asarray(vec, dtype=np.float32).reshape(-1, 128).T)


def run_layer_a(xs, i, inputs):
    j = i // 2
    ident, tril = consts()
    ntok = xs[0].shape[0]
    nc = prog_a(ntok)
    common = {
        "w_in": np.ascontiguousarray(inputs["a_w_in"][j]),
        "npre": col128(inputs["norm_pre"][i]),
        "ln_g": np.ascontiguousarray(inputs["a_ln_g"][j][None, :]),
        "ln_b": np.ascontiguousarray(inputs["a_ln_b"][j][None, :]),
        "w_s": np.ascontiguousarray(inputs["a_w_s"][j]),
        "bs": np.ascontiguousarray(np.asarray(inputs["a_b_s"][j]).T),
        "w_out": np.ascontiguousarray(inputs["a_w_out"][j]),
        "npost": np.ascontiguousarray(inputs["norm_post"][i][None, :]),
        "ident": ident, "tril": tril,
    }
    in_maps = [dict(common, x=np.ascontiguousarray(x)) for x in xs]
    res = run_bass_kernel_spmd(nc, in_maps, core_ids=list(range(len(xs))))
    return [r["y"] for r in res.results]


DILS = (1, 4, 16)
HALO = 2048


def rope_consts():
    rm = np.zeros((128, 128), np.float32)
    for fp in range(128):
        if fp % 64 < 32:
            rm[fp + 32, fp] = -1.0
        else:
            rm[fp - 32, fp] = 1.0
    mask = np.zeros((128, 2, 2, 128), np.float32)
    j = np.arange(128)[:, None]
    i = np.arange(128)[None, :]
    for hh in range(2):
        mask[:, hh, 0, :] = (j >= i)
        mask[:, hh, 1, :] = (j <= i)
    return rm.astype(ml_dtypes.bfloat16), mask.reshape(128, 512).astype(ml_dtypes.bfloat16)


def rope_tables_core(pos0, nown=4096):
    inv_freq = (1.0 / (np.float32(10000.0) ** (np.arange(0, 64, 2, dtype=np.float32) / np.float32(64)))).astype(np.float32)
    out = []
    for d in DILS:
        span = 128 * d
        res = []
        for (tau0, ncols) in ((HALO, nown), (HALO - span, nown + span)):
            tau = np.zeros(ncols, np.int64)
            for b0 in range(0, ncols, 512):
                n = min(512, ncols - b0)
                col = np.arange(n)
                r, ii = col // (n // d), col % (n // d)
                tau[b0:b0 + n] = tau0 + b0 + ii * d + r
            pos = (pos0 - HALO + tau).astype(np.float32)
            ang = pos[None, :] * inv_freq[:, None]
            c = np.tile(np.cos(ang), (4, 1)).astype(ml_dtypes.bfloat16)
            s_ = np.tile(np.sin(ang), (4, 1)).astype(ml_dtypes.bfloat16)
            res += [np.ascontiguousarray(c), np.ascontiguousarray(s_)]
        out.append(res)
    return out


DBG = {}


def prog_b():
    key = ("B",)
    if key not in _CACHE:
        P = Prog()
        x = P.inp("x", [TOK, D])
        xh = P.inp("xh", [HALO, D])
        w_in = P.inp("w_in", [D, 10240])
        gpre = P.inp("gpre", [1, D])
        w_out = P.inp("w_out", [D, D])
        npost = P.inp("npost", [1, D])
        hv = P.inp("hv", [128, 2])
        ident = P.inp("ident", [128, 128], BF16)
        rm = P.inp("rm", [128, 128], BF16)
        mask = P.inp("mask", [128, 512], BF16)
        tabs = []
        for g, d in enumerate(DILS):
            nk = 4096 + 128 * d
            tabs.append((P.inp("cq%d" % g, [128, 4096], BF16), P.inp("sq%d" % g, [128, 4096], BF16),
                         P.inp("ck%d" % g, [128, nk], BF16), P.inp("sk%d" % g, [128, nk], BF16)))
        ysc = P.nc.dram_tensor("ysc", [8, 128, 4096], BF16).ap()
        vsc = P.nc.dram_tensor("vsc", [3, 8, 128, 48, 128], BF16).ap()
        y = P.outp("y", [TOK, D])
        P.layer_b(x, xh, y, ysc, vsc, w_in, gpre, w_out, npost, hv, tabs, ident, rm, mask, **DBG)
        _CACHE[key] = P.finish()
    return _CACHE[key]


def run_layer_b(xs, i, inputs, core_ids=None):
    j = i // 2
    ident, _ = consts()
    rm, mask = rope_consts()
    nc = prog_b()
    common = {
        "w_in": np.ascontiguousarray(inputs["b_w_in"][j]),
        "gpre": np.ascontiguousarray(inputs["norm_pre"][i][None, :]),
        "w_out": np.ascontiguousarray(inputs["b_w_out"][j]),
        "npost": np.ascontiguousarray(inputs["norm_post"][i][None, :]),
        "ident": ident, "rm": rm, "mask": mask,
    }
    in_maps = []
    cores = list(range(len(xs))) if core_ids is None else core_ids
    for c in cores:
        m = dict(common)
        m["x"] = np.ascontiguousarray(xs[c])
        first = (c % 4 == 0)
        m["xh"] = np.zeros((HALO, D), np.float32) if first else np.ascontiguousarray(xs[c - 1][TOK - HALO:])
        m["hv"] = np.tile(np.array([[0.0 if first else 1.0, 1.0]], np.float32), (128, 1))
        tb = rope_tables_core((c % 4) * TOK)
        for g in range(3):
            m["cq%d" % g], m["sq%d" % g], m["ck%d" % g], m["sk%d" % g] = tb[g]
        in_maps.append(m)
    res = run_bass_kernel_spmd(nc, in_maps, core_ids=list(range(len(cores))))
    return [r["y"] for r in res.results]


def kernel_unfused(x, norm_pre, norm_post, a_w_in, a_ln_g, a_ln_b, a_w_s, a_b_s, a_w_out, b_w_in, b_w_out):
    inputs = dict(norm_pre=np.asarray(norm_pre), norm_post=np.asarray(norm_post), a_w_in=np.asarray(a_w_in),
                  a_ln_g=np.asarray(a_ln_g), a_ln_b=np.asarray(a_ln_b), a_w_s=np.asarray(a_w_s), a_b_s=np.asarray(a_b_s),
                  a_w_out=np.asarray(a_w_out), b_w_in=np.asarray(b_w_in), b_w_out=np.asarray(b_w_out))
    x = np.asarray(x, dtype=np.float32)
    B, S_, D_ = x.shape
    xs = [np.ascontiguousarray(c) for c in x.reshape(NCORES, TOK, D_)]
    for i in range(4):
        if i % 2 == 0:
            xs = run_layer_a(xs, i, inputs)
        else:
            xs = run_layer_b(xs, i, inputs)
    return np.stack(xs).reshape(B, S_, D_).astype(np.float32)


XR = 8192
B_CALLS = ((2048, 2048), (4096, -2048), (4096, 0))


def prog_fused():
    key = ("F",)
    if key in _CACHE:
        return _CACHE[key]
    P = Prog()
    x = P.inp("x", [XR, D])
    ident = P.inp("ident", [128, 128], BF16)
    tril = P.inp("tril", [128, 128])
    rm = P.inp("rm", [128, 128], BF16)
    mask = P.inp("mask", [128, 512], BF16)
    npost = [P.inp("npost%d" % i, [1, D]) for i in range(4)]
    A = []
    for j in range(2):
        A.append(dict(w_in=P.inp("a_w_in%d" % j, [D, 3 * EW]), npre=P.inp("a_npre%d" % j, [128, 8]),
                      ln_g=P.inp("a_ln_g%d" % j, [1, EW]), ln_b=P.inp("a_ln_b%d" % j, [1, EW]),
                      w_s=P.inp("a_w_s%d" % j, [8, 128, 128]), bs=P.inp("a_bs%d" % j, [128, 8]),
                      w_out=P.inp("a_w_out%d" % j, [EW, D])))
    Bw = []
    for j in range(2):
        Bw.append(dict(w_in=P.inp("b_w_in%d" % j, [D, 10240]), gpre=P.inp("b_gpre%d" % j, [1, D]),
                       w_out=P.inp("b_w_out%d" % j, [D, D])))
    fl = [P.inp("fl%d" % c, [128, 2]) for c in range(3)]
    tabs = []
    for c, (nown, _) in enumerate(B_CALLS):
        tc = []
        for g, d in enumerate(DILS):
            nk = nown + 128 * d
            tc.append((P.inp("cq%d_%d" % (c, g), [128, nown], BF16), P.inp("sq%d_%d" % (c, g), [128, nown], BF16),
                       P.inp("ck%d_%d" % (c, g), [128, nk], BF16), P.inp("sk%d_%d" % (c, g), [128, nk], BF16)))
        tabs.append(tc)
    y = P.outp("y", [TOK, D])
    xres = P.nc.dram_tensor("xres", [XR, D], F32).ap()
    ysc = P.nc.dram_tensor("ysc", [8, 128, 4096], BF16).ap()
    vsc = P.nc.dram_tensor("vsc", [3, 8, 128, 48, 128], BF16).ap()

    def la(j, i, src, dst, ntok):
        a = A[j]
        P.layer_a(src, dst, ntok, a["w_in"], a["npre"], a["ln_g"], a["ln_b"], a["w_s"], a["bs"], a["w_out"], npost[i],
                  ident, tril)

    def lb(j, i, c, src, halo, dst):
        nown = B_CALLS[c][0]
        b = Bw[j]
        P.layer_b(src, halo, dst, ysc[:, :, 0:nown], vsc, b["w_in"], b["gpre"], b["w_out"], npost[i], fl[c], tabs[c],
                  ident, rm, mask, NOWN=nown)

    la(0, 0, x, xres, XR)
    lb(0, 1, 0, xres[6144:8192], xres[4096:6144], xres[6144:8192])
    lb(0, 1, 1, xres[2048:6144], xres[0:2048], xres[2048:6144])
    la(1, 2, xres[2048:8192], xres[2048:8192], 6144)
    P.final = []
    lb(1, 3, 2, xres[4096:8192], xres[2048:4096], y)
    _CACHE[key] = P.finish()
    return _CACHE[key]


def kernel(x, norm_pre, norm_post, a_w_in, a_ln_g, a_ln_b, a_w_s, a_b_s, a_w_out, b_w_in, b_w_out):
    f32 = lambda a: np.ascontiguousarray(np.asarray(a, dtype=np.float32))
    x = f32(x)
    norm_pre, norm_post = f32(norm_pre), f32(norm_post)
    Bn, S_, D_ = x.shape
    per_seq = S_ // TOK
    ident, tril = consts()
    rm, mask = rope_consts()
    common = {"ident": ident, "tril": tril, "rm": rm, "mask": mask}
    for i in range(4):
        common["npost%d" % i] = f32(norm_post[i][None, :])
    for j in range(2):
        common["a_w_in%d" % j] = f32(a_w_in[j])
        common["a_npre%d" % j] = col128(norm_pre[2 * j])
        common["a_ln_g%d" % j] = f32(np.asarray(a_ln_g[j])[None, :])
        common["a_ln_b%d" % j] = f32(np.asarray(a_ln_b[j])[None, :])
        common["a_w_s%d" % j] = f32(a_w_s[j])
        common["a_bs%d" % j] = f32(np.asarray(a_b_s[j]).T)
        common["a_w_out%d" % j] = f32(a_w_out[j])
        common["b_w_in%d" % j] = f32(b_w_in[j])
        common["b_gpre%d" % j] = f32(norm_pre[2 * j + 1][None, :])
        common["b_w_out%d" % j] = f32(b_w_out[j])
    tab_cache = {}
    in_maps = []
    for c in range(NCORES):
        b, k = c // per_seq, c % per_seq
        m = dict(common)
        xe = np.zeros((XR, D_), np.float32)
        lo = k * TOK - TOK
        src_lo = max(lo, 0)
        xe[src_lo - lo:] = x[b, src_lo:(k + 1) * TOK]
        m["x"] = xe
        first = (k == 0)
        flags = ((1.0, 1.0), (0.0, 0.0) if first else (1.0, 1.0), (0.0, 1.0) if first else (1.0, 1.0))
        for ci in range(3):
            m["fl%d" % ci] = np.tile(np.array([flags[ci]], np.float32), (128, 1))
            nown, off = B_CALLS[ci]
            tk = (k, ci)
            if tk not in tab_cache:
                tab_cache[tk] = rope_tables_core(k * TOK + off, nown)
            tb = tab_cache[tk]
            for g in range(3):
                m["cq%d_%d" % (ci, g)], m["sq%d_%d" % (ci, g)], m["ck%d_%d" % (ci, g)], m["sk%d_%d" % (ci, g)] = tb[g]
        in_maps.append(m)
    nc = prog_fused()
    res = run_bass_kernel_spmd(nc, in_maps, core_ids=list(range(NCORES)))
    out = np.stack([r["y"] for r in res.results]).reshape(Bn, S_, D_)
    return out.astype(np.float32)
```

```python
import contextlib
import numpy as np
import ml_dtypes
import concourse.bass as bass
import concourse.mybir as mybir
from concourse.bass_utils import run_bass_kernel_spmd

F32 = mybir.dt.float32
BF16 = mybir.dt.bfloat16
I32 = mybir.dt.int32
AF = mybir.ActivationFunctionType
ALU = mybir.AluOpType

D = 1024
NCORES = 8
TOK = 4096
EW = 2048
RMS_EPS = 1e-6
LN_EPS = 1e-5

ENGS = ("pe", "act", "dve", "pool", "sp")


class _Op:
    __slots__ = ("eng", "fn", "sem", "val", "waits", "dma", "needs_inc")

    def __init__(self, eng, fn, dma):
        self.eng = eng
        self.fn = fn
        self.dma = dma
        self.sem = None
        self.val = 0
        self.waits = []
        self.needs_inc = False


class Sched:
    def __init__(self):
        self.ops = {e: [] for e in ENGS}
        self.last_w = {}
        self.readers = {}
        self.all_ops = []
        self.epoch = 0

    def _deps(self, op, reads, writes):
        deps = []
        raw = set()
        for b in reads:
            w = self.last_w.get(b)
            if w is not None:
                deps.append(w)
                raw.add(id(w))
        for b in writes:
            w = self.last_w.get(b)
            if w is not None:
                deps.append(w)
            deps.extend(self.readers.get(b, {}).values())
        for b in reads:
            self.readers.setdefault(b, {})[op.sem] = op
        for b in writes:
            self.last_w[b] = op
            self.readers[b] = {}
        return deps, raw

    def barrier(self):
        last = {}
        for o in self.all_ops:
            last[o.sem] = o
        self.pending = {e: list(last.values()) for e in ENGS}

    def _pend(self, o, eng):
        pend = getattr(self, "pending", None)
        if pend and pend.get(eng):
            for d in pend[eng]:
                if d.dma is None and d.eng == eng:
                    continue
                d.needs_inc = True
                o.waits.append(d)
            pend[eng] = []

    def op(self, eng, fn, reads=(), writes=()):
        o = _Op(eng, fn, None)
        o.sem = ("eng", eng, self.epoch)
        self._pend(o, eng)
        deps, raw = self._deps(o, reads, writes)
        for d in deps:
            if d is o:
                continue
            if d.dma is None and d.eng == eng:
                if eng == "pe" or id(d) not in raw:
                    continue
            d.needs_inc = True
            o.waits.append(d)
        self.ops[eng].append(o)
        self.all_ops.append(o)
        return o

    def dma(self, queue, stream, fn, reads=(), writes=()):
        o = _Op(queue, fn, stream)
        o.sem = ("dma", stream)
        o.needs_inc = True
        self._pend(o, queue)
        deps, _ = self._deps(o, reads, writes)
        for d in deps:
            if d is o:
                continue
            d.needs_inc = True
            o.waits.append(d)
        self.ops[queue].append(o)
        self.all_ops.append(o)
        return o

    def new_epoch(self):
        self.epoch += 1

    def finalize(self):
        counts = {}
        for o in self.all_ops:
            if o.needs_inc:
                step = 16 if o.dma is not None else 1
                counts[o.sem] = counts.get(o.sem, 0) + step
                o.val = counts[o.sem]
        self.sem_keys = list(counts.keys())
        for k, v in counts.items():
            assert v < 60000, (k, v)
        return counts

    def emit(self, block, sems, final_waits=()):
        handles = {"pe": "tensor", "act": "scalar", "dve": "vector", "pool": "gpsimd", "sp": "sync"}

        def make(engname):
            ops = self.ops[engname]

            def body(eng):
                waited = {}

                def wait(d):
                    if waited.get(d.sem, 0) >= d.val:
                        return
                    eng.wait_ge(sems[d.sem], d.val)
                    waited[d.sem] = d.val

                for o in ops:
                    for d in o.waits:
                        wait(d)
                    ins = o.fn(eng)
                    if o.needs_inc:
                        ins.then_inc(sems[o.sem], 16 if o.dma is not None else 1)
                if engname == "sp":
                    for d in final_waits:
                        wait(d)
            return body

        for engname in ENGS:
            getattr(block, handles[engname])(make(engname))


class Prog:
    def __init__(self):
        self.nc = bass.Bass("TRN2", target_bir_lowering=False)
        self.S = Sched()
        self.cap = 212000
        self.arena = self.nc.alloc_sbuf_tensor("arena", [128, self.cap // 2], BF16)[:]
        self.off = 0
        self.uid = 0
        self.ps_f32 = [self.nc.alloc_psum_tensor("psb%d" % i, [128, 512], F32)[:] for i in range(8)]
        self.ps_b16 = [p.bitcast(BF16) for p in self.ps_f32]
        self.final = []

    def inp(self, name, shape, dtype=F32):
        return self.nc.dram_tensor(name, list(shape), dtype, kind="ExternalInput").ap()

    def outp(self, name, shape, dtype=F32):
        return self.nc.dram_tensor(name, list(shape), dtype, kind="ExternalOutput").ap()

    def carve(self, cols, dtype):
        esz = 4 if dtype in (F32, I32) else 2
        nb = (cols * esz + 63) // 64 * 64
        assert self.off + nb <= self.cap, ("SBUF arena overflow", self.off, nb)
        a = self.arena[:, self.off // 2:(self.off + nb) // 2]
        self.off += nb
        if dtype != BF16:
            a = a.bitcast(dtype)
        return a[:, 0:cols]

    def key(self, base):
        self.uid += 1
        return "%s#%d" % (base, self.uid)

    def rsqrt(self, a_key, a, out_key, out, tmp_key, tmp):
        S = self.S
        S.op("dve", lambda e: e.tensor_scalar(out=out.bitcast(I32), in0=a.bitcast(I32), scalar1=1, scalar2=None,
                                              op0=ALU.arith_shift_right), reads=[a_key], writes=[out_key])
        S.op("dve", lambda e: e.tensor_scalar(out=out.bitcast(I32), in0=out.bitcast(I32), scalar1=0x5f3759df,
                                              scalar2=-1, op0=ALU.subtract, op1=ALU.mult),
             reads=[out_key], writes=[out_key])
        for _ in range(2):
            S.op("dve", lambda e: e.scalar_tensor_tensor(out=tmp, in0=out, scalar=-0.5, in1=out, op0=ALU.mult,
                                                         op1=ALU.mult), reads=[out_key], writes=[tmp_key])
            S.op("dve", lambda e: e.tensor_tensor(out=tmp, in0=tmp, in1=a, op=ALU.mult),
                 reads=[tmp_key, a_key], writes=[tmp_key])
            S.op("dve", lambda e: e.scalar_tensor_tensor(out=out, in0=tmp, scalar=1.5, in1=out, op0=ALU.add,
                                                         op1=ALU.mult), reads=[tmp_key, out_key], writes=[out_key])

    def layer_a(self, x_src, x_dst, ntok, w_in, npre_col, ln_g, ln_b, w_s, bs_col, w_out, npost, ident_d, tril_d):
        S, nc = self.S, self.nc
        off0 = self.off
        NT = ntok // 128
        L = self.key("A")
        K = lambda s: "%s/%s" % (L, s)

        Wb = [self.carve(6144, BF16) for _ in range(8)]
        Wo = [self.carve(1024, BF16) for _ in range(16)]
        lng = self.carve(2048, BF16)
        lnb = self.carve(2048, BF16)
        gpo = self.carve(1024, F32)
        WmT = self.carve(1024, BF16)
        bsc = self.carve(8, F32)
        gpre = self.carve(8, F32)
        idt = self.carve(128, BF16)
        tril = self.carve(128, F32)
        xt = [self.carve(1024, F32) for _ in range(3)]
        xb = self.carve(1024, BF16)
        xT = [self.carve(1024, BF16) for _ in range(2)]
        junk = self.carve(1024, BF16)
        u = self.carve(2048, BF16)
        sz = self.carve(2048, BF16)
        v = self.carve(2048, F32)
        vn = [self.carve(2048, BF16) for _ in range(2)]
        y = self.carve(2048, BF16)
        yT = self.carve(2048, BF16)
        tpost = self.carve(1024, F32)
        st = self.carve(64, F32)
        ss, ms, tmp1 = st[:, 0:1], st[:, 1:2], st[:, 3:4]
        rstd = [st[:, 4:5], st[:, 5:6]]
        bst = st[:, 8:32]
        mv = st[:, 32:34]
        va, rsv, tmp2 = st[:, 34:35], st[:, 35:36], st[:, 36:37]
        ss2, a2, rstd2, tmp3 = st[:, 40:42], st[:, 42:43], st[:, 43:44], st[:, 44:45]
        stage = [v[:, 0:1024], v[:, 1024:2048]]
        VK = [K("v0"), K("v1")]
        P_TX, P_IN, P_SV, P_TY, P_O = 0, (1, 2), (3, 4, 5), 5, (6, 7)
        P_TY2 = (5, 0)
        psf, psb = self.ps_f32, self.ps_b16

        S.dma("sp", "idt", lambda e: e.dma_start(out=idt, in_=ident_d), writes=[K("idt")])
        S.dma("sp", "tril", lambda e: e.dma_start(out=tril, in_=tril_d), writes=[K("tril")])
        S.dma("sp", "gpre", lambda e: e.dma_start(out=gpre, in_=npre_col), writes=[K("gpre")])
        S.dma("sp", "bsc", lambda e: e.dma_start(out=bsc, in_=bs_col), writes=[K("bsc")])
        S.dma("sp", "gpo", lambda e: e.dma_start(out=gpo, in_=npost.partition_broadcast(128)), writes=[K("gpo")])
        cnt = [0]

        def staged(src_ap, cols, consume, view=None):
            i = cnt[0] % 2
            cnt[0] += 1
            sb = stage[i][:, 0:cols]
            sbd = view(sb) if view is not None else sb
            S.dma("sp", "stg%d" % i, lambda e: e.dma_start(out=sbd, in_=src_ap), writes=[VK[i]])
            consume(sb, "act" if i == 0 else "dve", VK[i])

        for k in range(8):
            for q in range(6):
                dst = Wb[k][:, q * 1024:(q + 1) * 1024]

                def cons(sb, eng, skey, dst=dst, k=k):
                    if eng == "act":
                        S.op("act", lambda e: e.activation(out=dst, in_=sb, func=AF.Copy, scale=gpre[:, k:k + 1]),
                             reads=[skey, K("gpre")], writes=[K("Wb%d" % k)])
                    else:
                        S.op("dve", lambda e: e.tensor_scalar(out=dst, in0=sb, scalar1=gpre[:, k:k + 1], scalar2=None,
                                                              op0=ALU.mult), reads=[skey, K("gpre")], writes=[K("Wb%d" % k)])
                staged(w_in[k * 128:(k + 1) * 128, q * 1024:(q + 1) * 1024], 1024, cons)
        for k in range(16):
            def cons(sb, eng, skey, k=k):
                if eng == "act":
                    S.op("act", lambda e: e.activation(out=Wo[k], in_=sb, func=AF.Copy), reads=[skey], writes=[K("Wo")])
                else:
                    S.op("dve", lambda e: e.tensor_copy(out=Wo[k], in_=sb), reads=[skey], writes=[K("Wo")])
            staged(w_out[k * 128:(k + 1) * 128, :], 1024, cons)
        for (src, dstt, nm) in ((ln_g, lng, "lng"), (ln_b, lnb, "lnb")):
            for h in range(2):
                def cons(sb, eng, skey, dstt=dstt, h=h, nm=nm):
                    if eng == "act":
                        S.op("act", lambda e: e.activation(out=dstt[:, h * 1024:(h + 1) * 1024], in_=sb, func=AF.Copy),
                             reads=[skey], writes=[K(nm)])
                    else:
                        S.op("dve", lambda e: e.tensor_copy(out=dstt[:, h * 1024:(h + 1) * 1024], in_=sb),
                             reads=[skey], writes=[K(nm)])
                staged(src[:, h * 1024:(h + 1) * 1024].partition_broadcast(128), 1024, cons)

        def cons_ws(sb, eng, skey):
            sbv = sb.rearrange("p (g s) -> p g s", g=8)
            yv = y[:, 0:1024].rearrange("p (g s) -> p g s", g=8)
            for g in range(8):
                S.op("dve", lambda e, g=g: e.tensor_tensor(out=yv[:, g, :], in0=sbv[:, g, :], in1=tril, op=ALU.mult),
                     reads=[skey, K("tril")], writes=[K("y")])
            for g in range(8):
                S.op("pe", lambda e, g=g: e.transpose(out=psb[P_TX][:, g * 128:(g + 1) * 128], in_=yv[:, g, :], identity=idt),
                     reads=[K("y"), K("idt")], writes=[K("ps%d" % P_TX)])
            S.op("dve", lambda e: e.tensor_copy(out=WmT, in_=psb[P_TX]), reads=[K("ps%d" % P_TX)], writes=[K("WmT")])
        staged(w_s.rearrange("g t s -> t g s"), 1024, cons_ws, view=lambda sb: sb.rearrange("p (g s) -> p g s", g=8))

        def load(i):
            p = i % 3
            S.dma("sp", "xl%d" % p, lambda e: e.dma_start(out=xt[p], in_=x_src[i * 128:(i + 1) * 128, :]),
                  writes=[K("xt%d" % p)])

        def inproj(q, banks, r0):
            r = r0
            for n in banks:
                b = P_IN[r % 2]
                r += 1
                for k in range(8):
                    S.op("pe", lambda e, k=k, n=n, b=b: e.matmul(psf[b], lhsT=xT[q][:, k * 128:(k + 1) * 128],
                                                                rhs=Wb[k][:, n * 512:(n + 1) * 512],
                                                                start=(k == 0), stop=(k == 7)),
                         reads=[K("xT%d" % q), K("Wb%d" % k)], writes=[K("ps%d" % b)])
                if n < 4:
                    dst, func, dk = u[:, n * 512:(n + 1) * 512], AF.Gelu, [K("u%d" % n)]
                elif n < 8:
                    dst, func, dk = v[:, (n - 4) * 512:(n - 3) * 512], AF.Gelu, VK
                else:
                    dst, func, dk = sz[:, (n - 8) * 512:(n - 7) * 512], AF.Silu, [K("sz%d" % (n - 8))]
                S.op("act", lambda e, dst=dst, func=func, b=b: e.activation(out=dst, in_=psf[b], func=func, scale=rstd[q]),
                     reads=[K("ps%d" % b), K("rstd%d" % q)], writes=dk)

        def pre(i):
            p, q = i % 3, i % 2
            xk = K("xt%d" % p)
            S.op("act", lambda e: e.activation(out=junk, in_=xt[p], func=AF.Square, accum_out=ss),
                 reads=[xk], writes=[K("junk"), K("ss")])
            S.op("dve", lambda e: e.tensor_scalar(out=ms, in0=ss, scalar1=1.0 / D, scalar2=RMS_EPS, op0=ALU.mult,
                                                  op1=ALU.add), reads=[K("ss")], writes=[K("ms")])
            self.rsqrt(K("ms"), ms, K("rstd%d" % q), rstd[q], K("tmp1"), tmp1)
            S.op("pool", lambda e: e.tensor_copy(out=xb, in_=xt[p]), reads=[xk], writes=[K("xb")])

        def stage1_t(i):
            q = i % 2
            for k in range(8):
                S.op("pe", lambda e, k=k: e.transpose(out=psb[P_TX][:, k * 128:(k + 1) * 128],
                                                      in_=xb[:, k * 128:(k + 1) * 128], identity=idt),
                     reads=[K("xb"), K("idt")], writes=[K("ps%d" % P_TX)])
            S.op("act", lambda e: e.activation(out=xT[q], in_=psb[P_TX], func=AF.Copy),
                 reads=[K("ps%d" % P_TX)], writes=[K("xT%d" % q)])

        def stage1(i):
            q = i % 2
            inproj(q, (4, 5, 6, 7), 0)
            for c in range(4):
                S.op("dve", lambda e, c=c: e.bn_stats(out=bst[:, c * 6:(c + 1) * 6], in_=v[:, c * 512:(c + 1) * 512]),
                     reads=VK, writes=[K("bst")])
            S.op("dve", lambda e: e.bn_aggr(out=mv, in_=bst.rearrange("p (c s) -> p c s", c=4)),
                 reads=[K("bst")], writes=[K("mv")])
            S.op("dve", lambda e: e.tensor_scalar(out=va, in0=mv[:, 1:2], scalar1=LN_EPS, scalar2=None,
                                                  op0=ALU.add), reads=[K("mv")], writes=[K("va")])
            self.rsqrt(K("va"), va, K("rsv"), rsv, K("tmp2"), tmp2)
            S.op("dve", lambda e: e.scalar_tensor_tensor(out=v, in0=v, scalar=mv[:, 0:1], in1=lng,
                                                         op0=ALU.subtract, op1=ALU.mult),
                 reads=VK + [K("mv"), K("lng")], writes=VK)
            S.op("dve", lambda e: e.scalar_tensor_tensor(out=vn[q], in0=v, scalar=rsv, in1=lnb,
                                                         op0=ALU.mult, op1=ALU.add),
                 reads=VK + [K("rsv"), K("lnb")], writes=[K("vn%d" % q)])

        def stage2a(i):
            q = i % 2
            inproj(q, (0, 1, 2, 3, 8, 9, 10, 11), 0)
            if i + 1 < NT:
                stage1_t(i + 1)
            for qd in range(4):
                b = P_SV[qd % 3]
                for gg in range(2):
                    g = 2 * qd + gg
                    S.op("pe", lambda e, g=g, gg=gg, b=b: e.matmul(psf[b][:, gg * 256:(gg + 1) * 256],
                                                                  lhsT=WmT[:, g * 128:(g + 1) * 128],
                                                                  rhs=vn[q][:, g * 256:(g + 1) * 256], start=True, stop=True),
                         reads=[K("vn%d" % q), K("WmT")], writes=[K("ps%d" % b)])
                for gg in range(2):
                    g = 2 * qd + gg
                    S.op("dve", lambda e, g=g, gg=gg, b=b: e.scalar_tensor_tensor(
                        out=y[:, g * 256:(g + 1) * 256], in0=psf[b][:, gg * 256:(gg + 1) * 256], scalar=bsc[:, g:g + 1],
                        in1=u[:, g * 256:(g + 1) * 256], op0=ALU.add, op1=ALU.mult),
                        reads=[K("ps%d" % b), K("bsc"), K("u%d" % qd)], writes=[K("y%d" % qd)])
            S.op("dve", lambda e: e.tensor_tensor(out=y, in0=y, in1=sz, op=ALU.mult),
                 reads=[K("y")] + [K("y%d" % c) for c in range(4)] + [K("sz%d" % c) for c in range(4)],
                 writes=[K("y")] + [K("y%d" % c) for c in range(4)])

        def stage2b(i):
            p = i % 3
            xk = K("xt%d" % p)
            for h in range(2):
                for k in range(8):
                    kk = h * 8 + k
                    S.op("pe", lambda e, k=k, kk=kk, h=h: e.transpose(out=psb[P_TY2[h]][:, k * 128:(k + 1) * 128],
                                                                     in_=y[:, kk * 128:(kk + 1) * 128], identity=idt),
                         reads=[K("y"), K("idt")] + [K("y%d" % c) for c in range(4)], writes=[K("ps%d" % P_TY2[h])])
            for h in range(2):
                S.op("act", lambda e, h=h: e.activation(out=yT[:, h * 1024:(h + 1) * 1024], in_=psb[P_TY2[h]], func=AF.Copy),
                     reads=[K("ps%d" % P_TY2[h])], writes=[K("yT%d" % h)])
            for n in range(2):
                b = P_O[n]
                for k in range(16):
                    S.op("pe", lambda e, k=k, n=n, b=b: e.matmul(psf[b], lhsT=yT[:, k * 128:(k + 1) * 128],
                                                                rhs=Wo[k][:, n * 512:(n + 1) * 512],
                                                                start=(k == 0), stop=(k == 15)),
                         reads=[K("yT%d" % (k // 8)), K("Wo")], writes=[K("ps%d" % b)])
                S.op("act", lambda e, n=n, b=b: e.activation(out=junk[:, 0:512], in_=psf[b], func=AF.Square,
                                                             accum_out=ss2[:, n:n + 1]),
                     reads=[K("ps%d" % b)], writes=[K("junk"), K("ss2")])
            S.op("dve", lambda e: e.tensor_tensor(out=a2, in0=ss2[:, 0:1], in1=ss2[:, 1:2], op=ALU.add),
                 reads=[K("ss2")], writes=[K("a2")])
            S.op("dve", lambda e: e.tensor_scalar(out=a2, in0=a2, scalar1=1.0 / D, scalar2=RMS_EPS, op0=ALU.mult,
                                                  op1=ALU.add), reads=[K("a2")], writes=[K("a2")])
            self.rsqrt(K("a2"), a2, K("rstd2"), rstd2, K("tmp3"), tmp3)
            for n in range(2):
                b = P_O[n]
                S.op("dve", lambda e, n=n, b=b: e.scalar_tensor_tensor(out=tpost[:, n * 512:(n + 1) * 512], in0=psf[b],
                                                                       scalar=rstd2, in1=gpo[:, n * 512:(n + 1) * 512],
                                                                       op0=ALU.mult, op1=ALU.mult),
                     reads=[K("ps%d" % b), K("rstd2"), K("gpo")], writes=[K("tpost")])
            S.op("pool", lambda e: e.tensor_tensor(out=xt[p], in0=xt[p], in1=tpost, op=ALU.add),
                 reads=[xk, K("tpost")], writes=[xk])
            o = S.dma("pool", "xs%d" % p, lambda e: e.dma_start(out=x_dst[i * 128:(i + 1) * 128, :], in_=xt[p]),
                      reads=[xk], writes=[K("xdst")])
            self.final.append(o)

        load(0)
        if NT > 1:
            load(1)
        pre(0)
        stage1_t(0)
        stage1(0)
        for i in range(NT):
            if i + 2 < NT:
                load(i + 2)
            if i + 1 < NT:
                pre(i + 1)
            stage2a(i)
            if i + 1 < NT:
                stage1(i + 1)
            stage2b(i)
        S.barrier()
        self.peak = max(getattr(self, "peak", 0), self.off)
        self.off = off0


    def layer_b(self, x_src, x_halo, x_dst, ysc, vsc, w_in, gpre_d, w_out, npost, hv_d, tabs, ident_d, rm_d, mask_d,
                NOWN=4096, dbg_hps=range(8), dbg_gs=range(3), dbg_attn=True):
        S, nc = self.S, self.nc
        off0 = self.off
        L = self.key("B")
        K = lambda s: "%s/%s" % (L, s)
        psf, psb = self.ps_f32, self.ps_b16
        HAL = 2048
        NT = HAL + NOWN
        DIL = (1, 4, 16)
        P_PR, P_RT, P_S2, P_ND, P_V2 = (0, 1), 2, ((3, 4), (0, 1), (2, 7)), (5, 6), (7, 3)

        hT = self.carve(8 * NT, BF16).rearrange("p (k t) -> p k t", k=8)
        offq = self.off
        QT = self.carve(NOWN, BF16)
        KT = self.carve(NT, BF16)
        Vb = self.carve(NT, BF16)
        acc = self.carve(2 * NOWN, F32).rearrange("p (a t) -> p a t", a=2)
        offw = self.off
        wst = self.carve(3 * 1024, F32)
        wb = [self.carve(3 * 1024, BF16) for _ in range(2)]
        idt = self.carve(128, BF16)
        rm = self.carve(128, BF16)
        maskT = self.carve(512, BF16)
        ones = self.carve(64, BF16)
        hvones = self.carve(64, BF16)
        loones = self.carve(64, BF16)
        hv2 = self.carve(2, F32)
        hv, lo = hv2[:, 0:1], hv2[:, 1:2]
        PT = [self.carve(512, BF16) for _ in range(6)]
        qraw = [self.carve(512, BF16) for _ in range(3)]
        rt1 = [self.carve(512, BF16) for _ in range(3)]
        ctab = [self.carve(512, BF16) for _ in range(3)]
        stab = [self.carve(512, BF16) for _ in range(3)]
        st = self.carve(64, F32)
        print("layer B persistent SBUF bytes", self.off - off0)
        offp = self.off

        S.dma("sp", "idt", lambda e: e.dma_start(out=idt, in_=ident_d), writes=[K("idt")])
        S.dma("sp", "rm", lambda e: e.dma_start(out=rm, in_=rm_d), writes=[K("rm")])
        S.dma("sp", "mask", lambda e: e.dma_start(out=maskT, in_=mask_d), writes=[K("mask")])
        S.dma("sp", "hv", lambda e: e.dma_start(out=hv2, in_=hv_d), writes=[K("hv")])
        S.op("pool", lambda e: e.memset(ones, 1.0), writes=[K("ones")])
        S.op("pool", lambda e: e.memset(hvones, 1.0), writes=[K("hvones")])
        S.op("pool", lambda e: e.tensor_scalar(out=hvones, in0=hvones, scalar1=hv, scalar2=None, op0=ALU.mult),
             reads=[K("hv"), K("hvones")], writes=[K("hvones")])
        S.op("pool", lambda e: e.memset(loones, 1.0), writes=[K("loones")])
        S.op("pool", lambda e: e.tensor_scalar(out=loones, in0=loones, scalar1=lo, scalar2=None, op0=ALU.mult),
             reads=[K("hv"), K("loones")], writes=[K("loones")])

        need1 = 8 * 4096 + 8 * 2048 + 4096 + 2048 + 256
        def place(need):
            if offp + need <= self.cap:
                self.off = offp
            else:
                assert offq + need <= offw, ("overlay does not fit", need, offw - offq)
                self.off = offq
        place(need1)
        gb = self.carve(1024, F32)
        xt = [self.carve(1024, F32) for _ in range(8)]
        hb = [self.carve(1024, BF16) for _ in range(8)]
        junk = self.carve(1024, BF16)
        st1 = self.carve(32, F32)
        S.dma("sp", "gb", lambda e: e.dma_start(out=gb, in_=gpre_d.partition_broadcast(128)), writes=[K("gb")])

        def xrows(i):
            return x_halo[i * 128:(i + 1) * 128, :] if i < 16 else x_src[(i - 16) * 128:(i - 15) * 128, :]

        def p1_load(i):
            p = i % 8
            S.dma("sp", "x1l%d" % p, lambda e: e.dma_start(out=xt[p], in_=xrows(i)), writes=[K("x1t%d" % p)])

        NT1 = NT // 128
        NG1 = NT1 // 4

        def p1_x(gi):
            q = gi % 2
            ss, ms, rstd, tmp1 = (st1[:, q * 16 + c * 4:q * 16 + c * 4 + 4] for c in range(4))
            for t in range(4):
                i = gi * 4 + t
                p = i % 8
                S.op("act", lambda e, p=p, t=t, ss=ss: e.activation(out=junk, in_=xt[p], func=AF.Square, accum_out=ss[:, t:t + 1]),
                     reads=[K("x1t%d" % p)], writes=[K("junk"), K("ss%d" % q)])
            S.op("dve", lambda e: e.tensor_scalar(out=ms, in0=ss, scalar1=1.0 / D, scalar2=RMS_EPS, op0=ALU.mult,
                                                  op1=ALU.add), reads=[K("ss%d" % q)], writes=[K("ms%d" % q)])
            self.rsqrt(K("ms%d" % q), ms, K("rstd%d" % q), rstd, K("tmp1%d" % q), tmp1)
            for t in range(4):
                i = gi * 4 + t
                p = i % 8
                S.op("dve", lambda e, p=p, t=t, rstd=rstd: e.scalar_tensor_tensor(out=hb[p], in0=xt[p], scalar=rstd[:, t:t + 1],
                                                                                  in1=gb, op0=ALU.mult, op1=ALU.mult),
                     reads=[K("x1t%d" % p), K("rstd%d" % q), K("gb")], writes=[K("hb%d" % p)])

        def p1_y(gi):
            for t in range(4):
                i = gi * 4 + t
                p = i % 8
                pb = P_PR[i % 2]
                for k in range(8):
                    S.op("pe", lambda e, k=k, p=p, pb=pb: e.transpose(out=psb[pb][:, k * 128:(k + 1) * 128],
                                                                in_=hb[p][:, k * 128:(k + 1) * 128], identity=idt),
                         reads=[K("hb%d" % p), K("idt")], writes=[K("ps%d" % pb)])
                S.op("act", lambda e, i=i, pb=pb: e.activation(out=hT[:, :, i * 128:(i + 1) * 128],
                                                               in_=psb[pb].rearrange("p (k t) -> p k t", k=8), func=AF.Copy),
                     reads=[K("ps%d" % pb)], writes=[K("hT")])

        for i in range(min(8, NT1)):
            p1_load(i)
        p1_x(0)
        for gi in range(NG1):
            if gi + 1 < NG1:
                p1_x(gi + 1)
            p1_y(gi)
            for t in range(4):
                i = (gi + 2) * 4 + t
                if i < NT1:
                    p1_load(i)
        S.barrier()

        KBG = 8 if NOWN >= 4096 else 4
        need15 = 2 * 8 * 1024 * 2 + 2 * KBG * 1024 * 2
        place(need15)
        Wv2 = [self.carve(8 * 1024, BF16).rearrange("p (k c) -> p k c", k=8) for _ in range(2)]
        vstg = [self.carve(KBG * 1024, BF16).rearrange("p (b c) -> p b c", b=KBG) for _ in range(2)]
        vcnt = [0]
        def load_wv(gidx):
            g = list(dbg_gs)[gidx]
            Wv = Wv2[gidx % 2]
            for k in range(8):
                slot = k % 3
                stg = wst[:, slot * 1024:(slot + 1) * 1024]
                S.dma("sp", "wvs%d" % slot, lambda e, k=k, g=g, stg=stg: e.dma_start(
                    out=stg, in_=w_in[k * 128:(k + 1) * 128, g * 3072 + 2048:g * 3072 + 3072]), writes=[K("wst%d" % slot)])
                if k % 2 == 0:
                    S.op("act", lambda e, k=k, stg=stg, Wv=Wv: e.activation(out=Wv[:, k, :], in_=stg, func=AF.Copy),
                         reads=[K("wst%d" % slot)], writes=[K("Wv%d" % (gidx % 2))])
                else:
                    S.op("dve", lambda e, k=k, stg=stg, Wv=Wv: e.tensor_copy(out=Wv[:, k, :], in_=stg), reads=[K("wst%d" % slot)],
                         writes=[K("Wv%d" % (gidx % 2))])

        if len(list(dbg_gs)) > 0:
            load_wv(0)
        for gidx, g in enumerate(dbg_gs):
            d = DIL[g]
            span = 128 * d
            nkb = (NOWN // span + 1) * d
            kt0 = HAL - span
            Wv = Wv2[gidx % 2]
            wvkey = K("Wv%d" % (gidx % 2))
            if gidx + 1 < len(list(dbg_gs)):
                load_wv(gidx + 1)
            for kb0 in range(0, nkb, KBG):
                nb = min(KBG, nkb - kb0)
                vs = vcnt[0] % 2
                vcnt[0] += 1
                for bl in range(nb):
                    kb = kb0 + bl
                    sp_, r = kb // d, kb % d
                    t_start = kt0 + sp_ * span + r
                    banks = P_PR if kb % 2 == 0 else (P_RT, P_V2[0])
                    for k in range(8):
                        for n in range(2):
                            S.op("pe", lambda e, k=k, n=n, t_start=t_start, d=d, bk=banks[n], Wv=Wv: e.matmul(
                                psf[bk], lhsT=hT[:, k, t_start:t_start + 127 * d + 1:d], rhs=Wv[:, k, n * 512:(n + 1) * 512],
                                start=(k == 0), stop=(k == 7)), reads=[wvkey, K("hT")], writes=[K("ps%d" % banks[n])])
                    c = 0 if kb < d else (1 if kb < d + 16 else 2)
                    fl = hv if c == 0 else lo
                    for n in range(2):
                        dst = vstg[vs][:, bl, n * 512:(n + 1) * 512]
                        bk = banks[n]
                        if n == 0:
                            if c == 2:
                                S.op("act", lambda e, dst=dst, bk=bk: e.activation(out=dst, in_=psf[bk], func=AF.Copy),
                                     reads=[K("ps%d" % bk)], writes=[K("vstg%d" % vs)])
                            else:
                                S.op("act", lambda e, dst=dst, bk=bk, fl=fl: e.activation(out=dst, in_=psf[bk], func=AF.Copy,
                                                                                        scale=fl),
                                     reads=[K("ps%d" % bk), K("hv")], writes=[K("vstg%d" % vs)])
                        else:
                            if c == 2:
                                S.op("dve", lambda e, dst=dst, bk=bk: e.tensor_copy(out=dst, in_=psf[bk]),
                                     reads=[K("ps%d" % bk)], writes=[K("vstg%d" % vs)])
                            else:
                                S.op("dve", lambda e, dst=dst, bk=bk, fl=fl: e.tensor_scalar(out=dst, in0=psf[bk], scalar1=fl,
                                                                                          scalar2=None, op0=ALU.mult),
                                     reads=[K("ps%d" % bk), K("hv")], writes=[K("vstg%d" % vs)])
                for h8 in range(8):
                    S.dma("sp", "vst%d" % vs, lambda e, g=g, kb0=kb0, nb=nb, vs=vs, h8=h8: e.dma_start(
                        out=vsc[g, h8, :, kb0:kb0 + nb, :], in_=vstg[vs][:, 0:nb, h8 * 128:(h8 + 1) * 128]),
                        reads=[K("vstg%d" % vs)], writes=[K("vsc")])
        S.barrier()

        w5 = w_in[:, 0:9216].rearrange("(k p) (g t h f) -> p k g t h f", p=128, g=3, t=3, h=8)
        wz = w_in[:, 9216:10240].rearrange("(k p) (h f) -> p k h f", p=128, h=8)
        tabi = [0]
        bankc = [0]
        wcnt = [0]

        pending_cast = []

        def flush_cast():
            while pending_cast:
                pending_cast.pop(0)()

        wjobs = []
        for hp_ in dbg_hps:
            for g_ in dbg_gs:
                wjobs.append((w5[:, :, g_, :, hp_, :], 384))
            wjobs.append((wz[:, :, hp_, :], 128))
        wready = {}

        def load_w(src_ap=None, ncols=None):
            i = wcnt[0]
            wcnt[0] += 1
            if i == 0:
                wready[0] = _load_w(0, *wjobs[0])
            if i + 1 < len(wjobs):
                wready[i + 1] = _load_w(i + 1, *wjobs[i + 1])
            return wready.pop(i)

        def _load_w(i, src_ap, ncols):
            slot = i % 2
            dst32 = wst[:, 0:8 * ncols]
            if len(src_ap.shape) == 3:
                S.dma("sp", "wst", lambda e: e.dma_start(out=dst32.rearrange("p (k c) -> p k c", k=8), in_=src_ap),
                      writes=[K("wst")])
            else:
                d4 = dst32.rearrange("p (k t f) -> p k t f", k=8, t=3)
                for t in range(3):
                    S.dma("sp", "wst", lambda e, t=t: e.dma_start(out=d4[:, :, t, :], in_=src_ap[:, :, t, :]),
                          writes=[K("wst")])
            def cast():
                S.op("act", lambda e: e.activation(out=wb[slot][:, 0:8 * ncols], in_=dst32, func=AF.Copy), reads=[K("wst")],
                     writes=[K("wb%d" % slot)])
            if i == 0:
                cast()
            else:
                pending_cast.append(cast)
            return wb[slot][:, 0:8 * ncols].rearrange("p (k c) -> p k c", k=8), K("wb%d" % slot)

        def proj_bank(wv, wkey, c0, tau0, n):
            b = P_PR[bankc[0] % 2]
            bankc[0] += 1
            for k in range(8):
                S.op("pe", lambda e, k=k, b=b: e.matmul(psf[b][:, 0:n], lhsT=wv[:, k, c0:c0 + 128], rhs=hT[:, k, tau0:tau0 + n],
                                                       start=(k == 0), stop=(k == 7)),
                     reads=[wkey, K("hT")], writes=[K("ps%d" % b)])
            flush_rot()
            return b

        pending_rot = []

        def flush_rot():
            while pending_rot:
                pending_rot.pop(0)()

        def rope_bank(b, n, d, dest, dkey, ctd, std, t0):
            j = tabi[0] % 3
            tabi[0] += 1
            S.dma("sp", "ct%d" % j, lambda e: e.dma_start(out=ctab[j][:, 0:n], in_=ctd[:, t0:t0 + n]), writes=[K("ctab%d" % j)])
            S.dma("sp", "st%d" % j, lambda e: e.dma_start(out=stab[j][:, 0:n], in_=std[:, t0:t0 + n]), writes=[K("stab%d" % j)])
            if d == 1 or n < 512:
                ov, iv = qraw[j][:, 0:n], psf[b][:, 0:n]
            else:
                ov = qraw[j].rearrange("p (r i) -> p r i", r=d)
                iv = psf[b].rearrange("p (i r) -> p r i", r=d)
            S.op("act", lambda e: e.activation(out=ov, in_=iv, func=AF.Copy), reads=[K("ps%d" % b)], writes=[K("qraw%d" % j)])

            def second():
                S.op("pe", lambda e: e.matmul(psf[P_RT][:, 0:n], lhsT=rm, rhs=qraw[j][:, 0:n], start=True, stop=True),
                     reads=[K("rm"), K("qraw%d" % j)], writes=[K("ps%d" % P_RT)])
                S.op("dve", lambda e: e.tensor_tensor(out=rt1[j][:, 0:n], in0=psf[P_RT][:, 0:n], in1=stab[j][:, 0:n], op=ALU.mult),
                     reads=[K("ps%d" % P_RT), K("stab%d" % j)], writes=[K("rt1%d" % j)])
                S.op("pool", lambda e: e.tensor_tensor(out=qraw[j][:, 0:n], in0=qraw[j][:, 0:n], in1=ctab[j][:, 0:n], op=ALU.mult),
                     reads=[K("qraw%d" % j), K("ctab%d" % j)], writes=[K("qraw%d" % j)])
                if d == 16:
                    a0 = rt1[j].rearrange("p (r i) -> p r i", r=16)
                    a1 = qraw[j].rearrange("p (r i) -> p r i", r=16)
                else:
                    a0, a1 = rt1[j][:, 0:n], qraw[j][:, 0:n]
                S.op("dve", lambda e: e.tensor_tensor(out=dest, in0=a0, in1=a1, op=ALU.add),
                     reads=[K("rt1%d" % j), K("qraw%d" % j)], writes=[dkey])
            pending_rot.append(second)

        ptc = [0]
        sc = [0]
        ndc = [0]
        pending_tail = []

        def flush_tail():
            while pending_tail:
                pending_tail.pop(0)()

        for hp in dbg_hps:
            if not pending_tail:
                S.op("pool", lambda e: e.memset(acc.rearrange("p a t -> p (a t)"), 0.0), writes=[K("acc")])
            for g in dbg_gs:
                d = DIL[g]
                span = 128 * d
                nsp = NOWN // span
                nkb = (nsp + 1) * d
                kt0 = HAL - span
                nk = NOWN + span
                cq, sq, ck, sk = tabs[g]
                wv, wkey = load_w(w5[:, :, g, :, hp, :], 384)
                for bq in range(NOWN // 512):
                    if bq == 3 and pending_tail:
                        flush_tail()
                        S.op("pool", lambda e: e.memset(acc.rearrange("p a t -> p (a t)"), 0.0), writes=[K("acc")])
                    b = proj_bank(wv, wkey, 0, HAL + bq * 512, 512)
                    if d == 16:
                        sp_, i0 = (bq * 512) // span, ((bq * 512) % span) // 16
                        dest = QT[:, sp_ * span:(sp_ + 1) * span].rearrange("p (r i) -> p r i", r=16)[:, :, i0:i0 + 32]
                        rope_bank(b, 512, d, dest, K("QT"), cq, sq, bq * 512)
                    else:
                        rope_bank(b, 512, d, QT[:, bq * 512:(bq + 1) * 512], K("QT"), cq, sq, bq * 512)
                nb_full, rem = nk // 512, nk % 512
                for bk in range(nb_full + (1 if rem else 0)):
                    n = 512 if bk < nb_full else rem
                    b = proj_bank(wv, wkey, 128, kt0 + bk * 512, n)
                    if d == 16:
                        sp_, i0 = (bk * 512) // span, ((bk * 512) % span) // 16
                        dest = KT[:, sp_ * span:(sp_ + 1) * span].rearrange("p (r i) -> p r i", r=16)[:, :, i0:i0 + 32]
                        rope_bank(b, 512, d, dest, K("KT"), ck, sk, bk * 512)
                    else:
                        rope_bank(b, n, d, KT[:, bk * 512:bk * 512 + n], K("KT"), ck, sk, bk * 512)
                flush_rot()
                S.dma("sp", "vbl", lambda e, g=g, hp=hp, nkb=nkb: e.dma_start(
                    out=Vb[:, 0:nkb * 128], in_=vsc[g, hp, :, 0:nkb, :].rearrange("p b f -> p (b f)")),
                    reads=[K("vsc")], writes=[K("Vb")])
                nqb = nsp * d

                def s_pair(qb0):
                    ba, bb = P_S2[sc[0] % 3]
                    sc[0] += 1
                    for blk in range(2):
                        qb = qb0 + blk
                        for kbi in range(2):
                            kb = qb + kbi * d
                            for hh, b in ((0, ba), (1, bb)):
                                S.op("pe", lambda e, hh=hh, kbi=kbi, kb=kb, b=b, blk=blk, qb=qb: e.matmul(
                                    psf[b][:, (blk * 2 + kbi) * 128:(blk * 2 + kbi + 1) * 128],
                                    lhsT=KT[hh * 64:(hh + 1) * 64, kb * 128:(kb + 1) * 128],
                                    rhs=QT[hh * 64:(hh + 1) * 64, qb * 128:(qb + 1) * 128], start=True, stop=True,
                                    tile_position=(hh * 64, 0)),
                                    reads=[K("KT"), K("QT")], writes=[K("ps%d" % b)])
                    return ba, bb

                def e_pair(banks):
                    js = []
                    for b in banks:
                        j = ptc[0] % 6
                        ptc[0] += 1
                        S.op("act", lambda e, b=b, j=j: e.activation(out=PT[j], in_=psf[b], func=AF.Exp, scale=0.125),
                             reads=[K("ps%d" % b)], writes=[K("PT%d" % j)])
                        if j % 2 == 0:
                            S.op("pool", lambda e, j=j: e.tensor_tensor(out=PT[j], in0=PT[j], in1=maskT, op=ALU.mult),
                                 reads=[K("PT%d" % j), K("mask")], writes=[K("PT%d" % j)])
                        else:
                            S.op("dve", lambda e, j=j: e.tensor_tensor(out=PT[j], in0=PT[j], in1=maskT, op=ALU.mult),
                                 reads=[K("PT%d" % j), K("mask")], writes=[K("PT%d" % j)])
                        js.append(j)
                    return js

                def pv_pair(qb0, js, bnd):
                    for blk in range(2):
                        qb = qb0 + blk
                        for what in range(2):
                            for kbi in range(2):
                                kb = qb + kbi * d
                                for hh in range(2):
                                    j = js[hh]
                                    if what == 0:
                                        lhsT, lk = Vb[:, kb * 128 + hh * 64:kb * 128 + (hh + 1) * 64], K("Vb")
                                    elif kb < d:
                                        lhsT, lk = hvones, K("hvones")
                                    elif kb < d + 16:
                                        lhsT, lk = loones, K("loones")
                                    else:
                                        lhsT, lk = ones, K("ones")
                                    S.op("pe", lambda e, hh=hh, what=what, kbi=kbi, lhsT=lhsT, blk=blk, j=j: e.matmul(
                                        psf[bnd][hh * 64:(hh + 1) * 64, (what * 2 + blk) * 128:(what * 2 + blk + 1) * 128],
                                        lhsT=lhsT, rhs=PT[j][:, (blk * 2 + kbi) * 128:(blk * 2 + kbi + 1) * 128],
                                        start=(kbi == 0), stop=(kbi == 1), tile_position=(0, hh * 64)),
                                        reads=[lk, K("PT%d" % j)], writes=[K("ps%d" % bnd)])

                def acc_pair(qb0, bnd):
                    sp_, r = qb0 // d, qb0 % d
                    if d == 1:
                        av = acc[:, :, qb0 * 128:(qb0 + 2) * 128]
                        pv = psf[bnd].rearrange("p (a t) -> p a t", a=2)
                    else:
                        av = acc[:, :, sp_ * span:(sp_ + 1) * span].rearrange("p a (i r) -> p a r i", r=d)[:, :, r:r + 2, :]
                        pv = psf[bnd].rearrange("p (a s i) -> p a s i", a=2, s=2)
                    S.op("dve", lambda e: e.tensor_tensor(out=av, in0=av, in1=pv, op=ALU.add),
                         reads=[K("acc"), K("ps%d" % bnd)], writes=[K("acc")])

                flush_cast()
                if not dbg_attn:
                    continue
                npair = nqb // 2
                sb = {0: s_pair(0)}
                if npair > 1:
                    sb[1] = s_pair(2)
                jss = {0: e_pair(sb.pop(0))}
                for pi in range(npair):
                    if pi + 2 < npair:
                        sb[pi + 2] = s_pair(2 * (pi + 2))
                    if pi + 1 < npair:
                        jss[pi + 1] = e_pair(sb.pop(pi + 1))
                    bnd = P_ND[ndc[0] % 2]
                    ndc[0] += 1
                    pv_pair(2 * pi, jss.pop(pi), bnd)
                    acc_pair(2 * pi, bnd)
            wvz, wzkey = load_w(wz[:, :, hp, :], 128)
            zs = KT[:, 0:NOWN]
            for bq in range(NOWN // 512):
                b = proj_bank(wvz, wzkey, 0, HAL + bq * 512, 512)
                S.op("act", lambda e, b=b, bq=bq: e.activation(out=zs[:, bq * 512:(bq + 1) * 512], in_=psf[b], func=AF.Silu),
                     reads=[K("ps%d" % b)], writes=[K("KT")])
            def tail(hp=hp):
                S.op("dve", lambda e: e.tensor_scalar(out=acc[:, 1, :], in0=acc[:, 1, :], scalar1=1e-18, scalar2=None, op0=ALU.max),
                     reads=[K("acc")], writes=[K("acc")])
                S.op("act", lambda e: e.activation(out=acc[:, 1, :], in_=acc[:, 1, :], func=AF.Ln), reads=[K("acc")], writes=[K("acc")])
                S.op("act", lambda e: e.activation(out=acc[:, 1, :], in_=acc[:, 1, :], func=AF.Exp, scale=-1.0),
                     reads=[K("acc")], writes=[K("acc")])
                S.op("dve", lambda e: e.tensor_tensor(out=acc[:, 0, :], in0=acc[:, 0, :], in1=acc[:, 1, :], op=ALU.mult),
                     reads=[K("acc")], writes=[K("acc")])
                S.op("dve", lambda e: e.tensor_tensor(out=zs, in0=acc[:, 0, :], in1=zs, op=ALU.mult),
                     reads=[K("acc"), K("KT")], writes=[K("KT")])
                S.dma("pool", "ysc", lambda e: e.dma_start(out=ysc[hp], in_=zs), reads=[K("KT")], writes=[K("ysc")])
            pending_tail.append(tail)
            flush_tail()
            flush_cast()
        flush_tail()
        S.barrier()

        self.off = off0
        Wo = self.carve(8 * 1024, BF16).rearrange("p (k c) -> p k c", k=8)
        wos = self.carve(1024, F32)
        gpo = self.carve(1024, F32)
        yt = [self.carve(8 * 512, BF16).rearrange("p (h t) -> p h t", h=8) for _ in range(2)]
        x3 = [self.carve(1024, F32) for _ in range(8)]
        osb = [self.carve(1024, F32) for _ in range(8)]
        tpost = [self.carve(1024, F32) for _ in range(2)]
        junk3 = self.carve(512, BF16)
        st3 = self.carve(64, F32)
        P_O = ((0, 1), (2, 3))
        S.dma("sp", "gpo", lambda e: e.dma_start(out=gpo, in_=npost.partition_broadcast(128)), writes=[K("gpo")])
        for k in range(8):
            S.dma("sp", "wos", lambda e, k=k: e.dma_start(out=wos, in_=w_out[k * 128:(k + 1) * 128, :]), writes=[K("wos")])
            S.op("act", lambda e, k=k: e.activation(out=Wo[:, k, :], in_=wos, func=AF.Copy), reads=[K("wos")], writes=[K("Wo")])
        yv = ysc.rearrange("h p t -> p h t")
        NG3 = NOWN // 512

        def p3_yload(gi):
            q = gi % 2
            S.dma("sp", "ytl%d" % q, lambda e: e.dma_start(out=yt[q], in_=yv[:, :, gi * 512:(gi + 1) * 512]),
                  reads=[K("ysc")], writes=[K("yt%d" % q)])

        def p3_load(gi):
            for t in range(4):
                i = gi * 4 + t
                p = i % 8
                S.dma("sp", "x3l%d" % p, lambda e, i=i, p=p: e.dma_start(out=x3[p], in_=x_src[i * 128:(i + 1) * 128, :]),
                      writes=[K("x3%d" % p)])

        def p3_mm(gi):
            q = gi % 2
            ss2 = st3[:, q * 32:q * 32 + 8]
            for t in range(4):
                i = gi * 4 + t
                p = i % 8
                c0 = t * 128
                pb = P_O[i % 2]
                for n in range(2):
                    bk = pb[n]
                    for k in range(8):
                        S.op("pe", lambda e, k=k, n=n, bk=bk, c0=c0: e.matmul(psf[bk], lhsT=yt[q][:, k, c0:c0 + 128],
                                                                            rhs=Wo[:, k, n * 512:(n + 1) * 512],
                                                                            start=(k == 0), stop=(k == 7)),
                             reads=[K("yt%d" % q), K("Wo")], writes=[K("ps%d" % bk)])
                    S.op("act", lambda e, n=n, bk=bk, t=t, ss2=ss2: e.activation(out=junk3, in_=psf[bk], func=AF.Square,
                                                                               accum_out=ss2[:, 2 * t + n:2 * t + n + 1]),
                         reads=[K("ps%d" % bk)], writes=[K("junk3"), K("ss2%d" % q)])
                    S.op("act", lambda e, n=n, bk=bk, p=p: e.activation(out=osb[p][:, n * 512:(n + 1) * 512], in_=psf[bk],
                                                                       func=AF.Copy),
                         reads=[K("ps%d" % bk)], writes=[K("osb%d" % p)])

        def p3_post(gi):
            q = gi % 2
            ss2 = st3[:, q * 32:q * 32 + 8]
            a2, rstd2, tmp3 = (st3[:, q * 32 + 8 + c * 4:q * 32 + 12 + c * 4] for c in range(3))
            ssv = ss2.rearrange("p (t n) -> p t n", n=2)
            S.op("dve", lambda e: e.tensor_tensor(out=a2, in0=ssv[:, :, 0], in1=ssv[:, :, 1], op=ALU.add),
                 reads=[K("ss2%d" % q)], writes=[K("a2%d" % q)])
            S.op("dve", lambda e: e.tensor_scalar(out=a2, in0=a2, scalar1=1.0 / D, scalar2=RMS_EPS, op0=ALU.mult,
                                                  op1=ALU.add), reads=[K("a2%d" % q)], writes=[K("a2%d" % q)])
            self.rsqrt(K("a2%d" % q), a2, K("rstd2%d" % q), rstd2, K("tmp3%d" % q), tmp3)
            for t in range(4):
                i = gi * 4 + t
                p = i % 8
                tp = tpost[i % 2]
                S.op("dve", lambda e, p=p, t=t, tp=tp, rstd2=rstd2: e.scalar_tensor_tensor(out=tp, in0=osb[p], scalar=rstd2[:, t:t + 1],
                                                                                      in1=gpo, op0=ALU.mult, op1=ALU.mult),
                     reads=[K("osb%d" % p), K("rstd2%d" % q), K("gpo")], writes=[K("tpost%d" % (i % 2))])
                S.op("pool", lambda e, p=p, tp=tp: e.tensor_tensor(out=x3[p], in0=x3[p], in1=tp, op=ALU.add),
                     reads=[K("x3%d" % p), K("tpost%d" % (i % 2))], writes=[K("x3%d" % p)])
                o = S.dma("pool", "x3s%d" % p, lambda e, i=i, p=p: e.dma_start(out=x_dst[i * 128:(i + 1) * 128, :], in_=x3[p]),
                          reads=[K("x3%d" % p)], writes=[K("xdst")])
                self.final.append(o)

        p3_yload(0)
        p3_load(0)
        if NG3 > 1:
            p3_yload(1)
            p3_load(1)
        p3_mm(0)
        for gi in range(NG3):
            if gi + 1 < NG3:
                p3_mm(gi + 1)
            if gi + 2 < NG3:
                p3_yload(gi + 2)
            p3_post(gi)
            if gi + 2 < NG3:
                p3_load(gi + 2)
        S.barrier()
        self.peak = max(getattr(self, "peak", 0), self.off)
        self.off = off0

    def finish(self):
        S, nc = self.S, self.nc
        S.finalize()
        with contextlib.ExitStack() as es:
            sems = {k: es.enter_context(nc.semaphore("s%d" % i)) for i, k in enumerate(S.sem_keys)}
            block = es.enter_context(nc.Block())
            S.emit(block, sems, final_waits=self.final)
        return nc


_CACHE = {}


def consts():
    ident = np.eye(128, dtype=np.float32).astype(ml_dtypes.bfloat16)
    tril = np.tril(np.ones((128, 128), dtype=np.float32))
    return ident, tril


def prog_a(ntok):
    key = ("A", ntok)
    if key not in _CACHE:
        P = Prog()
        x = P.inp("x", [ntok, D])
        w_in = P.inp("w_in", [D, 3 * EW])
        npre = P.inp("npre", [128, 8])
        ln_g = P.inp("ln_g", [1, EW])
        ln_b = P.inp("ln_b", [1, EW])
        w_s = P.inp("w_s", [8, 128, 128])
        bs = P.inp("bs", [128, 8])
        w_out = P.inp("w_out", [EW, D])
        npost = P.inp("npost", [1, D])
        ident = P.inp("ident", [128, 128], BF16)
        tril = P.inp("tril", [128, 128])
        y = P.outp("y", [ntok, D])
        P.layer_a(x, y, ntok, w_in, npre, ln_g, ln_b, w_s, bs, w_out, npost, ident, tril)
        _CACHE[key] = P.finish()
    return _CACHE[key]


def col128(vec):
    return np.ascontiguousarray(np.asarray(vec, dtype=np.float32).reshape(-1, 128).T)


def run_layer_a(xs, i, inputs):
    j = i // 2
    ident, tril = consts()
    ntok = xs[0].shape[0]
    nc = prog_a(ntok)
    common = {
        "w_in": np.ascontiguousarray(inputs["a_w_in"][j]),
        "npre": col128(inputs["norm_pre"][i]),
        "ln_g": np.ascontiguousarray(inputs["a_ln_g"][j][None, :]),
        "ln_b": np.ascontiguousarray(inputs["a_ln_b"][j][None, :]),
        "w_s": np.ascontiguousarray(inputs["a_w_s"][j]),
        "bs": np.ascontiguousarray(np.asarray(inputs["a_b_s"][j]).T),
        "w_out": np.ascontiguousarray(inputs["a_w_out"][j]),
        "npost": np.ascontiguousarray(inputs["norm_post"][i][None, :]),
        "ident": ident, "tril": tril,
    }
    in_maps = [dict(common, x=np.ascontiguousarray(x)) for x in xs]
    res = run_bass_kernel_spmd(nc, in_maps, core_ids=list(range(len(xs))))
    return [r["y"] for r in res.results]


DILS = (1, 4, 16)
HALO = 2048


def rope_consts():
    rm = np.zeros((128, 128), np.float32)
    for fp in range(128):
        if fp % 64 < 32:
            rm[fp + 32, fp] = -1.0
        else:
            rm[fp - 32, fp] = 1.0
    mask = np.zeros((128, 2, 2, 128), np.float32)
    j = np.arange(128)[:, None]
    i = np.arange(128)[None, :]
    for hh in range(2):
        mask[:, hh, 0, :] = (j >= i)
        mask[:, hh, 1, :] = (j <= i)
    return rm.astype(ml_dtypes.bfloat16), mask.reshape(128, 512).astype(ml_dtypes.bfloat16)


def rope_tables_core(pos0, nown=4096):
    inv_freq = (1.0 / (np.float32(10000.0) ** (np.arange(0, 64, 2, dtype=np.float32) / np.float32(64)))).astype(np.float32)
    out = []
    for d in DILS:
        span = 128 * d
        res = []
        for (tau0, ncols) in ((HALO, nown), (HALO - span, nown + span)):
            tau = np.zeros(ncols, np.int64)
            for b0 in range(0, ncols, 512):
                n = min(512, ncols - b0)
                col = np.arange(n)
                r, ii = col // (n // d), col % (n // d)
                tau[b0:b0 + n] = tau0 + b0 + ii * d + r
            pos = (pos0 - HALO + tau).astype(np.float32)
            ang = pos[None, :] * inv_freq[:, None]
            c = np.tile(np.cos(ang), (4, 1)).astype(ml_dtypes.bfloat16)
            s_ = np.tile(np.sin(ang), (4, 1)).astype(ml_dtypes.bfloat16)
            res += [np.ascontiguousarray(c), np.ascontiguousarray(s_)]
        out.append(res)
    return out


DBG = {}


def prog_b():
    key = ("B",)
    if key not in _CACHE:
        P = Prog()
        x = P.inp("x", [TOK, D])
        xh = P.inp("xh", [HALO, D])
        w_in = P.inp("w_in", [D, 10240])
        gpre = P.inp("gpre", [1, D])
        w_out = P.inp("w_out", [D, D])
        npost = P.inp("npost", [1, D])
        hv = P.inp("hv", [128, 2])
        ident = P.inp("ident", [128, 128], BF16)
        rm = P.inp("rm", [128, 128], BF16)
        mask = P.inp("mask", [128, 512], BF16)
        tabs = []
        for g, d in enumerate(DILS):
            nk = 4096 + 128 * d
            tabs.append((P.inp("cq%d" % g, [128, 4096], BF16), P.inp("sq%d" % g, [128, 4096], BF16),
                         P.inp("ck%d" % g, [128, nk], BF16), P.inp("sk%d" % g, [128, nk], BF16)))
        ysc = P.nc.dram_tensor("ysc", [8, 128, 4096], BF16).ap()
        vsc = P.nc.dram_tensor("vsc", [3, 8, 128, 48, 128], BF16).ap()
        y = P.outp("y", [TOK, D])
        P.layer_b(x, xh, y, ysc, vsc, w_in, gpre, w_out, npost, hv, tabs, ident, rm, mask, **DBG)
        _CACHE[key] = P.finish()
    return _CACHE[key]


def run_layer_b(xs, i, inputs, core_ids=None):
    j = i // 2
    ident, _ = consts()
    rm, mask = rope_consts()
    nc = prog_b()
    common = {
        "w_in": np.ascontiguousarray(inputs["b_w_in"][j]),
        "gpre": np.ascontiguousarray(inputs["norm_pre"][i][None, :]),
        "w_out": np.ascontiguousarray(inputs["b_w_out"][j]),
        "npost": np.ascontiguousarray(inputs["norm_post"][i][None, :]),
        "ident": ident, "rm": rm, "mask": mask,
    }
    in_maps = []
    cores = list(range(len(xs))) if core_ids is None else core_ids
    for c in cores:
        m = dict(common)
        m["x"] = np.ascontiguousarray(xs[c])
        first = (c % 4 == 0)
        m["xh"] = np.zeros((HALO, D), np.float32) if first else np.ascontiguousarray(xs[c - 1][TOK - HALO:])
        m["hv"] = np.tile(np.array([[0.0 if first else 1.0, 1.0]], np.float32), (128, 1))
        tb = rope_tables_core((c % 4) * TOK)
        for g in range(3):
            m["cq%d" % g], m["sq%d" % g], m["ck%d" % g], m["sk%d" % g] = tb[g]
        in_maps.append(m)
    res = run_bass_kernel_spmd(nc, in_maps, core_ids=list(range(len(cores))))
    return [r["y"] for r in res.results]


def kernel_unfused(x, norm_pre, norm_post, a_w_in, a_ln_g, a_ln_b, a_w_s, a_b_s, a_w_out, b_w_in, b_w_out):
    inputs = dict(norm_pre=np.asarray(norm_pre), norm_post=np.asarray(norm_post), a_w_in=np.asarray(a_w_in),
                  a_ln_g=np.asarray(a_ln_g), a_ln_b=np.asarray(a_ln_b), a_w_s=np.asarray(a_w_s), a_b_s=np.asarray(a_b_s),
                  a_w_out=np.asarray(a_w_out), b_w_in=np.asarray(b_w_in), b_w_out=np.asarray(b_w_out))
    x = np.asarray(x, dtype=np.float32)
    B, S_, D_ = x.shape
    xs = [np.ascontiguousarray(c) for c in x.reshape(NCORES, TOK, D_)]
    for i in range(4):
        if i % 2 == 0:
            xs = run_layer_a(xs, i, inputs)
        else:
            xs = run_layer_b(xs, i, inputs)
    return np.stack(xs).reshape(B, S_, D_).astype(np.float32)


XR = 8192
B_CALLS = ((2048, 2048), (4096, -2048), (4096, 0))


def prog_fused():
    key = ("F",)
    if key in _CACHE:
        return _CACHE[key]
    P = Prog()
    x = P.inp("x", [XR, D])
    ident = P.inp("ident", [128, 128], BF16)
    tril = P.inp("tril", [128, 128])
    rm = P.inp("rm", [128, 128], BF16)
    mask = P.inp("mask", [128, 512], BF16)
    npost = [P.inp("npost%d" % i, [1, D]) for i in range(4)]
    A = []
    for j in range(2):
        A.append(dict(w_in=P.inp("a_w_in%d" % j, [D, 3 * EW]), npre=P.inp("a_npre%d" % j, [128, 8]),
                      ln_g=P.inp("a_ln_g%d" % j, [1, EW]), ln_b=P.inp("a_ln_b%d" % j, [1, EW]),
                      w_s=P.inp("a_w_s%d" % j, [8, 128, 128]), bs=P.inp("a_bs%d" % j, [128, 8]),
                      w_out=P.inp("a_w_out%d" % j, [EW, D])))
    Bw = []
    for j in range(2):
        Bw.append(dict(w_in=P.inp("b_w_in%d" % j, [D, 10240]), gpre=P.inp("b_gpre%d" % j, [1, D]),
                       w_out=P.inp("b_w_out%d" % j, [D, D])))
    fl = [P.inp("fl%d" % c, [128, 2]) for c in range(3)]
    tabs = []
    for c, (nown, _) in enumerate(B_CALLS):
        tc = []
        for g, d in enumerate(DILS):
            nk = nown + 128 * d
            tc.append((P.inp("cq%d_%d" % (c, g), [128, nown], BF16), P.inp("sq%d_%d" % (c, g), [128, nown], BF16),
                       P.inp("ck%d_%d" % (c, g), [128, nk], BF16), P.inp("sk%d_%d" % (c, g), [128, nk], BF16)))
        tabs.append(tc)
    y = P.outp("y", [TOK, D])
    xres = P.nc.dram_tensor("xres", [XR, D], F32).ap()
    ysc = P.nc.dram_tensor("ysc", [8, 128, 4096], BF16).ap()
    vsc = P.nc.dram_tensor("vsc", [3, 8, 128, 48, 128], BF16).ap()

    def la(j, i, src, dst, ntok):
        a = A[j]
        P.layer_a(src, dst, ntok, a["w_in"], a["npre"], a["ln_g"], a["ln_b"], a["w_s"], a["bs"], a["w_out"], npost[i],
                  ident, tril)

    def lb(j, i, c, src, halo, dst):
        nown = B_CALLS[c][0]
        b = Bw[j]
        P.layer_b(src, halo, dst, ysc[:, :, 0:nown], vsc, b["w_in"], b["gpre"], b["w_out"], npost[i], fl[c], tabs[c],
                  ident, rm, mask, NOWN=nown)

    la(0, 0, x, xres, XR)
    lb(0, 1, 0, xres[6144:8192], xres[4096:6144], xres[6144:8192])
    lb(0, 1, 1, xres[2048:6144], xres[0:2048], xres[2048:6144])
    la(1, 2, xres[2048:8192], xres[2048:8192], 6144)
    P.final = []
    lb(1, 3, 2, xres[4096:8192], xres[2048:4096], y)
    _CACHE[key] = P.finish()
    return _CACHE[key]


def kernel(x, norm_pre, norm_post, a_w_in, a_ln_g, a_ln_b, a_w_s, a_b_s, a_w_out, b_w_in, b_w_out):
    f32 = lambda a: np.ascontiguousarray(np.asarray(a, dtype=np.float32))
    x = f32(x)
    norm_pre, norm_post = f32(norm_pre), f32(norm_post)
    Bn, S_, D_ = x.shape
    per_seq = S_ // TOK
    ident, tril = consts()
    rm, mask = rope_consts()
    common = {"ident": ident, "tril": tril, "rm": rm, "mask": mask}
    for i in range(4):
        common["npost%d" % i] = f32(norm_post[i][None, :])
    for j in range(2):
        common["a_w_in%d" % j] = f32(a_w_in[j])
        common["a_npre%d" % j] = col128(norm_pre[2 * j])
        common["a_ln_g%d" % j] = f32(np.asarray(a_ln_g[j])[None, :])
        common["a_ln_b%d" % j] = f32(np.asarray(a_ln_b[j])[None, :])
        common["a_w_s%d" % j] = f32(a_w_s[j])
        common["a_bs%d" % j] = f32(np.asarray(a_b_s[j]).T)
        common["a_w_out%d" % j] = f32(a_w_out[j])
        common["b_w_in%d" % j] = f32(b_w_in[j])
        common["b_gpre%d" % j] = f32(norm_pre[2 * j + 1][None, :])
        common["b_w_out%d" % j] = f32(b_w_out[j])
    tab_cache = {}
    in_maps = []
    for c in range(NCORES):
        b, k = c // per_seq, c % per_seq
        m = dict(common)
        xe = np.zeros((XR, D_), np.float32)
        lo = k * TOK - TOK
        src_lo = max(lo, 0)
        xe[src_lo - lo:] = x[b, src_lo:(k + 1) * TOK]
        m["x"] = xe
        first = (k == 0)
        flags = ((1.0, 1.0), (0.0, 0.0) if first else (1.0, 1.0), (0.0, 1.0) if first else (1.0, 1.0))
        for ci in range(3):
            m["fl%d" % ci] = np.tile(np.array([flags[ci]], np.float32), (128, 1))
            nown, off = B_CALLS[ci]
            tk = (k, ci)
            if tk not in tab_cache:
                tab_cache[tk] = rope_tables_core(k * TOK + off, nown)
            tb = tab_cache[tk]
            for g in range(3):
                m["cq%d_%d" % (ci, g)], m["sq%d_%d" % (ci, g)], m["ck%d_%d" % (ci, g)], m["sk%d_%d" % (ci, g)] = tb[g]
        in_maps.append(m)
    nc = prog_fused()
    res = run_bass_kernel_spmd(nc, in_maps, core_ids=list(range(NCORES)))
    out = np.stack([r["y"] for r in res.results]).reshape(Bn, S_, D_)
    return out.astype(np.float32)
```
